# Optimizing a Trainium2 kernel written in Bass

```python
import jax, jax.numpy as jnp
from jax import lax
import numpy as np

D_MODEL = 2048
BATCH = 4
SEQ = 4096
DEPTH = 4

FNET_GROUPS = 4
FNET_GROUP_DIM = D_MODEL // 8
FNET_WIDTH = FNET_GROUPS * FNET_GROUP_DIM
GLA_HEADS = 4
GLA_KEY_DIM = D_MODEL // 16
GLA_VAL_DIM = D_MODEL // 8
GLA_QK = GLA_HEADS * GLA_KEY_DIM
GLA_V = GLA_HEADS * GLA_VAL_DIM
GLA_GATE_RANK = 16
GLA_GATE_TEMP = 16.0
GLA_CHUNK = 64
EVEN_SPLITS = (FNET_WIDTH, GLA_QK, GLA_QK, GLA_V, GLA_V, GLA_GATE_RANK, GLA_GATE_RANK)
EVEN_IN_WIDTH = sum(EVEN_SPLITS)
EVEN_MIX_WIDTH = FNET_WIDTH + GLA_V
ATT_HEADS = 16
ATT_HEAD_DIM = D_MODEL // 16
ATT_WIDTH = ATT_HEADS * ATT_HEAD_DIM
DILATED_GROUPS = ((128, 1), (512, 4), (2048, 16))
ATT_BLOCK = 128
REL_BUCKETS = 32
REL_MAX_DISTANCE = 1024
D_FF = ((8 * D_MODEL // 3 + 255) // 256) * 256
CONV_WIDTH = 3
EPS = 1e-6

kernel_name = "hybrid_fourier_gla_dilated_encoder"


def rms_norm(x, g):
    xf = x.astype(jnp.float32)
    y = xf * lax.rsqrt(jnp.mean(xf * xf, axis=-1, keepdims=True) + EPS)
    return (y * g.astype(jnp.float32)).astype(x.dtype)


def t5_bucket(rel):
    nb = REL_BUCKETS // 2
    ret = (rel > 0).astype(np.int32) * nb
    n = np.abs(rel)
    max_exact = nb // 2
    large = max_exact + (np.log(np.maximum(n, 1) / max_exact)
                         / np.log(REL_MAX_DISTANCE / max_exact)
                         * (nb - max_exact)).astype(np.int32)
    large = np.minimum(large, nb - 1)
    return (ret + np.where(n < max_exact, n, large)).astype(np.int32)


def gla_scan(q, k, v, log_a):
    bsz, seq, heads, dk = q.shape
    dv = v.shape[-1]
    n_chunks = seq // GLA_CHUNK

    def to_chunks(t):
        return t.reshape(bsz, n_chunks, GLA_CHUNK, heads, t.shape[-1]).transpose(1, 0, 3, 2, 4)

    tri = jnp.tril(jnp.ones((GLA_CHUNK, GLA_CHUNK), dtype=bool))[:, :, None]

    def step(state, inp):
        qc, kc, vc, ac = inp
        b = jnp.cumsum(ac, axis=2)
        b_last = b[:, :, -1:, :]
        diff = b[:, :, :, None, :] - b[:, :, None, :, :]
        decay = jnp.exp(jnp.where(tri, diff, -jnp.inf))
        scores = jnp.einsum('bhtk,bhsk,bhtsk->bhts', qc, kc, decay)
        o = (jnp.einsum('bhts,bhsv->bhtv', scores, vc)
             + jnp.einsum('bhtk,bhkv->bhtv', qc * jnp.exp(b), state))
        new_state = (jnp.exp(b_last[:, :, 0, :])[..., None] * state
                     + jnp.einsum('bhsk,bhsv->bhkv', kc * jnp.exp(b_last - b), vc))
        return new_state, o

    state0 = jnp.zeros((bsz, heads, dk, dv), jnp.float32)
    _, o = lax.scan(step, state0, (to_chunks(q), to_chunks(k), to_chunks(v), to_chunks(log_a)))
    return o.transpose(1, 0, 3, 2, 4).reshape(bsz, seq, heads, dv)


def fourier_gla_mixer(xn, w_in, gate_up_f, gate_bias_f, gate_up_b, gate_bias_b, gla_g, w_out):
    bsz, seq, _ = xn.shape
    proj = xn @ w_in
    a, q, k, v, g, zf, zb = jnp.split(proj, np.cumsum(EVEN_SPLITS)[:-1].tolist(), axis=-1)
    a = a.reshape(bsz, seq, FNET_GROUPS, FNET_GROUP_DIM).astype(jnp.float32)
    a = jnp.fft.fft2(a, axes=(1, 3), norm='ortho').real.reshape(bsz, seq, FNET_WIDTH)
    q = q.reshape(bsz, seq, GLA_HEADS, GLA_KEY_DIM).astype(jnp.float32) * (GLA_KEY_DIM ** -0.5)
    k = k.reshape(bsz, seq, GLA_HEADS, GLA_KEY_DIM).astype(jnp.float32)
    v = v.reshape(bsz, seq, GLA_HEADS, GLA_VAL_DIM).astype(jnp.float32)
    log_af = (jax.nn.log_sigmoid((zf @ gate_up_f + gate_bias_f).astype(jnp.float32))
              / GLA_GATE_TEMP).reshape(bsz, seq, GLA_HEADS, GLA_KEY_DIM)
    log_ab = (jax.nn.log_sigmoid((zb @ gate_up_b + gate_bias_b).astype(jnp.float32))
              / GLA_GATE_TEMP).reshape(bsz, seq, GLA_HEADS, GLA_KEY_DIM)
    flip = lambda t: jnp.flip(t, axis=1)
    o = gla_scan(q, k, v, log_af) + flip(gla_scan(flip(q), flip(k), flip(v), flip(log_ab)))
    gate = jax.nn.silu(g.reshape(bsz, seq, GLA_HEADS, GLA_VAL_DIM).astype(jnp.float32))
    o = (rms_norm(o, gla_g) * gate).reshape(bsz, seq, GLA_V)
    mixed = jnp.concatenate([a, o], axis=-1).astype(xn.dtype)
    return mixed @ w_out


def dilated_branch(q, k, v, rel_bias, window, dilation):
    bsz, seq, heads, hd = q.shape
    half = window // (2 * dilation)
    sub_len = seq // dilation
    n_blk = -(-sub_len // ATT_BLOCK)
    pad_len = n_blk * ATT_BLOCK
    kb_len = ATT_BLOCK + 2 * half

    def to_sub(t):
        return t.reshape(bsz, sub_len, dilation, heads, hd).transpose(0, 2, 1, 3, 4)

    qs = jnp.pad(to_sub(q), ((0, 0), (0, 0), (0, pad_len - sub_len), (0, 0), (0, 0)))
    qs = qs.reshape(bsz, dilation, n_blk, ATT_BLOCK, heads, hd)
    key_idx = np.arange(n_blk)[:, None] * ATT_BLOCK + np.arange(kb_len)[None, :]
    kv_pad = ((0, 0), (0, 0), (half, pad_len - sub_len + half), (0, 0), (0, 0))
    ks = jnp.pad(to_sub(k), kv_pad)[:, :, key_idx]
    vs = jnp.pad(to_sub(v), kv_pad)[:, :, key_idx]

    rel = np.arange(kb_len)[None, :] - half - np.arange(ATT_BLOCK)[:, None]
    key_pos = key_idx - half
    valid = ((np.abs(rel) <= half)[None]
             & ((key_pos >= 0) & (key_pos < sub_len))[:, None, :])
    bias = rel_bias[t5_bucket(rel * dilation)].astype(jnp.float32).transpose(2, 0, 1)

    s = jnp.einsum('brnqhc,brnkhc->brnhqk', qs, ks) + bias
    s = jnp.where(valid[:, None], s, -jnp.inf)
    m = jnp.max(s, axis=-1)
    m = jnp.where(jnp.isfinite(m), m, 0.0)
    p = jnp.exp(s - m[..., None])
    l = jnp.sum(p, axis=-1)
    num = jnp.einsum('brnhqk,brnkhc->brnqhc', p, vs)

    def from_sub(t):
        t = t.reshape(bsz, dilation, pad_len, *t.shape[4:])[:, :, :sub_len]
        t = jnp.swapaxes(t, 1, 2)
        return t.reshape(bsz, seq, *t.shape[3:])

    return (from_sub(jnp.swapaxes(m, 3, 4)), from_sub(jnp.swapaxes(l, 3, 4)), from_sub(num))


def dilated_mixer(xn, w_qkv, q_norm_g, k_norm_g, rel_bias, w_out):
    bsz, seq, _ = xn.shape
    q, k, v = jnp.split(xn @ w_qkv, 3, axis=-1)
    shp = (bsz, seq, ATT_HEADS, ATT_HEAD_DIM)
    q = rms_norm(q.reshape(shp), q_norm_g).astype(jnp.float32) * (ATT_HEAD_DIM ** -0.5)
    k = rms_norm(k.reshape(shp), k_norm_g).astype(jnp.float32)
    v = v.reshape(shp).astype(jnp.float32)
    branches = [dilated_branch(q, k, v, rel_bias, w, d) for (w, d) in DILATED_GROUPS]
    ms = jnp.stack([br[0] for br in branches])
    ls = jnp.stack([br[1] for br in branches])
    nums = jnp.stack([br[2] for br in branches])
    wts = jnp.exp(ms - jnp.max(ms, axis=0, keepdims=True))
    out = jnp.sum(wts[..., None] * nums, axis=0) / jnp.sum(wts * ls, axis=0)[..., None]
    return out.reshape(bsz, seq, ATT_WIDTH).astype(xn.dtype) @ w_out


def conv_ffn(xn, w_gate, w_up, conv_w, conv_b, w_down):
    g = xn @ w_gate
    g = lax.conv_general_dilated(g, conv_w[:, None, :], window_strides=(1,),
                                 padding=[((CONV_WIDTH - 1) // 2, (CONV_WIDTH - 1) // 2)],
                                 dimension_numbers=('NWC', 'WIO', 'NWC'),
                                 feature_group_count=D_FF) + conv_b
    h = jax.nn.gelu(g) * (xn @ w_up)
    return h @ w_down


def setup_inputs(seed: int = 0) -> dict:
    key = jax.random.key(seed)
    ks = jax.random.split(key, 24)
    ne = (DEPTH + 1) // 2
    no = DEPTH // 2
    nrm = lambda k, shape, scale: jax.random.normal(k, shape, jnp.float32) * scale
    return {
        "x": nrm(ks[0], (BATCH, SEQ, D_MODEL), 1.0),
        "mix_norm_g": 1.0 + nrm(ks[1], (DEPTH, D_MODEL), 0.01),
        "w_in_even": nrm(ks[2], (ne, D_MODEL, EVEN_IN_WIDTH), D_MODEL ** -0.5),
        "gate_up_fwd": nrm(ks[3], (ne, GLA_GATE_RANK, GLA_QK), GLA_GATE_RANK ** -0.5),
        "gate_bias_fwd": nrm(ks[4], (ne, GLA_QK), 0.1),
        "gate_up_bwd": nrm(ks[5], (ne, GLA_GATE_RANK, GLA_QK), GLA_GATE_RANK ** -0.5),
        "gate_bias_bwd": nrm(ks[6], (ne, GLA_QK), 0.1),
        "gla_norm_g": 1.0 + nrm(ks[7], (ne, GLA_VAL_DIM), 0.01),
        "w_out_even": nrm(ks[8], (ne, EVEN_MIX_WIDTH, D_MODEL), EVEN_MIX_WIDTH ** -0.5),
        "w_qkv_odd": nrm(ks[9], (no, D_MODEL, 3 * ATT_WIDTH), D_MODEL ** -0.5),
        "q_norm_g": 1.0 + nrm(ks[10], (no, ATT_HEAD_DIM), 0.01),
        "k_norm_g": 1.0 + nrm(ks[11], (no, ATT_HEAD_DIM), 0.01),
        "rel_bias": nrm(ks[12], (REL_BUCKETS, ATT_HEADS), 0.1),
        "w_out_odd": nrm(ks[13], (no, ATT_WIDTH, D_MODEL), ATT_WIDTH ** -0.5),
        "ffn_norm_g": 1.0 + nrm(ks[14], (DEPTH, D_MODEL), 0.01),
        "w_gate": nrm(ks[15], (DEPTH, D_MODEL, D_FF), D_MODEL ** -0.5),
        "w_up": nrm(ks[16], (DEPTH, D_MODEL, D_FF), D_MODEL ** -0.5),
        "conv_w": nrm(ks[17], (DEPTH, CONV_WIDTH, D_FF), CONV_WIDTH ** -0.5),
        "conv_b": nrm(ks[18], (DEPTH, D_FF), 0.01),
        "w_down": nrm(ks[19], (DEPTH, D_FF, D_MODEL), D_FF ** -0.5),
    }


def reference(x, mix_norm_g, w_in_even, gate_up_fwd, gate_bias_fwd, gate_up_bwd, gate_bias_bwd,
              gla_norm_g, w_out_even, w_qkv_odd, q_norm_g, k_norm_g, rel_bias, w_out_odd,
              ffn_norm_g, w_gate, w_up, conv_w, conv_b, w_down):
    h = x
    for layer in range(DEPTH):
        i = layer // 2
        xn = rms_norm(h, mix_norm_g[layer])
        if layer % 2 == 0:
            h = h + fourier_gla_mixer(xn, w_in_even[i], gate_up_fwd[i], gate_bias_fwd[i],
                                      gate_up_bwd[i], gate_bias_bwd[i], gla_norm_g[i], w_out_even[i])
        else:
            h = h + dilated_mixer(xn, w_qkv_odd[i], q_norm_g[i], k_norm_g[i], rel_bias, w_out_odd[i])
        h = h + conv_ffn(rms_norm(h, ffn_norm_g[layer]), w_gate[layer], w_up[layer],
                         conv_w[layer], conv_b[layer], w_down[layer])
    return h
```

```python
import contextlib
import numpy as np
import ml_dtypes
import concourse.bass as bass
import concourse.mybir as mybir
from concourse.bass_utils import run_bass_kernel_spmd

F32 = mybir.dt.float32
BF16 = mybir.dt.bfloat16
AF = mybir.ActivationFunctionType
ALU = mybir.AluOpType
NPBF = ml_dtypes.bfloat16

D = 2048
KT = D // 128
SEQ = 4096
BATCH = 4
TOK = 2048
EXT = TOK + 2
FF = 5632
FT = FF // 128
EPS = 1e-6
NCORES = 8


DBG = {}


class Buf:
    __slots__ = ("name", "w", "r", "multi")

    def __init__(self, name=""):
        self.name = name
        self.w = None
        self.r = []
        self.multi = None


class Trk:
    NDS = 20

    def __init__(self, nc, stack):
        self.nc = nc
        self.E = {"pe": nc.tensor, "act": nc.scalar, "dve": nc.vector, "pool": nc.gpsimd, "sp": nc.sync}
        self.sem = {k: stack.enter_context(nc.semaphore("s_" + k)) for k in self.E}
        self.cnt = {k: 0 for k in self.E}
        self.waited = {}
        self.dsem = {q: [stack.enter_context(nc.semaphore("d_%s%d" % (q, i))) for i in range(self.NDS)]
                     for q in ("sp", "pool")}
        self.dtot = {q: [0] * self.NDS for q in ("sp", "pool")}
        self.dnext = {"sp": 0, "pool": 0}
        self.n_instr = 0
        self.stack = stack
        self.ccsem = stack.enter_context(nc.semaphore("s_cc"))
        self.cccnt = 0
        self.epoch = 0

    def new_epoch(self):
        self.epoch += 1
        self.sem = {k: self.stack.enter_context(self.nc.semaphore("s%d_%s" % (self.epoch, k))) for k in self.E}
        self.cnt = {k: 0 for k in self.E}

    def collective(self, in_ap, out_ap, groups, reads=(), writes=(), extra=()):
        if DBG.get("no_cc"):
            return None
        self._sync("pool", reads, writes)
        for t in extra:
            self._wait("pool", t)
        if self.cccnt > 0:
            self._wait("pool", (self.ccsem, self.cccnt, "cc"))
        self.nc.gpsimd.collective_compute("AllGather", ALU.bypass, replica_groups=groups, ins=[in_ap], outs=[out_ap]
                                          ).then_inc(self.ccsem)
        self.cccnt += 1
        tok = (self.ccsem, self.cccnt, "cc")
        self._update(tok, reads, writes)
        return tok

    def _wait(self, eng, tok):
        sem, val, src = tok
        if src == eng and eng == "pe":
            return
        key = (eng, id(sem))
        if self.waited.get(key, 0) >= val:
            return
        self.waited[key] = val
        self.E[eng].wait_ge(sem, val)

    def _sync(self, eng, reads, writes):
        for b in reads:
            if b.w is not None:
                self._wait(eng, b.w)
            if b.multi:
                for t in b.multi:
                    self._wait(eng, t)
        for b in writes:
            if b.w is not None:
                self._wait(eng, b.w)
            if b.multi:
                for t in b.multi:
                    self._wait(eng, t)
            for t in b.r:
                self._wait(eng, t)

    def _update(self, tok, reads, writes):
        for b in reads:
            b.r.append(tok)
        for b in writes:
            b.w = tok
            b.r = []
            b.multi = None

    def op(self, eng, fn, reads=(), writes=(), inc=True):
        self._sync(eng, reads, writes)
        ins = fn(self.E[eng])
        self.n_instr += 1
        if inc:
            self.cnt[eng] += 1
            ins.then_inc(self.sem[eng], 1)
            tok = (self.sem[eng], self.cnt[eng], eng)
        else:
            tok = (self.sem[eng], self.cnt[eng] + 1, eng)
        self._update(tok, reads, writes)
        return tok

    def dma(self, q, out, in_, reads=(), writes=(), slow=False):
        self._sync(q, reads, writes)
        i = self.dnext[q]
        self.dnext[q] = (i + 1) % self.NDS
        sem = self.dsem[q][i]
        if self.dtot[q][i] > 0:
            self._wait(q, (sem, self.dtot[q][i], "dma"))
        if slow:
            self.E[q].dma_start(out=out, in_=in_, allow_slow_non_contiguous=True).then_inc(sem, 16)
        else:
            self.E[q].dma_start(out=out, in_=in_).then_inc(sem, 16)
        self.n_instr += 1
        self.dtot[q][i] += 16
        tok = (sem, self.dtot[q][i], "dma")
        self._update(tok, reads, writes)
        return tok

    def barrier(self):
        toks = []
        for q in ("sp", "pool"):
            for i in range(self.NDS):
                if self.dtot[q][i] > 0:
                    toks.append((self.dsem[q][i], self.dtot[q][i], "dma"))
        for e in ("pe", "act", "dve"):
            if self.cnt[e] > 0:
                toks.append((self.sem[e], self.cnt[e], e))
        if self.cccnt > 0:
            toks.append((self.ccsem, self.cccnt, "cc"))
        for eng in self.E:
            for t in toks:
                if t[2] == eng:
                    continue
                self._wait(eng, t)

    def finish(self):
        for q in ("sp", "pool"):
            for i in range(self.NDS):
                if self.dtot[q][i] > 0:
                    self._wait("sp", (self.dsem[q][i], self.dtot[q][i], "dma"))
        if self.cccnt > 0:
            self._wait("sp", (self.ccsem, self.cccnt, "cc"))
        for e in ("pe", "act", "dve"):
            if self.cnt[e] > 0:
                self._wait("sp", (self.sem[e], self.cnt[e], e))


_UNIQ = [0]


def uniq(name):
    _UNIQ[0] += 1
    return "%s_%d" % (name, _UNIQ[0])


def col_chunks(n, step=512):
    return [(c, min(c + step, n)) for c in range(0, n, step)]


def emit_post(nc, T, stack, ps, load_m, load_h, w_out, g2, w_gate, w_up, cw, cb, w_down, hT_out, hmid_scr, hm_scr,
              on_store=None):
    sb = lambda name, shape, dt: stack.enter_context(nc.sbuf_tensor(uniq(name), shape, dt))
    psb = [Buf("ps%d" % i) for i in range(8)]

    ones = sb("ones", [128, 128], F32)
    b_ones = Buf()
    T.op("dve", lambda e: e.memset(ones[:], 1.0), writes=[b_ones])
    g2s = sb("g2s", [128, KT], F32); b_g2 = Buf()
    T.dma("sp", g2s[:], g2, writes=[b_g2])
    cws = sb("cws", [128, FT * 3], F32); b_cw = Buf()
    T.dma("sp", cws[:], cw, writes=[b_cw])
    cbs = sb("cbs", [128, FT], F32); b_cb = Buf()
    T.dma("sp", cbs[:], cb, writes=[b_cb])
    rstd = sb("rstd", [128, EXT], F32); b_rstd = Buf()
    xn = sb("xn", [128, KT, EXT], BF16); b_xn = [Buf() for _ in range(KT)]

    chunks = col_chunks(EXT)
    hm_b = [Buf() for _ in range(KT)]

    with contextlib.ExitStack() as st2:
        sb2 = lambda name, shape, dt: st2.enter_context(nc.sbuf_tensor(uniq(name), shape, dt))
        mT = sb2("mT", [128, KT, EXT], BF16); b_mT = [Buf() for _ in range(KT // 4)]
        for k0 in range(0, KT, 4):
            load_m(T, sb2, mT, k0, b_mT[k0 // 4])
        wo = [sb2("wo%d" % i, [128, KT, 128], BF16) for i in range(2)]; b_wo = [Buf(), Buf()]
        hb = [sb2("hb%d" % i, [128, EXT], F32) for i in range(2)]; b_hb = [Buf(), Buf()]
        hm = [sb2("hm%d" % i, [128, EXT], F32) for i in range(2)]; b_hm = [Buf(), Buf()]
        sq = [sb2("sq%d" % i, [128, 512], F32) for i in range(2)]; b_sq = [Buf(), Buf()]
        wv = w_out.rearrange("(k p) n -> p k n", p=128)

        def load(n):
            T.dma("pool", wo[n % 2][:], wv[:, :, n * 128:(n + 1) * 128], writes=[b_wo[n % 2]])
            load_h(T, sb2, hb[n % 2], n, b_hb[n % 2])

        load(0)
        pend = None
        it = 0
        for n in range(KT):
            if n + 1 < KT:
                load(n + 1)
            for j, (c0, c1) in enumerate(chunks):
                w = c1 - c0
                pi = it % 2
                for k in range(KT):
                    T.op("pe", lambda e, k=k: e.matmul(ps[pi][:, :w], wo[n % 2][:, k, :], mT[:, k, c0:c1],
                                                       start=(k == 0), stop=(k == KT - 1)),
                         reads=[b_wo[n % 2], b_mT[k // 4]], writes=[psb[pi]], inc=(k == KT - 1))
                T.op("dve", lambda e: e.tensor_tensor(out=hm[n % 2][:, c0:c1], in0=ps[pi][:, :w],
                                                      in1=hb[n % 2][:, c0:c1], op=ALU.add),
                     reads=[psb[pi], b_hb[n % 2]], writes=[b_hm[n % 2]])
                T.op("act", lambda e: e.activation(out=sq[pi][:, :w], in_=hm[n % 2][:, c0:c1], func=AF.Square),
                     reads=[b_hm[n % 2]], writes=[b_sq[pi]])
                if pend is not None:
                    pend()

                def mk(n=n, j=j, pi=pi, w=w):
                    T.op("pe", lambda e: e.matmul(ps[2 + j][:, :w], ones[:], sq[pi][:, :w],
                                                  start=(n == 0), stop=(n == KT - 1)),
                         reads=[b_ones, b_sq[pi]], writes=[psb[2 + j]])
                pend = mk
                it += 1
            T.dma("sp", hm_scr[n * 128:(n + 1) * 128, :], hm[n % 2][:], reads=[b_hm[n % 2]], writes=[hm_b[n]])
        pend()
        for j, (c0, c1) in enumerate(chunks):
            w = c1 - c0
            T.op("act", lambda e: e.activation(out=rstd[:, c0:c1], in_=ps[2 + j][:, :w], func=AF.Sqrt,
                                               scale=1.0 / D, bias=EPS),
                 reads=[psb[2 + j]], writes=[b_rstd])
        T.op("dve", lambda e: e.reciprocal(out=rstd[:], in_=rstd[:]), reads=[b_rstd], writes=[b_rstd])

        for n in range(KT):
            T.dma("sp", hb[n % 2][:], hm_scr[n * 128:(n + 1) * 128, :], reads=[hm_b[n]], writes=[b_hb[n % 2]])
            T.op("dve", lambda e: e.scalar_tensor_tensor(out=xn[:, n, :], in0=hb[n % 2][:], scalar=g2s[:, n:n + 1],
                                                         in1=rstd[:], op0=ALU.mult, op1=ALU.mult),
                 reads=[b_hb[n % 2], b_g2, b_rstd], writes=[b_xn[n]])

    T.barrier()
    hmid_b = [Buf() for _ in range(FT)]
    with contextlib.ExitStack() as st2:
        sb2 = lambda name, shape, dt: st2.enter_context(nc.sbuf_tensor(uniq(name), shape, dt))
        wg = [sb2("wg%d" % i, [128, KT, 128], BF16) for i in range(2)]; b_wg = [Buf(), Buf()]
        wu = [sb2("wu%d" % i, [128, KT, 128], BF16) for i in range(2)]; b_wu = [Buf(), Buf()]
        ge = [sb2("ge%d" % i, [128, EXT], F32) for i in range(2)]; b_ge = [Buf(), Buf()]
        u = [sb2("u%d" % i, [128, TOK], F32) for i in range(2)]; b_u = [Buf(), Buf()]
        gl = [sb2("gl%d" % i, [128, TOK], F32) for i in range(2)]; b_gl = [Buf(), Buf()]
        hf = [sb2("hf%d" % i, [128, TOK], BF16) for i in range(2)]; b_hf = [Buf(), Buf()]
        wgv = w_gate.rearrange("(k p) n -> p k n", p=128)
        wuv = w_up.rearrange("(k p) n -> p k n", p=128)

        def loadg(f):
            T.dma("pool", wg[f % 2][:], wgv[:, :, f * 128:(f + 1) * 128], writes=[b_wg[f % 2]])
            T.dma("pool", wu[f % 2][:], wuv[:, :, f * 128:(f + 1) * 128], writes=[b_wu[f % 2]])

        loadg(0)
        it = 0
        for f in range(FT):
            if f + 1 < FT:
                loadg(f + 1)
            s = f % 2
            for j, (c0, c1) in enumerate(chunks):
                w = c1 - c0
                pi = it % 4; it += 1
                for k in range(KT):
                    T.op("pe", lambda e, k=k: e.matmul(ps[pi][:, :w], wg[s][:, k, :], xn[:, k, c0:c1],
                                                       start=(k == 0), stop=(k == KT - 1)),
                         reads=[b_wg[s], b_xn[k]], writes=[psb[pi]], inc=(k == KT - 1))
                T.op("act", lambda e: e.activation(out=ge[s][:, c0:c1], in_=ps[pi][:, :w], func=AF.Copy),
                     reads=[psb[pi]], writes=[b_ge[s]])
            T.op("dve", lambda e: e.tensor_scalar(out=u[s][:], in0=ge[s][:, 1:1 + TOK],
                                                  scalar1=cws[:, 3 * f + 1:3 * f + 2], scalar2=cbs[:, f:f + 1],
                                                  op0=ALU.mult, op1=ALU.add),
                 reads=[b_ge[s], b_cw, b_cb], writes=[b_u[s]])
            T.op("dve", lambda e: e.scalar_tensor_tensor(out=u[s][:], in0=ge[s][:, 0:TOK],
                                                         scalar=cws[:, 3 * f:3 * f + 1], in1=u[s][:],
                                                         op0=ALU.mult, op1=ALU.add),
                 reads=[b_ge[s], b_cw, b_u[s]], writes=[b_u[s]])
            T.op("dve", lambda e: e.scalar_tensor_tensor(out=u[s][:], in0=ge[s][:, 2:2 + TOK],
                                                         scalar=cws[:, 3 * f + 2:3 * f + 3], in1=u[s][:],
                                                         op0=ALU.mult, op1=ALU.add),
                 reads=[b_ge[s], b_cw, b_u[s]], writes=[b_u[s]])
            T.op("act", lambda e: e.activation(out=gl[s][:], in_=u[s][:], func=AF.Gelu_apprx_tanh),
                 reads=[b_u[s]], writes=[b_gl[s]])
            for j in range(TOK // 512):
                c0 = 1 + 512 * j
                pi = 4 + (it % 4); it += 1
                for k in range(KT):
                    T.op("pe", lambda e, k=k: e.matmul(ps[pi][:, :], wu[s][:, k, :], xn[:, k, c0:c0 + 512],
                                                       start=(k == 0), stop=(k == KT - 1)),
                         reads=[b_wu[s], b_xn[k]], writes=[psb[pi]], inc=(k == KT - 1))
                T.op("dve", lambda e: e.tensor_tensor(out=hf[s][:, 512 * j:512 * (j + 1)], in0=ps[pi][:, :],
                                                      in1=gl[s][:, 512 * j:512 * (j + 1)], op=ALU.mult),
                     reads=[psb[pi], b_gl[s]], writes=[b_hf[s]])
            T.dma("sp", hmid_scr[f * 128:(f + 1) * 128, :], hf[s][:], reads=[b_hf[s]], writes=[hmid_b[f]])

    T.barrier()
    with contextlib.ExitStack() as st2:
        sb2 = lambda name, shape, dt: st2.enter_context(nc.sbuf_tensor(uniq(name), shape, dt))
        CH = 1024
        hms = sb2("hms", [128, FT, CH], BF16); b_hms = [Buf() for _ in range(4)]
        wd = [sb2("wd%d" % i, [128, FT, 128], BF16) for i in range(2)]; b_wd = [Buf(), Buf()]
        hr = [sb2("hr%d" % i, [128, CH], F32) for i in range(2)]; b_hr = [Buf(), Buf()]
        ot = [sb2("ot%d" % i, [128, CH], F32) for i in range(2)]; b_ot = [Buf(), Buf()]
        hv = hmid_scr.rearrange("(f p) t -> p f t", p=128)
        wdv = w_down.rearrange("(f p) n -> p f n", p=128)
        it = 0
        items = [(c, n) for c in range(TOK // CH) for n in range(KT)]

        def loadd(idx):
            c, n = items[idx]
            T.dma("pool", wd[idx % 2][:], wdv[:, :, n * 128:(n + 1) * 128], writes=[b_wd[idx % 2]])
            T.dma("sp", hr[idx % 2][:], hm_scr[n * 128:(n + 1) * 128, 1 + c * CH:1 + (c + 1) * CH],
                  reads=[hm_b[n]], writes=[b_hr[idx % 2]])

        loadd(0)
        for idx, (c, n) in enumerate(items):
            if n == 0:
                for f0 in range(0, FT, 11):
                    T.dma("sp", hms[:, f0:f0 + 11, :], hv[:, f0:f0 + 11, c * CH:(c + 1) * CH],
                          reads=hmid_b[f0:f0 + 11], writes=[b_hms[f0 // 11]])
            if idx + 1 < len(items):
                loadd(idx + 1)
            s = idx % 2
            for jj in range(CH // 512):
                pi = it % 4; it += 1
                for f in range(FT):
                    T.op("pe", lambda e, f=f: e.matmul(ps[pi][:, :], wd[s][:, f, :], hms[:, f, 512 * jj:512 * (jj + 1)],
                                                       start=(f == 0), stop=(f == FT - 1)),
                         reads=[b_wd[s], b_hms[f // 11]], writes=[psb[pi]], inc=(f == FT - 1))
                T.op("dve", lambda e: e.tensor_tensor(out=ot[s][:, 512 * jj:512 * (jj + 1)], in0=ps[pi][:, :],
                                                      in1=hr[s][:, 512 * jj:512 * (jj + 1)], op=ALU.add),
                     reads=[psb[pi], b_hr[s]], writes=[b_ot[s]])
            tk = T.dma("sp", hT_out[n * 128:(n + 1) * 128, c * CH:(c + 1) * CH], ot[s][:], reads=[b_ot[s]])
            if on_store is not None:
                on_store(n, c, TOK // CH, tk)


def build_post(debug=False):
    nc = bass.Bass("TRN2", target_bir_lowering=False)
    dt = lambda name, shape, dtype, kind: nc.dram_tensor(name, shape, dtype, kind=kind).ap()
    hT_ext = dt("hT_ext", [D, EXT], F32, "ExternalInput")
    mT_ext = dt("mT_ext", [D, EXT], BF16, "ExternalInput")
    w_out = dt("w_out", [D, D], F32, "ExternalInput")
    g2 = dt("g2", [128, KT], F32, "ExternalInput")
    w_gate = dt("w_gate", [D, FF], F32, "ExternalInput")
    w_up = dt("w_up", [D, FF], F32, "ExternalInput")
    cw = dt("cw", [128, FT * 3], F32, "ExternalInput")
    cb = dt("cb", [128, FT], F32, "ExternalInput")
    w_down = dt("w_down", [FF, D], F32, "ExternalInput")
    hT_out = dt("hT_out", [D, TOK], F32, "ExternalOutput")
    hmid_scr = dt("hmid_scr", [FF, TOK], BF16, "ExternalOutput" if debug else "Internal")
    hm_scr = dt("hm_scr", [D, EXT], F32, "ExternalOutput" if debug else "Internal")
    with contextlib.ExitStack() as stack:
        T = Trk(nc, stack)
        ps = [stack.enter_context(nc.psum_tensor("ps%d" % i, [128, 512], F32)) for i in range(8)]
        mv = mT_ext.rearrange("(k p) t -> p k t", p=128)
        load_m = lambda T, sb2, mT, k0, b: T.dma("sp", mT[:, k0:k0 + 4, :], mv[:, k0:k0 + 4, :], writes=[b])
        load_h = lambda T, sb2, hb, n, b: T.dma("sp", hb[:], hT_ext[n * 128:(n + 1) * 128, :], writes=[b])
        emit_post(nc, T, stack, ps, load_m, load_h, w_out, g2, w_gate, w_up, cw, cb, w_down, hT_out, hmid_scr, hm_scr)
        T.finish()
    return nc


def lay128(v):
    v = np.asarray(v)
    return np.ascontiguousarray(v.reshape(-1, 128).T)


class Rot:
    def __init__(self, tiles):
        self.tiles = tiles
        self.bufs = [Buf() for _ in tiles]
        self.i = 0

    def next(self):
        i = self.i
        self.i = (i + 1) % len(self.tiles)
        return self.tiles[i], self.bufs[i]


def mm_group(T, out_ap, pairs, pbuf, reads):
    n = len(pairs)
    for i, (l, r) in enumerate(pairs):
        T.op("pe", lambda e: e.matmul(out_ap, l, r, start=(i == 0), stop=(i == n - 1)),
             reads=reads, writes=[pbuf], inc=(i == n - 1))


def emit_norm_chunk(nc, T, hc, b_hc, g1s, b_g1, ones, b_ones, PS, sqr, rstd_t, b_rstd, xc, b_xc, width):
    pt, pb = PS.next()
    for k in range(KT):
        sq, b_sq = sqr.next()
        T.op("act", lambda e: e.activation(out=sq[:, :width], in_=hc[:, k, :width], func=AF.Square),
             reads=[b_hc], writes=[b_sq])
        T.op("pe", lambda e: e.matmul(pt[:, :width], ones[:], sq[:, :width], start=(k == 0), stop=(k == KT - 1)),
             reads=[b_ones, b_sq], writes=[pb])
    T.op("act", lambda e: e.activation(out=rstd_t[:, :width], in_=pt[:, :width], func=AF.Sqrt, scale=1.0 / D, bias=EPS),
         reads=[pb], writes=[b_rstd])
    T.op("dve", lambda e: e.reciprocal(out=rstd_t[:, :width], in_=rstd_t[:, :width]), reads=[b_rstd], writes=[b_rstd])
    for k in range(KT):
        T.op("dve", lambda e: e.scalar_tensor_tensor(out=xc[:, k, :width], in0=hc[:, k, :width], scalar=g1s[:, k:k + 1],
                                                     in1=rstd_t[:, :width], op0=ALU.mult, op1=ALU.mult),
             reads=[b_hc, b_g1, b_rstd], writes=[b_xc])


NW_E = 2080
GLA_SCALE = 128 ** -0.5


def emit_even(nc, T, stack, ps, hload, g1, w_e, gua_f, gua_b, glag, cs, cosm, nsinm, mf, mb, mixT, scr, on_rows=None):
    sb = lambda name, shape, dt: stack.enter_context(nc.sbuf_tensor(uniq(name), shape, dt))
    PS = Rot(ps)
    aT_s, qT_s, kT_s, k_s, v_s, sgT_s, la_s = scr
    NCH = SEQ // 512

    ones = sb("ones", [128, 128], F32); b_ones = Buf()
    T.op("dve", lambda e: e.memset(ones[:], 1.0), writes=[b_ones])
    g1s = sb("g1s", [128, KT], F32); b_g1 = Buf()
    T.dma("sp", g1s[:], g1, writes=[b_g1])
    glags = sb("glags", [128, 2], F32); b_glag = Buf()
    T.dma("sp", glags[:], glag, writes=[b_glag])
    mfs = sb("mfs", [128, 128], BF16); b_mf = Buf()
    T.dma("sp", mfs[:], mf, writes=[b_mf])
    mbs = sb("mbs", [128, 128], BF16); b_mb = Buf()
    T.dma("sp", mbs[:], mb, writes=[b_mb])

    with contextlib.ExitStack() as st2:
        sb2 = lambda name, shape, dt: st2.enter_context(nc.sbuf_tensor(uniq(name), shape, dt))
        we = sb2("we", [128, KT, NW_E], BF16); b_we = [Buf() for _ in range(4)]
        wv = w_e.rearrange("(k p) n -> p k n", p=128)
        for k0 in range(0, KT, 4):
            T.dma("pool", we[:, k0:k0 + 4, :], wv[:, k0:k0 + 4, :], writes=[b_we[k0 // 4]])
        guaf = sb2("guaf", [17, 256], BF16); b_guaf = Buf()
        T.dma("pool", guaf[:], gua_f, writes=[b_guaf])
        guab = sb2("guab", [17, 256], BF16); b_guab = Buf()
        T.dma("pool", guab[:], gua_b, writes=[b_guab])
        zfa = sb2("zfa", [17, 512], BF16); b_zfa = Buf()
        zba = sb2("zba", [17, 512], BF16); b_zba = Buf()
        T.op("dve", lambda e: e.memset(zfa[:], 1.0), writes=[b_zfa])
        T.op("dve", lambda e: e.memset(zba[:], 1.0), writes=[b_zba])
        hcs = [sb2("hc%d" % i, [128, KT, 512], F32) for i in range(2)]; b_hcs = [Buf(), Buf()]
        xc = sb2("xc", [128, KT, 512], BF16); b_xc = Buf()
        sqr = Rot([sb2("sq%d" % i, [128, 512], F32) for i in range(2)])
        rstd_t = sb2("rstd", [128, 512], F32); b_rstd = Buf()
        stg = Rot([sb2("stg%d" % i, [128, 512], BF16) for i in range(4)])
        lt = Rot([sb2("lt%d" % i, [128, 256], F32) for i in range(2)])
        hload(T, hcs[0], 0, b_hcs[0])
        we_reads = list(b_we)
        for c in range(NCH):
            t0 = c * 512
            hc, b_hc = hcs[c % 2], b_hcs[c % 2]
            if c + 1 < NCH:
                hload(T, hcs[(c + 1) % 2], c + 1, b_hcs[(c + 1) % 2])
            emit_norm_chunk(nc, T, hc, b_hc, g1s, b_g1, ones, b_ones, PS, sqr, rstd_t, b_rstd, xc, b_xc, 512)

            def fm(col0, dst, row0, func=AF.Copy, scale=1.0):
                pt, pb = PS.next()
                mm_group(T, pt[:, :], [(we[:, k, col0:col0 + 128], xc[:, k, :]) for k in range(KT)], pb,
                         reads=we_reads + [b_xc])
                s, b_s = stg.next()
                T.op("act", lambda e: e.activation(out=s[:, :], in_=pt[:, :], func=func, scale=scale),
                     reads=[pb], writes=[b_s])
                T.dma("sp", dst[row0:row0 + 128, t0:t0 + 512], s[:, :], reads=[b_s])
            for i in range(4):
                fm(i * 128, aT_s, i * 128)
            for i in range(2):
                fm(512 + i * 128, qT_s, i * 128, scale=GLA_SCALE)
            for i in range(2):
                fm(768 + i * 128, kT_s, i * 128)
            for i in range(4):
                fm(1536 + i * 128, sgT_s, i * 128, func=AF.Silu)
            for (col0, za, b_za) in ((2048, zfa, b_zfa), (2064, zba, b_zba)):
                pt, pb = PS.next()
                mm_group(T, pt[0:16, :], [(we[:, k, col0:col0 + 16], xc[:, k, :]) for k in range(KT)], pb,
                         reads=we_reads + [b_xc])
                T.op("act", lambda e: e.activation(out=za[0:16, :], in_=pt[0:16, :], func=AF.Copy), reads=[pb], writes=[b_za])
            for ts in range(4):
                r0 = t0 + ts * 128
                for (col0, ncol, dst) in ((768, 256, k_s), (1024, 512, v_s)):
                    pt, pb = PS.next()
                    mm_group(T, pt[:, :ncol], [(xc[:, k, ts * 128:(ts + 1) * 128], we[:, k, col0:col0 + ncol])
                                               for k in range(KT)], pb, reads=we_reads + [b_xc])
                    s, b_s = stg.next()
                    T.op("dve", lambda e: e.tensor_copy(out=s[:, :ncol], in_=pt[:, :ncol]), reads=[pb], writes=[b_s])
                    T.dma("sp", dst[r0:r0 + 128, :], s[:, :ncol], reads=[b_s])
                for d, (za, b_za, gu, b_gu) in enumerate(((zfa, b_zfa, guaf, b_guaf), (zba, b_zba, guab, b_guab))):
                    pt, pb = PS.next()
                    T.op("pe", lambda e: e.matmul(pt[:, :256], za[0:17, ts * 128:(ts + 1) * 128], gu[0:17, :],
                                                  start=True, stop=True), reads=[b_za, b_gu], writes=[pb])
                    l, b_l = lt.next()
                    T.op("act", lambda e: e.activation(out=l[:, :], in_=pt[:, :256], func=AF.Exp, scale=-1.0),
                         reads=[pb], writes=[b_l])
                    T.op("act", lambda e: e.activation(out=l[:, :], in_=l[:, :], func=AF.Ln, bias=1.0),
                         reads=[b_l], writes=[b_l])
                    s, b_s = stg.next()
                    T.op("dve", lambda e: e.tensor_scalar(out=s[:, :256], in0=l[:, :], scalar1=-1.0 / 16.0, scalar2=None,
                                                          op0=ALU.mult), reads=[b_l], writes=[b_s])
                    T.dma("sp", la_s[d, r0:r0 + 128, :], s[:, :256], reads=[b_s])
    T.barrier()

    with contextlib.ExitStack() as st2:
        sb2 = lambda name, shape, dt: st2.enter_context(nc.sbuf_tensor(uniq(name), shape, dt))
        aT = sb2("aT", [128, 4, SEQ], BF16); b_aT = Buf()
        T.dma("sp", aT[:], aT_s.rearrange("(k p) t -> p k t", p=128), writes=[b_aT])
        css = sb2("css", [128, 2, 512], BF16); b_cs = Buf()
        T.dma("sp", css[:], cs.rearrange("(k p) n -> p k n", p=128), writes=[b_cs])
        Z = sb2("Z", [128, 32, 2, 512], BF16); b_Z = Buf()
        for g in range(2):
            for st in range(32):
                pt, pb = PS.next()
                mm_group(T, pt[:, :], [(aT[:, 2 * g + kk, st * 128:(st + 1) * 128], css[:, kk, :]) for kk in range(2)],
                         pb, reads=[b_aT, b_cs])
                if st % 2 == 0:
                    T.op("act", lambda e: e.activation(out=Z[:, st, g, :], in_=pt[:, :], func=AF.Copy),
                         reads=[pb], writes=[b_Z])
                else:
                    T.op("dve", lambda e: e.tensor_copy(out=Z[:, st, g, :], in_=pt[:, :]), reads=[pb], writes=[b_Z])
        SG = 4
        if DBG.get("skip_fnet2"):
            SEQ_ = 0
        else:
            SEQ_ = SEQ
        cbuf = Rot([sb2("cb%d" % i, [128, 2, SG, 512], BF16) for i in range(3)])
        stg = Rot([sb2("stg%d" % i, [128, 512], BF16) for i in range(4)])
        cv = cosm.rearrange("(st p) n -> p st n", p=128)
        sv = nsinm.rearrange("(st p) n -> p st n", p=128)
        pieces = [(sp_, sg) for sp_ in range(SEQ_ // 512) for sg in range(32 // SG)]

        def loadc(idx):
            sp_, sg = pieces[idx]
            t, b = cbuf.next()
            T.dma("sp", t[:, 0, :, :], cv[:, sg * SG:(sg + 1) * SG, sp_ * 512:(sp_ + 1) * 512], writes=[b])
            T.dma("sp", t[:, 1, :, :], sv[:, sg * SG:(sg + 1) * SG, sp_ * 512:(sp_ + 1) * 512], writes=[b])
            return t, b
        q = [loadc(0), loadc(1)] if pieces else []
        for idx, (sp_, sg) in enumerate(pieces):
            if idx + 2 < len(pieces):
                q.append(loadc(idx + 2))
            t, b = q.pop(0)
            bank0 = (sp_ % 2) * 4
            for ct in range(4):
                g, hf = ct // 2, ct % 2
                for s_ in range(SG):
                    stile = sg * SG + s_
                    for part in range(2):
                        first = (sg == 0 and s_ == 0 and part == 0)
                        last = (sg == 32 // SG - 1 and s_ == SG - 1 and part == 1)
                        T.op("pe", lambda e: e.matmul(ps[bank0 + ct][:, :],
                                                      Z[:, stile, g, part * 256 + hf * 128: part * 256 + hf * 128 + 128],
                                                      t[:, part, s_, :], start=first, stop=last),
                             reads=[b_Z, b], writes=[PS.bufs[bank0 + ct]], inc=last or (s_ == SG - 1 and part == 1 and ct == 3))
            if sg == 32 // SG - 1:
                for ct in range(4):
                    s, b_s = stg.next()
                    T.op("act" if ct % 2 == 0 else "dve",
                         (lambda e: e.activation(out=s[:, :], in_=ps[bank0 + ct][:, :], func=AF.Copy)) if ct % 2 == 0 else
                         (lambda e: e.tensor_copy(out=s[:, :], in_=ps[bank0 + ct][:, :])),
                         reads=[PS.bufs[bank0 + ct]], writes=[b_s])
                    T.dma("sp", mixT[ct * 128:(ct + 1) * 128, sp_ * 512:(sp_ + 1) * 512], s[:, :], reads=[b_s])
    T.barrier()
    if on_rows is not None:
        on_rows(0, []); on_rows(1, [])

    with contextlib.ExitStack() as st2:
        sb2 = lambda name, shape, dt: st2.enter_context(nc.sbuf_tensor(uniq(name), shape, dt))
        qT = sb2("qT", [128, SEQ], BF16); kT = sb2("kT", [128, SEQ], BF16)
        kk_ = sb2("ktok", [128, 32, 128], BF16); vv = sb2("vtok", [128, 32, 256], BF16)
        la = [sb2("la%d" % d, [128, 32, 128], BF16) for d in range(2)]
        sg_ = sb2("sgT", [128, 2, SEQ], BF16)
        oacc = sb2("oacc", [128, 2, SEQ], F32)
        b_in = Buf(); b_sg = Buf()
        b_o = [Buf() for _ in range(32)]
        S32 = [sb2("S32_%d" % d, [128, 256], F32) for d in range(2)]; b_S32 = [Buf(), Buf()]
        Sbf = [sb2("Sbf_%d" % d, [128, 256], BF16) for d in range(2)]; b_Sbf = [Buf(), Buf()]
        tmpS = [sb2("tmpS_%d" % d, [128, 256], F32) for d in range(2)]; b_tmpS = [Buf(), Buf()]
        mk_t = lambda nm, dt_, n=2: Rot([sb2("%s%d" % (nm, i), [128, 128], dt_) for i in range(n)])
        ebR, enbR, entR = mk_t("eb", F32, 4), mk_t("enb", F32), mk_t("ent", F32)
        qtR, ktR, ktokR, pR = mk_t("qt", BF16), mk_t("kt", BF16), mk_t("ktk", BF16), mk_t("pp", BF16)
        elR = Rot([sb2("el%d" % i, [128, 2], F32) for i in range(4)])
        sq2 = Rot([sb2("sqq%d" % i, [128, 512], F32) for i in range(2)])
        rs2 = sb2("rs2", [128, 512], F32); b_rs2 = Buf()
        tm2 = Rot([sb2("tm%d" % i, [128, 512], F32) for i in range(2)])
        stg = Rot([sb2("stg%d" % i, [128, 512], BF16) for i in range(2)])
        masks = ((mfs, b_mf), (mbs, b_mb))
        kmR = mk_t("km", BF16)
        NHEAD_ = 0 if DBG.get("skip_gla") else 2
        cmask = sb2("cmask", [128, 2], F32); b_cmask = Buf()
        T.op("dve", lambda e: e.memset(cmask[:], 0.0), writes=[b_cmask])
        T.op("dve", lambda e: e.memset(cmask[0:64, 0:1], 1.0), writes=[b_cmask])
        T.op("dve", lambda e: e.memset(cmask[64:128, 1:2], 1.0), writes=[b_cmask])
        for hh in range(NHEAD_):
            T.dma("sp", qT[:], qT_s[hh * 128:(hh + 1) * 128, :], writes=[b_in])
            T.dma("sp", kT[:], kT_s[hh * 128:(hh + 1) * 128, :], writes=[b_in])
            T.dma("sp", kk_[:], k_s.rearrange("(i p) c -> p i c", p=128)[:, :, hh * 128:(hh + 1) * 128], writes=[b_in])
            T.dma("sp", vv[:], v_s.rearrange("(i p) c -> p i c", p=128)[:, :, hh * 256:(hh + 1) * 256], writes=[b_in])
            for d in range(2):
                T.dma("sp", la[d][:], la_s[d].rearrange("(i p) c -> p i c", p=128)[:, :, hh * 128:(hh + 1) * 128],
                      writes=[b_in])
            T.dma("sp", sg_[:], sgT_s.rearrange("(k p) t -> p k t", p=128)[:, 2 * hh:2 * hh + 2, :], writes=[b_sg])
            for d in range(2):
                T.op("dve", lambda e: e.memset(S32[d][:], 0.0), writes=[b_S32[d]])
                T.op("dve", lambda e: e.memset(Sbf[d][:], 0.0), writes=[b_Sbf[d]])
            written = [False] * 32
            for step in range(32):
                ctx = []
                for d in range(2):
                    i = step if d == 0 else 31 - step
                    M, b_M = masks[d]
                    c0 = i * 128
                    pbT, bbT = PS.next()
                    T.op("pe", lambda e: e.matmul(pbT[:, :128], la[d][:, i, :], M[:, :], start=True, stop=True),
                         reads=[b_in, b_M], writes=[bbT])
                    pbk, bbk = PS.next()
                    T.op("pe", lambda e: e.matmul(pbk[:, :128], M[:, :], la[d][:, i, :], start=True, stop=True),
                         reads=[b_in, b_M], writes=[bbk])
                    eb, b_eb = ebR.next(); enb, b_enb = enbR.next(); ent, b_ent = entR.next()
                    T.op("act", lambda e: e.activation(out=eb[:, :], in_=pbT[:, :128], func=AF.Exp), reads=[bbT], writes=[b_eb])
                    T.op("act", lambda e: e.activation(out=enb[:, :], in_=pbT[:, :128], func=AF.Exp, scale=-1.0),
                         reads=[bbT], writes=[b_enb])
                    T.op("act", lambda e: e.activation(out=ent[:, :], in_=pbk[:, :128], func=AF.Exp, scale=-1.0),
                         reads=[bbk], writes=[b_ent])
                    qt, b_qt = qtR.next(); kt, b_kt = ktR.next(); ktok, b_ktok = ktokR.next()
                    T.op("dve", lambda e: e.tensor_tensor(out=qt[:, :], in0=qT[:, c0:c0 + 128], in1=eb[:, :], op=ALU.mult),
                         reads=[b_in, b_eb], writes=[b_qt])
                    T.op("dve", lambda e: e.tensor_tensor(out=kt[:, :], in0=kT[:, c0:c0 + 128], in1=enb[:, :], op=ALU.mult),
                         reads=[b_in, b_enb], writes=[b_kt])
                    T.op("dve", lambda e: e.tensor_tensor(out=ktok[:, :], in0=kk_[:, i, :], in1=ent[:, :], op=ALU.mult),
                         reads=[b_in, b_ent], writes=[b_ktok])
                    ecols = (63, 127) if d == 0 else (0, 64)
                    ctx.append((i, c0, M, b_M, qt, b_qt, kt, b_kt, ktok, b_ktok, (eb, ecols), b_eb))
                pps = []
                for d in range(2):
                    i, c0, M, b_M, qt, b_qt, kt, b_kt, ktok, b_ktok, el, b_el = ctx[d]
                    psc, bsc = PS.next()
                    T.op("pe", lambda e: e.matmul(psc[:, :128], kt[:, :], qt[:, :], start=True, stop=True),
                         reads=[b_kt, b_qt], writes=[bsc])
                    pp, b_pp = pR.next()
                    T.op("dve", lambda e: e.tensor_tensor(out=pp[:, :], in0=psc[:, :128], in1=M[:, :], op=ALU.mult),
                         reads=[bsc, b_M], writes=[b_pp])
                    pps.append((pp, b_pp))
                pos = []
                for d in range(2):
                    i = ctx[d][0]
                    po, bo = PS.next()
                    pp, b_pp = pps[d]
                    for vt in range(2):
                        T.op("pe", lambda e: e.matmul(po[:, vt * 128:(vt + 1) * 128], vv[:, i, vt * 128:(vt + 1) * 128], pp[:, :],
                                                      start=True, stop=True),
                             reads=[b_in, b_pp], writes=[bo], inc=False)
                    pos.append((po, bo))
                for half in range(2):
                    for d in range(2):
                        if DBG.get("no_inter"):
                            if half == 1:
                                po, bo = pos[d]
                                T.op("pe", lambda e: e.matmul(po[:, 256:320], Sbf[d][:, 0:128], ctx[d][4][:, 0:64], start=True, stop=True),
                                     reads=[b_Sbf[d]], writes=[bo])
                            continue
                        i, c0, M, b_M, qt, b_qt, kt, b_kt, ktok, b_ktok, el, b_el = ctx[d]
                        po, bo = pos[d]
                        ch = half if d == 0 else 1 - half
                        r0 = ch * 64
                        for vt in range(2):
                            T.op("pe", lambda e: e.matmul(po[:, 256 + vt * 128 + r0: 256 + vt * 128 + r0 + 64],
                                                          Sbf[d][:, vt * 128:(vt + 1) * 128], qt[:, r0:r0 + 64],
                                                          start=True, stop=True),
                                 reads=[b_Sbf[d], b_qt], writes=[bo], inc=(half == 1 and vt == 1))
                        pd, bd = PS.next()
                        T.op("pe", lambda e: e.matmul(pd[:, :256], ktok[r0:r0 + 64, :], vv[r0:r0 + 64, i, :],
                                                      start=True, stop=True), reads=[b_ktok, b_in], writes=[bd])
                        ebt, ecols = el
                        esc = ebt[:, ecols[ch]:ecols[ch] + 1]
                        T.op("act", lambda e: e.activation(out=tmpS[d][:], in_=S32[d][:], func=AF.Copy, scale=esc),
                             reads=[b_S32[d], b_el], writes=[b_tmpS[d]])
                        T.op("dve", lambda e: e.scalar_tensor_tensor(out=S32[d][:], in0=pd[:, :256], scalar=esc,
                                                                     in1=tmpS[d][:], op0=ALU.mult, op1=ALU.add),
                             reads=[bd, b_el, b_tmpS[d]], writes=[b_S32[d]])
                        T.op("act", lambda e: e.activation(out=Sbf[d][:], in_=S32[d][:], func=AF.Copy),
                             reads=[b_S32[d]], writes=[b_Sbf[d]])
                for d in range(2):
                    i, c0 = ctx[d][0], ctx[d][1]
                    po, bo = pos[d]
                    o3 = oacc[:, :, c0:c0 + 128]
                    p_intra = po[:, 0:256].rearrange("p (v t) -> p v t", v=2)
                    p_inter = po[:, 256:512].rearrange("p (v t) -> p v t", v=2)
                    if not written[i]:
                        T.op("act", lambda e: e.activation(out=o3, in_=p_intra, func=AF.Copy), reads=[bo], writes=[b_o[i]])
                    else:
                        T.op("dve", lambda e: e.tensor_tensor(out=o3, in0=p_intra, in1=o3, op=ALU.add),
                             reads=[bo, b_o[i]], writes=[b_o[i]])
                    T.op("dve", lambda e: e.tensor_tensor(out=o3, in0=p_inter, in1=o3, op=ALU.add),
                         reads=[bo, b_o[i]], writes=[b_o[i]])
                    written[i] = True
            if DBG.get("dump") is not None and hh == 0:
                do, dS = DBG["dump"]
                for vt in range(2):
                    T.dma("sp", do[vt * 128:(vt + 1) * 128, :], oacc[:, vt, :], reads=b_o)
                for d in range(2):
                    T.dma("sp", dS[d], S32[d][:], reads=[b_S32[d]])
            row_toks = []
            for c in range(NCH):
                t0 = c * 512
                pt, pb = PS.next()
                for vt in range(2):
                    sq, b_sq = sq2.next()
                    T.op("act", lambda e: e.activation(out=sq[:, :], in_=oacc[:, vt, t0:t0 + 512], func=AF.Square),
                         reads=b_o[4 * c:4 * c + 4], writes=[b_sq])
                    T.op("pe", lambda e: e.matmul(pt[:, :], ones[:], sq[:, :], start=(vt == 0), stop=(vt == 1)),
                         reads=[b_ones, b_sq], writes=[pb])
                T.op("act", lambda e: e.activation(out=rs2[:, :], in_=pt[:, :], func=AF.Sqrt, scale=1.0 / 256, bias=EPS),
                     reads=[pb], writes=[b_rs2])
                T.op("dve", lambda e: e.reciprocal(out=rs2[:, :], in_=rs2[:, :]), reads=[b_rs2], writes=[b_rs2])
                for vt in range(2):
                    tm, b_tm = tm2.next()
                    T.op("dve", lambda e: e.scalar_tensor_tensor(out=tm[:, :], in0=oacc[:, vt, t0:t0 + 512],
                                                                 scalar=glags[:, vt:vt + 1], in1=rs2[:, :],
                                                                 op0=ALU.mult, op1=ALU.mult),
                         reads=b_o[4 * c:4 * c + 4] + [b_glag, b_rs2], writes=[b_tm])
                    s, b_s = stg.next()
                    T.op("dve", lambda e: e.tensor_tensor(out=s[:, :], in0=tm[:, :], in1=sg_[:, vt, t0:t0 + 512], op=ALU.mult),
                         reads=[b_tm, b_sg], writes=[b_s])
                    r = 512 + hh * 256 + vt * 128
                    row_toks.append(T.dma("sp", mixT[r:r + 128, t0:t0 + 512], s[:, :], reads=[b_s]))
            if on_rows is not None:
                on_rows(2 + hh, row_toks)


def build_even(debug=False):
    nc = bass.Bass("TRN2", target_bir_lowering=False)
    dt = lambda name, shape, dtype, kind="ExternalInput": nc.dram_tensor(name, shape, dtype, kind=kind).ap()
    hT = dt("hT", [D, SEQ], F32)
    g1 = dt("g1", [128, KT], F32)
    w_e = dt("w_e", [D, NW_E], F32)
    gua_f = dt("gua_f", [17, 256], F32)
    gua_b = dt("gua_b", [17, 256], F32)
    glag = dt("glag", [128, 2], F32)
    cs = dt("cs", [256, 512], BF16)
    cosm = dt("cosm", [SEQ, SEQ], BF16)
    nsinm = dt("nsinm", [SEQ, SEQ], BF16)
    mf = dt("mf", [128, 128], BF16)
    mb = dt("mb", [128, 128], BF16)
    mixT = dt("mixT", [1024, SEQ], BF16, "ExternalOutput")
    kind = "ExternalOutput" if debug else "Internal"
    scr = (dt("aT_s", [512, SEQ], BF16, kind), dt("qT_s", [256, SEQ], BF16, kind), dt("kT_s", [256, SEQ], BF16, kind),
           dt("k_s", [SEQ, 256], BF16, kind), dt("v_s", [SEQ, 512], BF16, kind), dt("sgT_s", [512, SEQ], BF16, kind),
           dt("la_s", [2, SEQ, 256], BF16, kind))
    if debug:
        DBG["dump"] = (dt("dbg_o", [256, SEQ], F32, "ExternalOutput"), dt("dbg_S", [2, 128, 256], F32, "ExternalOutput"))
    with contextlib.ExitStack() as stack:
        T = Trk(nc, stack)
        ps = [stack.enter_context(nc.psum_tensor("ps%d" % i, [128, 512], F32)) for i in range(8)]
        hv = hT.rearrange("(k p) t -> p k t", p=128)
        hload = lambda T, t, c, b: T.dma("sp", t[:], hv[:, :, c * 512:(c + 1) * 512], writes=[b])
        emit_even(nc, T, stack, ps, hload, g1, w_e, gua_f, gua_b, glag, cs, cosm, nsinm, mf, mb, mixT, scr)
        T.finish()
    return nc


def even_consts():
    c = np.arange(256)
    ang = 2 * np.pi * ((c[:, None] * c[None, :]) % 256) / 256.0
    cs = np.concatenate([np.cos(ang), np.sin(ang)], axis=1) / 16.0
    s = np.arange(SEQ, dtype=np.int64)
    ang = 2 * np.pi * ((s[:, None] * s[None, :]) % SEQ) / float(SEQ)
    cosm = (np.cos(ang) / 64.0).astype(NPBF)
    nsinm = (-np.sin(ang) / 64.0).astype(NPBF)
    p = np.arange(128)
    same = (p[:, None] // 64) == (p[None, :] // 64)
    mf = (same & (p[:, None] <= p[None, :])).astype(np.float32).astype(NPBF)
    mb = (same & (p[:, None] >= p[None, :])).astype(np.float32).astype(NPBF)
    return dict(cs=cs.astype(NPBF), cosm=cosm, nsinm=nsinm, mf=mf, mb=mb)


def even_weights(j, w_in, gu_f, gb_f, gu_b, gb_b, gla_g):
    cols = np.r_[512 * j:512 * j + 512, 1024 + 256 * j:1024 + 256 * j + 256, 1536 + 256 * j:1536 + 256 * j + 256,
                 2048 + 512 * j:2048 + 512 * j + 512, 3072 + 512 * j:3072 + 512 * j + 512, 4096:4128]
    hc = slice(256 * j, 256 * j + 256)
    return dict(w_e=np.ascontiguousarray(w_in[:, cols]),
                gua_f=np.ascontiguousarray(np.concatenate([gu_f[:, hc], gb_f[None, hc]], 0)),
                gua_b=np.ascontiguousarray(np.concatenate([gu_b[:, hc], gb_b[None, hc]], 0)),
                glag=lay128(gla_g))


NBU = 3072
NKT = 20


def emit_odd(nc, T, stack, ps, hload, g1, w_o, qg, kg, rb, onehot, cmult, mixT, scr, on_rows=None):
    sb = lambda name, shape, dt: stack.enter_context(nc.sbuf_tensor(uniq(name), shape, dt))
    PS = Rot(ps)
    xn_s, qT_s, kT_s, v_s, u_s = scr
    NCH = SEQ // 512
    ones = sb("ones", [128, 128], F32); b_ones = Buf()
    T.op("dve", lambda e: e.memset(ones[:], 1.0), writes=[b_ones])
    onesb = sb("onesb", [128, 128], BF16); b_onesb = Buf()
    T.op("dve", lambda e: e.memset(onesb[:], 1.0), writes=[b_onesb])
    g1s = sb("g1s", [128, KT], F32); b_g1 = Buf()
    T.dma("sp", g1s[:], g1, writes=[b_g1])
    qgs = sb("qgs", [128, 1], F32); b_qg = Buf()
    T.dma("sp", qgs[:], qg, writes=[b_qg])
    kgs = sb("kgs", [128, 1], F32); b_kg = Buf()
    T.dma("sp", kgs[:], kg, writes=[b_kg])

    with contextlib.ExitStack() as st2:
        sb2 = lambda name, shape, dt: st2.enter_context(nc.sbuf_tensor(uniq(name), shape, dt))
        rbs = sb2("rbs", [32, 8], F32); b_rb = Buf()
        T.dma("sp", rbs[:], rb, writes=[b_rb])
        oh = sb2("oh", [32, NBU], F32); b_oh = Buf()
        T.dma("sp", oh[:], onehot, writes=[b_oh])
        cm = sb2("cm", [8, NBU], F32); b_cm = Buf()
        T.dma("sp", cm[:], cmult, writes=[b_cm])
        ue = sb2("ue", [8, NBU], F32); b_ue = Buf()
        ub = sb2("ub", [8, NBU], BF16); b_ub = Buf()
        for c in range(NBU // 512):
            pt, pb = PS.next()
            T.op("pe", lambda e: e.matmul(pt[0:8, :], rbs[:, :], oh[:, c * 512:(c + 1) * 512], start=True, stop=True),
                 reads=[b_rb, b_oh], writes=[pb])
            T.op("act", lambda e: e.activation(out=ue[:, c * 512:(c + 1) * 512], in_=pt[0:8, :], func=AF.Exp),
                 reads=[pb], writes=[b_ue])
        T.op("dve", lambda e: e.tensor_tensor(out=ub[:, :], in0=ue[:, :], in1=cm[:, :], op=ALU.mult),
             reads=[b_ue, b_cm], writes=[b_ub])
        b_us = Buf()
        T.dma("sp", u_s, ub[:, :], reads=[b_ub], writes=[b_us])

        hcs = [sb2("hc%d" % i, [128, KT, 512], F32) for i in range(2)]; b_hcs = [Buf(), Buf()]
        xcs = [sb2("xc%d" % i, [128, KT, 512], BF16) for i in range(2)]; b_xcs = [Buf(), Buf()]
        sqr = Rot([sb2("sq%d" % i, [128, 512], F32) for i in range(2)])
        rstd_t = sb2("rstd", [128, 512], F32); b_rstd = Buf()
        xv = xn_s.rearrange("(k p) t -> p k t", p=128)
        b_xn = [Buf() for _ in range(NCH)]
        hload(T, hcs[0], 0, b_hcs[0])
        for c in range(NCH):
            t0 = c * 512
            if c + 1 < NCH:
                hload(T, hcs[(c + 1) % 2], c + 1, b_hcs[(c + 1) % 2])
            emit_norm_chunk(nc, T, hcs[c % 2], b_hcs[c % 2], g1s, b_g1, ones, b_ones, PS, sqr, rstd_t, b_rstd,
                            xcs[c % 2], b_xcs[c % 2], 512)
            T.dma("sp", xv[:, :, t0:t0 + 512], xcs[c % 2][:], reads=[b_xcs[c % 2]], writes=[b_xn[c]])
    T.barrier()

    with contextlib.ExitStack() as st2:
        sb2 = lambda name, shape, dt: st2.enter_context(nc.sbuf_tensor(uniq(name), shape, dt))
        wp = [sb2("wp%d" % i, [128, KT, 1024], BF16) for i in range(2)]; b_wp = [Buf(), Buf()]
        xcs = [sb2("xc%d" % i, [128, KT, 512], BF16) for i in range(2)]; b_xcs = [Buf(), Buf()]
        sqr = Rot([sb2("sq%d" % i, [128, 512], F32) for i in range(3)])
        rsr = Rot([sb2("rs%d" % i, [128, 512], F32) for i in range(3)])
        stg = Rot([sb2("stg%d" % i, [128, 512], BF16) for i in range(4)])
        wv = w_o.rearrange("(k p) n -> p k n", p=128)
        xv = xn_s.rearrange("(k p) t -> p k t", p=128)
        T.dma("pool", wp[0][:], wv[:, :, 0:1024], writes=[b_wp[0]])
        items = [(part, c) for part in range(3) for c in range(NCH)]
        T.dma("sp", xcs[0][:], xv[:, :, 0:512], reads=[b_xn[0]], writes=[b_xcs[0]])
        for idx, (part, c) in enumerate(items):
            t0 = c * 512
            if c == 0 and part + 1 < 3:
                T.dma("pool", wp[(part + 1) % 2][:], wv[:, :, (part + 1) * 1024:(part + 2) * 1024],
                      writes=[b_wp[(part + 1) % 2]])
            if idx + 1 < len(items):
                c2 = items[idx + 1][1]
                T.dma("sp", xcs[(idx + 1) % 2][:], xv[:, :, c2 * 512:(c2 + 1) * 512], reads=[b_xn[c2]],
                      writes=[b_xcs[(idx + 1) % 2]])
            xc, b_xc = xcs[idx % 2], b_xcs[idx % 2]
            w_, b_w = wp[part % 2], b_wp[part % 2]
            if part < 2:
                gs, b_gs = (qgs, b_qg) if part == 0 else (kgs, b_kg)
                dst = qT_s if part == 0 else kT_s
                sc_, bi_ = (1.0, 128 * EPS) if part == 0 else (1.0 / 128, EPS)
                pend_n = None
                for hd in range(8):
                    pt, pb = PS.next()
                    mm_group(T, pt[:, :], [(w_[:, k, hd * 128:(hd + 1) * 128], xc[:, k, :]) for k in range(KT)], pb,
                             reads=[b_w, b_xc])
                    sq, b_sq = sqr.next()
                    T.op("act", lambda e: e.activation(out=sq[:, :], in_=pt[:, :], func=AF.Square), reads=[pb], writes=[b_sq])
                    if pend_n is not None:
                        pend_n()

                    def fin(hd=hd, pt=pt, pb=pb, sq=sq, b_sq=b_sq):
                        p2, pb2 = PS.next()
                        T.op("pe", lambda e: e.matmul(p2[:, :], ones[:], sq[:, :], start=True, stop=True),
                             reads=[b_ones, b_sq], writes=[pb2])
                        rs, b_rs = rsr.next()
                        T.op("act", lambda e: e.activation(out=rs[:, :], in_=p2[:, :], func=AF.Sqrt, scale=sc_, bias=bi_),
                             reads=[pb2], writes=[b_rs])
                        T.op("dve", lambda e: e.reciprocal(out=rs[:, :], in_=rs[:, :]), reads=[b_rs], writes=[b_rs])
                        s, b_s = stg.next()
                        T.op("dve", lambda e: e.scalar_tensor_tensor(out=s[:, :], in0=pt[:, :], scalar=gs[:, 0:1], in1=rs[:, :],
                                                                     op0=ALU.mult, op1=ALU.mult),
                             reads=[pb, b_gs, b_rs], writes=[b_s])
                        T.dma("sp", dst[hd * 128:(hd + 1) * 128, t0:t0 + 512], s[:, :], reads=[b_s])
                    pend_n = fin
                pend_n()
            else:
                for ts in range(4):
                    for half in range(2):
                        pt, pb = PS.next()
                        mm_group(T, pt[:, :], [(xc[:, k, ts * 128:(ts + 1) * 128], w_[:, k, half * 512:(half + 1) * 512])
                                               for k in range(KT)], pb, reads=[b_w, b_xc])
                        s, b_s = stg.next()
                        if half == 0:
                            T.op("act", lambda e: e.activation(out=s[:, :], in_=pt[:, :], func=AF.Copy), reads=[pb], writes=[b_s])
                        else:
                            T.op("dve", lambda e: e.tensor_copy(out=s[:, :], in_=pt[:, :]), reads=[pb], writes=[b_s])
                        T.dma("sp", v_s[t0 + ts * 128:t0 + (ts + 1) * 128, half * 512:(half + 1) * 512], s[:, :], reads=[b_s])
    T.barrier()

    with contextlib.ExitStack() as st2:
        sb2 = lambda name, shape, dt: st2.enter_context(nc.sbuf_tensor(uniq(name), shape, dt))
        qTs = [sb2("qT%d" % i, [128, SEQ], BF16) for i in range(2)]
        kTs = [sb2("kT%d" % i, [128, SEQ], BF16) for i in range(2)]
        vts = [sb2("vt%d" % i, [128, 32, 128], BF16) for i in range(2)]
        Es = [sb2("E%d" % i, [128, NKT, 512], BF16) for i in range(2)]
        b_q = [Buf(), Buf()]; b_k = [Buf(), Buf()]; b_v = [Buf(), Buf()]; b_E = [Buf(), Buf()]
        pex = Rot([sb2("pex%d" % i, [128, 512], BF16) for i in range(4)])
        pmr = Rot([sb2("pm%d" % i, [128, 512], BF16) for i in range(5)])
        rdr = Rot([sb2("rd%d" % i, [128, 512], F32) for i in range(2)])
        stg = Rot([sb2("stg%d" % i, [128, 512], BF16) for i in range(2)])
        vv_ = v_s.rearrange("(i p) c -> p i c", p=128)
        SPS = Rot(ps[0:4]); OPS = Rot(ps[4:6]); DPS = Rot(ps[6:8])

        def loadh(h):
            s = h % 2
            T.dma("sp", qTs[s][:], qT_s[h * 128:(h + 1) * 128, :], writes=[b_q[s]])
            T.dma("sp", kTs[s][:], kT_s[h * 128:(h + 1) * 128, :], writes=[b_k[s]])
            T.dma("sp", vts[s][:], vv_[:, :, h * 128:(h + 1) * 128], writes=[b_v[s]])
            T.dma("sp", Es[s][:], bass.AP(u_s.tensor, u_s.offset + h * NBU, [[1, 128], [128, NKT], [1, 512]]),
                  reads=[b_us], writes=[b_E[s]])

        def rev(ap):
            return bass.AP(ap.tensor, ap.offset + 511, [list(ap.ap[0]), [-1, 512]])

        loadh(0)
        loadh(1)
        iters = []
        for h in range(8):
            for qt in range(NCH):
                tiles = [a for a in range(NKT) if 0 <= qt * 512 - 1024 + 128 * a < SEQ]
                for n_, a in enumerate(tiles):
                    iters.append((h, qt, a, n_ == 0, n_ == len(tiles) - 1))
        live = {}
        acc = {}
        row_toks = []

        def emit_qk(i):
            h, qt, a, first, last = iters[i]
            s = h % 2
            t0 = qt * 512
            s0 = t0 - 1024 + 128 * a
            pt, pb = SPS.next()
            T.op("pe", lambda e: e.matmul(pt[:, :], kTs[s][:, s0:s0 + 128], rev(qTs[s][:, t0:t0 + 512]),
                                          start=True, stop=True), reads=[b_k[s], b_q[s]], writes=[pb])
            px, b_px = pex.next()
            T.op("act", lambda e: e.activation(out=px[:, :], in_=pt[:, :], func=AF.Exp), reads=[pb], writes=[b_px])
            pm, b_pm = pmr.next()
            T.op("dve", lambda e: e.tensor_tensor(out=pm[:, :], in0=px[:, :], in1=Es[s][:, a, :], op=ALU.mult),
                 reads=[b_px, b_E[s]], writes=[b_pm])
            live[i] = (pm, b_pm)

        def emit_pv(i):
            h, qt, a, first, last = iters[i]
            s = h % 2
            t0 = qt * 512
            s0 = t0 - 1024 + 128 * a
            if first:
                acc[(h, qt)] = (OPS.next(), DPS.next())
                if qt == 0 and 1 <= h < 7:
                    loadh(h + 1)
            (po, bo), (pd, bd) = acc[(h, qt)]
            pm, b_pm = live.pop(i)
            T.op("pe", lambda e: e.matmul(po[:, :], vts[s][:, s0 // 128, :], pm[:, :], start=first, stop=last),
                 reads=[b_v[s], b_pm], writes=[bo], inc=last)
            T.op("pe", lambda e: e.matmul(pd[:, :], onesb[:, :], pm[:, :], start=first, stop=last),
                 reads=[b_onesb, b_pm], writes=[bd], inc=True)
            if last:
                rd, b_rd = rdr.next()
                T.op("dve", lambda e: e.reciprocal(out=rd[:, :], in_=pd[:, :]), reads=[bd], writes=[b_rd])
                st_, b_st = stg.next()
                T.op("dve", lambda e: e.tensor_tensor(out=st_[:, :], in0=rev(po[:, :]), in1=rev(rd[:, :]), op=ALU.mult),
                     reads=[bo, b_rd], writes=[b_st])
                row_toks.append(T.dma("sp", mixT[h * 128:(h + 1) * 128, t0:t0 + 512], st_[:, :], reads=[b_st]))
                del acc[(h, qt)]
                if qt == NCH - 1 and h % 2 == 1 and on_rows is not None:
                    on_rows(h // 2, list(row_toks))
                    del row_toks[:]

        LOOK = 2
        for i in range(len(iters) + LOOK):
            if i < len(iters):
                emit_qk(i)
            if i - LOOK >= 0:
                emit_pv(i - LOOK)


def build_odd(debug=False):
    nc = bass.Bass("TRN2", target_bir_lowering=False)
    dt = lambda name, shape, dtype, kind="ExternalInput": nc.dram_tensor(name, shape, dtype, kind=kind).ap()
    hT = dt("hT", [D, SEQ], F32)
    g1 = dt("g1", [128, KT], F32)
    w_o = dt("w_o", [D, 3072], F32)
    qg = dt("qg", [128, 1], F32)
    kg = dt("kg", [128, 1], F32)
    rb = dt("rb", [32, 8], F32)
    onehot = dt("onehot", [32, NBU], F32)
    cmult = dt("cmult", [8, NBU], F32)
    mixT = dt("mixT", [1024, SEQ], BF16, "ExternalOutput")
    kind = "ExternalOutput" if debug else "Internal"
    scr = (dt("xn_s", [D, SEQ], BF16, kind), dt("qT_s", [1024, SEQ], BF16, kind), dt("kT_s", [1024, SEQ], BF16, kind),
           dt("v_s", [SEQ, 1024], BF16, kind), dt("u_s", [8, NBU], BF16, kind))
    with contextlib.ExitStack() as stack:
        T = Trk(nc, stack)
        ps = [stack.enter_context(nc.psum_tensor("ps%d" % i, [128, 512], F32)) for i in range(8)]
        hv = hT.rearrange("(k p) t -> p k t", p=128)
        hload = lambda T, t, c, b: T.dma("sp", t[:], hv[:, :, c * 512:(c + 1) * 512], writes=[b])
        emit_odd(nc, T, stack, ps, hload, g1, w_o, qg, kg, rb, onehot, cmult, mixT, scr)
        T.finish()
    return nc


def t5_bucket_np(rel):
    nb = 16
    ret = (rel > 0).astype(np.int32) * nb
    n = np.abs(rel)
    max_exact = nb // 2
    large = max_exact + (np.log(np.maximum(n, 1) / max_exact) / np.log(1024 / max_exact) * (nb - max_exact)).astype(np.int32)
    large = np.minimum(large, nb - 1)
    return (ret + np.where(n < max_exact, n, large)).astype(np.int32)


def odd_consts():
    m = np.arange(NBU)
    delta = m - 1535
    bkt = t5_bucket_np(delta)
    onehot = (bkt[None, :] == np.arange(32)[:, None]).astype(np.float32)
    mult = np.zeros(NBU, np.float32)
    for (w, d) in ((128, 1), (512, 4), (2048, 16)):
        mult += ((delta % d == 0) & (np.abs(delta) <= w // 2)).astype(np.float32)
    return dict(onehot=onehot, cmult=np.ascontiguousarray(np.broadcast_to(mult, (8, NBU))))


def odd_weights(j, w_qkv, q_g, k_g, rel_bias):
    cols = np.r_[1024 * j:1024 * j + 1024, 2048 + 1024 * j:2048 + 1024 * j + 1024, 4096 + 1024 * j:4096 + 1024 * j + 1024]
    return dict(w_o=np.ascontiguousarray(w_qkv[:, cols]), qg=np.ascontiguousarray(q_g.reshape(128, 1)),
                kg=np.ascontiguousarray(k_g.reshape(128, 1)), rb=np.ascontiguousarray(rel_bias[:, 8 * j:8 * j + 8]))


_PROGS = {}


def _prog(name):
    if name not in _PROGS:
        _PROGS[name] = {"even": build_even, "odd": build_odd, "post": build_post}[name]()
    return _PROGS[name]


def _run(name, in_maps):
    res = run_bass_kernel_spmd(_prog(name), in_maps, core_ids=list(range(NCORES)))
    return res.results


def kernel_unfused(x, mix_norm_g, w_in_even, gate_up_fwd, gate_bias_fwd, gate_up_bwd, gate_bias_bwd, gla_norm_g, w_out_even,
           w_qkv_odd, q_norm_g, k_norm_g, rel_bias, w_out_odd, ffn_norm_g, w_gate, w_up, conv_w, conv_b, w_down):
    f32 = lambda a: np.ascontiguousarray(np.asarray(a, dtype=np.float32))
    x = f32(x)
    hT = [np.ascontiguousarray(x[b].T) for b in range(BATCH)]
    ec = even_consts()
    oc = odd_consts()
    for layer in range(4):
        i = layer // 2
        g1 = lay128(f32(mix_norm_g[layer]))
        ims = []
        for b in range(BATCH):
            for j in range(2):
                im = dict(hT=hT[b], g1=g1)
                if layer % 2 == 0:
                    im.update(even_weights(j, f32(w_in_even[i]), f32(gate_up_fwd[i]), f32(gate_bias_fwd[i]),
                                           f32(gate_up_bwd[i]), f32(gate_bias_bwd[i]), f32(gla_norm_g[i])))
                    im.update(ec)
                else:
                    im.update(odd_weights(j, f32(w_qkv_odd[i]), f32(q_norm_g[i]), f32(k_norm_g[i]), f32(rel_bias)))
                    im.update(oc)
                ims.append(im)
        res = _run("even" if layer % 2 == 0 else "odd", ims)
        mT = []
        for b in range(BATCH):
            m0 = np.asarray(res[2 * b]["mixT"]); m1 = np.asarray(res[2 * b + 1]["mixT"])
            if layer % 2 == 0:
                mT.append(np.concatenate([m0[:512], m1[:512], m0[512:], m1[512:]], axis=0))
            else:
                mT.append(np.concatenate([m0, m1], axis=0))
        w_out = f32(w_out_even[i]) if layer % 2 == 0 else f32(w_out_odd[i])
        cw = np.ascontiguousarray(f32(conv_w[layer]).T.reshape(FT, 128, 3).transpose(1, 0, 2).reshape(128, FT * 3))
        common = dict(w_out=w_out, g2=lay128(f32(ffn_norm_g[layer])), w_gate=f32(w_gate[layer]), w_up=f32(w_up[layer]),
                      cw=cw, cb=lay128(f32(conv_b[layer])), w_down=f32(w_down[layer]))
        ims = []
        for b in range(BATCH):
            for half in range(2):
                t0 = half * TOK
                he = np.zeros((D, EXT), np.float32); me = np.zeros((D, EXT), mT[b].dtype)
                lo, hi = max(t0 - 1, 0), min(t0 + TOK + 1, SEQ)
                he[:, lo - (t0 - 1):hi - (t0 - 1)] = hT[b][:, lo:hi]
                me[:, lo - (t0 - 1):hi - (t0 - 1)] = mT[b][:, lo:hi]
                im = dict(hT_ext=he, mT_ext=me)
                im.update(common)
                ims.append(im)
        res = _run("post", ims)
        hT = [np.ascontiguousarray(np.concatenate([np.asarray(res[2 * b]["hT_out"]), np.asarray(res[2 * b + 1]["hT_out"])], axis=1))
              for b in range(BATCH)]
    return np.ascontiguousarray(np.stack([h.T for h in hT], axis=0)).astype(np.float32)


PAIRS = [[0, 1], [2, 3], [4, 5], [6, 7]]


def build_fused():
    nc = bass.Bass("TRN2", target_bir_lowering=False)
    dt = lambda name, shape, dtype, kind="ExternalInput": nc.dram_tensor(name, shape, dtype, kind=kind).ap()
    xT = dt("xT", [D, TOK], F32)
    sel = dt("sel", [128, 2], F32)
    cs = dt("cs", [256, 512], BF16); cosm = dt("cosm", [SEQ, SEQ], BF16); nsinm = dt("nsinm", [SEQ, SEQ], BF16)
    mf = dt("mf", [128, 128], BF16); mb = dt("mb", [128, 128], BF16)
    onehot = dt("onehot", [32, NBU], F32); cmult = dt("cmult", [8, NBU], F32); rb = dt("rb", [32, 8], F32)
    L = []
    for l in range(4):
        d_ = dict(g1=dt("g1_%d" % l, [128, KT], F32), w_out=dt("w_out_%d" % l, [D, D], F32), g2=dt("g2_%d" % l, [128, KT], F32),
                  w_gate=dt("w_gate_%d" % l, [D, FF], F32), w_up=dt("w_up_%d" % l, [D, FF], F32),
                  cw=dt("cw_%d" % l, [128, FT * 3], F32), cb=dt("cb_%d" % l, [128, FT], F32),
                  w_down=dt("w_down_%d" % l, [FF, D], F32))
        if l % 2 == 0:
            d_.update(w_e=dt("w_e_%d" % l, [D, NW_E], F32), gua_f=dt("gua_f_%d" % l, [17, 256], F32),
                      gua_b=dt("gua_b_%d" % l, [17, 256], F32), glag=dt("glag_%d" % l, [128, 2], F32))
        else:
            d_.update(w_o=dt("w_o_%d" % l, [D, 3072], F32), qg=dt("qg_%d" % l, [128, 1], F32), kg=dt("kg_%d" % l, [128, 1], F32))
        L.append(d_)
    outT = dt("outT", [D, TOK], F32, "ExternalOutput")
    I = "Internal"
    hown = dt("hown", [D, TOK], F32, I)
    hfull = dt("hfull", [8, 2, 2, 128, TOK], F32, I)
    mcore = dt("mcore", [1024, SEQ], BF16, I)
    mfull = dt("mfull", [4, 2, 256, SEQ], BF16, I)
    hmid_scr = dt("hmid_scr", [FF, TOK], BF16, I)
    hm_scr = dt("hm_scr", [D, EXT], F32, I)
    scr_e = (dt("aT_s", [512, SEQ], BF16, I), dt("qT_s", [256, SEQ], BF16, I), dt("kT_s", [256, SEQ], BF16, I),
             dt("k_s", [SEQ, 256], BF16, I), dt("v_s", [SEQ, 512], BF16, I), dt("sgT_s", [512, SEQ], BF16, I),
             dt("la_s", [2, SEQ, 256], BF16, I))
    scr_o = (dt("xn_s", [D, SEQ], BF16, I), dt("qTo_s", [1024, SEQ], BF16, I), dt("kTo_s", [1024, SEQ], BF16, I),
             dt("vo_s", [SEQ, 1024], BF16, I), dt("u_s", [8, NBU], BF16, I))

    with contextlib.ExitStack() as stack:
        T = Trk(nc, stack)
        ps = [stack.enter_context(nc.psum_tensor("ps%d" % i, [128, 512], F32)) for i in range(8)]
        sels = stack.enter_context(nc.sbuf_tensor("sels", [128, 2], F32)); b_sel = Buf()
        T.dma("sp", sels[:], sel, writes=[b_sel])
        T.dma("sp", hown, xT)

        def hload(T, t, c, b):
            r, tc = c // 4, (c % 4) * 512
            toks = []
            for q in range(8):
                toks.append(T.dma("sp", t[:, 2 * q:2 * q + 2, :], hfull[q, r].rearrange("k p t -> p k t")[:, :, tc:tc + 512],
                                  writes=([b] if q == 0 else [])))
            b.multi = toks

        def make_loaders():
            st = {}

            def get(sb2, key, shape, dtype):
                if key not in st:
                    t = sb2(key, shape, dtype); bb = Buf()
                    T.op("dve", lambda e: e.memset(t[:], 0.0), writes=[bb])
                    st[key] = (t, bb)
                return st[key]

            def load_m(T, sb2, mT, k0, b):
                kg = k0 // 4
                r_src, q0 = kg // 2, 2 * (kg % 2)
                A, bA = get(sb2, "mA", [128, 2, EXT], BF16)
                B, bB = get(sb2, "mB", [128, 2, EXT], BF16)
                for qq in range(2):
                    src = mfull[q0 + qq, r_src].rearrange("(k p) t -> p k t", p=128)
                    T.dma("sp", A[:, :, 1:EXT], src[:, :, 0:TOK + 1], writes=[bA])
                    T.dma("sp", B[:, :, 0:EXT - 1], src[:, :, TOK - 1:SEQ], writes=[bB])
                    T.op("dve", lambda e: e.tensor_scalar(out=A[:], in0=A[:], scalar1=sels[:, 0:1], scalar2=None, op0=ALU.mult),
                         reads=[b_sel], writes=[bA])
                    T.op("dve", lambda e: e.scalar_tensor_tensor(out=mT[:, k0 + 2 * qq:k0 + 2 * qq + 2, :], in0=B[:],
                                                                 scalar=sels[:, 1:2], in1=A[:], op0=ALU.mult, op1=ALU.add),
                         reads=[bA, bB, b_sel], writes=[b])

            def load_h(T, sb2, hb, n, b):
                q, k2 = n // 2, n % 2
                T.dma("sp", hb[:, 1:TOK + 1], hown[n * 128:(n + 1) * 128, :], writes=[b])
                T.dma("sp", hb[:, 0:1], hfull[q, 0, k2][:, TOK - 1:TOK], writes=[b], slow=True)
                T.dma("sp", hb[:, EXT - 1:EXT], hfull[q, 1, k2][:, 0:1], writes=[b], slow=True)
                T.op("dve", lambda e: e.tensor_scalar(out=hb[:, 0:1], in0=hb[:, 0:1], scalar1=sels[:, 1:2], scalar2=None,
                                                      op0=ALU.mult), reads=[b_sel], writes=[b])
                T.op("dve", lambda e: e.tensor_scalar(out=hb[:, EXT - 1:EXT], in0=hb[:, EXT - 1:EXT], scalar1=sels[:, 0:1],
                                                      scalar2=None, op0=ALU.mult), reads=[b_sel], writes=[b])
            return load_m, load_h

        def gather_h(q, extra=()):
            T.collective(hown[256 * q:256 * (q + 1), :].opt(), hfull[q].rearrange("r k p t -> (r k p) t").opt(), PAIRS,
                         extra=extra)

        def gather_m(q, extra=()):
            T.collective(mcore[256 * q:256 * (q + 1), :].opt(), mfull[q].rearrange("r p t -> (r p) t").opt(), PAIRS,
                         extra=extra)

        T.barrier()
        for q in range(8):
            gather_h(q)
        for l in range(4):
            W = L[l]
            T.barrier(); T.new_epoch()
            with contextlib.ExitStack() as st_l:
                if l % 2 == 0:
                    emit_even(nc, T, st_l, ps, hload, W["g1"], W["w_e"], W["gua_f"], W["gua_b"], W["glag"],
                              cs, cosm, nsinm, mf, mb, mcore, scr_e, on_rows=gather_m)
                else:
                    emit_odd(nc, T, st_l, ps, hload, W["g1"], W["w_o"], W["qg"], W["kg"], rb, onehot, cmult, mcore, scr_o,
                             on_rows=gather_m)
            T.barrier(); T.new_epoch()
            store_toks = {}

            def on_store(n, c, nchunk, tk):
                store_toks.setdefault(n // 2, []).append(tk)
                if c == nchunk - 1 and n % 2 == 1:
                    gather_h(n // 2, store_toks[n // 2])

            with contextlib.ExitStack() as st_l:
                load_m, load_h = make_loaders()
                emit_post(nc, T, st_l, ps, load_m, load_h, W["w_out"], W["g2"], W["w_gate"], W["w_up"], W["cw"], W["cb"],
                          W["w_down"], outT if l == 3 else hown, hmid_scr, hm_scr, on_store=(on_store if l < 3 else None))
        T.finish()
    return nc, T


def wout_perm(layer):
    perm = np.zeros(D, np.int64)
    for r in range(2):
        for row in range(1024):
            if layer % 2 == 0:
                ch = 512 * r + row if row < 512 else 1024 + 512 * r + (row - 512)
            else:
                ch = 1024 * r + row
            perm[r * 1024 + row] = ch
    return perm


def kernel(x, mix_norm_g, w_in_even, gate_up_fwd, gate_bias_fwd, gate_up_bwd, gate_bias_bwd, gla_norm_g, w_out_even,
                 w_qkv_odd, q_norm_g, k_norm_g, rel_bias, w_out_odd, ffn_norm_g, w_gate, w_up, conv_w, conv_b, w_down):
    f32 = lambda a: np.ascontiguousarray(np.asarray(a, dtype=np.float32))
    x = f32(x)
    if "fused" not in _PROGS:
        _PROGS["fused"] = build_fused()[0]
    common = {}
    common.update(even_consts()); common.update(odd_consts())
    for l in range(4):
        i = l // 2
        w_out = f32(w_out_even[i]) if l % 2 == 0 else f32(w_out_odd[i])
        common["w_out_%d" % l] = np.ascontiguousarray(w_out[wout_perm(l), :])
        common["g1_%d" % l] = lay128(f32(mix_norm_g[l])); common["g2_%d" % l] = lay128(f32(ffn_norm_g[l]))
        common["w_gate_%d" % l] = f32(w_gate[l]); common["w_up_%d" % l] = f32(w_up[l]); common["w_down_%d" % l] = f32(w_down[l])
        common["cw_%d" % l] = np.ascontiguousarray(f32(conv_w[l]).T.reshape(FT, 128, 3).transpose(1, 0, 2).reshape(128, FT * 3))
        common["cb_%d" % l] = lay128(f32(conv_b[l]))
    ims = []
    for b in range(BATCH):
        for r in range(2):
            im = dict(common)
            im["xT"] = np.ascontiguousarray(x[b, r * TOK:(r + 1) * TOK, :].T)
            s = np.zeros((128, 2), np.float32); s[:, r] = 1.0
            im["sel"] = s
            im["rb"] = np.ascontiguousarray(f32(rel_bias)[:, 8 * r:8 * r + 8])
            for l in range(4):
                i = l // 2
                if l % 2 == 0:
                    ew = even_weights(r, f32(w_in_even[i]), f32(gate_up_fwd[i]), f32(gate_bias_fwd[i]), f32(gate_up_bwd[i]),
                                      f32(gate_bias_bwd[i]), f32(gla_norm_g[i]))
                    for k_, v_ in ew.items():
                        im["%s_%d" % (k_, l)] = v_
                else:
                    ow = odd_weights(r, f32(w_qkv_odd[i]), f32(q_norm_g[i]), f32(k_norm_g[i]), f32(rel_bias))
                    for k_ in ("w_o", "qg", "kg"):
                        im["%s_%d" % (k_, l)] = ow[k_]
            ims.append(im)
    res = run_bass_kernel_spmd(_PROGS["fused"], ims, core_ids=list(range(NCORES))).results
    out = np.empty((BATCH, SEQ, D), np.float32)
    for b in range(BATCH):
        for r in range(2):
            out[b, r * TOK:(r + 1) * TOK, :] = np.asarray(res[2 * b + r]["outT"]).T
    return out
```

```python
import contextlib
import numpy as np
import ml_dtypes
import concourse.bass as bass
import concourse.mybir as mybir
from concourse.bass_utils import run_bass_kernel_spmd

F32 = mybir.dt.float32
BF16 = mybir.dt.bfloat16
AF = mybir.ActivationFunctionType
ALU = mybir.AluOpType
NPBF = ml_dtypes.bfloat16

D = 2048
KT = D // 128
SEQ = 4096
BATCH = 4
TOK = 2048
EXT = TOK + 2
FF = 5632
FT = FF // 128
EPS = 1e-6
NCORES = 8


DBG = {}
RELAX = set()


class Buf:
    __slots__ = ("name", "w", "r", "multi")

    def __init__(self, name=""):
        self.name = name
        self.w = None
        self.r = []
        self.multi = None


class Trk:
    NDS = 20

    def __init__(self, nc, stack):
        self.nc = nc
        self.E = {"pe": nc.tensor, "act": nc.scalar, "dve": nc.vector, "pool": nc.gpsimd, "sp": nc.sync}
        self.sem = {k: stack.enter_context(nc.semaphore("s_" + k)) for k in self.E}
        self.cnt = {k: 0 for k in self.E}
        self.waited = {}
        self.dsem = {q: [stack.enter_context(nc.semaphore("d_%s%d" % (q, i))) for i in range(self.NDS)]
                     for q in ("sp", "pool")}
        self.dtot = {q: [0] * self.NDS for q in ("sp", "pool")}
        self.dnext = {"sp": 0, "pool": 0}
        self.n_instr = 0
        self.stack = stack
        self.ccsem = stack.enter_context(nc.semaphore("s_cc"))
        self.cccnt = 0
        self.epoch = 0

    def new_epoch(self):
        self.epoch += 1
        self.sem = {k: self.stack.enter_context(self.nc.semaphore("s%d_%s" % (self.epoch, k))) for k in self.E}
        self.cnt = {k: 0 for k in self.E}

    def collective(self, in_ap, out_ap, groups, reads=(), writes=(), extra=()):
        if DBG.get("no_cc"):
            return None
        self._sync("pool", reads, writes)
        for t in extra:
            self._wait("pool", t)
        if self.cccnt > 0:
            self._wait("pool", (self.ccsem, self.cccnt, "cc"))
        self.nc.gpsimd.collective_compute("AllGather", ALU.bypass, replica_groups=groups, ins=[in_ap], outs=[out_ap]
                                          ).then_inc(self.ccsem)
        self.cccnt += 1
        tok = (self.ccsem, self.cccnt, "cc")
        self._update(tok, reads, writes)
        return tok

    def _wait(self, eng, tok):
        sem, val, src = tok
        if src == eng and (eng == "pe" or eng in RELAX):
            return
        key = (eng, id(sem))
        if self.waited.get(key, 0) >= val:
            return
        self.waited[key] = val
        self.E[eng].wait_ge(sem, val)

    def _sync(self, eng, reads, writes):
        for b in reads:
            if b.w is not None:
                self._wait(eng, b.w)
            if b.multi:
                for t in b.multi:
                    self._wait(eng, t)
        for b in writes:
            if b.w is not None:
                self._wait(eng, b.w)
            if b.multi:
                for t in b.multi:
                    self._wait(eng, t)
            for t in b.r:
                self._wait(eng, t)

    def _update(self, tok, reads, writes):
        for b in reads:
            b.r.append(tok)
        for b in writes:
            b.w = tok
            b.r = []
            b.multi = None

    def op(self, eng, fn, reads=(), writes=(), inc=True):
        self._sync(eng, reads, writes)
        ins = fn(self.E[eng])
        self.n_instr += 1
        if inc:
            self.cnt[eng] += 1
            ins.then_inc(self.sem[eng], 1)
            tok = (self.sem[eng], self.cnt[eng], eng)
        else:
            tok = (self.sem[eng], self.cnt[eng] + 1, eng)
        self._update(tok, reads, writes)
        return tok

    def dma(self, q, out, in_, reads=(), writes=(), slow=False):
        self._sync(q, reads, writes)
        i = self.dnext[q]
        self.dnext[q] = (i + 1) % self.NDS
        sem = self.dsem[q][i]
        if self.dtot[q][i] > 0:
            self._wait(q, (sem, self.dtot[q][i], "dma"))
        if slow:
            self.E[q].dma_start(out=out, in_=in_, allow_slow_non_contiguous=True).then_inc(sem, 16)
        else:
            self.E[q].dma_start(out=out, in_=in_).then_inc(sem, 16)
        self.n_instr += 1
        self.dtot[q][i] += 16
        tok = (sem, self.dtot[q][i], "dma")
        self._update(tok, reads, writes)
        return tok

    def barrier(self):
        toks = []
        for q in ("sp", "pool"):
            for i in range(self.NDS):
                if self.dtot[q][i] > 0:
                    toks.append((self.dsem[q][i], self.dtot[q][i], "dma"))
        for e in ("pe", "act", "dve"):
            if self.cnt[e] > 0:
                toks.append((self.sem[e], self.cnt[e], e))
        if self.cccnt > 0:
            toks.append((self.ccsem, self.cccnt, "cc"))
        for eng in self.E:
            for t in toks:
                if t[2] == eng:
                    continue
                self._wait(eng, t)

    def finish(self):
        for q in ("sp", "pool"):
            for i in range(self.NDS):
                if self.dtot[q][i] > 0:
                    self._wait("sp", (self.dsem[q][i], self.dtot[q][i], "dma"))
        if self.cccnt > 0:
            self._wait("sp", (self.ccsem, self.cccnt, "cc"))
        for e in ("pe", "act", "dve"):
            if self.cnt[e] > 0:
                self._wait("sp", (self.sem[e], self.cnt[e], e))


_UNIQ = [0]


def uniq(name):
    _UNIQ[0] += 1
    return "%s_%d" % (name, _UNIQ[0])


def col_chunks(n, step=512):
    return [(c, min(c + step, n)) for c in range(0, n, step)]


def emit_post(nc, T, stack, ps, load_m, load_h, w_out, g2, w_gate, w_up, cw, cb, w_down, hT_out, hmid_scr, hm_scr,
              on_store=None):
    sb = lambda name, shape, dt: stack.enter_context(nc.sbuf_tensor(uniq(name), shape, dt))
    psb = [Buf("ps%d" % i) for i in range(8)]

    ones = sb("ones", [128, 128], F32)
    b_ones = Buf()
    T.op("dve", lambda e: e.memset(ones[:], 1.0), writes=[b_ones])
    g2s = sb("g2s", [128, KT], F32); b_g2 = Buf()
    T.dma("sp", g2s[:], g2, writes=[b_g2])
    cws = sb("cws", [128, FT * 3], F32); b_cw = Buf()
    T.dma("sp", cws[:], cw, writes=[b_cw])
    cbs = sb("cbs", [128, FT], F32); b_cb = Buf()
    T.dma("sp", cbs[:], cb, writes=[b_cb])
    rstd = sb("rstd", [128, EXT], F32); b_rstd = Buf()
    xn = sb("xn", [128, KT, EXT], BF16); b_xn = [Buf() for _ in range(KT)]

    chunks = col_chunks(EXT)
    hm_b = [Buf() for _ in range(KT)]

    with contextlib.ExitStack() as st2:
        sb2 = lambda name, shape, dt: st2.enter_context(nc.sbuf_tensor(uniq(name), shape, dt))
        mT = sb2("mT", [128, KT, EXT], BF16); b_mT = [Buf() for _ in range(KT // 4)]
        for k0 in range(0, KT, 4):
            load_m(T, sb2, mT, k0, b_mT[k0 // 4])
        wo = [sb2("wo%d" % i, [128, KT, 128], BF16) for i in range(2)]; b_wo = [Buf(), Buf()]
        hb = [sb2("hb%d" % i, [128, EXT], F32) for i in range(2)]; b_hb = [Buf(), Buf()]
        hm = [sb2("hm%d" % i, [128, EXT], F32) for i in range(2)]; b_hm = [Buf(), Buf()]
        sq = [sb2("sq%d" % i, [128, 512], F32) for i in range(2)]; b_sq = [Buf(), Buf()]
        wv = w_out.rearrange("(k p) n -> p k n", p=128)

        def load(n):
            T.dma("pool", wo[n % 2][:], wv[:, :, n * 128:(n + 1) * 128], writes=[b_wo[n % 2]])
            load_h(T, sb2, hb[n % 2], n, b_hb[n % 2])

        load(0)
        pend = None
        it = 0
        for n in range(KT):
            if n + 1 < KT:
                load(n + 1)
            for j, (c0, c1) in enumerate(chunks):
                w = c1 - c0
                pi = it % 2
                for k in range(KT):
                    T.op("pe", lambda e, k=k: e.matmul(ps[pi][:, :w], wo[n % 2][:, k, :], mT[:, k, c0:c1],
                                                       start=(k == 0), stop=(k == KT - 1)),
                         reads=[b_wo[n % 2], b_mT[k // 4]], writes=[psb[pi]], inc=(k == KT - 1))
                T.op("dve", lambda e: e.tensor_tensor(out=hm[n % 2][:, c0:c1], in0=ps[pi][:, :w],
                                                      in1=hb[n % 2][:, c0:c1], op=ALU.add),
                     reads=[psb[pi], b_hb[n % 2]], writes=[b_hm[n % 2]])
                T.op("act", lambda e: e.activation(out=sq[pi][:, :w], in_=hm[n % 2][:, c0:c1], func=AF.Square),
                     reads=[b_hm[n % 2]], writes=[b_sq[pi]])
                if pend is not None:
                    pend()

                def mk(n=n, j=j, pi=pi, w=w):
                    T.op("pe", lambda e: e.matmul(ps[2 + j][:, :w], ones[:], sq[pi][:, :w],
                                                  start=(n == 0), stop=(n == KT - 1)),
                         reads=[b_ones, b_sq[pi]], writes=[psb[2 + j]])
                pend = mk
                it += 1
            T.dma("sp", hm_scr[n * 128:(n + 1) * 128, :], hm[n % 2][:], reads=[b_hm[n % 2]], writes=[hm_b[n]])
        pend()
        for j, (c0, c1) in enumerate(chunks):
            w = c1 - c0
            T.op("act", lambda e: e.activation(out=rstd[:, c0:c1], in_=ps[2 + j][:, :w], func=AF.Sqrt,
                                               scale=1.0 / D, bias=EPS),
                 reads=[psb[2 + j]], writes=[b_rstd])
        T.op("dve", lambda e: e.reciprocal(out=rstd[:], in_=rstd[:]), reads=[b_rstd], writes=[b_rstd])

        for n in range(KT):
            T.dma("sp", hb[n % 2][:], hm_scr[n * 128:(n + 1) * 128, :], reads=[hm_b[n]], writes=[b_hb[n % 2]])
            T.op("dve", lambda e: e.scalar_tensor_tensor(out=xn[:, n, :], in0=hb[n % 2][:], scalar=g2s[:, n:n + 1],
                                                         in1=rstd[:], op0=ALU.mult, op1=ALU.mult),
                 reads=[b_hb[n % 2], b_g2, b_rstd], writes=[b_xn[n]])

    T.barrier()
    hmid_b = [Buf() for _ in range(FT)]
    with contextlib.ExitStack() as st2:
        sb2 = lambda name, shape, dt: st2.enter_context(nc.sbuf_tensor(uniq(name), shape, dt))
        wg = [sb2("wg%d" % i, [128, KT, 128], BF16) for i in range(2)]; b_wg = [Buf(), Buf()]
        wu = [sb2("wu%d" % i, [128, KT, 128], BF16) for i in range(2)]; b_wu = [Buf(), Buf()]
        ge = [sb2("ge%d" % i, [128, EXT], F32) for i in range(2)]; b_ge = [Buf(), Buf()]
        u = [sb2("u%d" % i, [128, TOK], F32) for i in range(2)]; b_u = [Buf(), Buf()]
        gl = [sb2("gl%d" % i, [128, TOK], F32) for i in range(2)]; b_gl = [Buf(), Buf()]
        hf = [sb2("hf%d" % i, [128, TOK], BF16) for i in range(2)]; b_hf = [Buf(), Buf()]
        wgv = w_gate.rearrange("(k p) n -> p k n", p=128)
        wuv = w_up.rearrange("(k p) n -> p k n", p=128)

        def loadg(f):
            T.dma("pool", wg[f % 2][:], wgv[:, :, f * 128:(f + 1) * 128], writes=[b_wg[f % 2]])
            T.dma("pool", wu[f % 2][:], wuv[:, :, f * 128:(f + 1) * 128], writes=[b_wu[f % 2]])

        loadg(0)
        it = 0
        for f in range(FT):
            if f + 1 < FT:
                loadg(f + 1)
            s = f % 2
            for j, (c0, c1) in enumerate(chunks):
                w = c1 - c0
                pi = it % 4; it += 1
                for k in range(KT):
                    T.op("pe", lambda e, k=k: e.matmul(ps[pi][:, :w], wg[s][:, k, :], xn[:, k, c0:c1],
                                                       start=(k == 0), stop=(k == KT - 1)),
                         reads=[b_wg[s], b_xn[k]], writes=[psb[pi]], inc=(k == KT - 1))
                T.op("act", lambda e: e.activation(out=ge[s][:, c0:c1], in_=ps[pi][:, :w], func=AF.Copy),
                     reads=[psb[pi]], writes=[b_ge[s]])
            T.op("dve", lambda e: e.tensor_scalar(out=u[s][:], in0=ge[s][:, 1:1 + TOK],
                                                  scalar1=cws[:, 3 * f + 1:3 * f + 2], scalar2=cbs[:, f:f + 1],
                                                  op0=ALU.mult, op1=ALU.add),
                 reads=[b_ge[s], b_cw, b_cb], writes=[b_u[s]])
            T.op("dve", lambda e: e.scalar_tensor_tensor(out=u[s][:], in0=ge[s][:, 0:TOK],
                                                         scalar=cws[:, 3 * f:3 * f + 1], in1=u[s][:],
                                                         op0=ALU.mult, op1=ALU.add),
                 reads=[b_ge[s], b_cw, b_u[s]], writes=[b_u[s]])
            T.op("dve", lambda e: e.scalar_tensor_tensor(out=u[s][:], in0=ge[s][:, 2:2 + TOK],
                                                         scalar=cws[:, 3 * f + 2:3 * f + 3], in1=u[s][:],
                                                         op0=ALU.mult, op1=ALU.add),
                 reads=[b_ge[s], b_cw, b_u[s]], writes=[b_u[s]])
            T.op("act", lambda e: e.activation(out=gl[s][:], in_=u[s][:], func=AF.Gelu_apprx_tanh),
                 reads=[b_u[s]], writes=[b_gl[s]])
            for j in range(TOK // 512):
                c0 = 1 + 512 * j
                pi = 4 + (it % 4); it += 1
                for k in range(KT):
                    T.op("pe", lambda e, k=k: e.matmul(ps[pi][:, :], wu[s][:, k, :], xn[:, k, c0:c0 + 512],
                                                       start=(k == 0), stop=(k == KT - 1)),
                         reads=[b_wu[s], b_xn[k]], writes=[psb[pi]], inc=(k == KT - 1))
                T.op("dve", lambda e: e.tensor_tensor(out=hf[s][:, 512 * j:512 * (j + 1)], in0=ps[pi][:, :],
                                                      in1=gl[s][:, 512 * j:512 * (j + 1)], op=ALU.mult),
                     reads=[psb[pi], b_gl[s]], writes=[b_hf[s]])
            T.dma("sp", hmid_scr[f * 128:(f + 1) * 128, :], hf[s][:], reads=[b_hf[s]], writes=[hmid_b[f]])

    T.barrier()
    with contextlib.ExitStack() as st2:
        sb2 = lambda name, shape, dt: st2.enter_context(nc.sbuf_tensor(uniq(name), shape, dt))
        CH = 1024
        hms = sb2("hms", [128, FT, CH], BF16); b_hms = [Buf() for _ in range(4)]
        wd = [sb2("wd%d" % i, [128, FT, 128], BF16) for i in range(2)]; b_wd = [Buf(), Buf()]
        hr = [sb2("hr%d" % i, [128, CH], F32) for i in range(2)]; b_hr = [Buf(), Buf()]
        ot = [sb2("ot%d" % i, [128, CH], F32) for i in range(2)]; b_ot = [Buf(), Buf()]
        hv = hmid_scr.rearrange("(f p) t -> p f t", p=128)
        wdv = w_down.rearrange("(f p) n -> p f n", p=128)
        it = 0
        items = [(c, n) for c in range(TOK // CH) for n in range(KT)]

        def loadd(idx):
            c, n = items[idx]
            T.dma("pool", wd[idx % 2][:], wdv[:, :, n * 128:(n + 1) * 128], writes=[b_wd[idx % 2]])
            T.dma("sp", hr[idx % 2][:], hm_scr[n * 128:(n + 1) * 128, 1 + c * CH:1 + (c + 1) * CH],
                  reads=[hm_b[n]], writes=[b_hr[idx % 2]])

        loadd(0)
        for idx, (c, n) in enumerate(items):
            if n == 0:
                for f0 in range(0, FT, 11):
                    T.dma("sp", hms[:, f0:f0 + 11, :], hv[:, f0:f0 + 11, c * CH:(c + 1) * CH],
                          reads=hmid_b[f0:f0 + 11], writes=[b_hms[f0 // 11]])
            if idx + 1 < len(items):
                loadd(idx + 1)
            s = idx % 2
            for jj in range(CH // 512):
                pi = it % 4; it += 1
                for f in range(FT):
                    T.op("pe", lambda e, f=f: e.matmul(ps[pi][:, :], wd[s][:, f, :], hms[:, f, 512 * jj:512 * (jj + 1)],
                                                       start=(f == 0), stop=(f == FT - 1)),
                         reads=[b_wd[s], b_hms[f // 11]], writes=[psb[pi]], inc=(f == FT - 1))
                T.op("dve", lambda e: e.tensor_tensor(out=ot[s][:, 512 * jj:512 * (jj + 1)], in0=ps[pi][:, :],
                                                      in1=hr[s][:, 512 * jj:512 * (jj + 1)], op=ALU.add),
                     reads=[psb[pi], b_hr[s]], writes=[b_ot[s]])
            tk = T.dma("sp", hT_out[n * 128:(n + 1) * 128, c * CH:(c + 1) * CH], ot[s][:], reads=[b_ot[s]])
            if on_store is not None:
                on_store(n, c, TOK // CH, tk)


def build_post(debug=False):
    nc = bass.Bass("TRN2", target_bir_lowering=False)
    dt = lambda name, shape, dtype, kind: nc.dram_tensor(name, shape, dtype, kind=kind).ap()
    hT_ext = dt("hT_ext", [D, EXT], F32, "ExternalInput")
    mT_ext = dt("mT_ext", [D, EXT], BF16, "ExternalInput")
    w_out = dt("w_out", [D, D], F32, "ExternalInput")
    g2 = dt("g2", [128, KT], F32, "ExternalInput")
    w_gate = dt("w_gate", [D, FF], F32, "ExternalInput")
    w_up = dt("w_up", [D, FF], F32, "ExternalInput")
    cw = dt("cw", [128, FT * 3], F32, "ExternalInput")
    cb = dt("cb", [128, FT], F32, "ExternalInput")
    w_down = dt("w_down", [FF, D], F32, "ExternalInput")
    hT_out = dt("hT_out", [D, TOK], F32, "ExternalOutput")
    hmid_scr = dt("hmid_scr", [FF, TOK], BF16, "ExternalOutput" if debug else "Internal")
    hm_scr = dt("hm_scr", [D, EXT], F32, "ExternalOutput" if debug else "Internal")
    with contextlib.ExitStack() as stack:
        T = Trk(nc, stack)
        ps = [stack.enter_context(nc.psum_tensor("ps%d" % i, [128, 512], F32)) for i in range(8)]
        mv = mT_ext.rearrange("(k p) t -> p k t", p=128)
        load_m = lambda T, sb2, mT, k0, b: T.dma("sp", mT[:, k0:k0 + 4, :], mv[:, k0:k0 + 4, :], writes=[b])
        load_h = lambda T, sb2, hb, n, b: T.dma("sp", hb[:], hT_ext[n * 128:(n + 1) * 128, :], writes=[b])
        emit_post(nc, T, stack, ps, load_m, load_h, w_out, g2, w_gate, w_up, cw, cb, w_down, hT_out, hmid_scr, hm_scr)
        T.finish()
    return nc


def lay128(v):
    v = np.asarray(v)
    return np.ascontiguousarray(v.reshape(-1, 128).T)


class Rot:
    def __init__(self, tiles):
        self.tiles = tiles
        self.bufs = [Buf() for _ in tiles]
        self.i = 0

    def next(self):
        i = self.i
        self.i = (i + 1) % len(self.tiles)
        return self.tiles[i], self.bufs[i]


def mm_group(T, out_ap, pairs, pbuf, reads):
    n = len(pairs)
    for i, (l, r) in enumerate(pairs):
        T.op("pe", lambda e: e.matmul(out_ap, l, r, start=(i == 0), stop=(i == n - 1)),
             reads=reads, writes=[pbuf], inc=(i == n - 1))


def emit_norm_chunk(nc, T, hc, b_hc, g1s, b_g1, ones, b_ones, PS, sqr, rstd_t, b_rstd, xc, b_xc, width):
    pt, pb = PS.next()
    for k in range(KT):
        sq, b_sq = sqr.next()
        T.op("act", lambda e: e.activation(out=sq[:, :width], in_=hc[:, k, :width], func=AF.Square),
             reads=[b_hc], writes=[b_sq])
        T.op("pe", lambda e: e.matmul(pt[:, :width], ones[:], sq[:, :width], start=(k == 0), stop=(k == KT - 1)),
             reads=[b_ones, b_sq], writes=[pb])
    T.op("act", lambda e: e.activation(out=rstd_t[:, :width], in_=pt[:, :width], func=AF.Sqrt, scale=1.0 / D, bias=EPS),
         reads=[pb], writes=[b_rstd])
    T.op("dve", lambda e: e.reciprocal(out=rstd_t[:, :width], in_=rstd_t[:, :width]), reads=[b_rstd], writes=[b_rstd])
    for k in range(KT):
        T.op("dve", lambda e: e.scalar_tensor_tensor(out=xc[:, k, :width], in0=hc[:, k, :width], scalar=g1s[:, k:k + 1],
                                                     in1=rstd_t[:, :width], op0=ALU.mult, op1=ALU.mult),
             reads=[b_hc, b_g1, b_rstd], writes=[b_xc])


NW_E = 2080
GLA_SCALE = 128 ** -0.5


def emit_even(nc, T, stack, ps, hload, g1, w_e, gua_f, gua_b, glag, cs, cosm, nsinm, mf, mb, mixT, scr, on_rows=None):
    sb = lambda name, shape, dt: stack.enter_context(nc.sbuf_tensor(uniq(name), shape, dt))
    PS = Rot(ps)
    aT_s, qT_s, kT_s, k_s, v_s, sgT_s, la_s = scr
    NCH = SEQ // 512

    ones = sb("ones", [128, 128], F32); b_ones = Buf()
    T.op("dve", lambda e: e.memset(ones[:], 1.0), writes=[b_ones])
    g1s = sb("g1s", [128, KT], F32); b_g1 = Buf()
    T.dma("sp", g1s[:], g1, writes=[b_g1])
    glags = sb("glags", [128, 2], F32); b_glag = Buf()
    T.dma("sp", glags[:], glag, writes=[b_glag])
    mfs = sb("mfs", [128, 128], BF16); b_mf = Buf()
    T.dma("sp", mfs[:], mf, writes=[b_mf])
    mbs = sb("mbs", [128, 128], BF16); b_mb = Buf()
    T.dma("sp", mbs[:], mb, writes=[b_mb])

    with contextlib.ExitStack() as st2:
        sb2 = lambda name, shape, dt: st2.enter_context(nc.sbuf_tensor(uniq(name), shape, dt))
        we = sb2("we", [128, KT, NW_E], BF16); b_we = [Buf() for _ in range(4)]
        wv = w_e.rearrange("(k p) n -> p k n", p=128)
        for k0 in range(0, KT, 4):
            T.dma("pool", we[:, k0:k0 + 4, :], wv[:, k0:k0 + 4, :], writes=[b_we[k0 // 4]])
        guaf = sb2("guaf", [17, 256], BF16); b_guaf = Buf()
        T.dma("pool", guaf[:], gua_f, writes=[b_guaf])
        guab = sb2("guab", [17, 256], BF16); b_guab = Buf()
        T.dma("pool", guab[:], gua_b, writes=[b_guab])
        zfa = sb2("zfa", [17, 512], BF16); b_zfa = Buf()
        zba = sb2("zba", [17, 512], BF16); b_zba = Buf()
        T.op("dve", lambda e: e.memset(zfa[:], 1.0), writes=[b_zfa])
        T.op("dve", lambda e: e.memset(zba[:], 1.0), writes=[b_zba])
        hcs = [sb2("hc%d" % i, [128, KT, 512], F32) for i in range(2)]; b_hcs = [Buf(), Buf()]
        xc = sb2("xc", [128, KT, 512], BF16); b_xc = Buf()
        sqr = Rot([sb2("sq%d" % i, [128, 512], F32) for i in range(2)])
        rstd_t = sb2("rstd", [128, 512], F32); b_rstd = Buf()
        stg = Rot([sb2("stg%d" % i, [128, 512], BF16) for i in range(4)])
        lt = Rot([sb2("lt%d" % i, [128, 256], F32) for i in range(2)])
        hload(T, hcs[0], 0, b_hcs[0])
        we_reads = list(b_we)
        for c in range(NCH):
            t0 = c * 512
            hc, b_hc = hcs[c % 2], b_hcs[c % 2]
            if c + 1 < NCH:
                hload(T, hcs[(c + 1) % 2], c + 1, b_hcs[(c + 1) % 2])
            emit_norm_chunk(nc, T, hc, b_hc, g1s, b_g1, ones, b_ones, PS, sqr, rstd_t, b_rstd, xc, b_xc, 512)

            def fm(col0, dst, row0, func=AF.Copy, scale=1.0):
                pt, pb = PS.next()
                mm_group(T, pt[:, :], [(we[:, k, col0:col0 + 128], xc[:, k, :]) for k in range(KT)], pb,
                         reads=we_reads + [b_xc])
                s, b_s = stg.next()
                T.op("act", lambda e: e.activation(out=s[:, :], in_=pt[:, :], func=func, scale=scale),
                     reads=[pb], writes=[b_s])
                T.dma("sp", dst[row0:row0 + 128, t0:t0 + 512], s[:, :], reads=[b_s])
            for i in range(4):
                fm(i * 128, aT_s, i * 128)
            for i in range(2):
                fm(512 + i * 128, qT_s, i * 128, scale=GLA_SCALE)
            for i in range(2):
                fm(768 + i * 128, kT_s, i * 128)
            for i in range(4):
                fm(1536 + i * 128, sgT_s, i * 128, func=AF.Silu)
            for (col0, za, b_za) in ((2048, zfa, b_zfa), (2064, zba, b_zba)):
                pt, pb = PS.next()
                mm_group(T, pt[0:16, :], [(we[:, k, col0:col0 + 16], xc[:, k, :]) for k in range(KT)], pb,
                         reads=we_reads + [b_xc])
                T.op("act", lambda e: e.activation(out=za[0:16, :], in_=pt[0:16, :], func=AF.Copy), reads=[pb], writes=[b_za])
            for ts in range(4):
                r0 = t0 + ts * 128
                for (col0, ncol, dst) in ((768, 256, k_s), (1024, 512, v_s)):
                    pt, pb = PS.next()
                    mm_group(T, pt[:, :ncol], [(xc[:, k, ts * 128:(ts + 1) * 128], we[:, k, col0:col0 + ncol])
                                               for k in range(KT)], pb, reads=we_reads + [b_xc])
                    s, b_s = stg.next()
                    T.op("dve", lambda e: e.tensor_copy(out=s[:, :ncol], in_=pt[:, :ncol]), reads=[pb], writes=[b_s])
                    T.dma("sp", dst[r0:r0 + 128, :], s[:, :ncol], reads=[b_s])
                for d, (za, b_za, gu, b_gu) in enumerate(((zfa, b_zfa, guaf, b_guaf), (zba, b_zba, guab, b_guab))):
                    pt, pb = PS.next()
                    T.op("pe", lambda e: e.matmul(pt[:, :256], za[0:17, ts * 128:(ts + 1) * 128], gu[0:17, :],
                                                  start=True, stop=True), reads=[b_za, b_gu], writes=[pb])
                    l, b_l = lt.next()
                    T.op("act", lambda e: e.activation(out=l[:, :], in_=pt[:, :256], func=AF.Exp, scale=-1.0),
                         reads=[pb], writes=[b_l])
                    T.op("act", lambda e: e.activation(out=l[:, :], in_=l[:, :], func=AF.Ln, bias=1.0),
                         reads=[b_l], writes=[b_l])
                    s, b_s = stg.next()
                    T.op("dve", lambda e: e.tensor_scalar(out=s[:, :256], in0=l[:, :], scalar1=-1.0 / 16.0, scalar2=None,
                                                          op0=ALU.mult), reads=[b_l], writes=[b_s])
                    T.dma("sp", la_s[d, r0:r0 + 128, :], s[:, :256], reads=[b_s])
    T.barrier()

    with contextlib.ExitStack() as st2:
        sb2 = lambda name, shape, dt: st2.enter_context(nc.sbuf_tensor(uniq(name), shape, dt))
        aT = sb2("aT", [128, 4, SEQ], BF16); b_aT = Buf()
        T.dma("sp", aT[:], aT_s.rearrange("(k p) t -> p k t", p=128), writes=[b_aT])
        css = sb2("css", [128, 2, 512], BF16); b_cs = Buf()
        T.dma("sp", css[:], cs.rearrange("(k p) n -> p k n", p=128), writes=[b_cs])
        Z = sb2("Z", [128, 32, 2, 512], BF16); b_Z = Buf()
        for g in range(2):
            for st in range(32):
                pt, pb = PS.next()
                mm_group(T, pt[:, :], [(aT[:, 2 * g + kk, st * 128:(st + 1) * 128], css[:, kk, :]) for kk in range(2)],
                         pb, reads=[b_aT, b_cs])
                if st % 2 == 0:
                    T.op("act", lambda e: e.activation(out=Z[:, st, g, :], in_=pt[:, :], func=AF.Copy),
                         reads=[pb], writes=[b_Z])
                else:
                    T.op("dve", lambda e: e.tensor_copy(out=Z[:, st, g, :], in_=pt[:, :]), reads=[pb], writes=[b_Z])
        SG = 4
        if DBG.get("skip_fnet2"):
            SEQ_ = 0
        else:
            SEQ_ = SEQ
        cbuf = Rot([sb2("cb%d" % i, [128, 2, SG, 512], BF16) for i in range(3)])
        stg = Rot([sb2("stg%d" % i, [128, 512], BF16) for i in range(4)])
        cv = cosm.rearrange("(st p) n -> p st n", p=128)
        sv = nsinm.rearrange("(st p) n -> p st n", p=128)
        pieces = [(sp_, sg) for sp_ in range(SEQ_ // 512) for sg in range(32 // SG)]

        def loadc(idx):
            sp_, sg = pieces[idx]
            t, b = cbuf.next()
            T.dma("sp", t[:, 0, :, :], cv[:, sg * SG:(sg + 1) * SG, sp_ * 512:(sp_ + 1) * 512], writes=[b])
            T.dma("sp", t[:, 1, :, :], sv[:, sg * SG:(sg + 1) * SG, sp_ * 512:(sp_ + 1) * 512], writes=[b])
            return t, b
        q = [loadc(0), loadc(1)] if pieces else []
        for idx, (sp_, sg) in enumerate(pieces):
            if idx + 2 < len(pieces):
                q.append(loadc(idx + 2))
            t, b = q.pop(0)
            bank0 = (sp_ % 2) * 4
            for ct in range(4):
                g, hf = ct // 2, ct % 2
                for s_ in range(SG):
                    stile = sg * SG + s_
                    for part in range(2):
                        first = (sg == 0 and s_ == 0 and part == 0)
                        last = (sg == 32 // SG - 1 and s_ == SG - 1 and part == 1)
                        T.op("pe", lambda e: e.matmul(ps[bank0 + ct][:, :],
                                                      Z[:, stile, g, part * 256 + hf * 128: part * 256 + hf * 128 + 128],
                                                      t[:, part, s_, :], start=first, stop=last),
                             reads=[b_Z, b], writes=[PS.bufs[bank0 + ct]], inc=last or (s_ == SG - 1 and part == 1 and ct == 3))
            if sg == 32 // SG - 1:
                for ct in range(4):
                    s, b_s = stg.next()
                    T.op("act" if ct % 2 == 0 else "dve",
                         (lambda e: e.activation(out=s[:, :], in_=ps[bank0 + ct][:, :], func=AF.Copy)) if ct % 2 == 0 else
                         (lambda e: e.tensor_copy(out=s[:, :], in_=ps[bank0 + ct][:, :])),
                         reads=[PS.bufs[bank0 + ct]], writes=[b_s])
                    T.dma("sp", mixT[ct * 128:(ct + 1) * 128, sp_ * 512:(sp_ + 1) * 512], s[:, :], reads=[b_s])
    T.barrier()
    if on_rows is not None:
        on_rows(0, []); on_rows(1, [])

    with contextlib.ExitStack() as st2:
        sb2 = lambda name, shape, dt: st2.enter_context(nc.sbuf_tensor(uniq(name), shape, dt))
        qT = sb2("qT", [128, SEQ], BF16); kT = sb2("kT", [128, SEQ], BF16)
        kk_ = sb2("ktok", [128, 32, 128], BF16); vv = sb2("vtok", [128, 32, 256], BF16)
        la = [sb2("la%d" % d, [128, 32, 128], BF16) for d in range(2)]
        sg_ = sb2("sgT", [128, 2, SEQ], BF16)
        oacc = sb2("oacc", [128, 2, SEQ], F32)
        b_in = Buf(); b_sg = Buf()
        b_o = [Buf() for _ in range(32)]
        S32 = [sb2("S32_%d" % d, [128, 256], F32) for d in range(2)]; b_S32 = [Buf(), Buf()]
        Sbf = [sb2("Sbf_%d" % d, [128, 256], BF16) for d in range(2)]; b_Sbf = [Buf(), Buf()]
        tmpS = [sb2("tmpS_%d" % d, [128, 256], F32) for d in range(2)]; b_tmpS = [Buf(), Buf()]
        mk_t = lambda nm, dt_, n=6: Rot([sb2("%s%d" % (nm, i), [128, 128], dt_) for i in range(n)])
        ebR, enbR, entR = mk_t("eb", F32, 6), mk_t("enb", F32, 4), mk_t("ent", F32, 4)
        qtR, ktR, ktokR, pR = mk_t("qt", BF16), mk_t("kt", BF16), mk_t("ktk", BF16), mk_t("pp", BF16)
        elR = Rot([sb2("el%d" % i, [128, 2], F32) for i in range(4)])
        sq2 = Rot([sb2("sqq%d" % i, [128, 512], F32) for i in range(2)])
        rs2 = sb2("rs2", [128, 512], F32); b_rs2 = Buf()
        tm2 = Rot([sb2("tm%d" % i, [128, 512], F32) for i in range(2)])
        stg = Rot([sb2("stg%d" % i, [128, 512], BF16) for i in range(2)])
        masks = ((mfs, b_mf), (mbs, b_mb))
        kmR = mk_t("km", BF16)
        NHEAD_ = 0 if DBG.get("skip_gla") else 2
        cmask = sb2("cmask", [128, 2], F32); b_cmask = Buf()
        T.op("dve", lambda e: e.memset(cmask[:], 0.0), writes=[b_cmask])
        T.op("dve", lambda e: e.memset(cmask[0:64, 0:1], 1.0), writes=[b_cmask])
        T.op("dve", lambda e: e.memset(cmask[64:128, 1:2], 1.0), writes=[b_cmask])
        for hh in range(NHEAD_):
            T.dma("sp", qT[:], qT_s[hh * 128:(hh + 1) * 128, :], writes=[b_in])
            T.dma("sp", kT[:], kT_s[hh * 128:(hh + 1) * 128, :], writes=[b_in])
            T.dma("sp", kk_[:], k_s.rearrange("(i p) c -> p i c", p=128)[:, :, hh * 128:(hh + 1) * 128], writes=[b_in])
            T.dma("sp", vv[:], v_s.rearrange("(i p) c -> p i c", p=128)[:, :, hh * 256:(hh + 1) * 256], writes=[b_in])
            for d in range(2):
                T.dma("sp", la[d][:], la_s[d].rearrange("(i p) c -> p i c", p=128)[:, :, hh * 128:(hh + 1) * 128],
                      writes=[b_in])
            T.dma("sp", sg_[:], sgT_s.rearrange("(k p) t -> p k t", p=128)[:, 2 * hh:2 * hh + 2, :], writes=[b_sg])
            for d in range(2):
                T.op("dve", lambda e: e.memset(S32[d][:], 0.0), writes=[b_S32[d]])
                T.op("dve", lambda e: e.memset(Sbf[d][:], 0.0), writes=[b_Sbf[d]])
                T.op("dve", lambda e: e.memset(tmpS[d][:], 0.0), writes=[b_tmpS[d]])
            written = [False] * 32
            def phase12(step):
                    ctx = []
                    for d in range(2):
                        i = step if d == 0 else 31 - step
                        M, b_M = masks[d]
                        c0 = i * 128
                        pbT, bbT = PS.next()
                        T.op("pe", lambda e: e.matmul(pbT[:, :128], la[d][:, i, :], M[:, :], start=True, stop=True),
                             reads=[b_in, b_M], writes=[bbT])
                        pbk, bbk = PS.next()
                        T.op("pe", lambda e: e.matmul(pbk[:, :128], M[:, :], la[d][:, i, :], start=True, stop=True),
                             reads=[b_in, b_M], writes=[bbk])
                        eb, b_eb = ebR.next(); enb, b_enb = enbR.next(); ent, b_ent = entR.next()
                        T.op("act", lambda e: e.activation(out=eb[:, :], in_=pbT[:, :128], func=AF.Exp), reads=[bbT], writes=[b_eb])
                        T.op("act", lambda e: e.activation(out=enb[:, :], in_=pbT[:, :128], func=AF.Exp, scale=-1.0),
                             reads=[bbT], writes=[b_enb])
                        T.op("act", lambda e: e.activation(out=ent[:, :], in_=pbk[:, :128], func=AF.Exp, scale=-1.0),
                             reads=[bbk], writes=[b_ent])
                        qt, b_qt = qtR.next(); kt, b_kt = ktR.next(); ktok, b_ktok = ktokR.next()
                        T.op("dve", lambda e: e.tensor_tensor(out=qt[:, :], in0=qT[:, c0:c0 + 128], in1=eb[:, :], op=ALU.mult),
                             reads=[b_in, b_eb], writes=[b_qt])
                        T.op("dve", lambda e: e.tensor_tensor(out=kt[:, :], in0=kT[:, c0:c0 + 128], in1=enb[:, :], op=ALU.mult),
                             reads=[b_in, b_enb], writes=[b_kt])
                        T.op("dve", lambda e: e.tensor_tensor(out=ktok[:, :], in0=kk_[:, i, :], in1=ent[:, :], op=ALU.mult),
                             reads=[b_in, b_ent], writes=[b_ktok])
                        ecols = (63, 127) if d == 0 else (0, 64)
                        ctx.append((i, c0, M, b_M, qt, b_qt, kt, b_kt, ktok, b_ktok, (eb, ecols), b_eb))
                    pps = []
                    for d in range(2):
                        i, c0, M, b_M, qt, b_qt, kt, b_kt, ktok, b_ktok, el, b_el = ctx[d]
                        psc, bsc = PS.next()
                        T.op("pe", lambda e: e.matmul(psc[:, :128], kt[:, :], qt[:, :], start=True, stop=True),
                             reads=[b_kt, b_qt], writes=[bsc])
                        pp, b_pp = pR.next()
                        T.op("dve", lambda e: e.tensor_tensor(out=pp[:, :], in0=psc[:, :128], in1=M[:, :], op=ALU.mult),
                             reads=[bsc, b_M], writes=[b_pp])
                        pps.append((pp, b_pp))
                    return ctx, pps

            def phase34(step, ctx, pps, nctx):
                    pos = []
                    for d in range(2):
                        i = ctx[d][0]
                        po, bo = PS.next()
                        pp, b_pp = pps[d]
                        for vt in range(2):
                            T.op("pe", lambda e: e.matmul(po[:, vt * 128:(vt + 1) * 128], vv[:, i, vt * 128:(vt + 1) * 128], pp[:, :],
                                                          start=True, stop=True),
                                 reads=[b_in, b_pp], writes=[bo], inc=False)
                        pos.append((po, bo))
                    for half in range(2):
                        for d in range(2):
                            if DBG.get("no_inter"):
                                if half == 1:
                                    po, bo = pos[d]
                                    T.op("pe", lambda e: e.matmul(po[:, 256:320], Sbf[d][:, 0:128], ctx[d][4][:, 0:64], start=True, stop=True),
                                         reads=[b_Sbf[d]], writes=[bo])
                                continue
                            i, c0, M, b_M, qt, b_qt, kt, b_kt, ktok, b_ktok, el, b_el = ctx[d]
                            po, bo = pos[d]
                            ch = half if d == 0 else 1 - half
                            r0 = ch * 64
                            for vt in range(2):
                                T.op("pe", lambda e: e.matmul(po[:, 256 + vt * 128 + r0: 256 + vt * 128 + r0 + 64],
                                                              Sbf[d][:, vt * 128:(vt + 1) * 128], qt[:, r0:r0 + 64],
                                                              start=True, stop=True),
                                     reads=[b_Sbf[d], b_qt], writes=[bo], inc=(half == 1 and vt == 1))
                            pd, bd = PS.next()
                            T.op("pe", lambda e: e.matmul(pd[:, :256], ktok[r0:r0 + 64, :], vv[r0:r0 + 64, i, :],
                                                          start=True, stop=True), reads=[b_ktok, b_in], writes=[bd])
                            ebt, ecols = el
                            esc = ebt[:, ecols[ch]:ecols[ch] + 1]
                            T.op("dve", lambda e: e.scalar_tensor_tensor(out=Sbf[d][:], in0=pd[:, :256], scalar=esc,
                                                                         in1=tmpS[d][:], op0=ALU.mult, op1=ALU.add),
                                 reads=[bd, b_el, b_tmpS[d]], writes=[b_Sbf[d]])
                            T.op("dve", lambda e: e.scalar_tensor_tensor(out=S32[d][:], in0=pd[:, :256], scalar=esc,
                                                                         in1=tmpS[d][:], op0=ALU.mult, op1=ALU.add),
                                 reads=[bd, b_el, b_tmpS[d]], writes=[b_S32[d]])
                            if half == 0:
                                nel, nb_el, nch = el, b_el, 1 - ch
                            elif nctx is not None:
                                nel, nb_el = nctx[d][10], nctx[d][11]
                                nch = 0 if d == 0 else 1
                            else:
                                nel = None
                            if nel is not None:
                                nesc = nel[0][:, nel[1][nch]:nel[1][nch] + 1]
                                T.op("dve", lambda e: e.tensor_scalar(out=tmpS[d][:], in0=S32[d][:], scalar1=nesc, scalar2=None,
                                                                      op0=ALU.mult),
                                     reads=[b_S32[d], nb_el], writes=[b_tmpS[d]])
                    for d in range(2):
                        i, c0 = ctx[d][0], ctx[d][1]
                        po, bo = pos[d]
                        o3 = oacc[:, :, c0:c0 + 128]
                        p_intra = po[:, 0:256].rearrange("p (v t) -> p v t", v=2)
                        p_inter = po[:, 256:512].rearrange("p (v t) -> p v t", v=2)
                        if not written[i]:
                            T.op("act", lambda e: e.activation(out=o3, in_=p_intra, func=AF.Copy), reads=[bo], writes=[b_o[i]])
                        else:
                            T.op("dve", lambda e: e.tensor_tensor(out=o3, in0=p_intra, in1=o3, op=ALU.add),
                                 reads=[bo, b_o[i]], writes=[b_o[i]])
                        T.op("dve", lambda e: e.tensor_tensor(out=o3, in0=p_inter, in1=o3, op=ALU.add),
                             reads=[bo, b_o[i]], writes=[b_o[i]])
                        written[i] = True

            nxt = phase12(0)
            for step in range(32):
                cur = nxt
                nxt = phase12(step + 1) if step + 1 < 32 else None
                phase34(step, cur[0], cur[1], nxt[0] if nxt is not None else None)
            if DBG.get("dump") is not None and hh == 0:
                do, dS = DBG["dump"]
                for vt in range(2):
                    T.dma("sp", do[vt * 128:(vt + 1) * 128, :], oacc[:, vt, :], reads=b_o)
                for d in range(2):
                    T.dma("sp", dS[d], S32[d][:], reads=[b_S32[d]])
            row_toks = []
            for c in range(NCH):
                t0 = c * 512
                pt, pb = PS.next()
                for vt in range(2):
                    sq, b_sq = sq2.next()
                    T.op("act", lambda e: e.activation(out=sq[:, :], in_=oacc[:, vt, t0:t0 + 512], func=AF.Square),
                         reads=b_o[4 * c:4 * c + 4], writes=[b_sq])
                    T.op("pe", lambda e: e.matmul(pt[:, :], ones[:], sq[:, :], start=(vt == 0), stop=(vt == 1)),
                         reads=[b_ones, b_sq], writes=[pb])
                T.op("act", lambda e: e.activation(out=rs2[:, :], in_=pt[:, :], func=AF.Sqrt, scale=1.0 / 256, bias=EPS),
                     reads=[pb], writes=[b_rs2])
                T.op("dve", lambda e: e.reciprocal(out=rs2[:, :], in_=rs2[:, :]), reads=[b_rs2], writes=[b_rs2])
                for vt in range(2):
                    tm, b_tm = tm2.next()
                    T.op("dve", lambda e: e.scalar_tensor_tensor(out=tm[:, :], in0=oacc[:, vt, t0:t0 + 512],
                                                                 scalar=glags[:, vt:vt + 1], in1=rs2[:, :],
                                                                 op0=ALU.mult, op1=ALU.mult),
                         reads=b_o[4 * c:4 * c + 4] + [b_glag, b_rs2], writes=[b_tm])
                    s, b_s = stg.next()
                    T.op("dve", lambda e: e.tensor_tensor(out=s[:, :], in0=tm[:, :], in1=sg_[:, vt, t0:t0 + 512], op=ALU.mult),
                         reads=[b_tm, b_sg], writes=[b_s])
                    r = 512 + hh * 256 + vt * 128
                    row_toks.append(T.dma("sp", mixT[r:r + 128, t0:t0 + 512], s[:, :], reads=[b_s]))
            if on_rows is not None:
                on_rows(2 + hh, row_toks)


def build_even(debug=False):
    nc = bass.Bass("TRN2", target_bir_lowering=False)
    dt = lambda name, shape, dtype, kind="ExternalInput": nc.dram_tensor(name, shape, dtype, kind=kind).ap()
    hT = dt("hT", [D, SEQ], F32)
    g1 = dt("g1", [128, KT], F32)
    w_e = dt("w_e", [D, NW_E], F32)
    gua_f = dt("gua_f", [17, 256], F32)
    gua_b = dt("gua_b", [17, 256], F32)
    glag = dt("glag", [128, 2], F32)
    cs = dt("cs", [256, 512], BF16)
    cosm = dt("cosm", [SEQ, SEQ], BF16)
    nsinm = dt("nsinm", [SEQ, SEQ], BF16)
    mf = dt("mf", [128, 128], BF16)
    mb = dt("mb", [128, 128], BF16)
    mixT = dt("mixT", [1024, SEQ], BF16, "ExternalOutput")
    kind = "ExternalOutput" if debug else "Internal"
    scr = (dt("aT_s", [512, SEQ], BF16, kind), dt("qT_s", [256, SEQ], BF16, kind), dt("kT_s", [256, SEQ], BF16, kind),
           dt("k_s", [SEQ, 256], BF16, kind), dt("v_s", [SEQ, 512], BF16, kind), dt("sgT_s", [512, SEQ], BF16, kind),
           dt("la_s", [2, SEQ, 256], BF16, kind))
    if debug:
        DBG["dump"] = (dt("dbg_o", [256, SEQ], F32, "ExternalOutput"), dt("dbg_S", [2, 128, 256], F32, "ExternalOutput"))
    with contextlib.ExitStack() as stack:
        T = Trk(nc, stack)
        ps = [stack.enter_context(nc.psum_tensor("ps%d" % i, [128, 512], F32)) for i in range(8)]
        hv = hT.rearrange("(k p) t -> p k t", p=128)
        hload = lambda T, t, c, b: T.dma("sp", t[:], hv[:, :, c * 512:(c + 1) * 512], writes=[b])
        emit_even(nc, T, stack, ps, hload, g1, w_e, gua_f, gua_b, glag, cs, cosm, nsinm, mf, mb, mixT, scr)
        T.finish()
    return nc


def even_consts():
    c = np.arange(256)
    ang = 2 * np.pi * ((c[:, None] * c[None, :]) % 256) / 256.0
    cs = np.concatenate([np.cos(ang), np.sin(ang)], axis=1) / 16.0
    s = np.arange(SEQ, dtype=np.int64)
    ang = 2 * np.pi * ((s[:, None] * s[None, :]) % SEQ) / float(SEQ)
    cosm = (np.cos(ang) / 64.0).astype(NPBF)
    nsinm = (-np.sin(ang) / 64.0).astype(NPBF)
    p = np.arange(128)
    same = (p[:, None] // 64) == (p[None, :] // 64)
    mf = (same & (p[:, None] <= p[None, :])).astype(np.float32).astype(NPBF)
    mb = (same & (p[:, None] >= p[None, :])).astype(np.float32).astype(NPBF)
    return dict(cs=cs.astype(NPBF), cosm=cosm, nsinm=nsinm, mf=mf, mb=mb)


def even_weights(j, w_in, gu_f, gb_f, gu_b, gb_b, gla_g):
    cols = np.r_[512 * j:512 * j + 512, 1024 + 256 * j:1024 + 256 * j + 256, 1536 + 256 * j:1536 + 256 * j + 256,
                 2048 + 512 * j:2048 + 512 * j + 512, 3072 + 512 * j:3072 + 512 * j + 512, 4096:4128]
    hc = slice(256 * j, 256 * j + 256)
    return dict(w_e=np.ascontiguousarray(w_in[:, cols]),
                gua_f=np.ascontiguousarray(np.concatenate([gu_f[:, hc], gb_f[None, hc]], 0)),
                gua_b=np.ascontiguousarray(np.concatenate([gu_b[:, hc], gb_b[None, hc]], 0)),
                glag=lay128(gla_g))


NBU = 3072
NKT = 20


def emit_odd(nc, T, stack, ps, hload, g1, w_o, qg, kg, rb, onehot, cmult, mixT, scr, on_rows=None):
    sb = lambda name, shape, dt: stack.enter_context(nc.sbuf_tensor(uniq(name), shape, dt))
    PS = Rot(ps)
    xn_s, qT_s, kT_s, v_s, u_s = scr
    NCH = SEQ // 512
    ones = sb("ones", [128, 128], F32); b_ones = Buf()
    T.op("dve", lambda e: e.memset(ones[:], 1.0), writes=[b_ones])
    onesb = sb("onesb", [128, 128], BF16); b_onesb = Buf()
    T.op("dve", lambda e: e.memset(onesb[:], 1.0), writes=[b_onesb])
    g1s = sb("g1s", [128, KT], F32); b_g1 = Buf()
    T.dma("sp", g1s[:], g1, writes=[b_g1])
    qgs = sb("qgs", [128, 1], F32); b_qg = Buf()
    T.dma("sp", qgs[:], qg, writes=[b_qg])
    kgs = sb("kgs", [128, 1], F32); b_kg = Buf()
    T.dma("sp", kgs[:], kg, writes=[b_kg])

    with contextlib.ExitStack() as st2:
        sb2 = lambda name, shape, dt: st2.enter_context(nc.sbuf_tensor(uniq(name), shape, dt))
        rbs = sb2("rbs", [32, 8], F32); b_rb = Buf()
        T.dma("sp", rbs[:], rb, writes=[b_rb])
        oh = sb2("oh", [32, NBU], F32); b_oh = Buf()
        T.dma("sp", oh[:], onehot, writes=[b_oh])
        cm = sb2("cm", [8, NBU], F32); b_cm = Buf()
        T.dma("sp", cm[:], cmult, writes=[b_cm])
        ue = sb2("ue", [8, NBU], F32); b_ue = Buf()
        ub = sb2("ub", [8, NBU], BF16); b_ub = Buf()
        for c in range(NBU // 512):
            pt, pb = PS.next()
            T.op("pe", lambda e: e.matmul(pt[0:8, :], rbs[:, :], oh[:, c * 512:(c + 1) * 512], start=True, stop=True),
                 reads=[b_rb, b_oh], writes=[pb])
            T.op("act", lambda e: e.activation(out=ue[:, c * 512:(c + 1) * 512], in_=pt[0:8, :], func=AF.Exp),
                 reads=[pb], writes=[b_ue])
        T.op("dve", lambda e: e.tensor_tensor(out=ub[:, :], in0=ue[:, :], in1=cm[:, :], op=ALU.mult),
             reads=[b_ue, b_cm], writes=[b_ub])
        b_us = Buf()
        T.dma("sp", u_s, ub[:, :], reads=[b_ub], writes=[b_us])

        hcs = [sb2("hc%d" % i, [128, KT, 512], F32) for i in range(2)]; b_hcs = [Buf(), Buf()]
        xcs = [sb2("xc%d" % i, [128, KT, 512], BF16) for i in range(2)]; b_xcs = [Buf(), Buf()]
        sqr = Rot([sb2("sq%d" % i, [128, 512], F32) for i in range(2)])
        rstd_t = sb2("rstd", [128, 512], F32); b_rstd = Buf()
        xv = xn_s.rearrange("(k p) t -> p k t", p=128)
        b_xn = [Buf() for _ in range(NCH)]
        hload(T, hcs[0], 0, b_hcs[0])
        for c in range(NCH):
            t0 = c * 512
            if c + 1 < NCH:
                hload(T, hcs[(c + 1) % 2], c + 1, b_hcs[(c + 1) % 2])
            emit_norm_chunk(nc, T, hcs[c % 2], b_hcs[c % 2], g1s, b_g1, ones, b_ones, PS, sqr, rstd_t, b_rstd,
                            xcs[c % 2], b_xcs[c % 2], 512)
            T.dma("sp", xv[:, :, t0:t0 + 512], xcs[c % 2][:], reads=[b_xcs[c % 2]], writes=[b_xn[c]])
    T.barrier()

    with contextlib.ExitStack() as st2:
        sb2 = lambda name, shape, dt: st2.enter_context(nc.sbuf_tensor(uniq(name), shape, dt))
        wp = [sb2("wp%d" % i, [128, KT, 1024], BF16) for i in range(2)]; b_wp = [Buf(), Buf()]
        xcs = [sb2("xc%d" % i, [128, KT, 512], BF16) for i in range(2)]; b_xcs = [Buf(), Buf()]
        sqr = Rot([sb2("sq%d" % i, [128, 512], F32) for i in range(3)])
        rsr = Rot([sb2("rs%d" % i, [128, 512], F32) for i in range(3)])
        stg = Rot([sb2("stg%d" % i, [128, 512], BF16) for i in range(4)])
        wv = w_o.rearrange("(k p) n -> p k n", p=128)
        xv = xn_s.rearrange("(k p) t -> p k t", p=128)
        T.dma("pool", wp[0][:], wv[:, :, 0:1024], writes=[b_wp[0]])
        items = [(part, c) for part in range(3) for c in range(NCH)]
        T.dma("sp", xcs[0][:], xv[:, :, 0:512], reads=[b_xn[0]], writes=[b_xcs[0]])
        for idx, (part, c) in enumerate(items):
            t0 = c * 512
            if c == 0 and part + 1 < 3:
                T.dma("pool", wp[(part + 1) % 2][:], wv[:, :, (part + 1) * 1024:(part + 2) * 1024],
                      writes=[b_wp[(part + 1) % 2]])
            if idx + 1 < len(items):
                c2 = items[idx + 1][1]
                T.dma("sp", xcs[(idx + 1) % 2][:], xv[:, :, c2 * 512:(c2 + 1) * 512], reads=[b_xn[c2]],
                      writes=[b_xcs[(idx + 1) % 2]])
            xc, b_xc = xcs[idx % 2], b_xcs[idx % 2]
            w_, b_w = wp[part % 2], b_wp[part % 2]
            if part < 2:
                gs, b_gs = (qgs, b_qg) if part == 0 else (kgs, b_kg)
                dst = qT_s if part == 0 else kT_s
                sc_, bi_ = (1.0, 128 * EPS) if part == 0 else (1.0 / 128, EPS)
                pend_n = None
                for hd in range(8):
                    pt, pb = PS.next()
                    mm_group(T, pt[:, :], [(w_[:, k, hd * 128:(hd + 1) * 128], xc[:, k, :]) for k in range(KT)], pb,
                             reads=[b_w, b_xc])
                    sq, b_sq = sqr.next()
                    T.op("act", lambda e: e.activation(out=sq[:, :], in_=pt[:, :], func=AF.Square), reads=[pb], writes=[b_sq])
                    if pend_n is not None:
                        pend_n()

                    def fin(hd=hd, pt=pt, pb=pb, sq=sq, b_sq=b_sq):
                        p2, pb2 = PS.next()
                        T.op("pe", lambda e: e.matmul(p2[:, :], ones[:], sq[:, :], start=True, stop=True),
                             reads=[b_ones, b_sq], writes=[pb2])
                        rs, b_rs = rsr.next()
                        T.op("act", lambda e: e.activation(out=rs[:, :], in_=p2[:, :], func=AF.Sqrt, scale=sc_, bias=bi_),
                             reads=[pb2], writes=[b_rs])
                        T.op("dve", lambda e: e.reciprocal(out=rs[:, :], in_=rs[:, :]), reads=[b_rs], writes=[b_rs])
                        s, b_s = stg.next()
                        T.op("dve", lambda e: e.scalar_tensor_tensor(out=s[:, :], in0=pt[:, :], scalar=gs[:, 0:1], in1=rs[:, :],
                                                                     op0=ALU.mult, op1=ALU.mult),
                             reads=[pb, b_gs, b_rs], writes=[b_s])
                        T.dma("sp", dst[hd * 128:(hd + 1) * 128, t0:t0 + 512], s[:, :], reads=[b_s])
                    pend_n = fin
                pend_n()
            else:
                for ts in range(4):
                    for half in range(2):
                        pt, pb = PS.next()
                        mm_group(T, pt[:, :], [(xc[:, k, ts * 128:(ts + 1) * 128], w_[:, k, half * 512:(half + 1) * 512])
                                               for k in range(KT)], pb, reads=[b_w, b_xc])
                        s, b_s = stg.next()
                        if half == 0:
                            T.op("act", lambda e: e.activation(out=s[:, :], in_=pt[:, :], func=AF.Copy), reads=[pb], writes=[b_s])
                        else:
                            T.op("dve", lambda e: e.tensor_copy(out=s[:, :], in_=pt[:, :]), reads=[pb], writes=[b_s])
                        T.dma("sp", v_s[t0 + ts * 128:t0 + (ts + 1) * 128, half * 512:(half + 1) * 512], s[:, :], reads=[b_s])
    T.barrier()

    with contextlib.ExitStack() as st2:
        sb2 = lambda name, shape, dt: st2.enter_context(nc.sbuf_tensor(uniq(name), shape, dt))
        qTs = [sb2("qT%d" % i, [128, SEQ], BF16) for i in range(2)]
        kTs = [sb2("kT%d" % i, [128, SEQ], BF16) for i in range(2)]
        vts = [sb2("vt%d" % i, [128, 32, 128], BF16) for i in range(2)]
        Es = [sb2("E%d" % i, [128, NKT, 512], BF16) for i in range(2)]
        b_q = [Buf(), Buf()]; b_k = [Buf(), Buf()]; b_v = [Buf(), Buf()]; b_E = [Buf(), Buf()]
        pex = Rot([sb2("pex%d" % i, [128, 512], BF16) for i in range(4)])
        pmr = Rot([sb2("pm%d" % i, [128, 512], BF16) for i in range(5)])
        rdr = Rot([sb2("rd%d" % i, [128, 512], F32) for i in range(2)])
        stg = Rot([sb2("stg%d" % i, [128, 512], BF16) for i in range(2)])
        vv_ = v_s.rearrange("(i p) c -> p i c", p=128)
        SPS = Rot(ps[0:4]); OPS = Rot(ps[4:6]); DPS = Rot(ps[6:8])

        def loadh(h):
            s = h % 2
            T.dma("sp", qTs[s][:], qT_s[h * 128:(h + 1) * 128, :], writes=[b_q[s]])
            T.dma("sp", kTs[s][:], kT_s[h * 128:(h + 1) * 128, :], writes=[b_k[s]])
            T.dma("sp", vts[s][:], vv_[:, :, h * 128:(h + 1) * 128], writes=[b_v[s]])
            T.dma("sp", Es[s][:], bass.AP(u_s.tensor, u_s.offset + h * NBU, [[1, 128], [128, NKT], [1, 512]]),
                  reads=[b_us], writes=[b_E[s]])

        def rev(ap):
            return bass.AP(ap.tensor, ap.offset + 511, [list(ap.ap[0]), [-1, 512]])

        loadh(0)
        loadh(1)
        iters = []
        for h in range(8):
            for qt in range(NCH):
                tiles = [a for a in range(NKT) if 0 <= qt * 512 - 1024 + 128 * a < SEQ]
                for n_, a in enumerate(tiles):
                    iters.append((h, qt, a, n_ == 0, n_ == len(tiles) - 1))
        live = {}
        acc = {}
        row_toks = []

        def emit_qk(i):
            h, qt, a, first, last = iters[i]
            s = h % 2
            t0 = qt * 512
            s0 = t0 - 1024 + 128 * a
            pt, pb = SPS.next()
            T.op("pe", lambda e: e.matmul(pt[:, :], kTs[s][:, s0:s0 + 128], rev(qTs[s][:, t0:t0 + 512]),
                                          start=True, stop=True), reads=[b_k[s], b_q[s]], writes=[pb])
            px, b_px = pex.next()
            T.op("act", lambda e: e.activation(out=px[:, :], in_=pt[:, :], func=AF.Exp), reads=[pb], writes=[b_px])
            pm, b_pm = pmr.next()
            T.op("dve", lambda e: e.tensor_tensor(out=pm[:, :], in0=px[:, :], in1=Es[s][:, a, :], op=ALU.mult),
                 reads=[b_px, b_E[s]], writes=[b_pm])
            live[i] = (pm, b_pm)

        def emit_pv(i):
            h, qt, a, first, last = iters[i]
            s = h % 2
            t0 = qt * 512
            s0 = t0 - 1024 + 128 * a
            if first:
                acc[(h, qt)] = (OPS.next(), DPS.next())
                if qt == 0 and 1 <= h < 7:
                    loadh(h + 1)
            (po, bo), (pd, bd) = acc[(h, qt)]
            pm, b_pm = live.pop(i)
            T.op("pe", lambda e: e.matmul(po[:, :], vts[s][:, s0 // 128, :], pm[:, :], start=first, stop=last),
                 reads=[b_v[s], b_pm], writes=[bo], inc=last)
            T.op("pe", lambda e: e.matmul(pd[:, :], onesb[:, :], pm[:, :], start=first, stop=last),
                 reads=[b_onesb, b_pm], writes=[bd], inc=True)
            if last:
                rd, b_rd = rdr.next()
                T.op("dve", lambda e: e.reciprocal(out=rd[:, :], in_=pd[:, :]), reads=[bd], writes=[b_rd])
                st_, b_st = stg.next()
                T.op("dve", lambda e: e.tensor_tensor(out=st_[:, :], in0=rev(po[:, :]), in1=rev(rd[:, :]), op=ALU.mult),
                     reads=[bo, b_rd], writes=[b_st])
                row_toks.append(T.dma("sp", mixT[h * 128:(h + 1) * 128, t0:t0 + 512], st_[:, :], reads=[b_st]))
                del acc[(h, qt)]
                if qt == NCH - 1 and h % 2 == 1 and on_rows is not None:
                    on_rows(h // 2, list(row_toks))
                    del row_toks[:]

        LOOK = 2
        for i in range(len(iters) + LOOK):
            if i < len(iters):
                emit_qk(i)
            if i - LOOK >= 0:
                emit_pv(i - LOOK)


def build_odd(debug=False):
    nc = bass.Bass("TRN2", target_bir_lowering=False)
    dt = lambda name, shape, dtype, kind="ExternalInput": nc.dram_tensor(name, shape, dtype, kind=kind).ap()
    hT = dt("hT", [D, SEQ], F32)
    g1 = dt("g1", [128, KT], F32)
    w_o = dt("w_o", [D, 3072], F32)
    qg = dt("qg", [128, 1], F32)
    kg = dt("kg", [128, 1], F32)
    rb = dt("rb", [32, 8], F32)
    onehot = dt("onehot", [32, NBU], F32)
    cmult = dt("cmult", [8, NBU], F32)
    mixT = dt("mixT", [1024, SEQ], BF16, "ExternalOutput")
    kind = "ExternalOutput" if debug else "Internal"
    scr = (dt("xn_s", [D, SEQ], BF16, kind), dt("qT_s", [1024, SEQ], BF16, kind), dt("kT_s", [1024, SEQ], BF16, kind),
           dt("v_s", [SEQ, 1024], BF16, kind), dt("u_s", [8, NBU], BF16, kind))
    with contextlib.ExitStack() as stack:
        T = Trk(nc, stack)
        ps = [stack.enter_context(nc.psum_tensor("ps%d" % i, [128, 512], F32)) for i in range(8)]
        hv = hT.rearrange("(k p) t -> p k t", p=128)
        hload = lambda T, t, c, b: T.dma("sp", t[:], hv[:, :, c * 512:(c + 1) * 512], writes=[b])
        emit_odd(nc, T, stack, ps, hload, g1, w_o, qg, kg, rb, onehot, cmult, mixT, scr)
        T.finish()
    return nc


def t5_bucket_np(rel):
    nb = 16
    ret = (rel > 0).astype(np.int32) * nb
    n = np.abs(rel)
    max_exact = nb // 2
    large = max_exact + (np.log(np.maximum(n, 1) / max_exact) / np.log(1024 / max_exact) * (nb - max_exact)).astype(np.int32)
    large = np.minimum(large, nb - 1)
    return (ret + np.where(n < max_exact, n, large)).astype(np.int32)


def odd_consts():
    m = np.arange(NBU)
    delta = m - 1535
    bkt = t5_bucket_np(delta)
    onehot = (bkt[None, :] == np.arange(32)[:, None]).astype(np.float32)
    mult = np.zeros(NBU, np.float32)
    for (w, d) in ((128, 1), (512, 4), (2048, 16)):
        mult += ((delta % d == 0) & (np.abs(delta) <= w // 2)).astype(np.float32)
    return dict(onehot=onehot, cmult=np.ascontiguousarray(np.broadcast_to(mult, (8, NBU))))


def odd_weights(j, w_qkv, q_g, k_g, rel_bias):
    cols = np.r_[1024 * j:1024 * j + 1024, 2048 + 1024 * j:2048 + 1024 * j + 1024, 4096 + 1024 * j:4096 + 1024 * j + 1024]
    return dict(w_o=np.ascontiguousarray(w_qkv[:, cols]), qg=np.ascontiguousarray(q_g.reshape(128, 1)),
                kg=np.ascontiguousarray(k_g.reshape(128, 1)), rb=np.ascontiguousarray(rel_bias[:, 8 * j:8 * j + 8]))


_PROGS = {}


def _prog(name):
    if name not in _PROGS:
        _PROGS[name] = {"even": build_even, "odd": build_odd, "post": build_post}[name]()
    return _PROGS[name]


def _run(name, in_maps):
    res = run_bass_kernel_spmd(_prog(name), in_maps, core_ids=list(range(NCORES)))
    return res.results


def kernel_unfused(x, mix_norm_g, w_in_even, gate_up_fwd, gate_bias_fwd, gate_up_bwd, gate_bias_bwd, gla_norm_g, w_out_even,
           w_qkv_odd, q_norm_g, k_norm_g, rel_bias, w_out_odd, ffn_norm_g, w_gate, w_up, conv_w, conv_b, w_down):
    f32 = lambda a: np.ascontiguousarray(np.asarray(a, dtype=np.float32))
    x = f32(x)
    hT = [np.ascontiguousarray(x[b].T) for b in range(BATCH)]
    ec = even_consts()
    oc = odd_consts()
    for layer in range(4):
        i = layer // 2
        g1 = lay128(f32(mix_norm_g[layer]))
        ims = []
        for b in range(BATCH):
            for j in range(2):
                im = dict(hT=hT[b], g1=g1)
                if layer % 2 == 0:
                    im.update(even_weights(j, f32(w_in_even[i]), f32(gate_up_fwd[i]), f32(gate_bias_fwd[i]),
                                           f32(gate_up_bwd[i]), f32(gate_bias_bwd[i]), f32(gla_norm_g[i])))
                    im.update(ec)
                else:
                    im.update(odd_weights(j, f32(w_qkv_odd[i]), f32(q_norm_g[i]), f32(k_norm_g[i]), f32(rel_bias)))
                    im.update(oc)
                ims.append(im)
        res = _run("even" if layer % 2 == 0 else "odd", ims)
        mT = []
        for b in range(BATCH):
            m0 = np.asarray(res[2 * b]["mixT"]); m1 = np.asarray(res[2 * b + 1]["mixT"])
            if layer % 2 == 0:
                mT.append(np.concatenate([m0[:512], m1[:512], m0[512:], m1[512:]], axis=0))
            else:
                mT.append(np.concatenate([m0, m1], axis=0))
        w_out = f32(w_out_even[i]) if layer % 2 == 0 else f32(w_out_odd[i])
        cw = np.ascontiguousarray(f32(conv_w[layer]).T.reshape(FT, 128, 3).transpose(1, 0, 2).reshape(128, FT * 3))
        common = dict(w_out=w_out, g2=lay128(f32(ffn_norm_g[layer])), w_gate=f32(w_gate[layer]), w_up=f32(w_up[layer]),
                      cw=cw, cb=lay128(f32(conv_b[layer])), w_down=f32(w_down[layer]))
        ims = []
        for b in range(BATCH):
            for half in range(2):
                t0 = half * TOK
                he = np.zeros((D, EXT), np.float32); me = np.zeros((D, EXT), mT[b].dtype)
                lo, hi = max(t0 - 1, 0), min(t0 + TOK + 1, SEQ)
                he[:, lo - (t0 - 1):hi - (t0 - 1)] = hT[b][:, lo:hi]
                me[:, lo - (t0 - 1):hi - (t0 - 1)] = mT[b][:, lo:hi]
                im = dict(hT_ext=he, mT_ext=me)
                im.update(common)
                ims.append(im)
        res = _run("post", ims)
        hT = [np.ascontiguousarray(np.concatenate([np.asarray(res[2 * b]["hT_out"]), np.asarray(res[2 * b + 1]["hT_out"])], axis=1))
              for b in range(BATCH)]
    return np.ascontiguousarray(np.stack([h.T for h in hT], axis=0)).astype(np.float32)


PAIRS = [[0, 1], [2, 3], [4, 5], [6, 7]]


def build_fused():
    nc = bass.Bass("TRN2", target_bir_lowering=False)
    dt = lambda name, shape, dtype, kind="ExternalInput": nc.dram_tensor(name, shape, dtype, kind=kind).ap()
    xT = dt("xT", [D, TOK], F32)
    sel = dt("sel", [128, 2], F32)
    cs = dt("cs", [256, 512], BF16); cosm = dt("cosm", [SEQ, SEQ], BF16); nsinm = dt("nsinm", [SEQ, SEQ], BF16)
    mf = dt("mf", [128, 128], BF16); mb = dt("mb", [128, 128], BF16)
    onehot = dt("onehot", [32, NBU], F32); cmult = dt("cmult", [8, NBU], F32); rb = dt("rb", [32, 8], F32)
    L = []
    for l in range(4):
        d_ = dict(g1=dt("g1_%d" % l, [128, KT], F32), w_out=dt("w_out_%d" % l, [D, D], F32), g2=dt("g2_%d" % l, [128, KT], F32),
                  w_gate=dt("w_gate_%d" % l, [D, FF], F32), w_up=dt("w_up_%d" % l, [D, FF], F32),
                  cw=dt("cw_%d" % l, [128, FT * 3], F32), cb=dt("cb_%d" % l, [128, FT], F32),
                  w_down=dt("w_down_%d" % l, [FF, D], F32))
        if l % 2 == 0:
            d_.update(w_e=dt("w_e_%d" % l, [D, NW_E], F32), gua_f=dt("gua_f_%d" % l, [17, 256], F32),
                      gua_b=dt("gua_b_%d" % l, [17, 256], F32), glag=dt("glag_%d" % l, [128, 2], F32))
        else:
            d_.update(w_o=dt("w_o_%d" % l, [D, 3072], F32), qg=dt("qg_%d" % l, [128, 1], F32), kg=dt("kg_%d" % l, [128, 1], F32))
        L.append(d_)
    outT = dt("outT", [D, TOK], F32, "ExternalOutput")
    I = "Internal"
    hown = dt("hown", [D, TOK], F32, I)
    hfull = dt("hfull", [8, 2, 2, 128, TOK], F32, I)
    mcore = dt("mcore", [1024, SEQ], BF16, I)
    mfull = dt("mfull", [4, 2, 256, SEQ], BF16, I)
    hmid_scr = dt("hmid_scr", [FF, TOK], BF16, I)
    hm_scr = dt("hm_scr", [D, EXT], F32, I)
    scr_e = (dt("aT_s", [512, SEQ], BF16, I), dt("qT_s", [256, SEQ], BF16, I), dt("kT_s", [256, SEQ], BF16, I),
             dt("k_s", [SEQ, 256], BF16, I), dt("v_s", [SEQ, 512], BF16, I), dt("sgT_s", [512, SEQ], BF16, I),
             dt("la_s", [2, SEQ, 256], BF16, I))
    scr_o = (dt("xn_s", [D, SEQ], BF16, I), dt("qTo_s", [1024, SEQ], BF16, I), dt("kTo_s", [1024, SEQ], BF16, I),
             dt("vo_s", [SEQ, 1024], BF16, I), dt("u_s", [8, NBU], BF16, I))

    with contextlib.ExitStack() as stack:
        T = Trk(nc, stack)
        ps = [stack.enter_context(nc.psum_tensor("ps%d" % i, [128, 512], F32)) for i in range(8)]
        sels = stack.enter_context(nc.sbuf_tensor("sels", [128, 2], F32)); b_sel = Buf()
        T.dma("sp", sels[:], sel, writes=[b_sel])
        T.dma("sp", hown, xT)

        def hload(T, t, c, b):
            r, tc = c // 4, (c % 4) * 512
            toks = []
            for q in range(8):
                toks.append(T.dma("sp", t[:, 2 * q:2 * q + 2, :], hfull[q, r].rearrange("k p t -> p k t")[:, :, tc:tc + 512],
                                  writes=([b] if q == 0 else [])))
            b.multi = toks

        def make_loaders():
            st = {}

            def get(sb2, key, shape, dtype):
                if key not in st:
                    t = sb2(key, shape, dtype); bb = Buf()
                    T.op("dve", lambda e: e.memset(t[:], 0.0), writes=[bb])
                    st[key] = (t, bb)
                return st[key]

            def load_m(T, sb2, mT, k0, b):
                kg = k0 // 4
                r_src, q0 = kg // 2, 2 * (kg % 2)
                A, bA = get(sb2, "mA", [128, 2, EXT], BF16)
                B, bB = get(sb2, "mB", [128, 2, EXT], BF16)
                for qq in range(2):
                    src = mfull[q0 + qq, r_src].rearrange("(k p) t -> p k t", p=128)
                    T.dma("sp", A[:, :, 1:EXT], src[:, :, 0:TOK + 1], writes=[bA])
                    T.dma("sp", B[:, :, 0:EXT - 1], src[:, :, TOK - 1:SEQ], writes=[bB])
                    T.op("dve", lambda e: e.tensor_scalar(out=A[:], in0=A[:], scalar1=sels[:, 0:1], scalar2=None, op0=ALU.mult),
                         reads=[b_sel], writes=[bA])
                    T.op("dve", lambda e: e.scalar_tensor_tensor(out=mT[:, k0 + 2 * qq:k0 + 2 * qq + 2, :], in0=B[:],
                                                                 scalar=sels[:, 1:2], in1=A[:], op0=ALU.mult, op1=ALU.add),
                         reads=[bA, bB, b_sel], writes=[b])

            def load_h(T, sb2, hb, n, b):
                q, k2 = n // 2, n % 2
                T.dma("sp", hb[:, 1:TOK + 1], hown[n * 128:(n + 1) * 128, :], writes=[b])
                T.dma("sp", hb[:, 0:1], hfull[q, 0, k2][:, TOK - 1:TOK], writes=[b], slow=True)
                T.dma("sp", hb[:, EXT - 1:EXT], hfull[q, 1, k2][:, 0:1], writes=[b], slow=True)
                T.op("dve", lambda e: e.tensor_scalar(out=hb[:, 0:1], in0=hb[:, 0:1], scalar1=sels[:, 1:2], scalar2=None,
                                                      op0=ALU.mult), reads=[b_sel], writes=[b])
                T.op("dve", lambda e: e.tensor_scalar(out=hb[:, EXT - 1:EXT], in0=hb[:, EXT - 1:EXT], scalar1=sels[:, 0:1],
                                                      scalar2=None, op0=ALU.mult), reads=[b_sel], writes=[b])
            return load_m, load_h

        def gather_h(q, extra=()):
            T.collective(hown[256 * q:256 * (q + 1), :].opt(), hfull[q].rearrange("r k p t -> (r k p) t").opt(), PAIRS,
                         extra=extra)

        def gather_m(q, extra=()):
            T.collective(mcore[256 * q:256 * (q + 1), :].opt(), mfull[q].rearrange("r p t -> (r p) t").opt(), PAIRS,
                         extra=extra)

        T.barrier()
        for q in range(8):
            gather_h(q)
        for l in range(4):
            W = L[l]
            T.barrier(); T.new_epoch()
            with contextlib.ExitStack() as st_l:
                if l % 2 == 0:
                    emit_even(nc, T, st_l, ps, hload, W["g1"], W["w_e"], W["gua_f"], W["gua_b"], W["glag"],
                              cs, cosm, nsinm, mf, mb, mcore, scr_e, on_rows=gather_m)
                else:
                    emit_odd(nc, T, st_l, ps, hload, W["g1"], W["w_o"], W["qg"], W["kg"], rb, onehot, cmult, mcore, scr_o,
                             on_rows=gather_m)
            T.barrier(); T.new_epoch()
            store_toks = {}

            def on_store(n, c, nchunk, tk):
                store_toks.setdefault(n // 2, []).append(tk)
                if c == nchunk - 1 and n % 2 == 1:
                    gather_h(n // 2, store_toks[n // 2])

            with contextlib.ExitStack() as st_l:
                load_m, load_h = make_loaders()
                emit_post(nc, T, st_l, ps, load_m, load_h, W["w_out"], W["g2"], W["w_gate"], W["w_up"], W["cw"], W["cb"],
                          W["w_down"], outT if l == 3 else hown, hmid_scr, hm_scr, on_store=(on_store if l < 3 else None))
        T.finish()
    return nc, T


def wout_perm(layer):
    perm = np.zeros(D, np.int64)
    for r in range(2):
        for row in range(1024):
            if layer % 2 == 0:
                ch = 512 * r + row if row < 512 else 1024 + 512 * r + (row - 512)
            else:
                ch = 1024 * r + row
            perm[r * 1024 + row] = ch
    return perm


def kernel(x, mix_norm_g, w_in_even, gate_up_fwd, gate_bias_fwd, gate_up_bwd, gate_bias_bwd, gla_norm_g, w_out_even,
                 w_qkv_odd, q_norm_g, k_norm_g, rel_bias, w_out_odd, ffn_norm_g, w_gate, w_up, conv_w, conv_b, w_down):
    f32 = lambda a: np.ascontiguousarray(np.asarray(a, dtype=np.float32))
    x = f32(x)
    if "fused" not in _PROGS:
        _PROGS["fused"] = build_fused()[0]
    common = {}
    common.update(even_consts()); common.update(odd_consts())
    for l in range(4):
        i = l // 2
        w_out = f32(w_out_even[i]) if l % 2 == 0 else f32(w_out_odd[i])
        common["w_out_%d" % l] = np.ascontiguousarray(w_out[wout_perm(l), :])
        common["g1_%d" % l] = lay128(f32(mix_norm_g[l])); common["g2_%d" % l] = lay128(f32(ffn_norm_g[l]))
        common["w_gate_%d" % l] = f32(w_gate[l]); common["w_up_%d" % l] = f32(w_up[l]); common["w_down_%d" % l] = f32(w_down[l])
        common["cw_%d" % l] = np.ascontiguousarray(f32(conv_w[l]).T.reshape(FT, 128, 3).transpose(1, 0, 2).reshape(128, FT * 3))
        common["cb_%d" % l] = lay128(f32(conv_b[l]))
    ims = []
    for b in range(BATCH):
        for r in range(2):
            im = dict(common)
            im["xT"] = np.ascontiguousarray(x[b, r * TOK:(r + 1) * TOK, :].T)
            s = np.zeros((128, 2), np.float32); s[:, r] = 1.0
            im["sel"] = s
            im["rb"] = np.ascontiguousarray(f32(rel_bias)[:, 8 * r:8 * r + 8])
            for l in range(4):
                i = l // 2
                if l % 2 == 0:
                    ew = even_weights(r, f32(w_in_even[i]), f32(gate_up_fwd[i]), f32(gate_bias_fwd[i]), f32(gate_up_bwd[i]),
                                      f32(gate_bias_bwd[i]), f32(gla_norm_g[i]))
                    for k_, v_ in ew.items():
                        im["%s_%d" % (k_, l)] = v_
                else:
                    ow = odd_weights(r, f32(w_qkv_odd[i]), f32(q_norm_g[i]), f32(k_norm_g[i]), f32(rel_bias))
                    for k_ in ("w_o", "qg", "kg"):
                        im["%s_%d" % (k_, l)] = ow[k_]
            ims.append(im)
    res = run_bass_kernel_spmd(_PROGS["fused"], ims, core_ids=list(range(NCORES))).results
    out = np.empty((BATCH, SEQ, D), np.float32)
    for b in range(BATCH):
        for r in range(2):
            out[b, r * TOK:(r + 1) * TOK, :] = np.asarray(res[2 * b + r]["outT"]).T
    return out
```

```python
import contextlib
import numpy as np
import ml_dtypes
import concourse.bass as bass
import concourse.mybir as mybir
from concourse.bass_utils import run_bass_kernel_spmd

F32 = mybir.dt.float32
BF16 = mybir.dt.bfloat16
AF = mybir.ActivationFunctionType
ALU = mybir.AluOpType
NPBF = ml_dtypes.bfloat16

D = 2048
KT = D // 128
SEQ = 4096
BATCH = 4
TOK = 2048
EXT = TOK + 2
FF = 5632
FT = FF // 128
EPS = 1e-6
NCORES = 8


DBG = {}
RELAX = set()


class Buf:
    __slots__ = ("name", "w", "r", "multi")

    def __init__(self, name=""):
        self.name = name
        self.w = None
        self.r = []
        self.multi = None


class Trk:
    NDS = 20

    def __init__(self, nc, stack):
        self.nc = nc
        self.E = {"pe": nc.tensor, "act": nc.scalar, "dve": nc.vector, "pool": nc.gpsimd, "sp": nc.sync}
        self.sem = {k: stack.enter_context(nc.semaphore("s_" + k)) for k in self.E}
        self.cnt = {k: 0 for k in self.E}
        self.waited = {}
        self.dsem = {q: [stack.enter_context(nc.semaphore("d_%s%d" % (q, i))) for i in range(self.NDS)]
                     for q in ("sp", "pool")}
        self.dtot = {q: [0] * self.NDS for q in ("sp", "pool")}
        self.dnext = {"sp": 0, "pool": 0}
        self.n_instr = 0
        self.stack = stack
        self.ccsem = stack.enter_context(nc.semaphore("s_cc"))
        self.cccnt = 0
        self.epoch = 0

    def new_epoch(self):
        self.epoch += 1
        self.sem = {k: self.stack.enter_context(self.nc.semaphore("s%d_%s" % (self.epoch, k))) for k in self.E}
        self.cnt = {k: 0 for k in self.E}

    def collective(self, in_ap, out_ap, groups, reads=(), writes=(), extra=()):
        if DBG.get("no_cc"):
            return None
        self._sync("pool", reads, writes)
        for t in extra:
            self._wait("pool", t)
        if self.cccnt > 0:
            self._wait("pool", (self.ccsem, self.cccnt, "cc"))
        self.nc.gpsimd.collective_compute("AllGather", ALU.bypass, replica_groups=groups, ins=[in_ap], outs=[out_ap]
                                          ).then_inc(self.ccsem)
        self.cccnt += 1
        tok = (self.ccsem, self.cccnt, "cc")
        self._update(tok, reads, writes)
        return tok

    def _wait(self, eng, tok):
        sem, val, src = tok
        if src == eng and (eng == "pe" or eng in RELAX):
            return
        key = (eng, id(sem))
        if self.waited.get(key, 0) >= val:
            return
        self.waited[key] = val
        self.E[eng].wait_ge(sem, val)

    def _sync(self, eng, reads, writes):
        for b in reads:
            if b.w is not None:
                self._wait(eng, b.w)
            if b.multi:
                for t in b.multi:
                    self._wait(eng, t)
        for b in writes:
            if b.w is not None:
                self._wait(eng, b.w)
            if b.multi:
                for t in b.multi:
                    self._wait(eng, t)
            for t in b.r:
                self._wait(eng, t)

    def _update(self, tok, reads, writes):
        for b in reads:
            b.r.append(tok)
        for b in writes:
            b.w = tok
            b.r = []
            b.multi = None

    def op(self, eng, fn, reads=(), writes=(), inc=True):
        self._sync(eng, reads, writes)
        ins = fn(self.E[eng])
        self.n_instr += 1
        if inc:
            self.cnt[eng] += 1
            ins.then_inc(self.sem[eng], 1)
            tok = (self.sem[eng], self.cnt[eng], eng)
        else:
            tok = (self.sem[eng], self.cnt[eng] + 1, eng)
        self._update(tok, reads, writes)
        return tok

    def dma(self, q, out, in_, reads=(), writes=(), slow=False):
        self._sync(q, reads, writes)
        i = self.dnext[q]
        self.dnext[q] = (i + 1) % self.NDS
        sem = self.dsem[q][i]
        if self.dtot[q][i] > 0:
            self._wait(q, (sem, self.dtot[q][i], "dma"))
        if slow:
            self.E[q].dma_start(out=out, in_=in_, allow_slow_non_contiguous=True).then_inc(sem, 16)
        else:
            self.E[q].dma_start(out=out, in_=in_).then_inc(sem, 16)
        self.n_instr += 1
        self.dtot[q][i] += 16
        tok = (sem, self.dtot[q][i], "dma")
        self._update(tok, reads, writes)
        return tok

    def barrier(self):
        toks = []
        for q in ("sp", "pool"):
            for i in range(self.NDS):
                if self.dtot[q][i] > 0:
                    toks.append((self.dsem[q][i], self.dtot[q][i], "dma"))
        for e in ("pe", "act", "dve"):
            if self.cnt[e] > 0:
                toks.append((self.sem[e], self.cnt[e], e))
        if self.cccnt > 0:
            toks.append((self.ccsem, self.cccnt, "cc"))
        for eng in self.E:
            for t in toks:
                if t[2] == eng:
                    continue
                self._wait(eng, t)

    def finish(self):
        for q in ("sp", "pool"):
            for i in range(self.NDS):
                if self.dtot[q][i] > 0:
                    self._wait("sp", (self.dsem[q][i], self.dtot[q][i], "dma"))
        if self.cccnt > 0:
            self._wait("sp", (self.ccsem, self.cccnt, "cc"))
        for e in ("pe", "act", "dve"):
            if self.cnt[e] > 0:
                self._wait("sp", (self.sem[e], self.cnt[e], e))


_UNIQ = [0]


def uniq(name):
    _UNIQ[0] += 1
    return "%s_%d" % (name, _UNIQ[0])


def col_chunks(n, step=512):
    return [(c, min(c + step, n)) for c in range(0, n, step)]


def emit_post(nc, T, stack, ps, load_m, load_h, w_out, g2, w_gate, w_up, cw, cb, w_down, hT_out, hmid_scr, hm_scr,
              on_store=None):
    sb = lambda name, shape, dt: stack.enter_context(nc.sbuf_tensor(uniq(name), shape, dt))
    psb = [Buf("ps%d" % i) for i in range(8)]

    ones = sb("ones", [128, 128], F32)
    b_ones = Buf()
    T.op("dve", lambda e: e.memset(ones[:], 1.0), writes=[b_ones])
    g2s = sb("g2s", [128, KT], F32); b_g2 = Buf()
    T.dma("sp", g2s[:], g2, writes=[b_g2])
    cws = sb("cws", [128, FT * 3], F32); b_cw = Buf()
    T.dma("sp", cws[:], cw, writes=[b_cw])
    cbs = sb("cbs", [128, FT], F32); b_cb = Buf()
    T.dma("sp", cbs[:], cb, writes=[b_cb])
    rstd = sb("rstd", [128, EXT], F32); b_rstd = Buf()
    xn = sb("xn", [128, KT, EXT], BF16); b_xn = [Buf() for _ in range(KT)]

    chunks = col_chunks(EXT)
    hm_b = [Buf() for _ in range(KT)]

    with contextlib.ExitStack() as st2:
        sb2 = lambda name, shape, dt: st2.enter_context(nc.sbuf_tensor(uniq(name), shape, dt))
        mT = sb2("mT", [128, KT, EXT], BF16); b_mT = [Buf() for _ in range(KT // 4)]
        for k0 in range(0, KT, 4):
            load_m(T, sb2, mT, k0, b_mT[k0 // 4])
        wo = [sb2("wo%d" % i, [128, KT, 128], BF16) for i in range(2)]; b_wo = [Buf(), Buf()]
        hb = [sb2("hb%d" % i, [128, EXT], F32) for i in range(2)]; b_hb = [Buf(), Buf()]
        hm = [sb2("hm%d" % i, [128, EXT], F32) for i in range(2)]; b_hm = [Buf(), Buf()]
        sq = [sb2("sq%d" % i, [128, 512], F32) for i in range(2)]; b_sq = [Buf(), Buf()]
        wv = w_out.rearrange("(k p) n -> p k n", p=128)

        def load(n):
            T.dma("pool", wo[n % 2][:], wv[:, :, n * 128:(n + 1) * 128], writes=[b_wo[n % 2]])
            load_h(T, sb2, hb[n % 2], n, b_hb[n % 2])

        load(0)
        pend = None
        it = 0
        for n in range(KT):
            if n + 1 < KT:
                load(n + 1)
            for j, (c0, c1) in enumerate(chunks):
                w = c1 - c0
                pi = it % 2
                for k in range(KT):
                    T.op("pe", lambda e, k=k: e.matmul(ps[pi][:, :w], wo[n % 2][:, k, :], mT[:, k, c0:c1],
                                                       start=(k == 0), stop=(k == KT - 1)),
                         reads=[b_wo[n % 2], b_mT[k // 4]], writes=[psb[pi]], inc=(k == KT - 1))
                T.op("dve", lambda e: e.tensor_tensor(out=hm[n % 2][:, c0:c1], in0=ps[pi][:, :w],
                                                      in1=hb[n % 2][:, c0:c1], op=ALU.add),
                     reads=[psb[pi], b_hb[n % 2]], writes=[b_hm[n % 2]])
                T.op("act", lambda e: e.activation(out=sq[pi][:, :w], in_=hm[n % 2][:, c0:c1], func=AF.Square),
                     reads=[b_hm[n % 2]], writes=[b_sq[pi]])
                if pend is not None:
                    pend()

                def mk(n=n, j=j, pi=pi, w=w):
                    T.op("pe", lambda e: e.matmul(ps[2 + j][:, :w], ones[:], sq[pi][:, :w],
                                                  start=(n == 0), stop=(n == KT - 1)),
                         reads=[b_ones, b_sq[pi]], writes=[psb[2 + j]])
                pend = mk
                it += 1
            T.dma("sp", hm_scr[n * 128:(n + 1) * 128, :], hm[n % 2][:], reads=[b_hm[n % 2]], writes=[hm_b[n]])
        pend()
        for j, (c0, c1) in enumerate(chunks):
            w = c1 - c0
            T.op("act", lambda e: e.activation(out=rstd[:, c0:c1], in_=ps[2 + j][:, :w], func=AF.Sqrt,
                                               scale=1.0 / D, bias=EPS),
                 reads=[psb[2 + j]], writes=[b_rstd])
        T.op("dve", lambda e: e.reciprocal(out=rstd[:], in_=rstd[:]), reads=[b_rstd], writes=[b_rstd])

        for n in range(KT):
            T.dma("sp", hb[n % 2][:], hm_scr[n * 128:(n + 1) * 128, :], reads=[hm_b[n]], writes=[b_hb[n % 2]])
            T.op("dve", lambda e: e.scalar_tensor_tensor(out=xn[:, n, :], in0=hb[n % 2][:], scalar=g2s[:, n:n + 1],
                                                         in1=rstd[:], op0=ALU.mult, op1=ALU.mult),
                 reads=[b_hb[n % 2], b_g2, b_rstd], writes=[b_xn[n]])

    T.barrier()
    hmid_b = [Buf() for _ in range(FT)]
    with contextlib.ExitStack() as st2:
        sb2 = lambda name, shape, dt: st2.enter_context(nc.sbuf_tensor(uniq(name), shape, dt))
        wg = [sb2("wg%d" % i, [128, KT, 128], BF16) for i in range(2)]; b_wg = [Buf(), Buf()]
        wu = [sb2("wu%d" % i, [128, KT, 128], BF16) for i in range(2)]; b_wu = [Buf(), Buf()]
        ge = [sb2("ge%d" % i, [128, EXT], F32) for i in range(2)]; b_ge = [Buf(), Buf()]
        u = [sb2("u%d" % i, [128, TOK], F32) for i in range(2)]; b_u = [Buf(), Buf()]
        gl = [sb2("gl%d" % i, [128, TOK], F32) for i in range(2)]; b_gl = [Buf(), Buf()]
        hf = [sb2("hf%d" % i, [128, TOK], BF16) for i in range(2)]; b_hf = [Buf(), Buf()]
        wgv = w_gate.rearrange("(k p) n -> p k n", p=128)
        wuv = w_up.rearrange("(k p) n -> p k n", p=128)

        def loadg(f):
            T.dma("pool", wg[f % 2][:], wgv[:, :, f * 128:(f + 1) * 128], writes=[b_wg[f % 2]])
            T.dma("pool", wu[f % 2][:], wuv[:, :, f * 128:(f + 1) * 128], writes=[b_wu[f % 2]])

        loadg(0)
        it = 0
        for f in range(FT):
            if f + 1 < FT:
                loadg(f + 1)
            s = f % 2
            for j, (c0, c1) in enumerate(chunks):
                w = c1 - c0
                pi = it % 4; it += 1
                for k in range(KT):
                    T.op("pe", lambda e, k=k: e.matmul(ps[pi][:, :w], wg[s][:, k, :], xn[:, k, c0:c1],
                                                       start=(k == 0), stop=(k == KT - 1)),
                         reads=[b_wg[s], b_xn[k]], writes=[psb[pi]], inc=(k == KT - 1))
                T.op("act", lambda e: e.activation(out=ge[s][:, c0:c1], in_=ps[pi][:, :w], func=AF.Copy),
                     reads=[psb[pi]], writes=[b_ge[s]])
            T.op("dve", lambda e: e.tensor_scalar(out=u[s][:], in0=ge[s][:, 1:1 + TOK],
                                                  scalar1=cws[:, 3 * f + 1:3 * f + 2], scalar2=cbs[:, f:f + 1],
                                                  op0=ALU.mult, op1=ALU.add),
                 reads=[b_ge[s], b_cw, b_cb], writes=[b_u[s]])
            T.op("dve", lambda e: e.scalar_tensor_tensor(out=u[s][:], in0=ge[s][:, 0:TOK],
                                                         scalar=cws[:, 3 * f:3 * f + 1], in1=u[s][:],
                                                         op0=ALU.mult, op1=ALU.add),
                 reads=[b_ge[s], b_cw, b_u[s]], writes=[b_u[s]])
            T.op("dve", lambda e: e.scalar_tensor_tensor(out=u[s][:], in0=ge[s][:, 2:2 + TOK],
                                                         scalar=cws[:, 3 * f + 2:3 * f + 3], in1=u[s][:],
                                                         op0=ALU.mult, op1=ALU.add),
                 reads=[b_ge[s], b_cw, b_u[s]], writes=[b_u[s]])
            T.op("act", lambda e: e.activation(out=gl[s][:], in_=u[s][:], func=AF.Gelu_apprx_tanh),
                 reads=[b_u[s]], writes=[b_gl[s]])
            for j in range(TOK // 512):
                c0 = 1 + 512 * j
                pi = 4 + (it % 4); it += 1
                for k in range(KT):
                    T.op("pe", lambda e, k=k: e.matmul(ps[pi][:, :], wu[s][:, k, :], xn[:, k, c0:c0 + 512],
                                                       start=(k == 0), stop=(k == KT - 1)),
                         reads=[b_wu[s], b_xn[k]], writes=[psb[pi]], inc=(k == KT - 1))
                T.op("dve", lambda e: e.tensor_tensor(out=hf[s][:, 512 * j:512 * (j + 1)], in0=ps[pi][:, :],
                                                      in1=gl[s][:, 512 * j:512 * (j + 1)], op=ALU.mult),
                     reads=[psb[pi], b_gl[s]], writes=[b_hf[s]])
            T.dma("sp", hmid_scr[f * 128:(f + 1) * 128, :], hf[s][:], reads=[b_hf[s]], writes=[hmid_b[f]])

    T.barrier()
    with contextlib.ExitStack() as st2:
        sb2 = lambda name, shape, dt: st2.enter_context(nc.sbuf_tensor(uniq(name), shape, dt))
        CH = 1024
        hms = sb2("hms", [128, FT, CH], BF16); b_hms = [Buf() for _ in range(4)]
        wd = [sb2("wd%d" % i, [128, FT, 128], BF16) for i in range(2)]; b_wd = [Buf(), Buf()]
        hr = [sb2("hr%d" % i, [128, CH], F32) for i in range(2)]; b_hr = [Buf(), Buf()]
        ot = [sb2("ot%d" % i, [128, CH], F32) for i in range(2)]; b_ot = [Buf(), Buf()]
        hv = hmid_scr.rearrange("(f p) t -> p f t", p=128)
        wdv = w_down.rearrange("(f p) n -> p f n", p=128)
        it = 0
        items = [(c, n) for c in range(TOK // CH) for n in range(KT)]

        def loadd(idx):
            c, n = items[idx]
            T.dma("pool", wd[idx % 2][:], wdv[:, :, n * 128:(n + 1) * 128], writes=[b_wd[idx % 2]])
            T.dma("sp", hr[idx % 2][:], hm_scr[n * 128:(n + 1) * 128, 1 + c * CH:1 + (c + 1) * CH],
                  reads=[hm_b[n]], writes=[b_hr[idx % 2]])

        loadd(0)
        for idx, (c, n) in enumerate(items):
            if n == 0:
                for f0 in range(0, FT, 11):
                    T.dma("sp", hms[:, f0:f0 + 11, :], hv[:, f0:f0 + 11, c * CH:(c + 1) * CH],
                          reads=hmid_b[f0:f0 + 11], writes=[b_hms[f0 // 11]])
            if idx + 1 < len(items):
                loadd(idx + 1)
            s = idx % 2
            for jj in range(CH // 512):
                pi = it % 4; it += 1
                for f in range(FT):
                    T.op("pe", lambda e, f=f: e.matmul(ps[pi][:, :], wd[s][:, f, :], hms[:, f, 512 * jj:512 * (jj + 1)],
                                                       start=(f == 0), stop=(f == FT - 1)),
                         reads=[b_wd[s], b_hms[f // 11]], writes=[psb[pi]], inc=(f == FT - 1))
                T.op("dve", lambda e: e.tensor_tensor(out=ot[s][:, 512 * jj:512 * (jj + 1)], in0=ps[pi][:, :],
                                                      in1=hr[s][:, 512 * jj:512 * (jj + 1)], op=ALU.add),
                     reads=[psb[pi], b_hr[s]], writes=[b_ot[s]])
            tk = T.dma("sp", hT_out[n * 128:(n + 1) * 128, c * CH:(c + 1) * CH], ot[s][:], reads=[b_ot[s]])
            if on_store is not None:
                on_store(n, c, TOK // CH, tk)


def build_post(debug=False):
    nc = bass.Bass("TRN2", target_bir_lowering=False)
    dt = lambda name, shape, dtype, kind: nc.dram_tensor(name, shape, dtype, kind=kind).ap()
    hT_ext = dt("hT_ext", [D, EXT], F32, "ExternalInput")
    mT_ext = dt("mT_ext", [D, EXT], BF16, "ExternalInput")
    w_out = dt("w_out", [D, D], F32, "ExternalInput")
    g2 = dt("g2", [128, KT], F32, "ExternalInput")
    w_gate = dt("w_gate", [D, FF], F32, "ExternalInput")
    w_up = dt("w_up", [D, FF], F32, "ExternalInput")
    cw = dt("cw", [128, FT * 3], F32, "ExternalInput")
    cb = dt("cb", [128, FT], F32, "ExternalInput")
    w_down = dt("w_down", [FF, D], F32, "ExternalInput")
    hT_out = dt("hT_out", [D, TOK], F32, "ExternalOutput")
    hmid_scr = dt("hmid_scr", [FF, TOK], BF16, "ExternalOutput" if debug else "Internal")
    hm_scr = dt("hm_scr", [D, EXT], F32, "ExternalOutput" if debug else "Internal")
    with contextlib.ExitStack() as stack:
        T = Trk(nc, stack)
        ps = [stack.enter_context(nc.psum_tensor("ps%d" % i, [128, 512], F32)) for i in range(8)]
        mv = mT_ext.rearrange("(k p) t -> p k t", p=128)
        load_m = lambda T, sb2, mT, k0, b: T.dma("sp", mT[:, k0:k0 + 4, :], mv[:, k0:k0 + 4, :], writes=[b])
        load_h = lambda T, sb2, hb, n, b: T.dma("sp", hb[:], hT_ext[n * 128:(n + 1) * 128, :], writes=[b])
        emit_post(nc, T, stack, ps, load_m, load_h, w_out, g2, w_gate, w_up, cw, cb, w_down, hT_out, hmid_scr, hm_scr)
        T.finish()
    return nc


def lay128(v):
    v = np.asarray(v)
    return np.ascontiguousarray(v.reshape(-1, 128).T)


class Rot:
    def __init__(self, tiles):
        self.tiles = tiles
        self.bufs = [Buf() for _ in tiles]
        self.i = 0

    def next(self):
        i = self.i
        self.i = (i + 1) % len(self.tiles)
        return self.tiles[i], self.bufs[i]


def mm_group(T, out_ap, pairs, pbuf, reads):
    n = len(pairs)
    for i, (l, r) in enumerate(pairs):
        T.op("pe", lambda e: e.matmul(out_ap, l, r, start=(i == 0), stop=(i == n - 1)),
             reads=reads, writes=[pbuf], inc=(i == n - 1))


def emit_norm_chunk(nc, T, hc, b_hc, g1s, b_g1, ones, b_ones, PS, sqr, rstd_t, b_rstd, xc, b_xc, width):
    pt, pb = PS.next()
    for k in range(KT):
        sq, b_sq = sqr.next()
        T.op("act", lambda e: e.activation(out=sq[:, :width], in_=hc[:, k, :width], func=AF.Square),
             reads=[b_hc], writes=[b_sq])
        T.op("pe", lambda e: e.matmul(pt[:, :width], ones[:], sq[:, :width], start=(k == 0), stop=(k == KT - 1)),
             reads=[b_ones, b_sq], writes=[pb])
    T.op("act", lambda e: e.activation(out=rstd_t[:, :width], in_=pt[:, :width], func=AF.Sqrt, scale=1.0 / D, bias=EPS),
         reads=[pb], writes=[b_rstd])
    T.op("dve", lambda e: e.reciprocal(out=rstd_t[:, :width], in_=rstd_t[:, :width]), reads=[b_rstd], writes=[b_rstd])
    for k in range(KT):
        T.op("dve", lambda e: e.scalar_tensor_tensor(out=xc[:, k, :width], in0=hc[:, k, :width], scalar=g1s[:, k:k + 1],
                                                     in1=rstd_t[:, :width], op0=ALU.mult, op1=ALU.mult),
             reads=[b_hc, b_g1, b_rstd], writes=[b_xc])


NW_E = 2080
GLA_SCALE = 128 ** -0.5


def emit_even(nc, T, stack, ps, hload, g1, w_e, gua_f, gua_b, glag, cs, cosm, nsinm, mf, mb, mixT, scr, on_rows=None):
    sb = lambda name, shape, dt: stack.enter_context(nc.sbuf_tensor(uniq(name), shape, dt))
    PS = Rot(ps)
    aT_s, qT_s, kT_s, k_s, v_s, sgT_s, la_s = scr
    NCH = SEQ // 512

    ones = sb("ones", [128, 128], F32); b_ones = Buf()
    T.op("dve", lambda e: e.memset(ones[:], 1.0), writes=[b_ones])
    g1s = sb("g1s", [128, KT], F32); b_g1 = Buf()
    T.dma("sp", g1s[:], g1, writes=[b_g1])
    glags = sb("glags", [128, 2], F32); b_glag = Buf()
    T.dma("sp", glags[:], glag, writes=[b_glag])
    mfs = sb("mfs", [128, 128], BF16); b_mf = Buf()
    T.dma("sp", mfs[:], mf, writes=[b_mf])
    mbs = sb("mbs", [128, 128], BF16); b_mb = Buf()
    T.dma("sp", mbs[:], mb, writes=[b_mb])

    with contextlib.ExitStack() as st2:
        sb2 = lambda name, shape, dt: st2.enter_context(nc.sbuf_tensor(uniq(name), shape, dt))
        we = sb2("we", [128, KT, NW_E], BF16); b_we = [Buf() for _ in range(4)]
        wv = w_e.rearrange("(k p) n -> p k n", p=128)
        for k0 in range(0, KT, 4):
            T.dma("pool", we[:, k0:k0 + 4, :], wv[:, k0:k0 + 4, :], writes=[b_we[k0 // 4]])
        guaf = sb2("guaf", [17, 256], BF16); b_guaf = Buf()
        T.dma("pool", guaf[:], gua_f, writes=[b_guaf])
        guab = sb2("guab", [17, 256], BF16); b_guab = Buf()
        T.dma("pool", guab[:], gua_b, writes=[b_guab])
        zfa = sb2("zfa", [17, 512], BF16); b_zfa = Buf()
        zba = sb2("zba", [17, 512], BF16); b_zba = Buf()
        T.op("dve", lambda e: e.memset(zfa[:], 1.0), writes=[b_zfa])
        T.op("dve", lambda e: e.memset(zba[:], 1.0), writes=[b_zba])
        hcs = [sb2("hc%d" % i, [128, KT, 512], F32) for i in range(2)]; b_hcs = [Buf(), Buf()]
        xcs_e = [sb2("xc%d" % i, [128, KT, 512], BF16) for i in range(2)]; b_xcs_e = [Buf(), Buf()]
        sqr = Rot([sb2("sq%d" % i, [128, 512], F32) for i in range(2)])
        rstd_t = sb2("rstd", [128, 512], F32); b_rstd = Buf()
        stg = Rot([sb2("stg%d" % i, [128, 512], BF16) for i in range(4)])
        lt = Rot([sb2("lt%d" % i, [128, 256], F32) for i in range(2)])
        hload(T, hcs[0], 0, b_hcs[0])
        hload(T, hcs[1], 1, b_hcs[1])
        we_reads = list(b_we)
        emit_norm_chunk(nc, T, hcs[0], b_hcs[0], g1s, b_g1, ones, b_ones, PS, sqr, rstd_t, b_rstd, xcs_e[0], b_xcs_e[0], 512)
        for c in range(NCH):
            t0 = c * 512
            if c + 2 < NCH:
                hload(T, hcs[c % 2], c + 2, b_hcs[c % 2])
            if c + 1 < NCH:
                emit_norm_chunk(nc, T, hcs[(c + 1) % 2], b_hcs[(c + 1) % 2], g1s, b_g1, ones, b_ones, PS, sqr, rstd_t, b_rstd,
                                xcs_e[(c + 1) % 2], b_xcs_e[(c + 1) % 2], 512)
            xc, b_xc = xcs_e[c % 2], b_xcs_e[c % 2]

            def fm(col0, dst, row0, func=AF.Copy, scale=1.0):
                pt, pb = PS.next()
                mm_group(T, pt[:, :], [(we[:, k, col0:col0 + 128], xc[:, k, :]) for k in range(KT)], pb,
                         reads=we_reads + [b_xc])
                s, b_s = stg.next()
                T.op("act", lambda e: e.activation(out=s[:, :], in_=pt[:, :], func=func, scale=scale),
                     reads=[pb], writes=[b_s])
                T.dma("sp", dst[row0:row0 + 128, t0:t0 + 512], s[:, :], reads=[b_s])
            for i in range(4):
                fm(i * 128, aT_s, i * 128)
            for i in range(2):
                fm(512 + i * 128, qT_s, i * 128, scale=GLA_SCALE)
            for i in range(2):
                fm(768 + i * 128, kT_s, i * 128)
            for i in range(4):
                fm(1536 + i * 128, sgT_s, i * 128, func=AF.Silu)
            for (col0, za, b_za) in ((2048, zfa, b_zfa), (2064, zba, b_zba)):
                pt, pb = PS.next()
                mm_group(T, pt[0:16, :], [(we[:, k, col0:col0 + 16], xc[:, k, :]) for k in range(KT)], pb,
                         reads=we_reads + [b_xc])
                T.op("act", lambda e: e.activation(out=za[0:16, :], in_=pt[0:16, :], func=AF.Copy), reads=[pb], writes=[b_za])
            for ts in range(4):
                r0 = t0 + ts * 128
                for (col0, ncol, dst) in ((768, 256, k_s), (1024, 512, v_s)):
                    pt, pb = PS.next()
                    mm_group(T, pt[:, :ncol], [(xc[:, k, ts * 128:(ts + 1) * 128], we[:, k, col0:col0 + ncol])
                                               for k in range(KT)], pb, reads=we_reads + [b_xc])
                    s, b_s = stg.next()
                    T.op("dve", lambda e: e.tensor_copy(out=s[:, :ncol], in_=pt[:, :ncol]), reads=[pb], writes=[b_s])
                    T.dma("sp", dst[r0:r0 + 128, :], s[:, :ncol], reads=[b_s])
                for d, (za, b_za, gu, b_gu) in enumerate(((zfa, b_zfa, guaf, b_guaf), (zba, b_zba, guab, b_guab))):
                    pt, pb = PS.next()
                    T.op("pe", lambda e: e.matmul(pt[:, :256], za[0:17, ts * 128:(ts + 1) * 128], gu[0:17, :],
                                                  start=True, stop=True), reads=[b_za, b_gu], writes=[pb])
                    l, b_l = lt.next()
                    T.op("act", lambda e: e.activation(out=l[:, :], in_=pt[:, :256], func=AF.Exp, scale=-1.0),
                         reads=[pb], writes=[b_l])
                    T.op("act", lambda e: e.activation(out=l[:, :], in_=l[:, :], func=AF.Ln, bias=1.0),
                         reads=[b_l], writes=[b_l])
                    s, b_s = stg.next()
                    T.op("dve", lambda e: e.tensor_scalar(out=s[:, :256], in0=l[:, :], scalar1=-1.0 / 16.0, scalar2=None,
                                                          op0=ALU.mult), reads=[b_l], writes=[b_s])
                    T.dma("sp", la_s[d, r0:r0 + 128, :], s[:, :256], reads=[b_s])
    T.barrier()

    with contextlib.ExitStack() as st2:
        sb2 = lambda name, shape, dt: st2.enter_context(nc.sbuf_tensor(uniq(name), shape, dt))
        aT = sb2("aT", [128, 4, SEQ], BF16); b_aT = Buf()
        T.dma("sp", aT[:], aT_s.rearrange("(k p) t -> p k t", p=128), writes=[b_aT])
        css = sb2("css", [128, 2, 512], BF16); b_cs = Buf()
        T.dma("sp", css[:], cs.rearrange("(k p) n -> p k n", p=128), writes=[b_cs])
        Z = sb2("Z", [128, 32, 2, 512], BF16); b_Z = Buf()
        for g in range(2):
            for st in range(32):
                pt, pb = PS.next()
                mm_group(T, pt[:, :], [(aT[:, 2 * g + kk, st * 128:(st + 1) * 128], css[:, kk, :]) for kk in range(2)],
                         pb, reads=[b_aT, b_cs])
                if st % 2 == 0:
                    T.op("act", lambda e: e.activation(out=Z[:, st, g, :], in_=pt[:, :], func=AF.Copy),
                         reads=[pb], writes=[b_Z])
                else:
                    T.op("dve", lambda e: e.tensor_copy(out=Z[:, st, g, :], in_=pt[:, :]), reads=[pb], writes=[b_Z])
        SG = 4
        if DBG.get("skip_fnet2"):
            SEQ_ = 0
        else:
            SEQ_ = SEQ
        cbuf = Rot([sb2("cb%d" % i, [128, 2, SG, 512], BF16) for i in range(3)])
        stg = Rot([sb2("stg%d" % i, [128, 512], BF16) for i in range(4)])
        cv = cosm.rearrange("(st p) n -> p st n", p=128)
        sv = nsinm.rearrange("(st p) n -> p st n", p=128)
        pieces = [(sp_, sg) for sp_ in range(SEQ_ // 512) for sg in range(32 // SG)]

        def loadc(idx):
            sp_, sg = pieces[idx]
            t, b = cbuf.next()
            T.dma("sp", t[:, 0, :, :], cv[:, sg * SG:(sg + 1) * SG, sp_ * 512:(sp_ + 1) * 512], writes=[b])
            T.dma("sp", t[:, 1, :, :], sv[:, sg * SG:(sg + 1) * SG, sp_ * 512:(sp_ + 1) * 512], writes=[b])
            return t, b
        q = [loadc(0), loadc(1)] if pieces else []
        for idx, (sp_, sg) in enumerate(pieces):
            if idx + 2 < len(pieces):
                q.append(loadc(idx + 2))
            t, b = q.pop(0)
            bank0 = (sp_ % 2) * 4
            for ct in range(4):
                g, hf = ct // 2, ct % 2
                for s_ in range(SG):
                    stile = sg * SG + s_
                    for part in range(2):
                        first = (sg == 0 and s_ == 0 and part == 0)
                        last = (sg == 32 // SG - 1 and s_ == SG - 1 and part == 1)
                        T.op("pe", lambda e: e.matmul(ps[bank0 + ct][:, :],
                                                      Z[:, stile, g, part * 256 + hf * 128: part * 256 + hf * 128 + 128],
                                                      t[:, part, s_, :], start=first, stop=last),
                             reads=[b_Z, b], writes=[PS.bufs[bank0 + ct]], inc=last or (s_ == SG - 1 and part == 1 and ct == 3))
            if sg == 32 // SG - 1:
                for ct in range(4):
                    s, b_s = stg.next()
                    T.op("act" if ct % 2 == 0 else "dve",
                         (lambda e: e.activation(out=s[:, :], in_=ps[bank0 + ct][:, :], func=AF.Copy)) if ct % 2 == 0 else
                         (lambda e: e.tensor_copy(out=s[:, :], in_=ps[bank0 + ct][:, :])),
                         reads=[PS.bufs[bank0 + ct]], writes=[b_s])
                    T.dma("sp", mixT[ct * 128:(ct + 1) * 128, sp_ * 512:(sp_ + 1) * 512], s[:, :], reads=[b_s])
    T.barrier()
    if on_rows is not None:
        on_rows(0, []); on_rows(1, [])

    with contextlib.ExitStack() as st2:
        sb2 = lambda name, shape, dt: st2.enter_context(nc.sbuf_tensor(uniq(name), shape, dt))
        qT = sb2("qT", [128, SEQ], BF16); kT = sb2("kT", [128, SEQ], BF16)
        kk_ = sb2("ktok", [128, 32, 128], BF16); vv = sb2("vtok", [128, 32, 256], BF16)
        la = [sb2("la%d" % d, [128, 32, 128], BF16) for d in range(2)]
        sg_ = sb2("sgT", [128, 2, SEQ], BF16)
        oacc = sb2("oacc", [128, 2, SEQ], F32)
        b_in = Buf(); b_sg = Buf()
        b_o = [Buf() for _ in range(32)]
        S32 = [sb2("S32_%d" % d, [128, 256], F32) for d in range(2)]; b_S32 = [Buf(), Buf()]
        Sbf = [sb2("Sbf_%d" % d, [128, 256], BF16) for d in range(2)]; b_Sbf = [Buf(), Buf()]
        tmpS = [sb2("tmpS_%d" % d, [128, 256], F32) for d in range(2)]; b_tmpS = [Buf(), Buf()]
        mk_t = lambda nm, dt_, n=6: Rot([sb2("%s%d" % (nm, i), [128, 128], dt_) for i in range(n)])
        ebR, enbR, entR = mk_t("eb", F32, 6), mk_t("enb", F32, 4), mk_t("ent", F32, 4)
        qtR, ktR, ktokR, pR = mk_t("qt", BF16), mk_t("kt", BF16), mk_t("ktk", BF16), mk_t("pp", BF16)
        elR = Rot([sb2("el%d" % i, [128, 2], F32) for i in range(4)])
        sq2 = Rot([sb2("sqq%d" % i, [128, 512], F32) for i in range(2)])
        rs2 = sb2("rs2", [128, 512], F32); b_rs2 = Buf()
        tm2 = Rot([sb2("tm%d" % i, [128, 512], F32) for i in range(2)])
        stg = Rot([sb2("stg%d" % i, [128, 512], BF16) for i in range(2)])
        masks = ((mfs, b_mf), (mbs, b_mb))
        kmR = mk_t("km", BF16)
        NHEAD_ = 0 if DBG.get("skip_gla") else 2
        cmask = sb2("cmask", [128, 2], F32); b_cmask = Buf()
        T.op("dve", lambda e: e.memset(cmask[:], 0.0), writes=[b_cmask])
        T.op("dve", lambda e: e.memset(cmask[0:64, 0:1], 1.0), writes=[b_cmask])
        T.op("dve", lambda e: e.memset(cmask[64:128, 1:2], 1.0), writes=[b_cmask])
        for hh in range(NHEAD_):
            T.dma("sp", qT[:], qT_s[hh * 128:(hh + 1) * 128, :], writes=[b_in])
            T.dma("sp", kT[:], kT_s[hh * 128:(hh + 1) * 128, :], writes=[b_in])
            T.dma("sp", kk_[:], k_s.rearrange("(i p) c -> p i c", p=128)[:, :, hh * 128:(hh + 1) * 128], writes=[b_in])
            T.dma("sp", vv[:], v_s.rearrange("(i p) c -> p i c", p=128)[:, :, hh * 256:(hh + 1) * 256], writes=[b_in])
            for d in range(2):
                T.dma("sp", la[d][:], la_s[d].rearrange("(i p) c -> p i c", p=128)[:, :, hh * 128:(hh + 1) * 128],
                      writes=[b_in])
            T.dma("sp", sg_[:], sgT_s.rearrange("(k p) t -> p k t", p=128)[:, 2 * hh:2 * hh + 2, :], writes=[b_sg])
            for d in range(2):
                T.op("dve", lambda e: e.memset(S32[d][:], 0.0), writes=[b_S32[d]])
                T.op("dve", lambda e: e.memset(Sbf[d][:], 0.0), writes=[b_Sbf[d]])
                T.op("dve", lambda e: e.memset(tmpS[d][:], 0.0), writes=[b_tmpS[d]])
            written = [False] * 32
            def phase12(step):
                    ctx = []
                    for d in range(2):
                        i = step if d == 0 else 31 - step
                        M, b_M = masks[d]
                        c0 = i * 128
                        pbT, bbT = PS.next()
                        T.op("pe", lambda e: e.matmul(pbT[:, :128], la[d][:, i, :], M[:, :], start=True, stop=True),
                             reads=[b_in, b_M], writes=[bbT])
                        pbk, bbk = PS.next()
                        T.op("pe", lambda e: e.matmul(pbk[:, :128], M[:, :], la[d][:, i, :], start=True, stop=True),
                             reads=[b_in, b_M], writes=[bbk])
                        eb, b_eb = ebR.next(); enb, b_enb = enbR.next(); ent, b_ent = entR.next()
                        T.op("act", lambda e: e.activation(out=eb[:, :], in_=pbT[:, :128], func=AF.Exp), reads=[bbT], writes=[b_eb])
                        T.op("act", lambda e: e.activation(out=enb[:, :], in_=pbT[:, :128], func=AF.Exp, scale=-1.0),
                             reads=[bbT], writes=[b_enb])
                        T.op("act", lambda e: e.activation(out=ent[:, :], in_=pbk[:, :128], func=AF.Exp, scale=-1.0),
                             reads=[bbk], writes=[b_ent])
                        qt, b_qt = qtR.next(); kt, b_kt = ktR.next(); ktok, b_ktok = ktokR.next()
                        T.op("dve", lambda e: e.tensor_tensor(out=qt[:, :], in0=qT[:, c0:c0 + 128], in1=eb[:, :], op=ALU.mult),
                             reads=[b_in, b_eb], writes=[b_qt])
                        T.op("dve", lambda e: e.tensor_tensor(out=kt[:, :], in0=kT[:, c0:c0 + 128], in1=enb[:, :], op=ALU.mult),
                             reads=[b_in, b_enb], writes=[b_kt])
                        T.op("dve", lambda e: e.tensor_tensor(out=ktok[:, :], in0=kk_[:, i, :], in1=ent[:, :], op=ALU.mult),
                             reads=[b_in, b_ent], writes=[b_ktok])
                        ecols = (63, 127) if d == 0 else (0, 64)
                        ctx.append((i, c0, M, b_M, qt, b_qt, kt, b_kt, ktok, b_ktok, (eb, ecols), b_eb))
                    pps = []
                    for d in range(2):
                        i, c0, M, b_M, qt, b_qt, kt, b_kt, ktok, b_ktok, el, b_el = ctx[d]
                        psc, bsc = PS.next()
                        T.op("pe", lambda e: e.matmul(psc[:, :128], kt[:, :], qt[:, :], start=True, stop=True),
                             reads=[b_kt, b_qt], writes=[bsc])
                        pp, b_pp = pR.next()
                        T.op("dve", lambda e: e.tensor_tensor(out=pp[:, :], in0=psc[:, :128], in1=M[:, :], op=ALU.mult),
                             reads=[bsc, b_M], writes=[b_pp])
                        pps.append((pp, b_pp))
                    return ctx, pps

            def phase34(step, ctx, pps, nctx):
                    pos = []
                    for d in range(2):
                        i = ctx[d][0]
                        po, bo = PS.next()
                        pp, b_pp = pps[d]
                        for vt in range(2):
                            T.op("pe", lambda e: e.matmul(po[:, vt * 128:(vt + 1) * 128], vv[:, i, vt * 128:(vt + 1) * 128], pp[:, :],
                                                          start=True, stop=True),
                                 reads=[b_in, b_pp], writes=[bo], inc=False)
                        pos.append((po, bo))
                    for half in range(2):
                        for d in range(2):
                            if DBG.get("no_inter"):
                                if half == 1:
                                    po, bo = pos[d]
                                    T.op("pe", lambda e: e.matmul(po[:, 256:320], Sbf[d][:, 0:128], ctx[d][4][:, 0:64], start=True, stop=True),
                                         reads=[b_Sbf[d]], writes=[bo])
                                continue
                            i, c0, M, b_M, qt, b_qt, kt, b_kt, ktok, b_ktok, el, b_el = ctx[d]
                            po, bo = pos[d]
                            ch = half if d == 0 else 1 - half
                            r0 = ch * 64
                            for vt in range(2):
                                T.op("pe", lambda e: e.matmul(po[:, 256 + vt * 128 + r0: 256 + vt * 128 + r0 + 64],
                                                              Sbf[d][:, vt * 128:(vt + 1) * 128], qt[:, r0:r0 + 64],
                                                              start=True, stop=True),
                                     reads=[b_Sbf[d], b_qt], writes=[bo], inc=(half == 1 and vt == 1))
                            pd, bd = PS.next()
                            T.op("pe", lambda e: e.matmul(pd[:, :256], ktok[r0:r0 + 64, :], vv[r0:r0 + 64, i, :],
                                                          start=True, stop=True), reads=[b_ktok, b_in], writes=[bd])
                            ebt, ecols = el
                            esc = ebt[:, ecols[ch]:ecols[ch] + 1]
                            T.op("dve", lambda e: e.scalar_tensor_tensor(out=Sbf[d][:], in0=pd[:, :256], scalar=esc,
                                                                         in1=tmpS[d][:], op0=ALU.mult, op1=ALU.add),
                                 reads=[bd, b_el, b_tmpS[d]], writes=[b_Sbf[d]])
                            T.op("dve", lambda e: e.scalar_tensor_tensor(out=S32[d][:], in0=pd[:, :256], scalar=esc,
                                                                         in1=tmpS[d][:], op0=ALU.mult, op1=ALU.add),
                                 reads=[bd, b_el, b_tmpS[d]], writes=[b_S32[d]])
                            if half == 0:
                                nel, nb_el, nch = el, b_el, 1 - ch
                            elif nctx is not None:
                                nel, nb_el = nctx[d][10], nctx[d][11]
                                nch = 0 if d == 0 else 1
                            else:
                                nel = None
                            if nel is not None:
                                nesc = nel[0][:, nel[1][nch]:nel[1][nch] + 1]
                                T.op("dve", lambda e: e.tensor_scalar(out=tmpS[d][:], in0=S32[d][:], scalar1=nesc, scalar2=None,
                                                                      op0=ALU.mult),
                                     reads=[b_S32[d], nb_el], writes=[b_tmpS[d]])
                    for d in range(2):
                        i, c0 = ctx[d][0], ctx[d][1]
                        po, bo = pos[d]
                        o3 = oacc[:, :, c0:c0 + 128]
                        p_intra = po[:, 0:256].rearrange("p (v t) -> p v t", v=2)
                        p_inter = po[:, 256:512].rearrange("p (v t) -> p v t", v=2)
                        if not written[i]:
                            T.op("act", lambda e: e.activation(out=o3, in_=p_intra, func=AF.Copy), reads=[bo], writes=[b_o[i]])
                        else:
                            T.op("dve", lambda e: e.tensor_tensor(out=o3, in0=p_intra, in1=o3, op=ALU.add),
                                 reads=[bo, b_o[i]], writes=[b_o[i]])
                        T.op("dve", lambda e: e.tensor_tensor(out=o3, in0=p_inter, in1=o3, op=ALU.add),
                             reads=[bo, b_o[i]], writes=[b_o[i]])
                        written[i] = True

            nxt = phase12(0)
            for step in range(32):
                cur = nxt
                nxt = phase12(step + 1) if step + 1 < 32 else None
                phase34(step, cur[0], cur[1], nxt[0] if nxt is not None else None)
            if DBG.get("dump") is not None and hh == 0:
                do, dS = DBG["dump"]
                for vt in range(2):
                    T.dma("sp", do[vt * 128:(vt + 1) * 128, :], oacc[:, vt, :], reads=b_o)
                for d in range(2):
                    T.dma("sp", dS[d], S32[d][:], reads=[b_S32[d]])
            row_toks = []
            for c in range(NCH):
                t0 = c * 512
                pt, pb = PS.next()
                for vt in range(2):
                    sq, b_sq = sq2.next()
                    T.op("act", lambda e: e.activation(out=sq[:, :], in_=oacc[:, vt, t0:t0 + 512], func=AF.Square),
                         reads=b_o[4 * c:4 * c + 4], writes=[b_sq])
                    T.op("pe", lambda e: e.matmul(pt[:, :], ones[:], sq[:, :], start=(vt == 0), stop=(vt == 1)),
                         reads=[b_ones, b_sq], writes=[pb])
                T.op("act", lambda e: e.activation(out=rs2[:, :], in_=pt[:, :], func=AF.Sqrt, scale=1.0 / 256, bias=EPS),
                     reads=[pb], writes=[b_rs2])
                T.op("dve", lambda e: e.reciprocal(out=rs2[:, :], in_=rs2[:, :]), reads=[b_rs2], writes=[b_rs2])
                for vt in range(2):
                    tm, b_tm = tm2.next()
                    T.op("dve", lambda e: e.scalar_tensor_tensor(out=tm[:, :], in0=oacc[:, vt, t0:t0 + 512],
                                                                 scalar=glags[:, vt:vt + 1], in1=rs2[:, :],
                                                                 op0=ALU.mult, op1=ALU.mult),
                         reads=b_o[4 * c:4 * c + 4] + [b_glag, b_rs2], writes=[b_tm])
                    s, b_s = stg.next()
                    T.op("dve", lambda e: e.tensor_tensor(out=s[:, :], in0=tm[:, :], in1=sg_[:, vt, t0:t0 + 512], op=ALU.mult),
                         reads=[b_tm, b_sg], writes=[b_s])
                    r = 512 + hh * 256 + vt * 128
                    row_toks.append(T.dma("sp", mixT[r:r + 128, t0:t0 + 512], s[:, :], reads=[b_s]))
            if on_rows is not None:
                on_rows(2 + hh, row_toks)


def build_even(debug=False):
    nc = bass.Bass("TRN2", target_bir_lowering=False)
    dt = lambda name, shape, dtype, kind="ExternalInput": nc.dram_tensor(name, shape, dtype, kind=kind).ap()
    hT = dt("hT", [D, SEQ], F32)
    g1 = dt("g1", [128, KT], F32)
    w_e = dt("w_e", [D, NW_E], F32)
    gua_f = dt("gua_f", [17, 256], F32)
    gua_b = dt("gua_b", [17, 256], F32)
    glag = dt("glag", [128, 2], F32)
    cs = dt("cs", [256, 512], BF16)
    cosm = dt("cosm", [SEQ, SEQ], BF16)
    nsinm = dt("nsinm", [SEQ, SEQ], BF16)
    mf = dt("mf", [128, 128], BF16)
    mb = dt("mb", [128, 128], BF16)
    mixT = dt("mixT", [1024, SEQ], BF16, "ExternalOutput")
    kind = "ExternalOutput" if debug else "Internal"
    scr = (dt("aT_s", [512, SEQ], BF16, kind), dt("qT_s", [256, SEQ], BF16, kind), dt("kT_s", [256, SEQ], BF16, kind),
           dt("k_s", [SEQ, 256], BF16, kind), dt("v_s", [SEQ, 512], BF16, kind), dt("sgT_s", [512, SEQ], BF16, kind),
           dt("la_s", [2, SEQ, 256], BF16, kind))
    if debug:
        DBG["dump"] = (dt("dbg_o", [256, SEQ], F32, "ExternalOutput"), dt("dbg_S", [2, 128, 256], F32, "ExternalOutput"))
    with contextlib.ExitStack() as stack:
        T = Trk(nc, stack)
        ps = [stack.enter_context(nc.psum_tensor("ps%d" % i, [128, 512], F32)) for i in range(8)]
        hv = hT.rearrange("(k p) t -> p k t", p=128)
        hload = lambda T, t, c, b: T.dma("sp", t[:], hv[:, :, c * 512:(c + 1) * 512], writes=[b])
        emit_even(nc, T, stack, ps, hload, g1, w_e, gua_f, gua_b, glag, cs, cosm, nsinm, mf, mb, mixT, scr)
        T.finish()
    return nc


def even_consts():
    c = np.arange(256)
    ang = 2 * np.pi * ((c[:, None] * c[None, :]) % 256) / 256.0
    cs = np.concatenate([np.cos(ang), np.sin(ang)], axis=1) / 16.0
    s = np.arange(SEQ, dtype=np.int64)
    ang = 2 * np.pi * ((s[:, None] * s[None, :]) % SEQ) / float(SEQ)
    cosm = (np.cos(ang) / 64.0).astype(NPBF)
    nsinm = (-np.sin(ang) / 64.0).astype(NPBF)
    p = np.arange(128)
    same = (p[:, None] // 64) == (p[None, :] // 64)
    mf = (same & (p[:, None] <= p[None, :])).astype(np.float32).astype(NPBF)
    mb = (same & (p[:, None] >= p[None, :])).astype(np.float32).astype(NPBF)
    return dict(cs=cs.astype(NPBF), cosm=cosm, nsinm=nsinm, mf=mf, mb=mb)


def even_weights(j, w_in, gu_f, gb_f, gu_b, gb_b, gla_g):
    cols = np.r_[512 * j:512 * j + 512, 1024 + 256 * j:1024 + 256 * j + 256, 1536 + 256 * j:1536 + 256 * j + 256,
                 2048 + 512 * j:2048 + 512 * j + 512, 3072 + 512 * j:3072 + 512 * j + 512, 4096:4128]
    hc = slice(256 * j, 256 * j + 256)
    return dict(w_e=np.ascontiguousarray(w_in[:, cols]),
                gua_f=np.ascontiguousarray(np.concatenate([gu_f[:, hc], gb_f[None, hc]], 0)),
                gua_b=np.ascontiguousarray(np.concatenate([gu_b[:, hc], gb_b[None, hc]], 0)),
                glag=lay128(gla_g))


NBU = 3072
NKT = 20


def emit_odd(nc, T, stack, ps, hload, g1, w_o, qg, kg, rb, onehot, cmult, mixT, scr, on_rows=None):
    sb = lambda name, shape, dt: stack.enter_context(nc.sbuf_tensor(uniq(name), shape, dt))
    PS = Rot(ps)
    xn_s, qT_s, kT_s, v_s, u_s = scr
    NCH = SEQ // 512
    ones = sb("ones", [128, 128], F32); b_ones = Buf()
    T.op("dve", lambda e: e.memset(ones[:], 1.0), writes=[b_ones])
    onesb = sb("onesb", [128, 128], BF16); b_onesb = Buf()
    T.op("dve", lambda e: e.memset(onesb[:], 1.0), writes=[b_onesb])
    g1s = sb("g1s", [128, KT], F32); b_g1 = Buf()
    T.dma("sp", g1s[:], g1, writes=[b_g1])
    qgs = sb("qgs", [128, 1], F32); b_qg = Buf()
    T.dma("sp", qgs[:], qg, writes=[b_qg])
    kgs = sb("kgs", [128, 1], F32); b_kg = Buf()
    T.dma("sp", kgs[:], kg, writes=[b_kg])

    with contextlib.ExitStack() as st2:
        sb2 = lambda name, shape, dt: st2.enter_context(nc.sbuf_tensor(uniq(name), shape, dt))
        rbs = sb2("rbs", [32, 8], F32); b_rb = Buf()
        T.dma("sp", rbs[:], rb, writes=[b_rb])
        oh = sb2("oh", [32, NBU], F32); b_oh = Buf()
        T.dma("sp", oh[:], onehot, writes=[b_oh])
        cm = sb2("cm", [8, NBU], F32); b_cm = Buf()
        T.dma("sp", cm[:], cmult, writes=[b_cm])
        ue = sb2("ue", [8, NBU], F32); b_ue = Buf()
        ub = sb2("ub", [8, NBU], BF16); b_ub = Buf()
        for c in range(NBU // 512):
            pt, pb = PS.next()
            T.op("pe", lambda e: e.matmul(pt[0:8, :], rbs[:, :], oh[:, c * 512:(c + 1) * 512], start=True, stop=True),
                 reads=[b_rb, b_oh], writes=[pb])
            T.op("act", lambda e: e.activation(out=ue[:, c * 512:(c + 1) * 512], in_=pt[0:8, :], func=AF.Exp),
                 reads=[pb], writes=[b_ue])
        T.op("dve", lambda e: e.tensor_tensor(out=ub[:, :], in0=ue[:, :], in1=cm[:, :], op=ALU.mult),
             reads=[b_ue, b_cm], writes=[b_ub])
        b_us = Buf()
        T.dma("sp", u_s, ub[:, :], reads=[b_ub], writes=[b_us])

        hcs = [sb2("hc%d" % i, [128, KT, 512], F32) for i in range(2)]; b_hcs = [Buf(), Buf()]
        xcs = [sb2("xc%d" % i, [128, KT, 512], BF16) for i in range(2)]; b_xcs = [Buf(), Buf()]
        sqr = Rot([sb2("sq%d" % i, [128, 512], F32) for i in range(2)])
        rstd_t = sb2("rstd", [128, 512], F32); b_rstd = Buf()
        xv = xn_s.rearrange("(k p) t -> p k t", p=128)
        b_xn = [Buf() for _ in range(NCH)]
        hload(T, hcs[0], 0, b_hcs[0])
        for c in range(NCH):
            t0 = c * 512
            if c + 1 < NCH:
                hload(T, hcs[(c + 1) % 2], c + 1, b_hcs[(c + 1) % 2])
            emit_norm_chunk(nc, T, hcs[c % 2], b_hcs[c % 2], g1s, b_g1, ones, b_ones, PS, sqr, rstd_t, b_rstd,
                            xcs[c % 2], b_xcs[c % 2], 512)
            T.dma("sp", xv[:, :, t0:t0 + 512], xcs[c % 2][:], reads=[b_xcs[c % 2]], writes=[b_xn[c]])
    T.barrier()

    with contextlib.ExitStack() as st2:
        sb2 = lambda name, shape, dt: st2.enter_context(nc.sbuf_tensor(uniq(name), shape, dt))
        wp = [sb2("wp%d" % i, [128, KT, 1024], BF16) for i in range(2)]; b_wp = [Buf(), Buf()]
        xcs = [sb2("xc%d" % i, [128, KT, 512], BF16) for i in range(2)]; b_xcs = [Buf(), Buf()]
        sqr = Rot([sb2("sq%d" % i, [128, 512], F32) for i in range(3)])
        rsr = Rot([sb2("rs%d" % i, [128, 512], F32) for i in range(3)])
        stg = Rot([sb2("stg%d" % i, [128, 512], BF16) for i in range(4)])
        wv = w_o.rearrange("(k p) n -> p k n", p=128)
        xv = xn_s.rearrange("(k p) t -> p k t", p=128)
        T.dma("pool", wp[0][:], wv[:, :, 0:1024], writes=[b_wp[0]])
        items = [(part, c) for part in range(3) for c in range(NCH)]
        T.dma("sp", xcs[0][:], xv[:, :, 0:512], reads=[b_xn[0]], writes=[b_xcs[0]])
        for idx, (part, c) in enumerate(items):
            t0 = c * 512
            if c == 0 and part + 1 < 3:
                T.dma("pool", wp[(part + 1) % 2][:], wv[:, :, (part + 1) * 1024:(part + 2) * 1024],
                      writes=[b_wp[(part + 1) % 2]])
            if idx + 1 < len(items):
                c2 = items[idx + 1][1]
                T.dma("sp", xcs[(idx + 1) % 2][:], xv[:, :, c2 * 512:(c2 + 1) * 512], reads=[b_xn[c2]],
                      writes=[b_xcs[(idx + 1) % 2]])
            xc, b_xc = xcs[idx % 2], b_xcs[idx % 2]
            w_, b_w = wp[part % 2], b_wp[part % 2]
            if part < 2:
                gs, b_gs = (qgs, b_qg) if part == 0 else (kgs, b_kg)
                dst = qT_s if part == 0 else kT_s
                sc_, bi_ = (1.0, 128 * EPS) if part == 0 else (1.0 / 128, EPS)
                pend_n = None
                for hd in range(8):
                    pt, pb = PS.next()
                    mm_group(T, pt[:, :], [(w_[:, k, hd * 128:(hd + 1) * 128], xc[:, k, :]) for k in range(KT)], pb,
                             reads=[b_w, b_xc])
                    sq, b_sq = sqr.next()
                    T.op("act", lambda e: e.activation(out=sq[:, :], in_=pt[:, :], func=AF.Square), reads=[pb], writes=[b_sq])
                    if pend_n is not None:
                        pend_n()

                    def fin(hd=hd, pt=pt, pb=pb, sq=sq, b_sq=b_sq):
                        p2, pb2 = PS.next()
                        T.op("pe", lambda e: e.matmul(p2[:, :], ones[:], sq[:, :], start=True, stop=True),
                             reads=[b_ones, b_sq], writes=[pb2])
                        rs, b_rs = rsr.next()
                        T.op("act", lambda e: e.activation(out=rs[:, :], in_=p2[:, :], func=AF.Sqrt, scale=sc_, bias=bi_),
                             reads=[pb2], writes=[b_rs])
                        T.op("dve", lambda e: e.reciprocal(out=rs[:, :], in_=rs[:, :]), reads=[b_rs], writes=[b_rs])
                        s, b_s = stg.next()
                        T.op("dve", lambda e: e.scalar_tensor_tensor(out=s[:, :], in0=pt[:, :], scalar=gs[:, 0:1], in1=rs[:, :],
                                                                     op0=ALU.mult, op1=ALU.mult),
                             reads=[pb, b_gs, b_rs], writes=[b_s])
                        T.dma("sp", dst[hd * 128:(hd + 1) * 128, t0:t0 + 512], s[:, :], reads=[b_s])
                    pend_n = fin
                pend_n()
            else:
                for ts in range(4):
                    for half in range(2):
                        pt, pb = PS.next()
                        mm_group(T, pt[:, :], [(xc[:, k, ts * 128:(ts + 1) * 128], w_[:, k, half * 512:(half + 1) * 512])
                                               for k in range(KT)], pb, reads=[b_w, b_xc])
                        s, b_s = stg.next()
                        if half == 0:
                            T.op("act", lambda e: e.activation(out=s[:, :], in_=pt[:, :], func=AF.Copy), reads=[pb], writes=[b_s])
                        else:
                            T.op("dve", lambda e: e.tensor_copy(out=s[:, :], in_=pt[:, :]), reads=[pb], writes=[b_s])
                        T.dma("sp", v_s[t0 + ts * 128:t0 + (ts + 1) * 128, half * 512:(half + 1) * 512], s[:, :], reads=[b_s])
    T.barrier()

    with contextlib.ExitStack() as st2:
        sb2 = lambda name, shape, dt: st2.enter_context(nc.sbuf_tensor(uniq(name), shape, dt))
        qTs = [sb2("qT%d" % i, [128, SEQ], BF16) for i in range(2)]
        kTs = [sb2("kT%d" % i, [128, SEQ], BF16) for i in range(2)]
        vts = [sb2("vt%d" % i, [128, 32, 128], BF16) for i in range(2)]
        Es = [sb2("E%d" % i, [128, NKT, 512], BF16) for i in range(2)]
        b_q = [Buf(), Buf()]; b_k = [Buf(), Buf()]; b_v = [Buf(), Buf()]; b_E = [Buf(), Buf()]
        pex = Rot([sb2("pex%d" % i, [128, 512], BF16) for i in range(4)])
        pmr = Rot([sb2("pm%d" % i, [128, 512], BF16) for i in range(5)])
        rdr = Rot([sb2("rd%d" % i, [128, 512], F32) for i in range(2)])
        stg = Rot([sb2("stg%d" % i, [128, 512], BF16) for i in range(2)])
        vv_ = v_s.rearrange("(i p) c -> p i c", p=128)
        SPS = Rot(ps[0:4]); OPS = Rot(ps[4:6]); DPS = Rot(ps[6:8])

        def loadh(h):
            s = h % 2
            T.dma("sp", qTs[s][:], qT_s[h * 128:(h + 1) * 128, :], writes=[b_q[s]])
            T.dma("sp", kTs[s][:], kT_s[h * 128:(h + 1) * 128, :], writes=[b_k[s]])
            T.dma("sp", vts[s][:], vv_[:, :, h * 128:(h + 1) * 128], writes=[b_v[s]])
            T.dma("sp", Es[s][:], bass.AP(u_s.tensor, u_s.offset + h * NBU, [[1, 128], [128, NKT], [1, 512]]),
                  reads=[b_us], writes=[b_E[s]])

        def rev(ap):
            return bass.AP(ap.tensor, ap.offset + 511, [list(ap.ap[0]), [-1, 512]])

        loadh(0)
        loadh(1)
        iters = []
        for h in range(8):
            for qt in range(NCH):
                tiles = [a for a in range(NKT) if 0 <= qt * 512 - 1024 + 128 * a < SEQ]
                for n_, a in enumerate(tiles):
                    iters.append((h, qt, a, n_ == 0, n_ == len(tiles) - 1))
        live = {}
        acc = {}
        row_toks = []

        def emit_qk(i):
            h, qt, a, first, last = iters[i]
            s = h % 2
            t0 = qt * 512
            s0 = t0 - 1024 + 128 * a
            pt, pb = SPS.next()
            T.op("pe", lambda e: e.matmul(pt[:, :], kTs[s][:, s0:s0 + 128], rev(qTs[s][:, t0:t0 + 512]),
                                          start=True, stop=True), reads=[b_k[s], b_q[s]], writes=[pb])
            px, b_px = pex.next()
            T.op("act", lambda e: e.activation(out=px[:, :], in_=pt[:, :], func=AF.Exp), reads=[pb], writes=[b_px])
            pm, b_pm = pmr.next()
            T.op("dve", lambda e: e.tensor_tensor(out=pm[:, :], in0=px[:, :], in1=Es[s][:, a, :], op=ALU.mult),
                 reads=[b_px, b_E[s]], writes=[b_pm])
            live[i] = (pm, b_pm)

        def emit_pv(i):
            h, qt, a, first, last = iters[i]
            s = h % 2
            t0 = qt * 512
            s0 = t0 - 1024 + 128 * a
            if first:
                acc[(h, qt)] = (OPS.next(), DPS.next())
                if qt == 0 and 1 <= h < 7:
                    loadh(h + 1)
            (po, bo), (pd, bd) = acc[(h, qt)]
            pm, b_pm = live.pop(i)
            T.op("pe", lambda e: e.matmul(po[:, :], vts[s][:, s0 // 128, :], pm[:, :], start=first, stop=last),
                 reads=[b_v[s], b_pm], writes=[bo], inc=last)
            T.op("pe", lambda e: e.matmul(pd[:, :], onesb[:, :], pm[:, :], start=first, stop=last),
                 reads=[b_onesb, b_pm], writes=[bd], inc=True)
            if last:
                rd, b_rd = rdr.next()
                T.op("dve", lambda e: e.reciprocal(out=rd[:, :], in_=pd[:, :]), reads=[bd], writes=[b_rd])
                st_, b_st = stg.next()
                T.op("dve", lambda e: e.tensor_tensor(out=st_[:, :], in0=rev(po[:, :]), in1=rev(rd[:, :]), op=ALU.mult),
                     reads=[bo, b_rd], writes=[b_st])
                row_toks.append(T.dma("sp", mixT[h * 128:(h + 1) * 128, t0:t0 + 512], st_[:, :], reads=[b_st]))
                del acc[(h, qt)]
                if qt == NCH - 1 and h % 2 == 1 and on_rows is not None:
                    on_rows(h // 2, list(row_toks))
                    del row_toks[:]

        LOOK = 2
        for i in range(len(iters) + LOOK):
            if i < len(iters):
                emit_qk(i)
            if i - LOOK >= 0:
                emit_pv(i - LOOK)


def build_odd(debug=False):
    nc = bass.Bass("TRN2", target_bir_lowering=False)
    dt = lambda name, shape, dtype, kind="ExternalInput": nc.dram_tensor(name, shape, dtype, kind=kind).ap()
    hT = dt("hT", [D, SEQ], F32)
    g1 = dt("g1", [128, KT], F32)
    w_o = dt("w_o", [D, 3072], F32)
    qg = dt("qg", [128, 1], F32)
    kg = dt("kg", [128, 1], F32)
    rb = dt("rb", [32, 8], F32)
    onehot = dt("onehot", [32, NBU], F32)
    cmult = dt("cmult", [8, NBU], F32)
    mixT = dt("mixT", [1024, SEQ], BF16, "ExternalOutput")
    kind = "ExternalOutput" if debug else "Internal"
    scr = (dt("xn_s", [D, SEQ], BF16, kind), dt("qT_s", [1024, SEQ], BF16, kind), dt("kT_s", [1024, SEQ], BF16, kind),
           dt("v_s", [SEQ, 1024], BF16, kind), dt("u_s", [8, NBU], BF16, kind))
    with contextlib.ExitStack() as stack:
        T = Trk(nc, stack)
        ps = [stack.enter_context(nc.psum_tensor("ps%d" % i, [128, 512], F32)) for i in range(8)]
        hv = hT.rearrange("(k p) t -> p k t", p=128)
        hload = lambda T, t, c, b: T.dma("sp", t[:], hv[:, :, c * 512:(c + 1) * 512], writes=[b])
        emit_odd(nc, T, stack, ps, hload, g1, w_o, qg, kg, rb, onehot, cmult, mixT, scr)
        T.finish()
    return nc


def t5_bucket_np(rel):
    nb = 16
    ret = (rel > 0).astype(np.int32) * nb
    n = np.abs(rel)
    max_exact = nb // 2
    large = max_exact + (np.log(np.maximum(n, 1) / max_exact) / np.log(1024 / max_exact) * (nb - max_exact)).astype(np.int32)
    large = np.minimum(large, nb - 1)
    return (ret + np.where(n < max_exact, n, large)).astype(np.int32)


def odd_consts():
    m = np.arange(NBU)
    delta = m - 1535
    bkt = t5_bucket_np(delta)
    onehot = (bkt[None, :] == np.arange(32)[:, None]).astype(np.float32)
    mult = np.zeros(NBU, np.float32)
    for (w, d) in ((128, 1), (512, 4), (2048, 16)):
        mult += ((delta % d == 0) & (np.abs(delta) <= w // 2)).astype(np.float32)
    return dict(onehot=onehot, cmult=np.ascontiguousarray(np.broadcast_to(mult, (8, NBU))))


def odd_weights(j, w_qkv, q_g, k_g, rel_bias):
    cols = np.r_[1024 * j:1024 * j + 1024, 2048 + 1024 * j:2048 + 1024 * j + 1024, 4096 + 1024 * j:4096 + 1024 * j + 1024]
    return dict(w_o=np.ascontiguousarray(w_qkv[:, cols]), qg=np.ascontiguousarray(q_g.reshape(128, 1)),
                kg=np.ascontiguousarray(k_g.reshape(128, 1)), rb=np.ascontiguousarray(rel_bias[:, 8 * j:8 * j + 8]))


_PROGS = {}


def _prog(name):
    if name not in _PROGS:
        _PROGS[name] = {"even": build_even, "odd": build_odd, "post": build_post}[name]()
    return _PROGS[name]


def _run(name, in_maps):
    res = run_bass_kernel_spmd(_prog(name), in_maps, core_ids=list(range(NCORES)))
    return res.results


def kernel_unfused(x, mix_norm_g, w_in_even, gate_up_fwd, gate_bias_fwd, gate_up_bwd, gate_bias_bwd, gla_norm_g, w_out_even,
           w_qkv_odd, q_norm_g, k_norm_g, rel_bias, w_out_odd, ffn_norm_g, w_gate, w_up, conv_w, conv_b, w_down):
    f32 = lambda a: np.ascontiguousarray(np.asarray(a, dtype=np.float32))
    x = f32(x)
    hT = [np.ascontiguousarray(x[b].T) for b in range(BATCH)]
    ec = even_consts()
    oc = odd_consts()
    for layer in range(4):
        i = layer // 2
        g1 = lay128(f32(mix_norm_g[layer]))
        ims = []
        for b in range(BATCH):
            for j in range(2):
                im = dict(hT=hT[b], g1=g1)
                if layer % 2 == 0:
                    im.update(even_weights(j, f32(w_in_even[i]), f32(gate_up_fwd[i]), f32(gate_bias_fwd[i]),
                                           f32(gate_up_bwd[i]), f32(gate_bias_bwd[i]), f32(gla_norm_g[i])))
                    im.update(ec)
                else:
                    im.update(odd_weights(j, f32(w_qkv_odd[i]), f32(q_norm_g[i]), f32(k_norm_g[i]), f32(rel_bias)))
                    im.update(oc)
                ims.append(im)
        res = _run("even" if layer % 2 == 0 else "odd", ims)
        mT = []
        for b in range(BATCH):
            m0 = np.asarray(res[2 * b]["mixT"]); m1 = np.asarray(res[2 * b + 1]["mixT"])
            if layer % 2 == 0:
                mT.append(np.concatenate([m0[:512], m1[:512], m0[512:], m1[512:]], axis=0))
            else:
                mT.append(np.concatenate([m0, m1], axis=0))
        w_out = f32(w_out_even[i]) if layer % 2 == 0 else f32(w_out_odd[i])
        cw = np.ascontiguousarray(f32(conv_w[layer]).T.reshape(FT, 128, 3).transpose(1, 0, 2).reshape(128, FT * 3))
        common = dict(w_out=w_out, g2=lay128(f32(ffn_norm_g[layer])), w_gate=f32(w_gate[layer]), w_up=f32(w_up[layer]),
                      cw=cw, cb=lay128(f32(conv_b[layer])), w_down=f32(w_down[layer]))
        ims = []
        for b in range(BATCH):
            for half in range(2):
                t0 = half * TOK
                he = np.zeros((D, EXT), np.float32); me = np.zeros((D, EXT), mT[b].dtype)
                lo, hi = max(t0 - 1, 0), min(t0 + TOK + 1, SEQ)
                he[:, lo - (t0 - 1):hi - (t0 - 1)] = hT[b][:, lo:hi]
                me[:, lo - (t0 - 1):hi - (t0 - 1)] = mT[b][:, lo:hi]
                im = dict(hT_ext=he, mT_ext=me)
                im.update(common)
                ims.append(im)
        res = _run("post", ims)
        hT = [np.ascontiguousarray(np.concatenate([np.asarray(res[2 * b]["hT_out"]), np.asarray(res[2 * b + 1]["hT_out"])], axis=1))
              for b in range(BATCH)]
    return np.ascontiguousarray(np.stack([h.T for h in hT], axis=0)).astype(np.float32)


PAIRS = [[0, 1], [2, 3], [4, 5], [6, 7]]


def build_fused():
    nc = bass.Bass("TRN2", target_bir_lowering=False)
    dt = lambda name, shape, dtype, kind="ExternalInput": nc.dram_tensor(name, shape, dtype, kind=kind).ap()
    xT = dt("xT", [D, TOK], F32)
    sel = dt("sel", [128, 2], F32)
    cs = dt("cs", [256, 512], BF16); cosm = dt("cosm", [SEQ, SEQ], BF16); nsinm = dt("nsinm", [SEQ, SEQ], BF16)
    mf = dt("mf", [128, 128], BF16); mb = dt("mb", [128, 128], BF16)
    onehot = dt("onehot", [32, NBU], F32); cmult = dt("cmult", [8, NBU], F32); rb = dt("rb", [32, 8], F32)
    L = []
    for l in range(4):
        d_ = dict(g1=dt("g1_%d" % l, [128, KT], F32), w_out=dt("w_out_%d" % l, [D, D], F32), g2=dt("g2_%d" % l, [128, KT], F32),
                  w_gate=dt("w_gate_%d" % l, [D, FF], F32), w_up=dt("w_up_%d" % l, [D, FF], F32),
                  cw=dt("cw_%d" % l, [128, FT * 3], F32), cb=dt("cb_%d" % l, [128, FT], F32),
                  w_down=dt("w_down_%d" % l, [FF, D], F32))
        if l % 2 == 0:
            d_.update(w_e=dt("w_e_%d" % l, [D, NW_E], F32), gua_f=dt("gua_f_%d" % l, [17, 256], F32),
                      gua_b=dt("gua_b_%d" % l, [17, 256], F32), glag=dt("glag_%d" % l, [128, 2], F32))
        else:
            d_.update(w_o=dt("w_o_%d" % l, [D, 3072], F32), qg=dt("qg_%d" % l, [128, 1], F32), kg=dt("kg_%d" % l, [128, 1], F32))
        L.append(d_)
    outT = dt("outT", [D, TOK], F32, "ExternalOutput")
    I = "Internal"
    hown = dt("hown", [D, TOK], F32, I)
    hfull = dt("hfull", [8, 2, 2, 128, TOK], F32, I)
    mcore = dt("mcore", [1024, SEQ], BF16, I)
    mfull = dt("mfull", [4, 2, 256, SEQ], BF16, I)
    hmid_scr = dt("hmid_scr", [FF, TOK], BF16, I)
    hm_scr = dt("hm_scr", [D, EXT], F32, I)
    scr_e = (dt("aT_s", [512, SEQ], BF16, I), dt("qT_s", [256, SEQ], BF16, I), dt("kT_s", [256, SEQ], BF16, I),
             dt("k_s", [SEQ, 256], BF16, I), dt("v_s", [SEQ, 512], BF16, I), dt("sgT_s", [512, SEQ], BF16, I),
             dt("la_s", [2, SEQ, 256], BF16, I))
    scr_o = (dt("xn_s", [D, SEQ], BF16, I), dt("qTo_s", [1024, SEQ], BF16, I), dt("kTo_s", [1024, SEQ], BF16, I),
             dt("vo_s", [SEQ, 1024], BF16, I), dt("u_s", [8, NBU], BF16, I))

    with contextlib.ExitStack() as stack:
        T = Trk(nc, stack)
        ps = [stack.enter_context(nc.psum_tensor("ps%d" % i, [128, 512], F32)) for i in range(8)]
        sels = stack.enter_context(nc.sbuf_tensor("sels", [128, 2], F32)); b_sel = Buf()
        T.dma("sp", sels[:], sel, writes=[b_sel])
        T.dma("sp", hown, xT)

        def hload(T, t, c, b):
            r, tc = c // 4, (c % 4) * 512
            toks = []
            for q in range(8):
                toks.append(T.dma("sp", t[:, 2 * q:2 * q + 2, :], hfull[q, r].rearrange("k p t -> p k t")[:, :, tc:tc + 512],
                                  writes=([b] if q == 0 else [])))
            b.multi = toks

        def make_loaders():
            st = {}

            def get(sb2, key, shape, dtype):
                if key not in st:
                    t = sb2(key, shape, dtype); bb = Buf()
                    T.op("dve", lambda e: e.memset(t[:], 0.0), writes=[bb])
                    st[key] = (t, bb)
                return st[key]

            def load_m(T, sb2, mT, k0, b):
                kg = k0 // 4
                r_src, q0 = kg // 2, 2 * (kg % 2)
                A, bA = get(sb2, "mA", [128, 2, EXT], BF16)
                B, bB = get(sb2, "mB", [128, 2, EXT], BF16)
                for qq in range(2):
                    src = mfull[q0 + qq, r_src].rearrange("(k p) t -> p k t", p=128)
                    T.dma("sp", A[:, :, 1:EXT], src[:, :, 0:TOK + 1], writes=[bA])
                    T.dma("sp", B[:, :, 0:EXT - 1], src[:, :, TOK - 1:SEQ], writes=[bB])
                    T.op("dve", lambda e: e.tensor_scalar(out=A[:], in0=A[:], scalar1=sels[:, 0:1], scalar2=None, op0=ALU.mult),
                         reads=[b_sel], writes=[bA])
                    T.op("dve", lambda e: e.scalar_tensor_tensor(out=mT[:, k0 + 2 * qq:k0 + 2 * qq + 2, :], in0=B[:],
                                                                 scalar=sels[:, 1:2], in1=A[:], op0=ALU.mult, op1=ALU.add),
                         reads=[bA, bB, b_sel], writes=[b])

            def load_h(T, sb2, hb, n, b):
                q, k2 = n // 2, n % 2
                T.dma("sp", hb[:, 1:TOK + 1], hown[n * 128:(n + 1) * 128, :], writes=[b])
                T.dma("sp", hb[:, 0:1], hfull[q, 0, k2][:, TOK - 1:TOK], writes=[b], slow=True)
                T.dma("sp", hb[:, EXT - 1:EXT], hfull[q, 1, k2][:, 0:1], writes=[b], slow=True)
                T.op("dve", lambda e: e.tensor_scalar(out=hb[:, 0:1], in0=hb[:, 0:1], scalar1=sels[:, 1:2], scalar2=None,
                                                      op0=ALU.mult), reads=[b_sel], writes=[b])
                T.op("dve", lambda e: e.tensor_scalar(out=hb[:, EXT - 1:EXT], in0=hb[:, EXT - 1:EXT], scalar1=sels[:, 0:1],
                                                      scalar2=None, op0=ALU.mult), reads=[b_sel], writes=[b])
            return load_m, load_h

        def gather_h(q, extra=()):
            T.collective(hown[256 * q:256 * (q + 1), :].opt(), hfull[q].rearrange("r k p t -> (r k p) t").opt(), PAIRS,
                         extra=extra)

        def gather_m(q, extra=()):
            T.collective(mcore[256 * q:256 * (q + 1), :].opt(), mfull[q].rearrange("r p t -> (r p) t").opt(), PAIRS,
                         extra=extra)

        T.barrier()
        for q in range(8):
            gather_h(q)
        for l in range(4):
            W = L[l]
            T.barrier(); T.new_epoch()
            with contextlib.ExitStack() as st_l:
                if l % 2 == 0:
                    emit_even(nc, T, st_l, ps, hload, W["g1"], W["w_e"], W["gua_f"], W["gua_b"], W["glag"],
                              cs, cosm, nsinm, mf, mb, mcore, scr_e, on_rows=gather_m)
                else:
                    emit_odd(nc, T, st_l, ps, hload, W["g1"], W["w_o"], W["qg"], W["kg"], rb, onehot, cmult, mcore, scr_o,
                             on_rows=gather_m)
            T.barrier(); T.new_epoch()
            store_toks = {}

            def on_store(n, c, nchunk, tk):
                store_toks.setdefault(n // 2, []).append(tk)
                if c == nchunk - 1 and n % 2 == 1:
                    gather_h(n // 2, store_toks[n // 2])

            with contextlib.ExitStack() as st_l:
                load_m, load_h = make_loaders()
                emit_post(nc, T, st_l, ps, load_m, load_h, W["w_out"], W["g2"], W["w_gate"], W["w_up"], W["cw"], W["cb"],
                          W["w_down"], outT if l == 3 else hown, hmid_scr, hm_scr, on_store=(on_store if l < 3 else None))
        T.finish()
    return nc, T


def wout_perm(layer):
    perm = np.zeros(D, np.int64)
    for r in range(2):
        for row in range(1024):
            if layer % 2 == 0:
                ch = 512 * r + row if row < 512 else 1024 + 512 * r + (row - 512)
            else:
                ch = 1024 * r + row
            perm[r * 1024 + row] = ch
    return perm


def kernel(x, mix_norm_g, w_in_even, gate_up_fwd, gate_bias_fwd, gate_up_bwd, gate_bias_bwd, gla_norm_g, w_out_even,
                 w_qkv_odd, q_norm_g, k_norm_g, rel_bias, w_out_odd, ffn_norm_g, w_gate, w_up, conv_w, conv_b, w_down):
    f32 = lambda a: np.ascontiguousarray(np.asarray(a, dtype=np.float32))
    x = f32(x)
    if "fused" not in _PROGS:
        _PROGS["fused"] = build_fused()[0]
    common = {}
    common.update(even_consts()); common.update(odd_consts())
    for l in range(4):
        i = l // 2
        w_out = f32(w_out_even[i]) if l % 2 == 0 else f32(w_out_odd[i])
        common["w_out_%d" % l] = np.ascontiguousarray(w_out[wout_perm(l), :])
        common["g1_%d" % l] = lay128(f32(mix_norm_g[l])); common["g2_%d" % l] = lay128(f32(ffn_norm_g[l]))
        common["w_gate_%d" % l] = f32(w_gate[l]); common["w_up_%d" % l] = f32(w_up[l]); common["w_down_%d" % l] = f32(w_down[l])
        common["cw_%d" % l] = np.ascontiguousarray(f32(conv_w[l]).T.reshape(FT, 128, 3).transpose(1, 0, 2).reshape(128, FT * 3))
        common["cb_%d" % l] = lay128(f32(conv_b[l]))
    ims = []
    for b in range(BATCH):
        for r in range(2):
            im = dict(common)
            im["xT"] = np.ascontiguousarray(x[b, r * TOK:(r + 1) * TOK, :].T)
            s = np.zeros((128, 2), np.float32); s[:, r] = 1.0
            im["sel"] = s
            im["rb"] = np.ascontiguousarray(f32(rel_bias)[:, 8 * r:8 * r + 8])
            for l in range(4):
                i = l // 2
                if l % 2 == 0:
                    ew = even_weights(r, f32(w_in_even[i]), f32(gate_up_fwd[i]), f32(gate_bias_fwd[i]), f32(gate_up_bwd[i]),
                                      f32(gate_bias_bwd[i]), f32(gla_norm_g[i]))
                    for k_, v_ in ew.items():
                        im["%s_%d" % (k_, l)] = v_
                else:
                    ow = odd_weights(r, f32(w_qkv_odd[i]), f32(q_norm_g[i]), f32(k_norm_g[i]), f32(rel_bias))
                    for k_ in ("w_o", "qg", "kg"):
                        im["%s_%d" % (k_, l)] = ow[k_]
            ims.append(im)
    res = run_bass_kernel_spmd(_PROGS["fused"], ims, core_ids=list(range(NCORES))).results
    out = np.empty((BATCH, SEQ, D), np.float32)
    for b in range(BATCH):
        for r in range(2):
            out[b, r * TOK:(r + 1) * TOK, :] = np.asarray(res[2 * b + r]["outT"]).T
    return out
```

```python
import contextlib
import numpy as np
import ml_dtypes
import concourse.bass as bass
import concourse.mybir as mybir
from concourse.bass_utils import run_bass_kernel_spmd

F32 = mybir.dt.float32
BF16 = mybir.dt.bfloat16
AF = mybir.ActivationFunctionType
ALU = mybir.AluOpType
NPBF = ml_dtypes.bfloat16

D = 2048
KT = D // 128
SEQ = 4096
BATCH = 4
TOK = 2048
EXT = TOK + 2
FF = 5632
FT = FF // 128
EPS = 1e-6
NCORES = 8


DBG = {}
RELAX = set()


class Buf:
    __slots__ = ("name", "w", "r", "multi")

    def __init__(self, name=""):
        self.name = name
        self.w = None
        self.r = []
        self.multi = None


class Trk:
    NDS = 20

    def __init__(self, nc, stack):
        self.nc = nc
        self.E = {"pe": nc.tensor, "act": nc.scalar, "dve": nc.vector, "pool": nc.gpsimd, "sp": nc.sync}
        self.sem = {k: stack.enter_context(nc.semaphore("s_" + k)) for k in self.E}
        self.cnt = {k: 0 for k in self.E}
        self.waited = {}
        self.dsem = {q: [stack.enter_context(nc.semaphore("d_%s%d" % (q, i))) for i in range(self.NDS)]
                     for q in ("sp", "pool")}
        self.dtot = {q: [0] * self.NDS for q in ("sp", "pool")}
        self.dnext = {"sp": 0, "pool": 0}
        self.n_instr = 0
        self.stack = stack
        self.ccsem = stack.enter_context(nc.semaphore("s_cc"))
        self.cccnt = 0
        self.epoch = 0

    def new_epoch(self):
        self.epoch += 1
        self.sem = {k: self.stack.enter_context(self.nc.semaphore("s%d_%s" % (self.epoch, k))) for k in self.E}
        self.cnt = {k: 0 for k in self.E}

    def collective(self, in_ap, out_ap, groups, reads=(), writes=(), extra=()):
        if DBG.get("no_cc"):
            return None
        self._sync("pool", reads, writes)
        for t in extra:
            self._wait("pool", t)
        if self.cccnt > 0:
            self._wait("pool", (self.ccsem, self.cccnt, "cc"))
        self.nc.gpsimd.collective_compute("AllGather", ALU.bypass, replica_groups=groups, ins=[in_ap], outs=[out_ap]
                                          ).then_inc(self.ccsem)
        self.cccnt += 1
        tok = (self.ccsem, self.cccnt, "cc")
        self._update(tok, reads, writes)
        return tok

    def _wait(self, eng, tok):
        sem, val, src = tok
        if src == eng and (eng == "pe" or eng in RELAX):
            return
        key = (eng, id(sem))
        if self.waited.get(key, 0) >= val:
            return
        self.waited[key] = val
        self.E[eng].wait_ge(sem, val)

    def _sync(self, eng, reads, writes):
        for b in reads:
            if b.w is not None:
                self._wait(eng, b.w)
            if b.multi:
                for t in b.multi:
                    self._wait(eng, t)
        for b in writes:
            if b.w is not None:
                self._wait(eng, b.w)
            if b.multi:
                for t in b.multi:
                    self._wait(eng, t)
            for t in b.r:
                self._wait(eng, t)

    def _update(self, tok, reads, writes):
        for b in reads:
            b.r.append(tok)
        for b in writes:
            b.w = tok
            b.r = []
            b.multi = None

    def op(self, eng, fn, reads=(), writes=(), inc=True):
        self._sync(eng, reads, writes)
        ins = fn(self.E[eng])
        self.n_instr += 1
        if inc:
            self.cnt[eng] += 1
            ins.then_inc(self.sem[eng], 1)
            tok = (self.sem[eng], self.cnt[eng], eng)
        else:
            tok = (self.sem[eng], self.cnt[eng] + 1, eng)
        self._update(tok, reads, writes)
        return tok

    def dma(self, q, out, in_, reads=(), writes=(), slow=False):
        self._sync(q, reads, writes)
        i = self.dnext[q]
        self.dnext[q] = (i + 1) % self.NDS
        sem = self.dsem[q][i]
        if self.dtot[q][i] > 0:
            self._wait(q, (sem, self.dtot[q][i], "dma"))
        if slow:
            self.E[q].dma_start(out=out, in_=in_, allow_slow_non_contiguous=True).then_inc(sem, 16)
        else:
            self.E[q].dma_start(out=out, in_=in_).then_inc(sem, 16)
        self.n_instr += 1
        self.dtot[q][i] += 16
        tok = (sem, self.dtot[q][i], "dma")
        self._update(tok, reads, writes)
        return tok

    def barrier(self):
        toks = []
        for q in ("sp", "pool"):
            for i in range(self.NDS):
                if self.dtot[q][i] > 0:
                    toks.append((self.dsem[q][i], self.dtot[q][i], "dma"))
        for e in ("pe", "act", "dve"):
            if self.cnt[e] > 0:
                toks.append((self.sem[e], self.cnt[e], e))
        if self.cccnt > 0:
            toks.append((self.ccsem, self.cccnt, "cc"))
        for eng in self.E:
            for t in toks:
                if t[2] == eng:
                    continue
                self._wait(eng, t)

    def finish(self):
        for q in ("sp", "pool"):
            for i in range(self.NDS):
                if self.dtot[q][i] > 0:
                    self._wait("sp", (self.dsem[q][i], self.dtot[q][i], "dma"))
        if self.cccnt > 0:
            self._wait("sp", (self.ccsem, self.cccnt, "cc"))
        for e in ("pe", "act", "dve"):
            if self.cnt[e] > 0:
                self._wait("sp", (self.sem[e], self.cnt[e], e))


_UNIQ = [0]


def uniq(name):
    _UNIQ[0] += 1
    return "%s_%d" % (name, _UNIQ[0])


def col_chunks(n, step=512):
    return [(c, min(c + step, n)) for c in range(0, n, step)]


def emit_post(nc, T, stack, ps, load_m, load_h, w_out, g2, w_gate, w_up, cw, cb, w_down, hT_out, hmid_scr, hm_scr,
              on_store=None):
    sb = lambda name, shape, dt: stack.enter_context(nc.sbuf_tensor(uniq(name), shape, dt))
    psb = [Buf("ps%d" % i) for i in range(8)]

    ones = sb("ones", [128, 128], F32)
    b_ones = Buf()
    T.op("dve", lambda e: e.memset(ones[:], 1.0), writes=[b_ones])
    g2s = sb("g2s", [128, KT], F32); b_g2 = Buf()
    T.dma("sp", g2s[:], g2, writes=[b_g2])
    cws = sb("cws", [128, FT * 3], F32); b_cw = Buf()
    T.dma("sp", cws[:], cw, writes=[b_cw])
    cbs = sb("cbs", [128, FT], F32); b_cb = Buf()
    T.dma("sp", cbs[:], cb, writes=[b_cb])
    rstd = sb("rstd", [128, EXT], F32); b_rstd = Buf()
    xn = sb("xn", [128, KT, EXT], BF16); b_xn = [Buf() for _ in range(KT)]

    chunks = col_chunks(EXT)
    hm_b = [Buf() for _ in range(KT)]

    with contextlib.ExitStack() as st2:
        sb2 = lambda name, shape, dt: st2.enter_context(nc.sbuf_tensor(uniq(name), shape, dt))
        mT = sb2("mT", [128, KT, EXT], BF16); b_mT = [Buf() for _ in range(KT // 4)]
        for k0 in range(0, KT, 4):
            load_m(T, sb2, mT, k0, b_mT[k0 // 4])
        wo = [sb2("wo%d" % i, [128, KT, 128], BF16) for i in range(2)]; b_wo = [Buf(), Buf()]
        hb = [sb2("hb%d" % i, [128, EXT], F32) for i in range(2)]; b_hb = [Buf(), Buf()]
        hm = [sb2("hm%d" % i, [128, EXT], F32) for i in range(2)]; b_hm = [Buf(), Buf()]
        sq = [sb2("sq%d" % i, [128, 512], F32) for i in range(2)]; b_sq = [Buf(), Buf()]
        wv = w_out.rearrange("(k p) n -> p k n", p=128)

        def load(n):
            T.dma("pool", wo[n % 2][:], wv[:, :, n * 128:(n + 1) * 128], writes=[b_wo[n % 2]])
            load_h(T, sb2, hb[n % 2], n, b_hb[n % 2])

        load(0)
        pend = None
        it = 0
        for n in range(KT):
            if n + 1 < KT:
                load(n + 1)
            for j, (c0, c1) in enumerate(chunks):
                w = c1 - c0
                pi = it % 2
                for k in range(KT):
                    T.op("pe", lambda e, k=k: e.matmul(ps[pi][:, :w], wo[n % 2][:, k, :], mT[:, k, c0:c1],
                                                       start=(k == 0), stop=(k == KT - 1)),
                         reads=[b_wo[n % 2], b_mT[k // 4]], writes=[psb[pi]], inc=(k == KT - 1))
                T.op("dve", lambda e: e.tensor_tensor(out=hm[n % 2][:, c0:c1], in0=ps[pi][:, :w],
                                                      in1=hb[n % 2][:, c0:c1], op=ALU.add),
                     reads=[psb[pi], b_hb[n % 2]], writes=[b_hm[n % 2]])
                T.op("act", lambda e: e.activation(out=sq[pi][:, :w], in_=hm[n % 2][:, c0:c1], func=AF.Square),
                     reads=[b_hm[n % 2]], writes=[b_sq[pi]])
                if pend is not None:
                    pend()

                def mk(n=n, j=j, pi=pi, w=w):
                    T.op("pe", lambda e: e.matmul(ps[2 + j][:, :w], ones[:], sq[pi][:, :w],
                                                  start=(n == 0), stop=(n == KT - 1)),
                         reads=[b_ones, b_sq[pi]], writes=[psb[2 + j]])
                pend = mk
                it += 1
            T.dma("sp", hm_scr[n * 128:(n + 1) * 128, :], hm[n % 2][:], reads=[b_hm[n % 2]], writes=[hm_b[n]])
        pend()
        for j, (c0, c1) in enumerate(chunks):
            w = c1 - c0
            T.op("act", lambda e: e.activation(out=rstd[:, c0:c1], in_=ps[2 + j][:, :w], func=AF.Sqrt,
                                               scale=1.0 / D, bias=EPS),
                 reads=[psb[2 + j]], writes=[b_rstd])
        T.op("dve", lambda e: e.reciprocal(out=rstd[:], in_=rstd[:]), reads=[b_rstd], writes=[b_rstd])

        for n in range(KT):
            T.dma("sp", hb[n % 2][:], hm_scr[n * 128:(n + 1) * 128, :], reads=[hm_b[n]], writes=[b_hb[n % 2]])
            T.op("dve", lambda e: e.scalar_tensor_tensor(out=xn[:, n, :], in0=hb[n % 2][:], scalar=g2s[:, n:n + 1],
                                                         in1=rstd[:], op0=ALU.mult, op1=ALU.mult),
                 reads=[b_hb[n % 2], b_g2, b_rstd], writes=[b_xn[n]])

    T.barrier()
    hmid_b = [Buf() for _ in range(FT)]
    with contextlib.ExitStack() as st2:
        sb2 = lambda name, shape, dt: st2.enter_context(nc.sbuf_tensor(uniq(name), shape, dt))
        wg = [sb2("wg%d" % i, [128, KT, 128], BF16) for i in range(2)]; b_wg = [Buf(), Buf()]
        wu = [sb2("wu%d" % i, [128, KT, 128], BF16) for i in range(2)]; b_wu = [Buf(), Buf()]
        ge = [sb2("ge%d" % i, [128, EXT], F32) for i in range(2)]; b_ge = [Buf(), Buf()]
        u = [sb2("u%d" % i, [128, TOK], F32) for i in range(2)]; b_u = [Buf(), Buf()]
        gl = [sb2("gl%d" % i, [128, TOK], F32) for i in range(2)]; b_gl = [Buf(), Buf()]
        hf = [sb2("hf%d" % i, [128, TOK], BF16) for i in range(2)]; b_hf = [Buf(), Buf()]
        wgv = w_gate.rearrange("(k p) n -> p k n", p=128)
        wuv = w_up.rearrange("(k p) n -> p k n", p=128)

        def loadg(f):
            T.dma("pool", wg[f % 2][:], wgv[:, :, f * 128:(f + 1) * 128], writes=[b_wg[f % 2]])
            T.dma("pool", wu[f % 2][:], wuv[:, :, f * 128:(f + 1) * 128], writes=[b_wu[f % 2]])

        loadg(0)
        it = 0
        for f in range(FT):
            if f + 1 < FT:
                loadg(f + 1)
            s = f % 2
            for j, (c0, c1) in enumerate(chunks):
                w = c1 - c0
                pi = it % 4; it += 1
                for k in range(KT):
                    T.op("pe", lambda e, k=k: e.matmul(ps[pi][:, :w], wg[s][:, k, :], xn[:, k, c0:c1],
                                                       start=(k == 0), stop=(k == KT - 1)),
                         reads=[b_wg[s], b_xn[k]], writes=[psb[pi]], inc=(k == KT - 1))
                T.op("act", lambda e: e.activation(out=ge[s][:, c0:c1], in_=ps[pi][:, :w], func=AF.Copy),
                     reads=[psb[pi]], writes=[b_ge[s]])
            T.op("dve", lambda e: e.tensor_scalar(out=u[s][:], in0=ge[s][:, 1:1 + TOK],
                                                  scalar1=cws[:, 3 * f + 1:3 * f + 2], scalar2=cbs[:, f:f + 1],
                                                  op0=ALU.mult, op1=ALU.add),
                 reads=[b_ge[s], b_cw, b_cb], writes=[b_u[s]])
            T.op("dve", lambda e: e.scalar_tensor_tensor(out=u[s][:], in0=ge[s][:, 0:TOK],
                                                         scalar=cws[:, 3 * f:3 * f + 1], in1=u[s][:],
                                                         op0=ALU.mult, op1=ALU.add),
                 reads=[b_ge[s], b_cw, b_u[s]], writes=[b_u[s]])
            T.op("dve", lambda e: e.scalar_tensor_tensor(out=u[s][:], in0=ge[s][:, 2:2 + TOK],
                                                         scalar=cws[:, 3 * f + 2:3 * f + 3], in1=u[s][:],
                                                         op0=ALU.mult, op1=ALU.add),
                 reads=[b_ge[s], b_cw, b_u[s]], writes=[b_u[s]])
            T.op("act", lambda e: e.activation(out=gl[s][:], in_=u[s][:], func=AF.Gelu_apprx_tanh),
                 reads=[b_u[s]], writes=[b_gl[s]])
            for j in range(TOK // 512):
                c0 = 1 + 512 * j
                pi = 4 + (it % 4); it += 1
                for k in range(KT):
                    T.op("pe", lambda e, k=k: e.matmul(ps[pi][:, :], wu[s][:, k, :], xn[:, k, c0:c0 + 512],
                                                       start=(k == 0), stop=(k == KT - 1)),
                         reads=[b_wu[s], b_xn[k]], writes=[psb[pi]], inc=(k == KT - 1))
                T.op("dve", lambda e: e.tensor_tensor(out=hf[s][:, 512 * j:512 * (j + 1)], in0=ps[pi][:, :],
                                                      in1=gl[s][:, 512 * j:512 * (j + 1)], op=ALU.mult),
                     reads=[psb[pi], b_gl[s]], writes=[b_hf[s]])
            T.dma("sp", hmid_scr[f * 128:(f + 1) * 128, :], hf[s][:], reads=[b_hf[s]], writes=[hmid_b[f]])

    T.barrier()
    with contextlib.ExitStack() as st2:
        sb2 = lambda name, shape, dt: st2.enter_context(nc.sbuf_tensor(uniq(name), shape, dt))
        CH = 1024
        hms = sb2("hms", [128, FT, CH], BF16); b_hms = [Buf() for _ in range(4)]
        wd = [sb2("wd%d" % i, [128, FT, 128], BF16) for i in range(2)]; b_wd = [Buf(), Buf()]
        hr = [sb2("hr%d" % i, [128, CH], F32) for i in range(2)]; b_hr = [Buf(), Buf()]
        ot = [sb2("ot%d" % i, [128, CH], F32) for i in range(2)]; b_ot = [Buf(), Buf()]
        hv = hmid_scr.rearrange("(f p) t -> p f t", p=128)
        wdv = w_down.rearrange("(f p) n -> p f n", p=128)
        it = 0
        items = [(c, n) for c in range(TOK // CH) for n in range(KT)]

        def loadd(idx):
            c, n = items[idx]
            T.dma("pool", wd[idx % 2][:], wdv[:, :, n * 128:(n + 1) * 128], writes=[b_wd[idx % 2]])
            T.dma("sp", hr[idx % 2][:], hm_scr[n * 128:(n + 1) * 128, 1 + c * CH:1 + (c + 1) * CH],
                  reads=[hm_b[n]], writes=[b_hr[idx % 2]])

        loadd(0)
        for idx, (c, n) in enumerate(items):
            if n == 0:
                for f0 in range(0, FT, 11):
                    T.dma("sp", hms[:, f0:f0 + 11, :], hv[:, f0:f0 + 11, c * CH:(c + 1) * CH],
                          reads=hmid_b[f0:f0 + 11], writes=[b_hms[f0 // 11]])
            if idx + 1 < len(items):
                loadd(idx + 1)
            s = idx % 2
            for jj in range(CH // 512):
                pi = it % 4; it += 1
                for f in range(FT):
                    T.op("pe", lambda e, f=f: e.matmul(ps[pi][:, :], wd[s][:, f, :], hms[:, f, 512 * jj:512 * (jj + 1)],
                                                       start=(f == 0), stop=(f == FT - 1)),
                         reads=[b_wd[s], b_hms[f // 11]], writes=[psb[pi]], inc=(f == FT - 1))
                T.op("dve", lambda e: e.tensor_tensor(out=ot[s][:, 512 * jj:512 * (jj + 1)], in0=ps[pi][:, :],
                                                      in1=hr[s][:, 512 * jj:512 * (jj + 1)], op=ALU.add),
                     reads=[psb[pi], b_hr[s]], writes=[b_ot[s]])
            tk = T.dma("sp", hT_out[n * 128:(n + 1) * 128, c * CH:(c + 1) * CH], ot[s][:], reads=[b_ot[s]])
            if on_store is not None:
                on_store(n, c, TOK // CH, tk)


def build_post(debug=False):
    nc = bass.Bass("TRN2", target_bir_lowering=False)
    dt = lambda name, shape, dtype, kind: nc.dram_tensor(name, shape, dtype, kind=kind).ap()
    hT_ext = dt("hT_ext", [D, EXT], F32, "ExternalInput")
    mT_ext = dt("mT_ext", [D, EXT], BF16, "ExternalInput")
    w_out = dt("w_out", [D, D], F32, "ExternalInput")
    g2 = dt("g2", [128, KT], F32, "ExternalInput")
    w_gate = dt("w_gate", [D, FF], F32, "ExternalInput")
    w_up = dt("w_up", [D, FF], F32, "ExternalInput")
    cw = dt("cw", [128, FT * 3], F32, "ExternalInput")
    cb = dt("cb", [128, FT], F32, "ExternalInput")
    w_down = dt("w_down", [FF, D], F32, "ExternalInput")
    hT_out = dt("hT_out", [D, TOK], F32, "ExternalOutput")
    hmid_scr = dt("hmid_scr", [FF, TOK], BF16, "ExternalOutput" if debug else "Internal")
    hm_scr = dt("hm_scr", [D, EXT], F32, "ExternalOutput" if debug else "Internal")
    with contextlib.ExitStack() as stack:
        T = Trk(nc, stack)
        ps = [stack.enter_context(nc.psum_tensor("ps%d" % i, [128, 512], F32)) for i in range(8)]
        mv = mT_ext.rearrange("(k p) t -> p k t", p=128)
        load_m = lambda T, sb2, mT, k0, b: T.dma("sp", mT[:, k0:k0 + 4, :], mv[:, k0:k0 + 4, :], writes=[b])
        load_h = lambda T, sb2, hb, n, b: T.dma("sp", hb[:], hT_ext[n * 128:(n + 1) * 128, :], writes=[b])
        emit_post(nc, T, stack, ps, load_m, load_h, w_out, g2, w_gate, w_up, cw, cb, w_down, hT_out, hmid_scr, hm_scr)
        T.finish()
    return nc


def lay128(v):
    v = np.asarray(v)
    return np.ascontiguousarray(v.reshape(-1, 128).T)


class Rot:
    def __init__(self, tiles):
        self.tiles = tiles
        self.bufs = [Buf() for _ in tiles]
        self.i = 0

    def next(self):
        i = self.i
        self.i = (i + 1) % len(self.tiles)
        return self.tiles[i], self.bufs[i]


def mm_group(T, out_ap, pairs, pbuf, reads):
    n = len(pairs)
    for i, (l, r) in enumerate(pairs):
        T.op("pe", lambda e: e.matmul(out_ap, l, r, start=(i == 0), stop=(i == n - 1)),
             reads=reads, writes=[pbuf], inc=(i == n - 1))


def emit_norm_chunk(nc, T, hc, b_hc, g1s, b_g1, ones, b_ones, PS, sqr, rstd_t, b_rstd, xc, b_xc, width):
    pt, pb = PS.next()
    for k in range(KT):
        sq, b_sq = sqr.next()
        T.op("act", lambda e: e.activation(out=sq[:, :width], in_=hc[:, k, :width], func=AF.Square),
             reads=[b_hc], writes=[b_sq])
        T.op("pe", lambda e: e.matmul(pt[:, :width], ones[:], sq[:, :width], start=(k == 0), stop=(k == KT - 1)),
             reads=[b_ones, b_sq], writes=[pb])
    T.op("act", lambda e: e.activation(out=rstd_t[:, :width], in_=pt[:, :width], func=AF.Sqrt, scale=1.0 / D, bias=EPS),
         reads=[pb], writes=[b_rstd])
    T.op("dve", lambda e: e.reciprocal(out=rstd_t[:, :width], in_=rstd_t[:, :width]), reads=[b_rstd], writes=[b_rstd])
    for k in range(KT):
        T.op("dve", lambda e: e.scalar_tensor_tensor(out=xc[:, k, :width], in0=hc[:, k, :width], scalar=g1s[:, k:k + 1],
                                                     in1=rstd_t[:, :width], op0=ALU.mult, op1=ALU.mult),
             reads=[b_hc, b_g1, b_rstd], writes=[b_xc])


NW_E = 2080
GLA_SCALE = 128 ** -0.5


def emit_even(nc, T, stack, ps, hload, g1, w_e, gua_f, gua_b, glag, cs, cosm, nsinm, mf, mb, mixT, scr, on_rows=None):
    sb = lambda name, shape, dt: stack.enter_context(nc.sbuf_tensor(uniq(name), shape, dt))
    PS = Rot(ps)
    aT_s, qT_s, kT_s, k_s, v_s, sgT_s, la_s = scr
    NCH = SEQ // 512

    ones = sb("ones", [128, 128], F32); b_ones = Buf()
    T.op("dve", lambda e: e.memset(ones[:], 1.0), writes=[b_ones])
    g1s = sb("g1s", [128, KT], F32); b_g1 = Buf()
    T.dma("sp", g1s[:], g1, writes=[b_g1])
    glags = sb("glags", [128, 2], F32); b_glag = Buf()
    T.dma("sp", glags[:], glag, writes=[b_glag])
    mfs = sb("mfs", [128, 128], BF16); b_mf = Buf()
    T.dma("sp", mfs[:], mf, writes=[b_mf])
    mbs = sb("mbs", [128, 128], BF16); b_mb = Buf()
    T.dma("sp", mbs[:], mb, writes=[b_mb])

    with contextlib.ExitStack() as st2:
        sb2 = lambda name, shape, dt: st2.enter_context(nc.sbuf_tensor(uniq(name), shape, dt))
        we = sb2("we", [128, KT, NW_E], BF16); b_we = [Buf() for _ in range(4)]
        wv = w_e.rearrange("(k p) n -> p k n", p=128)
        for k0 in range(0, KT, 4):
            T.dma("pool", we[:, k0:k0 + 4, :], wv[:, k0:k0 + 4, :], writes=[b_we[k0 // 4]])
        guaf = sb2("guaf", [17, 256], BF16); b_guaf = Buf()
        T.dma("pool", guaf[:], gua_f, writes=[b_guaf])
        guab = sb2("guab", [17, 256], BF16); b_guab = Buf()
        T.dma("pool", guab[:], gua_b, writes=[b_guab])
        zfa = sb2("zfa", [17, 512], BF16); b_zfa = Buf()
        zba = sb2("zba", [17, 512], BF16); b_zba = Buf()
        T.op("dve", lambda e: e.memset(zfa[:], 1.0), writes=[b_zfa])
        T.op("dve", lambda e: e.memset(zba[:], 1.0), writes=[b_zba])
        hcs = [sb2("hc%d" % i, [128, KT, 512], F32) for i in range(2)]; b_hcs = [Buf(), Buf()]
        xcs_e = [sb2("xc%d" % i, [128, KT, 512], BF16) for i in range(2)]; b_xcs_e = [Buf(), Buf()]
        sqr = Rot([sb2("sq%d" % i, [128, 512], F32) for i in range(2)])
        rstd_t = sb2("rstd", [128, 512], F32); b_rstd = Buf()
        stg = Rot([sb2("stg%d" % i, [128, 512], BF16) for i in range(4)])
        lt = Rot([sb2("lt%d" % i, [128, 256], F32) for i in range(2)])
        hload(T, hcs[0], 0, b_hcs[0])
        hload(T, hcs[1], 1, b_hcs[1])
        we_reads = list(b_we)
        emit_norm_chunk(nc, T, hcs[0], b_hcs[0], g1s, b_g1, ones, b_ones, PS, sqr, rstd_t, b_rstd, xcs_e[0], b_xcs_e[0], 512)
        for c in range(NCH):
            t0 = c * 512
            if c + 2 < NCH:
                hload(T, hcs[c % 2], c + 2, b_hcs[c % 2])
            if c + 1 < NCH:
                emit_norm_chunk(nc, T, hcs[(c + 1) % 2], b_hcs[(c + 1) % 2], g1s, b_g1, ones, b_ones, PS, sqr, rstd_t, b_rstd,
                                xcs_e[(c + 1) % 2], b_xcs_e[(c + 1) % 2], 512)
            xc, b_xc = xcs_e[c % 2], b_xcs_e[c % 2]

            def fm(col0, dst, row0, func=AF.Copy, scale=1.0):
                pt, pb = PS.next()
                mm_group(T, pt[:, :], [(we[:, k, col0:col0 + 128], xc[:, k, :]) for k in range(KT)], pb,
                         reads=we_reads + [b_xc])
                s, b_s = stg.next()
                T.op("act", lambda e: e.activation(out=s[:, :], in_=pt[:, :], func=func, scale=scale),
                     reads=[pb], writes=[b_s])
                T.dma("sp", dst[row0:row0 + 128, t0:t0 + 512], s[:, :], reads=[b_s])
            for i in range(4):
                fm(i * 128, aT_s, i * 128)
            for i in range(2):
                fm(512 + i * 128, qT_s, i * 128, scale=GLA_SCALE)
            for i in range(2):
                fm(768 + i * 128, kT_s, i * 128)
            for i in range(4):
                fm(1536 + i * 128, sgT_s, i * 128, func=AF.Silu)
            for (col0, za, b_za) in ((2048, zfa, b_zfa), (2064, zba, b_zba)):
                pt, pb = PS.next()
                mm_group(T, pt[0:16, :], [(we[:, k, col0:col0 + 16], xc[:, k, :]) for k in range(KT)], pb,
                         reads=we_reads + [b_xc])
                T.op("act", lambda e: e.activation(out=za[0:16, :], in_=pt[0:16, :], func=AF.Copy), reads=[pb], writes=[b_za])
            for ts in range(4):
                r0 = t0 + ts * 128
                for (col0, ncol, dst) in ((768, 256, k_s), (1024, 512, v_s)):
                    pt, pb = PS.next()
                    mm_group(T, pt[:, :ncol], [(xc[:, k, ts * 128:(ts + 1) * 128], we[:, k, col0:col0 + ncol])
                                               for k in range(KT)], pb, reads=we_reads + [b_xc])
                    s, b_s = stg.next()
                    T.op("dve", lambda e: e.tensor_copy(out=s[:, :ncol], in_=pt[:, :ncol]), reads=[pb], writes=[b_s])
                    T.dma("sp", dst[r0:r0 + 128, :], s[:, :ncol], reads=[b_s])
                for d, (za, b_za, gu, b_gu) in enumerate(((zfa, b_zfa, guaf, b_guaf), (zba, b_zba, guab, b_guab))):
                    pt, pb = PS.next()
                    T.op("pe", lambda e: e.matmul(pt[:, :256], za[0:17, ts * 128:(ts + 1) * 128], gu[0:17, :],
                                                  start=True, stop=True), reads=[b_za, b_gu], writes=[pb])
                    l, b_l = lt.next()
                    T.op("act", lambda e: e.activation(out=l[:, :], in_=pt[:, :256], func=AF.Exp, scale=-1.0),
                         reads=[pb], writes=[b_l])
                    T.op("act", lambda e: e.activation(out=l[:, :], in_=l[:, :], func=AF.Ln, bias=1.0),
                         reads=[b_l], writes=[b_l])
                    s, b_s = stg.next()
                    T.op("dve", lambda e: e.tensor_scalar(out=s[:, :256], in0=l[:, :], scalar1=-1.0 / 16.0, scalar2=None,
                                                          op0=ALU.mult), reads=[b_l], writes=[b_s])
                    T.dma("sp", la_s[d, r0:r0 + 128, :], s[:, :256], reads=[b_s])
    T.barrier()

    with contextlib.ExitStack() as st2:
        sb2 = lambda name, shape, dt: st2.enter_context(nc.sbuf_tensor(uniq(name), shape, dt))
        aT = sb2("aT", [128, 4, SEQ], BF16); b_aT = Buf()
        T.dma("sp", aT[:], aT_s.rearrange("(k p) t -> p k t", p=128), writes=[b_aT])
        css = sb2("css", [128, 2, 512], BF16); b_cs = Buf()
        T.dma("sp", css[:], cs.rearrange("(k p) n -> p k n", p=128), writes=[b_cs])
        Z = sb2("Z", [128, 32, 2, 512], BF16); b_Z = Buf()
        for g in range(2):
            for st in range(32):
                pt, pb = PS.next()
                mm_group(T, pt[:, :], [(aT[:, 2 * g + kk, st * 128:(st + 1) * 128], css[:, kk, :]) for kk in range(2)],
                         pb, reads=[b_aT, b_cs])
                if st % 2 == 0:
                    T.op("act", lambda e: e.activation(out=Z[:, st, g, :], in_=pt[:, :], func=AF.Copy),
                         reads=[pb], writes=[b_Z])
                else:
                    T.op("dve", lambda e: e.tensor_copy(out=Z[:, st, g, :], in_=pt[:, :]), reads=[pb], writes=[b_Z])
        SG = 4
        if DBG.get("skip_fnet2"):
            SEQ_ = 0
        else:
            SEQ_ = SEQ
        cbuf = Rot([sb2("cb%d" % i, [128, 2, SG, 512], BF16) for i in range(3)])
        stg = Rot([sb2("stg%d" % i, [128, 512], BF16) for i in range(4)])
        cv = cosm.rearrange("(st p) n -> p st n", p=128)
        sv = nsinm.rearrange("(st p) n -> p st n", p=128)
        pieces = [(sp_, sg) for sp_ in range(SEQ_ // 512) for sg in range(32 // SG)]

        def loadc(idx):
            sp_, sg = pieces[idx]
            t, b = cbuf.next()
            T.dma("sp", t[:, 0, :, :], cv[:, sg * SG:(sg + 1) * SG, sp_ * 512:(sp_ + 1) * 512], writes=[b])
            T.dma("sp", t[:, 1, :, :], sv[:, sg * SG:(sg + 1) * SG, sp_ * 512:(sp_ + 1) * 512], writes=[b])
            return t, b
        q = [loadc(0), loadc(1)] if pieces else []
        for idx, (sp_, sg) in enumerate(pieces):
            if idx + 2 < len(pieces):
                q.append(loadc(idx + 2))
            t, b = q.pop(0)
            bank0 = (sp_ % 2) * 4
            for ct in range(4):
                g, hf = ct // 2, ct % 2
                for s_ in range(SG):
                    stile = sg * SG + s_
                    for part in range(2):
                        first = (sg == 0 and s_ == 0 and part == 0)
                        last = (sg == 32 // SG - 1 and s_ == SG - 1 and part == 1)
                        T.op("pe", lambda e: e.matmul(ps[bank0 + ct][:, :],
                                                      Z[:, stile, g, part * 256 + hf * 128: part * 256 + hf * 128 + 128],
                                                      t[:, part, s_, :], start=first, stop=last),
                             reads=[b_Z, b], writes=[PS.bufs[bank0 + ct]], inc=last or (s_ == SG - 1 and part == 1 and ct == 3))
            if sg == 32 // SG - 1:
                for ct in range(4):
                    s, b_s = stg.next()
                    T.op("act" if ct % 2 == 0 else "dve",
                         (lambda e: e.activation(out=s[:, :], in_=ps[bank0 + ct][:, :], func=AF.Copy)) if ct % 2 == 0 else
                         (lambda e: e.tensor_copy(out=s[:, :], in_=ps[bank0 + ct][:, :])),
                         reads=[PS.bufs[bank0 + ct]], writes=[b_s])
                    T.dma("sp", mixT[ct * 128:(ct + 1) * 128, sp_ * 512:(sp_ + 1) * 512], s[:, :], reads=[b_s])
    T.barrier()
    if on_rows is not None:
        on_rows(0, []); on_rows(1, [])

    with contextlib.ExitStack() as st2:
        sb2 = lambda name, shape, dt: st2.enter_context(nc.sbuf_tensor(uniq(name), shape, dt))
        qT = sb2("qT", [128, SEQ], BF16); kT = sb2("kT", [128, SEQ], BF16)
        kk_ = sb2("ktok", [128, 32, 128], BF16); vv = sb2("vtok", [128, 32, 256], BF16)
        la = [sb2("la%d" % d, [128, 32, 128], BF16) for d in range(2)]
        sg_ = sb2("sgT", [128, 2, SEQ], BF16)
        oacc = sb2("oacc", [128, 2, SEQ], F32)
        b_in = Buf(); b_sg = Buf()
        b_o = [Buf() for _ in range(32)]
        S32 = [sb2("S32_%d" % d, [128, 256], F32) for d in range(2)]; b_S32 = [Buf(), Buf()]
        Sbf = [sb2("Sbf_%d" % d, [128, 256], BF16) for d in range(2)]; b_Sbf = [Buf(), Buf()]
        tmpS = [sb2("tmpS_%d" % d, [128, 256], F32) for d in range(2)]; b_tmpS = [Buf(), Buf()]
        mk_t = lambda nm, dt_, n=6: Rot([sb2("%s%d" % (nm, i), [128, 128], dt_) for i in range(n)])
        ebR, enbR, entR = mk_t("eb", F32, 6), mk_t("enb", F32, 4), mk_t("ent", F32, 4)
        qtR, ktR, ktokR, pR = mk_t("qt", BF16), mk_t("kt", BF16), mk_t("ktk", BF16), mk_t("pp", BF16)
        elR = Rot([sb2("el%d" % i, [128, 2], F32) for i in range(4)])
        sq2 = Rot([sb2("sqq%d" % i, [128, 512], F32) for i in range(2)])
        rs2 = sb2("rs2", [128, 512], F32); b_rs2 = Buf()
        tm2 = Rot([sb2("tm%d" % i, [128, 512], F32) for i in range(2)])
        stg = Rot([sb2("stg%d" % i, [128, 512], BF16) for i in range(2)])
        masks = ((mfs, b_mf), (mbs, b_mb))
        kmR = mk_t("km", BF16)
        NHEAD_ = 0 if DBG.get("skip_gla") else 2
        cmask = sb2("cmask", [128, 2], F32); b_cmask = Buf()
        T.op("dve", lambda e: e.memset(cmask[:], 0.0), writes=[b_cmask])
        T.op("dve", lambda e: e.memset(cmask[0:64, 0:1], 1.0), writes=[b_cmask])
        T.op("dve", lambda e: e.memset(cmask[64:128, 1:2], 1.0), writes=[b_cmask])
        for hh in range(NHEAD_):
            T.dma("sp", qT[:], qT_s[hh * 128:(hh + 1) * 128, :], writes=[b_in])
            T.dma("sp", kT[:], kT_s[hh * 128:(hh + 1) * 128, :], writes=[b_in])
            T.dma("sp", kk_[:], k_s.rearrange("(i p) c -> p i c", p=128)[:, :, hh * 128:(hh + 1) * 128], writes=[b_in])
            T.dma("sp", vv[:], v_s.rearrange("(i p) c -> p i c", p=128)[:, :, hh * 256:(hh + 1) * 256], writes=[b_in])
            for d in range(2):
                T.dma("sp", la[d][:], la_s[d].rearrange("(i p) c -> p i c", p=128)[:, :, hh * 128:(hh + 1) * 128],
                      writes=[b_in])
            T.dma("sp", sg_[:], sgT_s.rearrange("(k p) t -> p k t", p=128)[:, 2 * hh:2 * hh + 2, :], writes=[b_sg])
            for d in range(2):
                T.op("dve", lambda e: e.memset(S32[d][:], 0.0), writes=[b_S32[d]])
                T.op("dve", lambda e: e.memset(Sbf[d][:], 0.0), writes=[b_Sbf[d]])
                T.op("dve", lambda e: e.memset(tmpS[d][:], 0.0), writes=[b_tmpS[d]])
            written = [False] * 32
            def phase12(step):
                    ctx = []
                    for d in range(2):
                        i = step if d == 0 else 31 - step
                        M, b_M = masks[d]
                        c0 = i * 128
                        pbT, bbT = PS.next()
                        T.op("pe", lambda e: e.matmul(pbT[:, :128], la[d][:, i, :], M[:, :], start=True, stop=True),
                             reads=[b_in, b_M], writes=[bbT])
                        pbk, bbk = PS.next()
                        T.op("pe", lambda e: e.matmul(pbk[:, :128], M[:, :], la[d][:, i, :], start=True, stop=True),
                             reads=[b_in, b_M], writes=[bbk])
                        eb, b_eb = ebR.next(); enb, b_enb = enbR.next(); ent, b_ent = entR.next()
                        T.op("act", lambda e: e.activation(out=eb[:, :], in_=pbT[:, :128], func=AF.Exp), reads=[bbT], writes=[b_eb])
                        T.op("act", lambda e: e.activation(out=enb[:, :], in_=pbT[:, :128], func=AF.Exp, scale=-1.0),
                             reads=[bbT], writes=[b_enb])
                        T.op("act", lambda e: e.activation(out=ent[:, :], in_=pbk[:, :128], func=AF.Exp, scale=-1.0),
                             reads=[bbk], writes=[b_ent])
                        qt, b_qt = qtR.next(); kt, b_kt = ktR.next(); ktok, b_ktok = ktokR.next()
                        T.op("dve", lambda e: e.tensor_tensor(out=qt[:, :], in0=qT[:, c0:c0 + 128], in1=eb[:, :], op=ALU.mult),
                             reads=[b_in, b_eb], writes=[b_qt])
                        T.op("dve", lambda e: e.tensor_tensor(out=kt[:, :], in0=kT[:, c0:c0 + 128], in1=enb[:, :], op=ALU.mult),
                             reads=[b_in, b_enb], writes=[b_kt])
                        T.op("dve", lambda e: e.tensor_tensor(out=ktok[:, :], in0=kk_[:, i, :], in1=ent[:, :], op=ALU.mult),
                             reads=[b_in, b_ent], writes=[b_ktok])
                        ecols = (63, 127) if d == 0 else (0, 64)
                        ctx.append((i, c0, M, b_M, qt, b_qt, kt, b_kt, ktok, b_ktok, (eb, ecols), b_eb))
                    pps = []
                    for d in range(2):
                        i, c0, M, b_M, qt, b_qt, kt, b_kt, ktok, b_ktok, el, b_el = ctx[d]
                        psc, bsc = PS.next()
                        T.op("pe", lambda e: e.matmul(psc[:, :128], kt[:, :], qt[:, :], start=True, stop=True),
                             reads=[b_kt, b_qt], writes=[bsc])
                        pp, b_pp = pR.next()
                        T.op("dve", lambda e: e.tensor_tensor(out=pp[:, :], in0=psc[:, :128], in1=M[:, :], op=ALU.mult),
                             reads=[bsc, b_M], writes=[b_pp])
                        pps.append((pp, b_pp))
                    return ctx, pps

            def phase34(step, ctx, pps, nctx):
                    pos = []
                    for d in range(2):
                        i = ctx[d][0]
                        po, bo = PS.next()
                        pp, b_pp = pps[d]
                        for vt in range(2):
                            T.op("pe", lambda e: e.matmul(po[:, vt * 128:(vt + 1) * 128], vv[:, i, vt * 128:(vt + 1) * 128], pp[:, :],
                                                          start=True, stop=True),
                                 reads=[b_in, b_pp], writes=[bo], inc=False)
                        pos.append((po, bo))
                    for half in range(2):
                        for d in range(2):
                            if DBG.get("no_inter"):
                                if half == 1:
                                    po, bo = pos[d]
                                    T.op("pe", lambda e: e.matmul(po[:, 256:320], Sbf[d][:, 0:128], ctx[d][4][:, 0:64], start=True, stop=True),
                                         reads=[b_Sbf[d]], writes=[bo])
                                continue
                            i, c0, M, b_M, qt, b_qt, kt, b_kt, ktok, b_ktok, el, b_el = ctx[d]
                            po, bo = pos[d]
                            ch = half if d == 0 else 1 - half
                            r0 = ch * 64
                            for vt in range(2):
                                T.op("pe", lambda e: e.matmul(po[:, 256 + vt * 128 + r0: 256 + vt * 128 + r0 + 64],
                                                              Sbf[d][:, vt * 128:(vt + 1) * 128], qt[:, r0:r0 + 64],
                                                              start=True, stop=True),
                                     reads=[b_Sbf[d], b_qt], writes=[bo], inc=(half == 1 and vt == 1))
                            pd, bd = PS.next()
                            T.op("pe", lambda e: e.matmul(pd[:, :256], ktok[r0:r0 + 64, :], vv[r0:r0 + 64, i, :],
                                                          start=True, stop=True), reads=[b_ktok, b_in], writes=[bd])
                            ebt, ecols = el
                            esc = ebt[:, ecols[ch]:ecols[ch] + 1]
                            T.op("dve", lambda e: e.scalar_tensor_tensor(out=Sbf[d][:], in0=pd[:, :256], scalar=esc,
                                                                         in1=tmpS[d][:], op0=ALU.mult, op1=ALU.add),
                                 reads=[bd, b_el, b_tmpS[d]], writes=[b_Sbf[d]])
                            T.op("dve", lambda e: e.scalar_tensor_tensor(out=S32[d][:], in0=pd[:, :256], scalar=esc,
                                                                         in1=tmpS[d][:], op0=ALU.mult, op1=ALU.add),
                                 reads=[bd, b_el, b_tmpS[d]], writes=[b_S32[d]])
                            if half == 0:
                                nel, nb_el, nch = el, b_el, 1 - ch
                            elif nctx is not None:
                                nel, nb_el = nctx[d][10], nctx[d][11]
                                nch = 0 if d == 0 else 1
                            else:
                                nel = None
                            if nel is not None:
                                nesc = nel[0][:, nel[1][nch]:nel[1][nch] + 1]
                                T.op("dve", lambda e: e.tensor_scalar(out=tmpS[d][:], in0=S32[d][:], scalar1=nesc, scalar2=None,
                                                                      op0=ALU.mult),
                                     reads=[b_S32[d], nb_el], writes=[b_tmpS[d]])
                    for d in range(2):
                        i, c0 = ctx[d][0], ctx[d][1]
                        po, bo = pos[d]
                        o3 = oacc[:, :, c0:c0 + 128]
                        p_intra = po[:, 0:256].rearrange("p (v t) -> p v t", v=2)
                        p_inter = po[:, 256:512].rearrange("p (v t) -> p v t", v=2)
                        if not written[i]:
                            T.op("act", lambda e: e.activation(out=o3, in_=p_intra, func=AF.Copy), reads=[bo], writes=[b_o[i]])
                        else:
                            T.op("dve", lambda e: e.tensor_tensor(out=o3, in0=p_intra, in1=o3, op=ALU.add),
                                 reads=[bo, b_o[i]], writes=[b_o[i]])
                        T.op("dve", lambda e: e.tensor_tensor(out=o3, in0=p_inter, in1=o3, op=ALU.add),
                             reads=[bo, b_o[i]], writes=[b_o[i]])
                        written[i] = True

            nxt = phase12(0)
            for step in range(32):
                cur = nxt
                nxt = phase12(step + 1) if step + 1 < 32 else None
                phase34(step, cur[0], cur[1], nxt[0] if nxt is not None else None)
            if DBG.get("dump") is not None and hh == 0:
                do, dS = DBG["dump"]
                for vt in range(2):
                    T.dma("sp", do[vt * 128:(vt + 1) * 128, :], oacc[:, vt, :], reads=b_o)
                for d in range(2):
                    T.dma("sp", dS[d], S32[d][:], reads=[b_S32[d]])
            row_toks = []
            for c in range(NCH):
                t0 = c * 512
                pt, pb = PS.next()
                for vt in range(2):
                    sq, b_sq = sq2.next()
                    T.op("act", lambda e: e.activation(out=sq[:, :], in_=oacc[:, vt, t0:t0 + 512], func=AF.Square),
                         reads=b_o[4 * c:4 * c + 4], writes=[b_sq])
                    T.op("pe", lambda e: e.matmul(pt[:, :], ones[:], sq[:, :], start=(vt == 0), stop=(vt == 1)),
                         reads=[b_ones, b_sq], writes=[pb])
                T.op("act", lambda e: e.activation(out=rs2[:, :], in_=pt[:, :], func=AF.Sqrt, scale=1.0 / 256, bias=EPS),
                     reads=[pb], writes=[b_rs2])
                T.op("dve", lambda e: e.reciprocal(out=rs2[:, :], in_=rs2[:, :]), reads=[b_rs2], writes=[b_rs2])
                for vt in range(2):
                    tm, b_tm = tm2.next()
                    T.op("dve", lambda e: e.scalar_tensor_tensor(out=tm[:, :], in0=oacc[:, vt, t0:t0 + 512],
                                                                 scalar=glags[:, vt:vt + 1], in1=rs2[:, :],
                                                                 op0=ALU.mult, op1=ALU.mult),
                         reads=b_o[4 * c:4 * c + 4] + [b_glag, b_rs2], writes=[b_tm])
                    s, b_s = stg.next()
                    T.op("dve", lambda e: e.tensor_tensor(out=s[:, :], in0=tm[:, :], in1=sg_[:, vt, t0:t0 + 512], op=ALU.mult),
                         reads=[b_tm, b_sg], writes=[b_s])
                    r = 512 + hh * 256 + vt * 128
                    row_toks.append(T.dma("sp", mixT[r:r + 128, t0:t0 + 512], s[:, :], reads=[b_s]))
            if on_rows is not None:
                on_rows(2 + hh, row_toks)


def build_even(debug=False):
    nc = bass.Bass("TRN2", target_bir_lowering=False)
    dt = lambda name, shape, dtype, kind="ExternalInput": nc.dram_tensor(name, shape, dtype, kind=kind).ap()
    hT = dt("hT", [D, SEQ], F32)
    g1 = dt("g1", [128, KT], F32)
    w_e = dt("w_e", [D, NW_E], F32)
    gua_f = dt("gua_f", [17, 256], F32)
    gua_b = dt("gua_b", [17, 256], F32)
    glag = dt("glag", [128, 2], F32)
    cs = dt("cs", [256, 512], BF16)
    cosm = dt("cosm", [SEQ, SEQ], BF16)
    nsinm = dt("nsinm", [SEQ, SEQ], BF16)
    mf = dt("mf", [128, 128], BF16)
    mb = dt("mb", [128, 128], BF16)
    mixT = dt("mixT", [1024, SEQ], BF16, "ExternalOutput")
    kind = "ExternalOutput" if debug else "Internal"
    scr = (dt("aT_s", [512, SEQ], BF16, kind), dt("qT_s", [256, SEQ], BF16, kind), dt("kT_s", [256, SEQ], BF16, kind),
           dt("k_s", [SEQ, 256], BF16, kind), dt("v_s", [SEQ, 512], BF16, kind), dt("sgT_s", [512, SEQ], BF16, kind),
           dt("la_s", [2, SEQ, 256], BF16, kind))
    if debug:
        DBG["dump"] = (dt("dbg_o", [256, SEQ], F32, "ExternalOutput"), dt("dbg_S", [2, 128, 256], F32, "ExternalOutput"))
    with contextlib.ExitStack() as stack:
        T = Trk(nc, stack)
        ps = [stack.enter_context(nc.psum_tensor("ps%d" % i, [128, 512], F32)) for i in range(8)]
        hv = hT.rearrange("(k p) t -> p k t", p=128)
        hload = lambda T, t, c, b: T.dma("sp", t[:], hv[:, :, c * 512:(c + 1) * 512], writes=[b])
        emit_even(nc, T, stack, ps, hload, g1, w_e, gua_f, gua_b, glag, cs, cosm, nsinm, mf, mb, mixT, scr)
        T.finish()
    return nc


def even_consts():
    c = np.arange(256)
    ang = 2 * np.pi * ((c[:, None] * c[None, :]) % 256) / 256.0
    cs = np.concatenate([np.cos(ang), np.sin(ang)], axis=1) / 16.0
    s = np.arange(SEQ, dtype=np.int64)
    ang = 2 * np.pi * ((s[:, None] * s[None, :]) % SEQ) / float(SEQ)
    cosm = (np.cos(ang) / 64.0).astype(NPBF)
    nsinm = (-np.sin(ang) / 64.0).astype(NPBF)
    p = np.arange(128)
    same = (p[:, None] // 64) == (p[None, :] // 64)
    mf = (same & (p[:, None] <= p[None, :])).astype(np.float32).astype(NPBF)
    mb = (same & (p[:, None] >= p[None, :])).astype(np.float32).astype(NPBF)
    return dict(cs=cs.astype(NPBF), cosm=cosm, nsinm=nsinm, mf=mf, mb=mb)


def even_weights(j, w_in, gu_f, gb_f, gu_b, gb_b, gla_g):
    cols = np.r_[512 * j:512 * j + 512, 1024 + 256 * j:1024 + 256 * j + 256, 1536 + 256 * j:1536 + 256 * j + 256,
                 2048 + 512 * j:2048 + 512 * j + 512, 3072 + 512 * j:3072 + 512 * j + 512, 4096:4128]
    hc = slice(256 * j, 256 * j + 256)
    return dict(w_e=np.ascontiguousarray(w_in[:, cols]),
                gua_f=np.ascontiguousarray(np.concatenate([gu_f[:, hc], gb_f[None, hc]], 0)),
                gua_b=np.ascontiguousarray(np.concatenate([gu_b[:, hc], gb_b[None, hc]], 0)),
                glag=lay128(gla_g))


NBU = 3072
NKT = 20


def emit_odd(nc, T, stack, ps, hload, g1, w_o, qg, kg, rb, onehot, cmult, mixT, scr, on_rows=None):
    sb = lambda name, shape, dt: stack.enter_context(nc.sbuf_tensor(uniq(name), shape, dt))
    PS = Rot(ps)
    xn_s, qT_s, kT_s, v_s, u_s = scr
    NCH = SEQ // 512
    ones = sb("ones", [128, 128], F32); b_ones = Buf()
    T.op("dve", lambda e: e.memset(ones[:], 1.0), writes=[b_ones])
    onesb = sb("onesb", [128, 128], BF16); b_onesb = Buf()
    T.op("dve", lambda e: e.memset(onesb[:], 1.0), writes=[b_onesb])
    g1s = sb("g1s", [128, KT], F32); b_g1 = Buf()
    T.dma("sp", g1s[:], g1, writes=[b_g1])
    qgs = sb("qgs", [128, 1], F32); b_qg = Buf()
    T.dma("sp", qgs[:], qg, writes=[b_qg])
    kgs = sb("kgs", [128, 1], F32); b_kg = Buf()
    T.dma("sp", kgs[:], kg, writes=[b_kg])

    with contextlib.ExitStack() as st2:
        sb2 = lambda name, shape, dt: st2.enter_context(nc.sbuf_tensor(uniq(name), shape, dt))
        rbs = sb2("rbs", [32, 8], F32); b_rb = Buf()
        T.dma("sp", rbs[:], rb, writes=[b_rb])
        oh = sb2("oh", [32, NBU], F32); b_oh = Buf()
        T.dma("sp", oh[:], onehot, writes=[b_oh])
        cm = sb2("cm", [8, NBU], F32); b_cm = Buf()
        T.dma("sp", cm[:], cmult, writes=[b_cm])
        ue = sb2("ue", [8, NBU], F32); b_ue = Buf()
        ub = sb2("ub", [8, NBU], BF16); b_ub = Buf()
        for c in range(NBU // 512):
            pt, pb = PS.next()
            T.op("pe", lambda e: e.matmul(pt[0:8, :], rbs[:, :], oh[:, c * 512:(c + 1) * 512], start=True, stop=True),
                 reads=[b_rb, b_oh], writes=[pb])
            T.op("act", lambda e: e.activation(out=ue[:, c * 512:(c + 1) * 512], in_=pt[0:8, :], func=AF.Exp),
                 reads=[pb], writes=[b_ue])
        T.op("dve", lambda e: e.tensor_tensor(out=ub[:, :], in0=ue[:, :], in1=cm[:, :], op=ALU.mult),
             reads=[b_ue, b_cm], writes=[b_ub])
        b_us = Buf()
        T.dma("sp", u_s, ub[:, :], reads=[b_ub], writes=[b_us])

        hcs = [sb2("hc%d" % i, [128, KT, 512], F32) for i in range(2)]; b_hcs = [Buf(), Buf()]
        xcs = [sb2("xc%d" % i, [128, KT, 512], BF16) for i in range(2)]; b_xcs = [Buf(), Buf()]
        sqr = Rot([sb2("sq%d" % i, [128, 512], F32) for i in range(2)])
        rstd_t = sb2("rstd", [128, 512], F32); b_rstd = Buf()
        xv = xn_s.rearrange("(k p) t -> p k t", p=128)
        b_xn = [Buf() for _ in range(NCH)]
        hload(T, hcs[0], 0, b_hcs[0])
        for c in range(NCH):
            t0 = c * 512
            if c + 1 < NCH:
                hload(T, hcs[(c + 1) % 2], c + 1, b_hcs[(c + 1) % 2])
            emit_norm_chunk(nc, T, hcs[c % 2], b_hcs[c % 2], g1s, b_g1, ones, b_ones, PS, sqr, rstd_t, b_rstd,
                            xcs[c % 2], b_xcs[c % 2], 512)
            T.dma("sp", xv[:, :, t0:t0 + 512], xcs[c % 2][:], reads=[b_xcs[c % 2]], writes=[b_xn[c]])
    T.barrier()

    with contextlib.ExitStack() as st2:
        sb2 = lambda name, shape, dt: st2.enter_context(nc.sbuf_tensor(uniq(name), shape, dt))
        wp = [sb2("wp%d" % i, [128, KT, 1024], BF16) for i in range(2)]; b_wp = [Buf(), Buf()]
        xcs = [sb2("xc%d" % i, [128, KT, 512], BF16) for i in range(2)]; b_xcs = [Buf(), Buf()]
        sqr = Rot([sb2("sq%d" % i, [128, 512], F32) for i in range(3)])
        rsr = Rot([sb2("rs%d" % i, [128, 512], F32) for i in range(3)])
        stg = Rot([sb2("stg%d" % i, [128, 512], BF16) for i in range(4)])
        wv = w_o.rearrange("(k p) n -> p k n", p=128)
        xv = xn_s.rearrange("(k p) t -> p k t", p=128)
        T.dma("pool", wp[0][:], wv[:, :, 0:1024], writes=[b_wp[0]])
        items = [(part, c) for part in range(3) for c in range(NCH)]
        T.dma("sp", xcs[0][:], xv[:, :, 0:512], reads=[b_xn[0]], writes=[b_xcs[0]])
        for idx, (part, c) in enumerate(items):
            t0 = c * 512
            if c == 0 and part + 1 < 3:
                T.dma("pool", wp[(part + 1) % 2][:], wv[:, :, (part + 1) * 1024:(part + 2) * 1024],
                      writes=[b_wp[(part + 1) % 2]])
            if idx + 1 < len(items):
                c2 = items[idx + 1][1]
                T.dma("sp", xcs[(idx + 1) % 2][:], xv[:, :, c2 * 512:(c2 + 1) * 512], reads=[b_xn[c2]],
                      writes=[b_xcs[(idx + 1) % 2]])
            xc, b_xc = xcs[idx % 2], b_xcs[idx % 2]
            w_, b_w = wp[part % 2], b_wp[part % 2]
            if part < 2:
                gs, b_gs = (qgs, b_qg) if part == 0 else (kgs, b_kg)
                dst = qT_s if part == 0 else kT_s
                sc_, bi_ = (1.0, 128 * EPS) if part == 0 else (1.0 / 128, EPS)
                pend_n = None
                for hd in range(8):
                    pt, pb = PS.next()
                    mm_group(T, pt[:, :], [(w_[:, k, hd * 128:(hd + 1) * 128], xc[:, k, :]) for k in range(KT)], pb,
                             reads=[b_w, b_xc])
                    sq, b_sq = sqr.next()
                    T.op("act", lambda e: e.activation(out=sq[:, :], in_=pt[:, :], func=AF.Square), reads=[pb], writes=[b_sq])
                    if pend_n is not None:
                        pend_n()

                    def fin(hd=hd, pt=pt, pb=pb, sq=sq, b_sq=b_sq):
                        p2, pb2 = PS.next()
                        T.op("pe", lambda e: e.matmul(p2[:, :], ones[:], sq[:, :], start=True, stop=True),
                             reads=[b_ones, b_sq], writes=[pb2])
                        rs, b_rs = rsr.next()
                        T.op("act", lambda e: e.activation(out=rs[:, :], in_=p2[:, :], func=AF.Sqrt, scale=sc_, bias=bi_),
                             reads=[pb2], writes=[b_rs])
                        T.op("dve", lambda e: e.reciprocal(out=rs[:, :], in_=rs[:, :]), reads=[b_rs], writes=[b_rs])
                        s, b_s = stg.next()
                        T.op("dve", lambda e: e.scalar_tensor_tensor(out=s[:, :], in0=pt[:, :], scalar=gs[:, 0:1], in1=rs[:, :],
                                                                     op0=ALU.mult, op1=ALU.mult),
                             reads=[pb, b_gs, b_rs], writes=[b_s])
                        T.dma("sp", dst[hd * 128:(hd + 1) * 128, t0:t0 + 512], s[:, :], reads=[b_s])
                    pend_n = fin
                pend_n()
            else:
                for ts in range(4):
                    for half in range(2):
                        pt, pb = PS.next()
                        mm_group(T, pt[:, :], [(xc[:, k, ts * 128:(ts + 1) * 128], w_[:, k, half * 512:(half + 1) * 512])
                                               for k in range(KT)], pb, reads=[b_w, b_xc])
                        s, b_s = stg.next()
                        if half == 0:
                            T.op("act", lambda e: e.activation(out=s[:, :], in_=pt[:, :], func=AF.Copy), reads=[pb], writes=[b_s])
                        else:
                            T.op("dve", lambda e: e.tensor_copy(out=s[:, :], in_=pt[:, :]), reads=[pb], writes=[b_s])
                        T.dma("sp", v_s[t0 + ts * 128:t0 + (ts + 1) * 128, half * 512:(half + 1) * 512], s[:, :], reads=[b_s])
    T.barrier()

    with contextlib.ExitStack() as st2:
        sb2 = lambda name, shape, dt: st2.enter_context(nc.sbuf_tensor(uniq(name), shape, dt))
        qTs = [sb2("qT%d" % i, [128, SEQ], BF16) for i in range(2)]
        kTs = [sb2("kT%d" % i, [128, SEQ], BF16) for i in range(2)]
        vts = [sb2("vt%d" % i, [128, 32, 128], BF16) for i in range(2)]
        Es = [sb2("E%d" % i, [128, NKT, 512], BF16) for i in range(2)]
        b_q = [Buf(), Buf()]; b_k = [Buf(), Buf()]; b_v = [Buf(), Buf()]; b_E = [Buf(), Buf()]
        pex = Rot([sb2("pex%d" % i, [128, 512], BF16) for i in range(5)])
        pmr = Rot([sb2("pm%d" % i, [128, 512], BF16) for i in range(7)])
        rdr = Rot([sb2("rd%d" % i, [128, 512], F32) for i in range(2)])
        stg = Rot([sb2("stg%d" % i, [128, 512], BF16) for i in range(2)])
        vv_ = v_s.rearrange("(i p) c -> p i c", p=128)
        SPS = Rot(ps[0:4]); OPS = Rot(ps[4:6]); DPS = Rot(ps[6:8])

        def loadh(h):
            s = h % 2
            T.dma("sp", qTs[s][:], qT_s[h * 128:(h + 1) * 128, :], writes=[b_q[s]])
            T.dma("sp", kTs[s][:], kT_s[h * 128:(h + 1) * 128, :], writes=[b_k[s]])
            T.dma("sp", vts[s][:], vv_[:, :, h * 128:(h + 1) * 128], writes=[b_v[s]])
            T.dma("sp", Es[s][:], bass.AP(u_s.tensor, u_s.offset + h * NBU, [[1, 128], [128, NKT], [1, 512]]),
                  reads=[b_us], writes=[b_E[s]])

        def rev(ap):
            return bass.AP(ap.tensor, ap.offset + 511, [list(ap.ap[0]), [-1, 512]])

        loadh(0)
        loadh(1)
        iters = []
        for h in range(8):
            for qt in range(NCH):
                tiles = [a for a in range(NKT) if 0 <= qt * 512 - 1024 + 128 * a < SEQ]
                for n_, a in enumerate(tiles):
                    iters.append((h, qt, a, n_ == 0, n_ == len(tiles) - 1))
        live = {}
        acc = {}
        row_toks = []

        def emit_qk(i):
            h, qt, a, first, last = iters[i]
            s = h % 2
            t0 = qt * 512
            s0 = t0 - 1024 + 128 * a
            pt, pb = SPS.next()
            T.op("pe", lambda e: e.matmul(pt[:, :], kTs[s][:, s0:s0 + 128], rev(qTs[s][:, t0:t0 + 512]),
                                          start=True, stop=True), reads=[b_k[s], b_q[s]], writes=[pb])
            px, b_px = pex.next()
            T.op("act", lambda e: e.activation(out=px[:, :], in_=pt[:, :], func=AF.Exp), reads=[pb], writes=[b_px])
            pm, b_pm = pmr.next()
            T.op("dve", lambda e: e.tensor_tensor(out=pm[:, :], in0=px[:, :], in1=Es[s][:, a, :], op=ALU.mult),
                 reads=[b_px, b_E[s]], writes=[b_pm])
            live[i] = (pm, b_pm)

        def emit_pv(i):
            h, qt, a, first, last = iters[i]
            s = h % 2
            t0 = qt * 512
            s0 = t0 - 1024 + 128 * a
            if first:
                acc[(h, qt)] = (OPS.next(), DPS.next())
                if qt == 0 and 1 <= h < 7:
                    loadh(h + 1)
            (po, bo), (pd, bd) = acc[(h, qt)]
            pm, b_pm = live.pop(i)
            T.op("pe", lambda e: e.matmul(po[:, :], vts[s][:, s0 // 128, :], pm[:, :], start=first, stop=last),
                 reads=[b_v[s], b_pm], writes=[bo], inc=last)
            T.op("pe", lambda e: e.matmul(pd[:, :], onesb[:, :], pm[:, :], start=first, stop=last),
                 reads=[b_onesb, b_pm], writes=[bd], inc=True)
            if last:
                rd, b_rd = rdr.next()
                T.op("dve", lambda e: e.reciprocal(out=rd[:, :], in_=pd[:, :]), reads=[bd], writes=[b_rd])
                st_, b_st = stg.next()
                T.op("dve", lambda e: e.tensor_tensor(out=st_[:, :], in0=rev(po[:, :]), in1=rev(rd[:, :]), op=ALU.mult),
                     reads=[bo, b_rd], writes=[b_st])
                row_toks.append(T.dma("sp", mixT[h * 128:(h + 1) * 128, t0:t0 + 512], st_[:, :], reads=[b_st]))
                del acc[(h, qt)]
                if qt == NCH - 1 and h % 2 == 1 and on_rows is not None:
                    on_rows(h // 2, list(row_toks))
                    del row_toks[:]

        LOOK = 4
        for i in range(len(iters) + LOOK):
            if i < len(iters):
                emit_qk(i)
            if i - LOOK >= 0:
                emit_pv(i - LOOK)


def build_odd(debug=False):
    nc = bass.Bass("TRN2", target_bir_lowering=False)
    dt = lambda name, shape, dtype, kind="ExternalInput": nc.dram_tensor(name, shape, dtype, kind=kind).ap()
    hT = dt("hT", [D, SEQ], F32)
    g1 = dt("g1", [128, KT], F32)
    w_o = dt("w_o", [D, 3072], F32)
    qg = dt("qg", [128, 1], F32)
    kg = dt("kg", [128, 1], F32)
    rb = dt("rb", [32, 8], F32)
    onehot = dt("onehot", [32, NBU], F32)
    cmult = dt("cmult", [8, NBU], F32)
    mixT = dt("mixT", [1024, SEQ], BF16, "ExternalOutput")
    kind = "ExternalOutput" if debug else "Internal"
    scr = (dt("xn_s", [D, SEQ], BF16, kind), dt("qT_s", [1024, SEQ], BF16, kind), dt("kT_s", [1024, SEQ], BF16, kind),
           dt("v_s", [SEQ, 1024], BF16, kind), dt("u_s", [8, NBU], BF16, kind))
    with contextlib.ExitStack() as stack:
        T = Trk(nc, stack)
        ps = [stack.enter_context(nc.psum_tensor("ps%d" % i, [128, 512], F32)) for i in range(8)]
        hv = hT.rearrange("(k p) t -> p k t", p=128)
        hload = lambda T, t, c, b: T.dma("sp", t[:], hv[:, :, c * 512:(c + 1) * 512], writes=[b])
        emit_odd(nc, T, stack, ps, hload, g1, w_o, qg, kg, rb, onehot, cmult, mixT, scr)
        T.finish()
    return nc


def t5_bucket_np(rel):
    nb = 16
    ret = (rel > 0).astype(np.int32) * nb
    n = np.abs(rel)
    max_exact = nb // 2
    large = max_exact + (np.log(np.maximum(n, 1) / max_exact) / np.log(1024 / max_exact) * (nb - max_exact)).astype(np.int32)
    large = np.minimum(large, nb - 1)
    return (ret + np.where(n < max_exact, n, large)).astype(np.int32)


def odd_consts():
    m = np.arange(NBU)
    delta = m - 1535
    bkt = t5_bucket_np(delta)
    onehot = (bkt[None, :] == np.arange(32)[:, None]).astype(np.float32)
    mult = np.zeros(NBU, np.float32)
    for (w, d) in ((128, 1), (512, 4), (2048, 16)):
        mult += ((delta % d == 0) & (np.abs(delta) <= w // 2)).astype(np.float32)
    return dict(onehot=onehot, cmult=np.ascontiguousarray(np.broadcast_to(mult, (8, NBU))))


def odd_weights(j, w_qkv, q_g, k_g, rel_bias):
    cols = np.r_[1024 * j:1024 * j + 1024, 2048 + 1024 * j:2048 + 1024 * j + 1024, 4096 + 1024 * j:4096 + 1024 * j + 1024]
    return dict(w_o=np.ascontiguousarray(w_qkv[:, cols]), qg=np.ascontiguousarray(q_g.reshape(128, 1)),
                kg=np.ascontiguousarray(k_g.reshape(128, 1)), rb=np.ascontiguousarray(rel_bias[:, 8 * j:8 * j + 8]))


_PROGS = {}


def _prog(name):
    if name not in _PROGS:
        _PROGS[name] = {"even": build_even, "odd": build_odd, "post": build_post}[name]()
    return _PROGS[name]


def _run(name, in_maps):
    res = run_bass_kernel_spmd(_prog(name), in_maps, core_ids=list(range(NCORES)))
    return res.results


def kernel_unfused(x, mix_norm_g, w_in_even, gate_up_fwd, gate_bias_fwd, gate_up_bwd, gate_bias_bwd, gla_norm_g, w_out_even,
           w_qkv_odd, q_norm_g, k_norm_g, rel_bias, w_out_odd, ffn_norm_g, w_gate, w_up, conv_w, conv_b, w_down):
    f32 = lambda a: np.ascontiguousarray(np.asarray(a, dtype=np.float32))
    x = f32(x)
    hT = [np.ascontiguousarray(x[b].T) for b in range(BATCH)]
    ec = even_consts()
    oc = odd_consts()
    for layer in range(4):
        i = layer // 2
        g1 = lay128(f32(mix_norm_g[layer]))
        ims = []
        for b in range(BATCH):
            for j in range(2):
                im = dict(hT=hT[b], g1=g1)
                if layer % 2 == 0:
                    im.update(even_weights(j, f32(w_in_even[i]), f32(gate_up_fwd[i]), f32(gate_bias_fwd[i]),
                                           f32(gate_up_bwd[i]), f32(gate_bias_bwd[i]), f32(gla_norm_g[i])))
                    im.update(ec)
                else:
                    im.update(odd_weights(j, f32(w_qkv_odd[i]), f32(q_norm_g[i]), f32(k_norm_g[i]), f32(rel_bias)))
                    im.update(oc)
                ims.append(im)
        res = _run("even" if layer % 2 == 0 else "odd", ims)
        mT = []
        for b in range(BATCH):
            m0 = np.asarray(res[2 * b]["mixT"]); m1 = np.asarray(res[2 * b + 1]["mixT"])
            if layer % 2 == 0:
                mT.append(np.concatenate([m0[:512], m1[:512], m0[512:], m1[512:]], axis=0))
            else:
                mT.append(np.concatenate([m0, m1], axis=0))
        w_out = f32(w_out_even[i]) if layer % 2 == 0 else f32(w_out_odd[i])
        cw = np.ascontiguousarray(f32(conv_w[layer]).T.reshape(FT, 128, 3).transpose(1, 0, 2).reshape(128, FT * 3))
        common = dict(w_out=w_out, g2=lay128(f32(ffn_norm_g[layer])), w_gate=f32(w_gate[layer]), w_up=f32(w_up[layer]),
                      cw=cw, cb=lay128(f32(conv_b[layer])), w_down=f32(w_down[layer]))
        ims = []
        for b in range(BATCH):
            for half in range(2):
                t0 = half * TOK
                he = np.zeros((D, EXT), np.float32); me = np.zeros((D, EXT), mT[b].dtype)
                lo, hi = max(t0 - 1, 0), min(t0 + TOK + 1, SEQ)
                he[:, lo - (t0 - 1):hi - (t0 - 1)] = hT[b][:, lo:hi]
                me[:, lo - (t0 - 1):hi - (t0 - 1)] = mT[b][:, lo:hi]
                im = dict(hT_ext=he, mT_ext=me)
                im.update(common)
                ims.append(im)
        res = _run("post", ims)
        hT = [np.ascontiguousarray(np.concatenate([np.asarray(res[2 * b]["hT_out"]), np.asarray(res[2 * b + 1]["hT_out"])], axis=1))
              for b in range(BATCH)]
    return np.ascontiguousarray(np.stack([h.T for h in hT], axis=0)).astype(np.float32)


PAIRS = [[0, 1], [2, 3], [4, 5], [6, 7]]


def build_fused():
    nc = bass.Bass("TRN2", target_bir_lowering=False)
    dt = lambda name, shape, dtype, kind="ExternalInput": nc.dram_tensor(name, shape, dtype, kind=kind).ap()
    xT = dt("xT", [D, TOK], F32)
    sel = dt("sel", [128, 2], F32)
    cs = dt("cs", [256, 512], BF16); cosm = dt("cosm", [SEQ, SEQ], BF16); nsinm = dt("nsinm", [SEQ, SEQ], BF16)
    mf = dt("mf", [128, 128], BF16); mb = dt("mb", [128, 128], BF16)
    onehot = dt("onehot", [32, NBU], F32); cmult = dt("cmult", [8, NBU], F32); rb = dt("rb", [32, 8], F32)
    L = []
    for l in range(4):
        d_ = dict(g1=dt("g1_%d" % l, [128, KT], F32), w_out=dt("w_out_%d" % l, [D, D], F32), g2=dt("g2_%d" % l, [128, KT], F32),
                  w_gate=dt("w_gate_%d" % l, [D, FF], F32), w_up=dt("w_up_%d" % l, [D, FF], F32),
                  cw=dt("cw_%d" % l, [128, FT * 3], F32), cb=dt("cb_%d" % l, [128, FT], F32),
                  w_down=dt("w_down_%d" % l, [FF, D], F32))
        if l % 2 == 0:
            d_.update(w_e=dt("w_e_%d" % l, [D, NW_E], F32), gua_f=dt("gua_f_%d" % l, [17, 256], F32),
                      gua_b=dt("gua_b_%d" % l, [17, 256], F32), glag=dt("glag_%d" % l, [128, 2], F32))
        else:
            d_.update(w_o=dt("w_o_%d" % l, [D, 3072], F32), qg=dt("qg_%d" % l, [128, 1], F32), kg=dt("kg_%d" % l, [128, 1], F32))
        L.append(d_)
    outT = dt("outT", [D, TOK], F32, "ExternalOutput")
    I = "Internal"
    hown = dt("hown", [D, TOK], F32, I)
    hfull = dt("hfull", [8, 2, 2, 128, TOK], F32, I)
    mcore = dt("mcore", [1024, SEQ], BF16, I)
    mfull = dt("mfull", [4, 2, 256, SEQ], BF16, I)
    hmid_scr = dt("hmid_scr", [FF, TOK], BF16, I)
    hm_scr = dt("hm_scr", [D, EXT], F32, I)
    scr_e = (dt("aT_s", [512, SEQ], BF16, I), dt("qT_s", [256, SEQ], BF16, I), dt("kT_s", [256, SEQ], BF16, I),
             dt("k_s", [SEQ, 256], BF16, I), dt("v_s", [SEQ, 512], BF16, I), dt("sgT_s", [512, SEQ], BF16, I),
             dt("la_s", [2, SEQ, 256], BF16, I))
    scr_o = (dt("xn_s", [D, SEQ], BF16, I), dt("qTo_s", [1024, SEQ], BF16, I), dt("kTo_s", [1024, SEQ], BF16, I),
             dt("vo_s", [SEQ, 1024], BF16, I), dt("u_s", [8, NBU], BF16, I))

    with contextlib.ExitStack() as stack:
        T = Trk(nc, stack)
        ps = [stack.enter_context(nc.psum_tensor("ps%d" % i, [128, 512], F32)) for i in range(8)]
        sels = stack.enter_context(nc.sbuf_tensor("sels", [128, 2], F32)); b_sel = Buf()
        T.dma("sp", sels[:], sel, writes=[b_sel])
        T.dma("sp", hown, xT)

        def hload(T, t, c, b):
            r, tc = c // 4, (c % 4) * 512
            toks = []
            for q in range(8):
                toks.append(T.dma("sp", t[:, 2 * q:2 * q + 2, :], hfull[q, r].rearrange("k p t -> p k t")[:, :, tc:tc + 512],
                                  writes=([b] if q == 0 else [])))
            b.multi = toks

        def make_loaders():
            st = {}

            def get(sb2, key, shape, dtype):
                if key not in st:
                    t = sb2(key, shape, dtype); bb = Buf()
                    T.op("dve", lambda e: e.memset(t[:], 0.0), writes=[bb])
                    st[key] = (t, bb)
                return st[key]

            def load_m(T, sb2, mT, k0, b):
                kg = k0 // 4
                r_src, q0 = kg // 2, 2 * (kg % 2)
                A, bA = get(sb2, "mA", [128, 2, EXT], BF16)
                B, bB = get(sb2, "mB", [128, 2, EXT], BF16)
                for qq in range(2):
                    src = mfull[q0 + qq, r_src].rearrange("(k p) t -> p k t", p=128)
                    T.dma("sp", A[:, :, 1:EXT], src[:, :, 0:TOK + 1], writes=[bA])
                    T.dma("sp", B[:, :, 0:EXT - 1], src[:, :, TOK - 1:SEQ], writes=[bB])
                    T.op("dve", lambda e: e.tensor_scalar(out=A[:], in0=A[:], scalar1=sels[:, 0:1], scalar2=None, op0=ALU.mult),
                         reads=[b_sel], writes=[bA])
                    T.op("dve", lambda e: e.scalar_tensor_tensor(out=mT[:, k0 + 2 * qq:k0 + 2 * qq + 2, :], in0=B[:],
                                                                 scalar=sels[:, 1:2], in1=A[:], op0=ALU.mult, op1=ALU.add),
                         reads=[bA, bB, b_sel], writes=[b])

            def load_h(T, sb2, hb, n, b):
                q, k2 = n // 2, n % 2
                T.dma("sp", hb[:, 1:TOK + 1], hown[n * 128:(n + 1) * 128, :], writes=[b])
                T.dma("sp", hb[:, 0:1], hfull[q, 0, k2][:, TOK - 1:TOK], writes=[b], slow=True)
                T.dma("sp", hb[:, EXT - 1:EXT], hfull[q, 1, k2][:, 0:1], writes=[b], slow=True)
                T.op("dve", lambda e: e.tensor_scalar(out=hb[:, 0:1], in0=hb[:, 0:1], scalar1=sels[:, 1:2], scalar2=None,
                                                      op0=ALU.mult), reads=[b_sel], writes=[b])
                T.op("dve", lambda e: e.tensor_scalar(out=hb[:, EXT - 1:EXT], in0=hb[:, EXT - 1:EXT], scalar1=sels[:, 0:1],
                                                      scalar2=None, op0=ALU.mult), reads=[b_sel], writes=[b])
            return load_m, load_h

        def gather_h(q, extra=()):
            T.collective(hown[256 * q:256 * (q + 1), :].opt(), hfull[q].rearrange("r k p t -> (r k p) t").opt(), PAIRS,
                         extra=extra)

        def gather_m(q, extra=()):
            T.collective(mcore[256 * q:256 * (q + 1), :].opt(), mfull[q].rearrange("r p t -> (r p) t").opt(), PAIRS,
                         extra=extra)

        T.barrier()
        for q in range(8):
            gather_h(q)
        for l in range(4):
            W = L[l]
            T.barrier(); T.new_epoch()
            with contextlib.ExitStack() as st_l:
                if l % 2 == 0:
                    emit_even(nc, T, st_l, ps, hload, W["g1"], W["w_e"], W["gua_f"], W["gua_b"], W["glag"],
                              cs, cosm, nsinm, mf, mb, mcore, scr_e, on_rows=gather_m)
                else:
                    emit_odd(nc, T, st_l, ps, hload, W["g1"], W["w_o"], W["qg"], W["kg"], rb, onehot, cmult, mcore, scr_o,
                             on_rows=gather_m)
            T.barrier(); T.new_epoch()
            store_toks = {}

            def on_store(n, c, nchunk, tk):
                store_toks.setdefault(n // 2, []).append(tk)
                if c == nchunk - 1 and n % 2 == 1:
                    gather_h(n // 2, store_toks[n // 2])

            with contextlib.ExitStack() as st_l:
                load_m, load_h = make_loaders()
                emit_post(nc, T, st_l, ps, load_m, load_h, W["w_out"], W["g2"], W["w_gate"], W["w_up"], W["cw"], W["cb"],
                          W["w_down"], outT if l == 3 else hown, hmid_scr, hm_scr, on_store=(on_store if l < 3 else None))
        T.finish()
    return nc, T


def wout_perm(layer):
    perm = np.zeros(D, np.int64)
    for r in range(2):
        for row in range(1024):
            if layer % 2 == 0:
                ch = 512 * r + row if row < 512 else 1024 + 512 * r + (row - 512)
            else:
                ch = 1024 * r + row
            perm[r * 1024 + row] = ch
    return perm


def kernel(x, mix_norm_g, w_in_even, gate_up_fwd, gate_bias_fwd, gate_up_bwd, gate_bias_bwd, gla_norm_g, w_out_even,
                 w_qkv_odd, q_norm_g, k_norm_g, rel_bias, w_out_odd, ffn_norm_g, w_gate, w_up, conv_w, conv_b, w_down):
    f32 = lambda a: np.ascontiguousarray(np.asarray(a, dtype=np.float32))
    x = f32(x)
    if "fused" not in _PROGS:
        _PROGS["fused"] = build_fused()[0]
    common = {}
    common.update(even_consts()); common.update(odd_consts())
    for l in range(4):
        i = l // 2
        w_out = f32(w_out_even[i]) if l % 2 == 0 else f32(w_out_odd[i])
        common["w_out_%d" % l] = np.ascontiguousarray(w_out[wout_perm(l), :])
        common["g1_%d" % l] = lay128(f32(mix_norm_g[l])); common["g2_%d" % l] = lay128(f32(ffn_norm_g[l]))
        common["w_gate_%d" % l] = f32(w_gate[l]); common["w_up_%d" % l] = f32(w_up[l]); common["w_down_%d" % l] = f32(w_down[l])
        common["cw_%d" % l] = np.ascontiguousarray(f32(conv_w[l]).T.reshape(FT, 128, 3).transpose(1, 0, 2).reshape(128, FT * 3))
        common["cb_%d" % l] = lay128(f32(conv_b[l]))
    ims = []
    for b in range(BATCH):
        for r in range(2):
            im = dict(common)
            im["xT"] = np.ascontiguousarray(x[b, r * TOK:(r + 1) * TOK, :].T)
            s = np.zeros((128, 2), np.float32); s[:, r] = 1.0
            im["sel"] = s
            im["rb"] = np.ascontiguousarray(f32(rel_bias)[:, 8 * r:8 * r + 8])
            for l in range(4):
                i = l // 2
                if l % 2 == 0:
                    ew = even_weights(r, f32(w_in_even[i]), f32(gate_up_fwd[i]), f32(gate_bias_fwd[i]), f32(gate_up_bwd[i]),
                                      f32(gate_bias_bwd[i]), f32(gla_norm_g[i]))
                    for k_, v_ in ew.items():
                        im["%s_%d" % (k_, l)] = v_
                else:
                    ow = odd_weights(r, f32(w_qkv_odd[i]), f32(q_norm_g[i]), f32(k_norm_g[i]), f32(rel_bias))
                    for k_ in ("w_o", "qg", "kg"):
                        im["%s_%d" % (k_, l)] = ow[k_]
            ims.append(im)
    res = run_bass_kernel_spmd(_PROGS["fused"], ims, core_ids=list(range(NCORES))).results
    out = np.empty((BATCH, SEQ, D), np.float32)
    for b in range(BATCH):
        for r in range(2):
            out[b, r * TOK:(r + 1) * TOK, :] = np.asarray(res[2 * b + r]["outT"]).T
    return out
```

```python
import contextlib
import numpy as np
import ml_dtypes
import concourse.bass as bass
import concourse.mybir as mybir
from concourse.bass_utils import run_bass_kernel_spmd

F32 = mybir.dt.float32
BF16 = mybir.dt.bfloat16
AF = mybir.ActivationFunctionType
ALU = mybir.AluOpType
NPBF = ml_dtypes.bfloat16

D = 2048
KT = D // 128
SEQ = 4096
BATCH = 4
TOK = 2048
EXT = TOK + 2
FF = 5632
FT = FF // 128
EPS = 1e-6
NCORES = 8


DBG = {}
RELAX = set()


class Buf:
    __slots__ = ("name", "w", "r", "multi")

    def __init__(self, name=""):
        self.name = name
        self.w = None
        self.r = []
        self.multi = None


class Trk:
    NDS = 20

    def __init__(self, nc, stack):
        self.nc = nc
        self.E = {"pe": nc.tensor, "act": nc.scalar, "dve": nc.vector, "pool": nc.gpsimd, "sp": nc.sync}
        self.sem = {k: stack.enter_context(nc.semaphore("s_" + k)) for k in self.E}
        self.cnt = {k: 0 for k in self.E}
        self.waited = {}
        self.dsem = {q: [stack.enter_context(nc.semaphore("d_%s%d" % (q, i))) for i in range(self.NDS)]
                     for q in ("sp", "pool")}
        self.dtot = {q: [0] * self.NDS for q in ("sp", "pool")}
        self.dnext = {"sp": 0, "pool": 0}
        self.n_instr = 0
        self.stack = stack
        self.ccsem = stack.enter_context(nc.semaphore("s_cc"))
        self.cccnt = 0
        self.epoch = 0

    def new_epoch(self):
        self.epoch += 1
        self.sem = {k: self.stack.enter_context(self.nc.semaphore("s%d_%s" % (self.epoch, k))) for k in self.E}
        self.cnt = {k: 0 for k in self.E}

    def collective(self, in_ap, out_ap, groups, reads=(), writes=(), extra=()):
        if DBG.get("no_cc"):
            return None
        self._sync("pool", reads, writes)
        for t in extra:
            self._wait("pool", t)
        if self.cccnt > 0:
            self._wait("pool", (self.ccsem, self.cccnt, "cc"))
        self.nc.gpsimd.collective_compute("AllGather", ALU.bypass, replica_groups=groups, ins=[in_ap], outs=[out_ap]
                                          ).then_inc(self.ccsem)
        self.cccnt += 1
        tok = (self.ccsem, self.cccnt, "cc")
        self._update(tok, reads, writes)
        return tok

    def _wait(self, eng, tok):
        sem, val, src = tok
        if src == eng and (eng == "pe" or eng in RELAX):
            return
        key = (eng, id(sem))
        if self.waited.get(key, 0) >= val:
            return
        self.waited[key] = val
        self.E[eng].wait_ge(sem, val)

    def _sync(self, eng, reads, writes):
        for b in reads:
            if b.w is not None:
                self._wait(eng, b.w)
            if b.multi:
                for t in b.multi:
                    self._wait(eng, t)
        for b in writes:
            if b.w is not None:
                self._wait(eng, b.w)
            if b.multi:
                for t in b.multi:
                    self._wait(eng, t)
            for t in b.r:
                self._wait(eng, t)

    def _update(self, tok, reads, writes):
        for b in reads:
            b.r.append(tok)
        for b in writes:
            b.w = tok
            b.r = []
            b.multi = None

    def op(self, eng, fn, reads=(), writes=(), inc=True):
        self._sync(eng, reads, writes)
        ins = fn(self.E[eng])
        self.n_instr += 1
        if inc:
            self.cnt[eng] += 1
            ins.then_inc(self.sem[eng], 1)
            tok = (self.sem[eng], self.cnt[eng], eng)
        else:
            tok = (self.sem[eng], self.cnt[eng] + 1, eng)
        self._update(tok, reads, writes)
        return tok

    def dma(self, q, out, in_, reads=(), writes=(), slow=False):
        self._sync(q, reads, writes)
        i = self.dnext[q]
        self.dnext[q] = (i + 1) % self.NDS
        sem = self.dsem[q][i]
        if self.dtot[q][i] > 0:
            self._wait(q, (sem, self.dtot[q][i], "dma"))
        if slow:
            self.E[q].dma_start(out=out, in_=in_, allow_slow_non_contiguous=True).then_inc(sem, 16)
        else:
            self.E[q].dma_start(out=out, in_=in_).then_inc(sem, 16)
        self.n_instr += 1
        self.dtot[q][i] += 16
        tok = (sem, self.dtot[q][i], "dma")
        self._update(tok, reads, writes)
        return tok

    def barrier(self):
        toks = []
        for q in ("sp", "pool"):
            for i in range(self.NDS):
                if self.dtot[q][i] > 0:
                    toks.append((self.dsem[q][i], self.dtot[q][i], "dma"))
        for e in ("pe", "act", "dve"):
            if self.cnt[e] > 0:
                toks.append((self.sem[e], self.cnt[e], e))
        if self.cccnt > 0:
            toks.append((self.ccsem, self.cccnt, "cc"))
        for eng in self.E:
            for t in toks:
                if t[2] == eng:
                    continue
                self._wait(eng, t)

    def finish(self):
        for q in ("sp", "pool"):
            for i in range(self.NDS):
                if self.dtot[q][i] > 0:
                    self._wait("sp", (self.dsem[q][i], self.dtot[q][i], "dma"))
        if self.cccnt > 0:
            self._wait("sp", (self.ccsem, self.cccnt, "cc"))
        for e in ("pe", "act", "dve"):
            if self.cnt[e] > 0:
                self._wait("sp", (self.sem[e], self.cnt[e], e))


_UNIQ = [0]


def uniq(name):
    _UNIQ[0] += 1
    return "%s_%d" % (name, _UNIQ[0])


def col_chunks(n, step=512):
    return [(c, min(c + step, n)) for c in range(0, n, step)]


def emit_post(nc, T, stack, ps, load_m, load_h, w_out, g2, w_gate, w_up, cw, cb, w_down, hT_out, hmid_scr, hm_scr,
              on_store=None):
    sb = lambda name, shape, dt: stack.enter_context(nc.sbuf_tensor(uniq(name), shape, dt))
    psb = [Buf("ps%d" % i) for i in range(8)]

    ones = sb("ones", [128, 128], F32)
    b_ones = Buf()
    T.op("dve", lambda e: e.memset(ones[:], 1.0), writes=[b_ones])
    g2s = sb("g2s", [128, KT], F32); b_g2 = Buf()
    T.dma("sp", g2s[:], g2, writes=[b_g2])
    cws = sb("cws", [128, FT * 3], F32); b_cw = Buf()
    T.dma("sp", cws[:], cw, writes=[b_cw])
    cbs = sb("cbs", [128, FT], F32); b_cb = Buf()
    T.dma("sp", cbs[:], cb, writes=[b_cb])
    rstd = sb("rstd", [128, EXT], F32); b_rstd = Buf()
    xn = sb("xn", [128, KT, EXT], BF16); b_xn = [Buf() for _ in range(KT)]

    chunks = col_chunks(EXT)
    hm_b = [Buf() for _ in range(KT)]

    with contextlib.ExitStack() as st2:
        sb2 = lambda name, shape, dt: st2.enter_context(nc.sbuf_tensor(uniq(name), shape, dt))
        mT = sb2("mT", [128, KT, EXT], BF16); b_mT = [Buf() for _ in range(KT // 4)]
        for k0 in range(0, KT, 4):
            load_m(T, sb2, mT, k0, b_mT[k0 // 4])
        wo = [sb2("wo%d" % i, [128, KT, 128], BF16) for i in range(2)]; b_wo = [Buf(), Buf()]
        hb = [sb2("hb%d" % i, [128, EXT], F32) for i in range(2)]; b_hb = [Buf(), Buf()]
        hm = [sb2("hm%d" % i, [128, EXT], F32) for i in range(2)]; b_hm = [Buf(), Buf()]
        sq = [sb2("sq%d" % i, [128, 512], F32) for i in range(2)]; b_sq = [Buf(), Buf()]
        wv = w_out.rearrange("(k p) n -> p k n", p=128)

        def load(n):
            T.dma("pool", wo[n % 2][:], wv[:, :, n * 128:(n + 1) * 128], writes=[b_wo[n % 2]])
            load_h(T, sb2, hb[n % 2], n, b_hb[n % 2])

        load(0)
        pend = None
        it = 0
        for n in range(KT):
            if n + 1 < KT:
                load(n + 1)
            for j, (c0, c1) in enumerate(chunks):
                w = c1 - c0
                pi = it % 2
                for k in range(KT):
                    T.op("pe", lambda e, k=k: e.matmul(ps[pi][:, :w], wo[n % 2][:, k, :], mT[:, k, c0:c1],
                                                       start=(k == 0), stop=(k == KT - 1)),
                         reads=[b_wo[n % 2], b_mT[k // 4]], writes=[psb[pi]], inc=(k == KT - 1))
                T.op("dve", lambda e: e.tensor_tensor(out=hm[n % 2][:, c0:c1], in0=ps[pi][:, :w],
                                                      in1=hb[n % 2][:, c0:c1], op=ALU.add),
                     reads=[psb[pi], b_hb[n % 2]], writes=[b_hm[n % 2]])
                T.op("act", lambda e: e.activation(out=sq[pi][:, :w], in_=hm[n % 2][:, c0:c1], func=AF.Square),
                     reads=[b_hm[n % 2]], writes=[b_sq[pi]])
                if pend is not None:
                    pend()

                def mk(n=n, j=j, pi=pi, w=w):
                    T.op("pe", lambda e: e.matmul(ps[2 + j][:, :w], ones[:], sq[pi][:, :w],
                                                  start=(n == 0), stop=(n == KT - 1)),
                         reads=[b_ones, b_sq[pi]], writes=[psb[2 + j]])
                pend = mk
                it += 1
            T.dma("sp", hm_scr[n * 128:(n + 1) * 128, :], hm[n % 2][:], reads=[b_hm[n % 2]], writes=[hm_b[n]])
        pend()
        for j, (c0, c1) in enumerate(chunks):
            w = c1 - c0
            T.op("act", lambda e: e.activation(out=rstd[:, c0:c1], in_=ps[2 + j][:, :w], func=AF.Sqrt,
                                               scale=1.0 / D, bias=EPS),
                 reads=[psb[2 + j]], writes=[b_rstd])
        T.op("dve", lambda e: e.reciprocal(out=rstd[:], in_=rstd[:]), reads=[b_rstd], writes=[b_rstd])

        for n in range(KT):
            T.dma("sp", hb[n % 2][:], hm_scr[n * 128:(n + 1) * 128, :], reads=[hm_b[n]], writes=[b_hb[n % 2]])
            T.op("dve", lambda e: e.scalar_tensor_tensor(out=xn[:, n, :], in0=hb[n % 2][:], scalar=g2s[:, n:n + 1],
                                                         in1=rstd[:], op0=ALU.mult, op1=ALU.mult),
                 reads=[b_hb[n % 2], b_g2, b_rstd], writes=[b_xn[n]])

    T.barrier()
    hmid_b = [Buf() for _ in range(FT)]
    with contextlib.ExitStack() as st2:
        sb2 = lambda name, shape, dt: st2.enter_context(nc.sbuf_tensor(uniq(name), shape, dt))
        wg = [sb2("wg%d" % i, [128, KT, 128], BF16) for i in range(2)]; b_wg = [Buf(), Buf()]
        wu = [sb2("wu%d" % i, [128, KT, 128], BF16) for i in range(2)]; b_wu = [Buf(), Buf()]
        ge = [sb2("ge%d" % i, [128, EXT], F32) for i in range(2)]; b_ge = [Buf(), Buf()]
        u = [sb2("u%d" % i, [128, TOK], F32) for i in range(2)]; b_u = [Buf(), Buf()]
        gl = [sb2("gl%d" % i, [128, TOK], F32) for i in range(2)]; b_gl = [Buf(), Buf()]
        hf = [sb2("hf%d" % i, [128, TOK], BF16) for i in range(2)]; b_hf = [Buf(), Buf()]
        wgv = w_gate.rearrange("(k p) n -> p k n", p=128)
        wuv = w_up.rearrange("(k p) n -> p k n", p=128)

        def loadg(f):
            T.dma("pool", wg[f % 2][:], wgv[:, :, f * 128:(f + 1) * 128], writes=[b_wg[f % 2]])
            T.dma("pool", wu[f % 2][:], wuv[:, :, f * 128:(f + 1) * 128], writes=[b_wu[f % 2]])

        loadg(0)
        it = 0
        for f in range(FT):
            if f + 1 < FT:
                loadg(f + 1)
            s = f % 2
            for j, (c0, c1) in enumerate(chunks):
                w = c1 - c0
                pi = it % 4; it += 1
                for k in range(KT):
                    T.op("pe", lambda e, k=k: e.matmul(ps[pi][:, :w], wg[s][:, k, :], xn[:, k, c0:c1],
                                                       start=(k == 0), stop=(k == KT - 1)),
                         reads=[b_wg[s], b_xn[k]], writes=[psb[pi]], inc=(k == KT - 1))
                T.op("act", lambda e: e.activation(out=ge[s][:, c0:c1], in_=ps[pi][:, :w], func=AF.Copy),
                     reads=[psb[pi]], writes=[b_ge[s]])
            T.op("dve", lambda e: e.tensor_scalar(out=u[s][:], in0=ge[s][:, 1:1 + TOK],
                                                  scalar1=cws[:, 3 * f + 1:3 * f + 2], scalar2=cbs[:, f:f + 1],
                                                  op0=ALU.mult, op1=ALU.add),
                 reads=[b_ge[s], b_cw, b_cb], writes=[b_u[s]])
            T.op("dve", lambda e: e.scalar_tensor_tensor(out=u[s][:], in0=ge[s][:, 0:TOK],
                                                         scalar=cws[:, 3 * f:3 * f + 1], in1=u[s][:],
                                                         op0=ALU.mult, op1=ALU.add),
                 reads=[b_ge[s], b_cw, b_u[s]], writes=[b_u[s]])
            T.op("dve", lambda e: e.scalar_tensor_tensor(out=u[s][:], in0=ge[s][:, 2:2 + TOK],
                                                         scalar=cws[:, 3 * f + 2:3 * f + 3], in1=u[s][:],
                                                         op0=ALU.mult, op1=ALU.add),
                 reads=[b_ge[s], b_cw, b_u[s]], writes=[b_u[s]])
            T.op("act", lambda e: e.activation(out=gl[s][:], in_=u[s][:], func=AF.Gelu_apprx_tanh),
                 reads=[b_u[s]], writes=[b_gl[s]])
            for j in range(TOK // 512):
                c0 = 1 + 512 * j
                pi = 4 + (it % 4); it += 1
                for k in range(KT):
                    T.op("pe", lambda e, k=k: e.matmul(ps[pi][:, :], wu[s][:, k, :], xn[:, k, c0:c0 + 512],
                                                       start=(k == 0), stop=(k == KT - 1)),
                         reads=[b_wu[s], b_xn[k]], writes=[psb[pi]], inc=(k == KT - 1))
                T.op("dve", lambda e: e.tensor_tensor(out=hf[s][:, 512 * j:512 * (j + 1)], in0=ps[pi][:, :],
                                                      in1=gl[s][:, 512 * j:512 * (j + 1)], op=ALU.mult),
                     reads=[psb[pi], b_gl[s]], writes=[b_hf[s]])
            T.dma("sp", hmid_scr[f * 128:(f + 1) * 128, :], hf[s][:], reads=[b_hf[s]], writes=[hmid_b[f]])

    T.barrier()
    with contextlib.ExitStack() as st2:
        sb2 = lambda name, shape, dt: st2.enter_context(nc.sbuf_tensor(uniq(name), shape, dt))
        CH = 1024
        hms = sb2("hms", [128, FT, CH], BF16); b_hms = [Buf() for _ in range(4)]
        wd = [sb2("wd%d" % i, [128, FT, 128], BF16) for i in range(2)]; b_wd = [Buf(), Buf()]
        hr = [sb2("hr%d" % i, [128, CH], F32) for i in range(2)]; b_hr = [Buf(), Buf()]
        ot = [sb2("ot%d" % i, [128, CH], F32) for i in range(2)]; b_ot = [Buf(), Buf()]
        hv = hmid_scr.rearrange("(f p) t -> p f t", p=128)
        wdv = w_down.rearrange("(f p) n -> p f n", p=128)
        it = 0
        items = [(c, n) for c in range(TOK // CH) for n in range(KT)]

        def loadd(idx):
            c, n = items[idx]
            T.dma("pool", wd[idx % 2][:], wdv[:, :, n * 128:(n + 1) * 128], writes=[b_wd[idx % 2]])
            T.dma("sp", hr[idx % 2][:], hm_scr[n * 128:(n + 1) * 128, 1 + c * CH:1 + (c + 1) * CH],
                  reads=[hm_b[n]], writes=[b_hr[idx % 2]])

        loadd(0)
        for idx, (c, n) in enumerate(items):
            if n == 0:
                for f0 in range(0, FT, 11):
                    T.dma("sp", hms[:, f0:f0 + 11, :], hv[:, f0:f0 + 11, c * CH:(c + 1) * CH],
                          reads=hmid_b[f0:f0 + 11], writes=[b_hms[f0 // 11]])
            if idx + 1 < len(items):
                loadd(idx + 1)
            s = idx % 2
            for jj in range(CH // 512):
                pi = it % 4; it += 1
                for f in range(FT):
                    T.op("pe", lambda e, f=f: e.matmul(ps[pi][:, :], wd[s][:, f, :], hms[:, f, 512 * jj:512 * (jj + 1)],
                                                       start=(f == 0), stop=(f == FT - 1)),
                         reads=[b_wd[s], b_hms[f // 11]], writes=[psb[pi]], inc=(f == FT - 1))
                T.op("dve", lambda e: e.tensor_tensor(out=ot[s][:, 512 * jj:512 * (jj + 1)], in0=ps[pi][:, :],
                                                      in1=hr[s][:, 512 * jj:512 * (jj + 1)], op=ALU.add),
                     reads=[psb[pi], b_hr[s]], writes=[b_ot[s]])
            tk = T.dma("sp", hT_out[n * 128:(n + 1) * 128, c * CH:(c + 1) * CH], ot[s][:], reads=[b_ot[s]])
            if on_store is not None:
                on_store(n, c, TOK // CH, tk)


def build_post(debug=False):
    nc = bass.Bass("TRN2", target_bir_lowering=False)
    dt = lambda name, shape, dtype, kind: nc.dram_tensor(name, shape, dtype, kind=kind).ap()
    hT_ext = dt("hT_ext", [D, EXT], F32, "ExternalInput")
    mT_ext = dt("mT_ext", [D, EXT], BF16, "ExternalInput")
    w_out = dt("w_out", [D, D], F32, "ExternalInput")
    g2 = dt("g2", [128, KT], F32, "ExternalInput")
    w_gate = dt("w_gate", [D, FF], F32, "ExternalInput")
    w_up = dt("w_up", [D, FF], F32, "ExternalInput")
    cw = dt("cw", [128, FT * 3], F32, "ExternalInput")
    cb = dt("cb", [128, FT], F32, "ExternalInput")
    w_down = dt("w_down", [FF, D], F32, "ExternalInput")
    hT_out = dt("hT_out", [D, TOK], F32, "ExternalOutput")
    hmid_scr = dt("hmid_scr", [FF, TOK], BF16, "ExternalOutput" if debug else "Internal")
    hm_scr = dt("hm_scr", [D, EXT], F32, "ExternalOutput" if debug else "Internal")
    with contextlib.ExitStack() as stack:
        T = Trk(nc, stack)
        ps = [stack.enter_context(nc.psum_tensor("ps%d" % i, [128, 512], F32)) for i in range(8)]
        mv = mT_ext.rearrange("(k p) t -> p k t", p=128)
        load_m = lambda T, sb2, mT, k0, b: T.dma("sp", mT[:, k0:k0 + 4, :], mv[:, k0:k0 + 4, :], writes=[b])
        load_h = lambda T, sb2, hb, n, b: T.dma("sp", hb[:], hT_ext[n * 128:(n + 1) * 128, :], writes=[b])
        emit_post(nc, T, stack, ps, load_m, load_h, w_out, g2, w_gate, w_up, cw, cb, w_down, hT_out, hmid_scr, hm_scr)
        T.finish()
    return nc


def lay128(v):
    v = np.asarray(v)
    return np.ascontiguousarray(v.reshape(-1, 128).T)


class Rot:
    def __init__(self, tiles):
        self.tiles = tiles
        self.bufs = [Buf() for _ in tiles]
        self.i = 0

    def next(self):
        i = self.i
        self.i = (i + 1) % len(self.tiles)
        return self.tiles[i], self.bufs[i]


def mm_group(T, out_ap, pairs, pbuf, reads):
    n = len(pairs)
    for i, (l, r) in enumerate(pairs):
        T.op("pe", lambda e: e.matmul(out_ap, l, r, start=(i == 0), stop=(i == n - 1)),
             reads=reads, writes=[pbuf], inc=(i == n - 1))


def emit_norm_chunk(nc, T, hc, b_hc, g1s, b_g1, ones, b_ones, PS, sqr, rstd_t, b_rstd, xc, b_xc, width):
    pt, pb = PS.next()
    for k in range(KT):
        sq, b_sq = sqr.next()
        T.op("act", lambda e: e.activation(out=sq[:, :width], in_=hc[:, k, :width], func=AF.Square),
             reads=[b_hc], writes=[b_sq])
        T.op("pe", lambda e: e.matmul(pt[:, :width], ones[:], sq[:, :width], start=(k == 0), stop=(k == KT - 1)),
             reads=[b_ones, b_sq], writes=[pb])
    T.op("act", lambda e: e.activation(out=rstd_t[:, :width], in_=pt[:, :width], func=AF.Sqrt, scale=1.0 / D, bias=EPS),
         reads=[pb], writes=[b_rstd])
    T.op("dve", lambda e: e.reciprocal(out=rstd_t[:, :width], in_=rstd_t[:, :width]), reads=[b_rstd], writes=[b_rstd])
    for k in range(KT):
        T.op("dve", lambda e: e.scalar_tensor_tensor(out=xc[:, k, :width], in0=hc[:, k, :width], scalar=g1s[:, k:k + 1],
                                                     in1=rstd_t[:, :width], op0=ALU.mult, op1=ALU.mult),
             reads=[b_hc, b_g1, b_rstd], writes=[b_xc])


NW_E = 2080
GLA_SCALE = 128 ** -0.5


def emit_even(nc, T, stack, ps, hload, g1, w_e, gua_f, gua_b, glag, cs, cosm, nsinm, mf, mb, mixT, scr, on_rows=None):
    sb = lambda name, shape, dt: stack.enter_context(nc.sbuf_tensor(uniq(name), shape, dt))
    PS = Rot(ps)
    aT_s, qT_s, kT_s, k_s, v_s, sgT_s, la_s = scr
    NCH = SEQ // 512

    ones = sb("ones", [128, 128], F32); b_ones = Buf()
    T.op("dve", lambda e: e.memset(ones[:], 1.0), writes=[b_ones])
    g1s = sb("g1s", [128, KT], F32); b_g1 = Buf()
    T.dma("sp", g1s[:], g1, writes=[b_g1])
    glags = sb("glags", [128, 2], F32); b_glag = Buf()
    T.dma("sp", glags[:], glag, writes=[b_glag])
    mfs = sb("mfs", [128, 128], BF16); b_mf = Buf()
    T.dma("sp", mfs[:], mf, writes=[b_mf])
    mbs = sb("mbs", [128, 128], BF16); b_mb = Buf()
    T.dma("sp", mbs[:], mb, writes=[b_mb])

    with contextlib.ExitStack() as st2:
        sb2 = lambda name, shape, dt: st2.enter_context(nc.sbuf_tensor(uniq(name), shape, dt))
        we = sb2("we", [128, KT, NW_E], BF16); b_we = [Buf() for _ in range(4)]
        wv = w_e.rearrange("(k p) n -> p k n", p=128)
        for k0 in range(0, KT, 4):
            T.dma("pool", we[:, k0:k0 + 4, :], wv[:, k0:k0 + 4, :], writes=[b_we[k0 // 4]])
        guaf = sb2("guaf", [17, 256], BF16); b_guaf = Buf()
        T.dma("pool", guaf[:], gua_f, writes=[b_guaf])
        guab = sb2("guab", [17, 256], BF16); b_guab = Buf()
        T.dma("pool", guab[:], gua_b, writes=[b_guab])
        zfa = sb2("zfa", [17, 512], BF16); b_zfa = Buf()
        zba = sb2("zba", [17, 512], BF16); b_zba = Buf()
        T.op("dve", lambda e: e.memset(zfa[:], 1.0), writes=[b_zfa])
        T.op("dve", lambda e: e.memset(zba[:], 1.0), writes=[b_zba])
        hcs = [sb2("hc%d" % i, [128, KT, 512], F32) for i in range(2)]; b_hcs = [Buf(), Buf()]
        xcs_e = [sb2("xc%d" % i, [128, KT, 512], BF16) for i in range(2)]; b_xcs_e = [Buf(), Buf()]
        sqr = Rot([sb2("sq%d" % i, [128, 512], F32) for i in range(2)])
        rstd_t = sb2("rstd", [128, 512], F32); b_rstd = Buf()
        stg = Rot([sb2("stg%d" % i, [128, 512], BF16) for i in range(4)])
        lt = Rot([sb2("lt%d" % i, [128, 256], F32) for i in range(2)])
        hload(T, hcs[0], 0, b_hcs[0])
        hload(T, hcs[1], 1, b_hcs[1])
        we_reads = list(b_we)
        emit_norm_chunk(nc, T, hcs[0], b_hcs[0], g1s, b_g1, ones, b_ones, PS, sqr, rstd_t, b_rstd, xcs_e[0], b_xcs_e[0], 512)
        for c in range(NCH):
            t0 = c * 512
            if c + 2 < NCH:
                hload(T, hcs[c % 2], c + 2, b_hcs[c % 2])
            if c + 1 < NCH:
                emit_norm_chunk(nc, T, hcs[(c + 1) % 2], b_hcs[(c + 1) % 2], g1s, b_g1, ones, b_ones, PS, sqr, rstd_t, b_rstd,
                                xcs_e[(c + 1) % 2], b_xcs_e[(c + 1) % 2], 512)
            xc, b_xc = xcs_e[c % 2], b_xcs_e[c % 2]

            def fm(col0, dst, row0, func=AF.Copy, scale=1.0):
                pt, pb = PS.next()
                mm_group(T, pt[:, :], [(we[:, k, col0:col0 + 128], xc[:, k, :]) for k in range(KT)], pb,
                         reads=we_reads + [b_xc])
                s, b_s = stg.next()
                T.op("act", lambda e: e.activation(out=s[:, :], in_=pt[:, :], func=func, scale=scale),
                     reads=[pb], writes=[b_s])
                T.dma("sp", dst[row0:row0 + 128, t0:t0 + 512], s[:, :], reads=[b_s])
            for i in range(4):
                fm(i * 128, aT_s, i * 128)
            for i in range(2):
                fm(512 + i * 128, qT_s, i * 128, scale=GLA_SCALE)
            for i in range(2):
                fm(768 + i * 128, kT_s, i * 128)
            for i in range(4):
                fm(1536 + i * 128, sgT_s, i * 128, func=AF.Silu)
            for (col0, za, b_za) in ((2048, zfa, b_zfa), (2064, zba, b_zba)):
                pt, pb = PS.next()
                mm_group(T, pt[0:16, :], [(we[:, k, col0:col0 + 16], xc[:, k, :]) for k in range(KT)], pb,
                         reads=we_reads + [b_xc])
                T.op("act", lambda e: e.activation(out=za[0:16, :], in_=pt[0:16, :], func=AF.Copy), reads=[pb], writes=[b_za])
            for ts in range(4):
                r0 = t0 + ts * 128
                for (col0, ncol, dst) in ((768, 256, k_s), (1024, 512, v_s)):
                    pt, pb = PS.next()
                    mm_group(T, pt[:, :ncol], [(xc[:, k, ts * 128:(ts + 1) * 128], we[:, k, col0:col0 + ncol])
                                               for k in range(KT)], pb, reads=we_reads + [b_xc])
                    s, b_s = stg.next()
                    T.op("dve", lambda e: e.tensor_copy(out=s[:, :ncol], in_=pt[:, :ncol]), reads=[pb], writes=[b_s])
                    T.dma("sp", dst[r0:r0 + 128, :], s[:, :ncol], reads=[b_s])
                for d, (za, b_za, gu, b_gu) in enumerate(((zfa, b_zfa, guaf, b_guaf), (zba, b_zba, guab, b_guab))):
                    pt, pb = PS.next()
                    T.op("pe", lambda e: e.matmul(pt[:, :256], za[0:17, ts * 128:(ts + 1) * 128], gu[0:17, :],
                                                  start=True, stop=True), reads=[b_za, b_gu], writes=[pb])
                    l, b_l = lt.next()
                    T.op("act", lambda e: e.activation(out=l[:, :], in_=pt[:, :256], func=AF.Exp, scale=-1.0),
                         reads=[pb], writes=[b_l])
                    T.op("act", lambda e: e.activation(out=l[:, :], in_=l[:, :], func=AF.Ln, bias=1.0),
                         reads=[b_l], writes=[b_l])
                    s, b_s = stg.next()
                    T.op("dve", lambda e: e.tensor_scalar(out=s[:, :256], in0=l[:, :], scalar1=-1.0 / 16.0, scalar2=None,
                                                          op0=ALU.mult), reads=[b_l], writes=[b_s])
                    T.dma("sp", la_s[d, r0:r0 + 128, :], s[:, :256], reads=[b_s])
    T.barrier()

    with contextlib.ExitStack() as st2:
        sb2 = lambda name, shape, dt: st2.enter_context(nc.sbuf_tensor(uniq(name), shape, dt))
        aT = sb2("aT", [128, 4, SEQ], BF16); b_aT = Buf()
        T.dma("sp", aT[:], aT_s.rearrange("(k p) t -> p k t", p=128), writes=[b_aT])
        css = sb2("css", [128, 2, 512], BF16); b_cs = Buf()
        T.dma("sp", css[:], cs.rearrange("(k p) n -> p k n", p=128), writes=[b_cs])
        Z = sb2("Z", [128, 32, 2, 512], BF16); b_Z = Buf()
        for g in range(2):
            for st in range(32):
                pt, pb = PS.next()
                mm_group(T, pt[:, :], [(aT[:, 2 * g + kk, st * 128:(st + 1) * 128], css[:, kk, :]) for kk in range(2)],
                         pb, reads=[b_aT, b_cs])
                if st % 2 == 0:
                    T.op("act", lambda e: e.activation(out=Z[:, st, g, :], in_=pt[:, :], func=AF.Copy),
                         reads=[pb], writes=[b_Z])
                else:
                    T.op("dve", lambda e: e.tensor_copy(out=Z[:, st, g, :], in_=pt[:, :]), reads=[pb], writes=[b_Z])
        SG = 4
        if DBG.get("skip_fnet2"):
            SEQ_ = 0
        else:
            SEQ_ = SEQ
        cbuf = Rot([sb2("cb%d" % i, [128, 2, SG, 512], BF16) for i in range(3)])
        stg = Rot([sb2("stg%d" % i, [128, 512], BF16) for i in range(4)])
        cv = cosm.rearrange("(st p) n -> p st n", p=128)
        sv = nsinm.rearrange("(st p) n -> p st n", p=128)
        pieces = [(sp_, sg) for sp_ in range(SEQ_ // 512) for sg in range(32 // SG)]

        def loadc(idx):
            sp_, sg = pieces[idx]
            t, b = cbuf.next()
            T.dma("sp", t[:, 0, :, :], cv[:, sg * SG:(sg + 1) * SG, sp_ * 512:(sp_ + 1) * 512], writes=[b])
            T.dma("sp", t[:, 1, :, :], sv[:, sg * SG:(sg + 1) * SG, sp_ * 512:(sp_ + 1) * 512], writes=[b])
            return t, b
        q = [loadc(0), loadc(1)] if pieces else []
        for idx, (sp_, sg) in enumerate(pieces):
            if idx + 2 < len(pieces):
                q.append(loadc(idx + 2))
            t, b = q.pop(0)
            bank0 = (sp_ % 2) * 4
            for ct in range(4):
                g, hf = ct // 2, ct % 2
                for s_ in range(SG):
                    stile = sg * SG + s_
                    for part in range(2):
                        first = (sg == 0 and s_ == 0 and part == 0)
                        last = (sg == 32 // SG - 1 and s_ == SG - 1 and part == 1)
                        T.op("pe", lambda e: e.matmul(ps[bank0 + ct][:, :],
                                                      Z[:, stile, g, part * 256 + hf * 128: part * 256 + hf * 128 + 128],
                                                      t[:, part, s_, :], start=first, stop=last),
                             reads=[b_Z, b], writes=[PS.bufs[bank0 + ct]], inc=last or (s_ == SG - 1 and part == 1 and ct == 3))
            if sg == 32 // SG - 1:
                for ct in range(4):
                    s, b_s = stg.next()
                    T.op("act" if ct % 2 == 0 else "dve",
                         (lambda e: e.activation(out=s[:, :], in_=ps[bank0 + ct][:, :], func=AF.Copy)) if ct % 2 == 0 else
                         (lambda e: e.tensor_copy(out=s[:, :], in_=ps[bank0 + ct][:, :])),
                         reads=[PS.bufs[bank0 + ct]], writes=[b_s])
                    T.dma("sp", mixT[ct * 128:(ct + 1) * 128, sp_ * 512:(sp_ + 1) * 512], s[:, :], reads=[b_s])
    T.barrier()
    if on_rows is not None:
        on_rows(0, []); on_rows(1, [])

    with contextlib.ExitStack() as st2:
        sb2 = lambda name, shape, dt: st2.enter_context(nc.sbuf_tensor(uniq(name), shape, dt))
        qT = sb2("qT", [128, SEQ], BF16); kT = sb2("kT", [128, SEQ], BF16)
        kk_ = sb2("ktok", [128, 32, 128], BF16); vv = sb2("vtok", [128, 32, 256], BF16)
        la = [sb2("la%d" % d, [128, 32, 128], BF16) for d in range(2)]
        sg_ = sb2("sgT", [128, 2, SEQ], BF16)
        oacc = sb2("oacc", [128, 2, SEQ], F32)
        b_in = Buf(); b_sg = Buf()
        b_o = [Buf() for _ in range(32)]
        S32 = [sb2("S32_%d" % d, [128, 256], F32) for d in range(2)]; b_S32 = [Buf(), Buf()]
        Sbf = [sb2("Sbf_%d" % d, [128, 256], BF16) for d in range(2)]; b_Sbf = [Buf(), Buf()]
        tmpS = [sb2("tmpS_%d" % d, [128, 256], F32) for d in range(2)]; b_tmpS = [Buf(), Buf()]
        mk_t = lambda nm, dt_, n=6: Rot([sb2("%s%d" % (nm, i), [128, 128], dt_) for i in range(n)])
        ebR, enbR, entR = mk_t("eb", F32, 6), mk_t("enb", F32, 4), mk_t("ent", F32, 4)
        qtR, ktR, ktokR, pR = mk_t("qt", BF16), mk_t("kt", BF16), mk_t("ktk", BF16), mk_t("pp", BF16)
        elR = Rot([sb2("el%d" % i, [128, 2], F32) for i in range(4)])
        sq2 = Rot([sb2("sqq%d" % i, [128, 512], F32) for i in range(2)])
        rs2 = sb2("rs2", [128, 512], F32); b_rs2 = Buf()
        tm2 = Rot([sb2("tm%d" % i, [128, 512], F32) for i in range(2)])
        stg = Rot([sb2("stg%d" % i, [128, 512], BF16) for i in range(2)])
        masks = ((mfs, b_mf), (mbs, b_mb))
        kmR = mk_t("km", BF16)
        NHEAD_ = 0 if DBG.get("skip_gla") else 2
        cmask = sb2("cmask", [128, 2], F32); b_cmask = Buf()
        T.op("dve", lambda e: e.memset(cmask[:], 0.0), writes=[b_cmask])
        T.op("dve", lambda e: e.memset(cmask[0:64, 0:1], 1.0), writes=[b_cmask])
        T.op("dve", lambda e: e.memset(cmask[64:128, 1:2], 1.0), writes=[b_cmask])
        for hh in range(NHEAD_):
            T.dma("sp", qT[:], qT_s[hh * 128:(hh + 1) * 128, :], writes=[b_in])
            T.dma("sp", kT[:], kT_s[hh * 128:(hh + 1) * 128, :], writes=[b_in])
            T.dma("sp", kk_[:], k_s.rearrange("(i p) c -> p i c", p=128)[:, :, hh * 128:(hh + 1) * 128], writes=[b_in])
            T.dma("sp", vv[:], v_s.rearrange("(i p) c -> p i c", p=128)[:, :, hh * 256:(hh + 1) * 256], writes=[b_in])
            for d in range(2):
                T.dma("sp", la[d][:], la_s[d].rearrange("(i p) c -> p i c", p=128)[:, :, hh * 128:(hh + 1) * 128],
                      writes=[b_in])
            T.dma("sp", sg_[:], sgT_s.rearrange("(k p) t -> p k t", p=128)[:, 2 * hh:2 * hh + 2, :], writes=[b_sg])
            for d in range(2):
                T.op("dve", lambda e: e.memset(S32[d][:], 0.0), writes=[b_S32[d]])
                T.op("dve", lambda e: e.memset(Sbf[d][:], 0.0), writes=[b_Sbf[d]])
                T.op("dve", lambda e: e.memset(tmpS[d][:], 0.0), writes=[b_tmpS[d]])
            written = [False] * 32
            def phase12(step):
                    ctx = []
                    for d in range(2):
                        i = step if d == 0 else 31 - step
                        M, b_M = masks[d]
                        c0 = i * 128
                        pbT, bbT = PS.next()
                        T.op("pe", lambda e: e.matmul(pbT[:, :128], la[d][:, i, :], M[:, :], start=True, stop=True),
                             reads=[b_in, b_M], writes=[bbT])
                        pbk, bbk = PS.next()
                        T.op("pe", lambda e: e.matmul(pbk[:, :128], M[:, :], la[d][:, i, :], start=True, stop=True),
                             reads=[b_in, b_M], writes=[bbk])
                        eb, b_eb = ebR.next(); enb, b_enb = enbR.next(); ent, b_ent = entR.next()
                        T.op("act", lambda e: e.activation(out=eb[:, :], in_=pbT[:, :128], func=AF.Exp), reads=[bbT], writes=[b_eb])
                        T.op("act", lambda e: e.activation(out=enb[:, :], in_=pbT[:, :128], func=AF.Exp, scale=-1.0),
                             reads=[bbT], writes=[b_enb])
                        T.op("act", lambda e: e.activation(out=ent[:, :], in_=pbk[:, :128], func=AF.Exp, scale=-1.0),
                             reads=[bbk], writes=[b_ent])
                        qt, b_qt = qtR.next(); kt, b_kt = ktR.next(); ktok, b_ktok = ktokR.next()
                        T.op("dve", lambda e: e.tensor_tensor(out=qt[:, :], in0=qT[:, c0:c0 + 128], in1=eb[:, :], op=ALU.mult),
                             reads=[b_in, b_eb], writes=[b_qt])
                        T.op("dve", lambda e: e.tensor_tensor(out=kt[:, :], in0=kT[:, c0:c0 + 128], in1=enb[:, :], op=ALU.mult),
                             reads=[b_in, b_enb], writes=[b_kt])
                        T.op("dve", lambda e: e.tensor_tensor(out=ktok[:, :], in0=kk_[:, i, :], in1=ent[:, :], op=ALU.mult),
                             reads=[b_in, b_ent], writes=[b_ktok])
                        ecols = (63, 127) if d == 0 else (0, 64)
                        ctx.append((i, c0, M, b_M, qt, b_qt, kt, b_kt, ktok, b_ktok, (eb, ecols), b_eb))
                    pps = []
                    for d in range(2):
                        i, c0, M, b_M, qt, b_qt, kt, b_kt, ktok, b_ktok, el, b_el = ctx[d]
                        psc, bsc = PS.next()
                        T.op("pe", lambda e: e.matmul(psc[:, :128], kt[:, :], qt[:, :], start=True, stop=True),
                             reads=[b_kt, b_qt], writes=[bsc])
                        pp, b_pp = pR.next()
                        T.op("dve", lambda e: e.tensor_tensor(out=pp[:, :], in0=psc[:, :128], in1=M[:, :], op=ALU.mult),
                             reads=[bsc, b_M], writes=[b_pp])
                        pps.append((pp, b_pp))
                    return ctx, pps

            def phase34(step, ctx, pps, nctx):
                    pos = []
                    for d in range(2):
                        i = ctx[d][0]
                        po, bo = PS.next()
                        pp, b_pp = pps[d]
                        for vt in range(2):
                            T.op("pe", lambda e: e.matmul(po[:, vt * 128:(vt + 1) * 128], vv[:, i, vt * 128:(vt + 1) * 128], pp[:, :],
                                                          start=True, stop=True),
                                 reads=[b_in, b_pp], writes=[bo], inc=False)
                        pos.append((po, bo))
                    for half in range(2):
                        for d in range(2):
                            if DBG.get("no_inter"):
                                if half == 1:
                                    po, bo = pos[d]
                                    T.op("pe", lambda e: e.matmul(po[:, 256:320], Sbf[d][:, 0:128], ctx[d][4][:, 0:64], start=True, stop=True),
                                         reads=[b_Sbf[d]], writes=[bo])
                                continue
                            i, c0, M, b_M, qt, b_qt, kt, b_kt, ktok, b_ktok, el, b_el = ctx[d]
                            po, bo = pos[d]
                            ch = half if d == 0 else 1 - half
                            r0 = ch * 64
                            for vt in range(2):
                                T.op("pe", lambda e: e.matmul(po[:, 256 + vt * 128 + r0: 256 + vt * 128 + r0 + 64],
                                                              Sbf[d][:, vt * 128:(vt + 1) * 128], qt[:, r0:r0 + 64],
                                                              start=True, stop=True),
                                     reads=[b_Sbf[d], b_qt], writes=[bo], inc=(half == 1 and vt == 1))
                            pd, bd = PS.next()
                            T.op("pe", lambda e: e.matmul(pd[:, :256], ktok[r0:r0 + 64, :], vv[r0:r0 + 64, i, :],
                                                          start=True, stop=True), reads=[b_ktok, b_in], writes=[bd])
                            ebt, ecols = el
                            esc = ebt[:, ecols[ch]:ecols[ch] + 1]
                            T.op("dve", lambda e: e.scalar_tensor_tensor(out=Sbf[d][:], in0=pd[:, :256], scalar=esc,
                                                                         in1=tmpS[d][:], op0=ALU.mult, op1=ALU.add),
                                 reads=[bd, b_el, b_tmpS[d]], writes=[b_Sbf[d]])
                            T.op("dve", lambda e: e.scalar_tensor_tensor(out=S32[d][:], in0=pd[:, :256], scalar=esc,
                                                                         in1=tmpS[d][:], op0=ALU.mult, op1=ALU.add),
                                 reads=[bd, b_el, b_tmpS[d]], writes=[b_S32[d]])
                            if half == 0:
                                nel, nb_el, nch = el, b_el, 1 - ch
                            elif nctx is not None:
                                nel, nb_el = nctx[d][10], nctx[d][11]
                                nch = 0 if d == 0 else 1
                            else:
                                nel = None
                            if nel is not None:
                                nesc = nel[0][:, nel[1][nch]:nel[1][nch] + 1]
                                T.op("dve", lambda e: e.tensor_scalar(out=tmpS[d][:], in0=S32[d][:], scalar1=nesc, scalar2=None,
                                                                      op0=ALU.mult),
                                     reads=[b_S32[d], nb_el], writes=[b_tmpS[d]])
                    for d in range(2):
                        i, c0 = ctx[d][0], ctx[d][1]
                        po, bo = pos[d]
                        o3 = oacc[:, :, c0:c0 + 128]
                        p_intra = po[:, 0:256].rearrange("p (v t) -> p v t", v=2)
                        p_inter = po[:, 256:512].rearrange("p (v t) -> p v t", v=2)
                        if not written[i]:
                            T.op("act", lambda e: e.activation(out=o3, in_=p_intra, func=AF.Copy), reads=[bo], writes=[b_o[i]])
                        else:
                            T.op("dve", lambda e: e.tensor_tensor(out=o3, in0=p_intra, in1=o3, op=ALU.add),
                                 reads=[bo, b_o[i]], writes=[b_o[i]])
                        T.op("dve", lambda e: e.tensor_tensor(out=o3, in0=p_inter, in1=o3, op=ALU.add),
                             reads=[bo, b_o[i]], writes=[b_o[i]])
                        written[i] = True

            nxt = phase12(0)
            for step in range(32):
                cur = nxt
                nxt = phase12(step + 1) if step + 1 < 32 else None
                phase34(step, cur[0], cur[1], nxt[0] if nxt is not None else None)
            if DBG.get("dump") is not None and hh == 0:
                do, dS = DBG["dump"]
                for vt in range(2):
                    T.dma("sp", do[vt * 128:(vt + 1) * 128, :], oacc[:, vt, :], reads=b_o)
                for d in range(2):
                    T.dma("sp", dS[d], S32[d][:], reads=[b_S32[d]])
            row_toks = []
            for c in range(NCH):
                t0 = c * 512
                pt, pb = PS.next()
                for vt in range(2):
                    sq, b_sq = sq2.next()
                    T.op("act", lambda e: e.activation(out=sq[:, :], in_=oacc[:, vt, t0:t0 + 512], func=AF.Square),
                         reads=b_o[4 * c:4 * c + 4], writes=[b_sq])
                    T.op("pe", lambda e: e.matmul(pt[:, :], ones[:], sq[:, :], start=(vt == 0), stop=(vt == 1)),
                         reads=[b_ones, b_sq], writes=[pb])
                T.op("act", lambda e: e.activation(out=rs2[:, :], in_=pt[:, :], func=AF.Sqrt, scale=1.0 / 256, bias=EPS),
                     reads=[pb], writes=[b_rs2])
                T.op("dve", lambda e: e.reciprocal(out=rs2[:, :], in_=rs2[:, :]), reads=[b_rs2], writes=[b_rs2])
                for vt in range(2):
                    tm, b_tm = tm2.next()
                    T.op("dve", lambda e: e.scalar_tensor_tensor(out=tm[:, :], in0=oacc[:, vt, t0:t0 + 512],
                                                                 scalar=glags[:, vt:vt + 1], in1=rs2[:, :],
                                                                 op0=ALU.mult, op1=ALU.mult),
                         reads=b_o[4 * c:4 * c + 4] + [b_glag, b_rs2], writes=[b_tm])
                    s, b_s = stg.next()
                    T.op("dve", lambda e: e.tensor_tensor(out=s[:, :], in0=tm[:, :], in1=sg_[:, vt, t0:t0 + 512], op=ALU.mult),
                         reads=[b_tm, b_sg], writes=[b_s])
                    r = 512 + hh * 256 + vt * 128
                    row_toks.append(T.dma("sp", mixT[r:r + 128, t0:t0 + 512], s[:, :], reads=[b_s]))
            if on_rows is not None:
                on_rows(2 + hh, row_toks)


def build_even(debug=False):
    nc = bass.Bass("TRN2", target_bir_lowering=False)
    dt = lambda name, shape, dtype, kind="ExternalInput": nc.dram_tensor(name, shape, dtype, kind=kind).ap()
    hT = dt("hT", [D, SEQ], F32)
    g1 = dt("g1", [128, KT], F32)
    w_e = dt("w_e", [D, NW_E], F32)
    gua_f = dt("gua_f", [17, 256], F32)
    gua_b = dt("gua_b", [17, 256], F32)
    glag = dt("glag", [128, 2], F32)
    cs = dt("cs", [256, 512], BF16)
    cosm = dt("cosm", [SEQ, SEQ], BF16)
    nsinm = dt("nsinm", [SEQ, SEQ], BF16)
    mf = dt("mf", [128, 128], BF16)
    mb = dt("mb", [128, 128], BF16)
    mixT = dt("mixT", [1024, SEQ], BF16, "ExternalOutput")
    kind = "ExternalOutput" if debug else "Internal"
    scr = (dt("aT_s", [512, SEQ], BF16, kind), dt("qT_s", [256, SEQ], BF16, kind), dt("kT_s", [256, SEQ], BF16, kind),
           dt("k_s", [SEQ, 256], BF16, kind), dt("v_s", [SEQ, 512], BF16, kind), dt("sgT_s", [512, SEQ], BF16, kind),
           dt("la_s", [2, SEQ, 256], BF16, kind))
    if debug:
        DBG["dump"] = (dt("dbg_o", [256, SEQ], F32, "ExternalOutput"), dt("dbg_S", [2, 128, 256], F32, "ExternalOutput"))
    with contextlib.ExitStack() as stack:
        T = Trk(nc, stack)
        ps = [stack.enter_context(nc.psum_tensor("ps%d" % i, [128, 512], F32)) for i in range(8)]
        hv = hT.rearrange("(k p) t -> p k t", p=128)
        hload = lambda T, t, c, b: T.dma("sp", t[:], hv[:, :, c * 512:(c + 1) * 512], writes=[b])
        emit_even(nc, T, stack, ps, hload, g1, w_e, gua_f, gua_b, glag, cs, cosm, nsinm, mf, mb, mixT, scr)
        T.finish()
    return nc


def even_consts():
    c = np.arange(256)
    ang = 2 * np.pi * ((c[:, None] * c[None, :]) % 256) / 256.0
    cs = np.concatenate([np.cos(ang), np.sin(ang)], axis=1) / 16.0
    s = np.arange(SEQ, dtype=np.int64)
    ang = 2 * np.pi * ((s[:, None] * s[None, :]) % SEQ) / float(SEQ)
    cosm = (np.cos(ang) / 64.0).astype(NPBF)
    nsinm = (-np.sin(ang) / 64.0).astype(NPBF)
    p = np.arange(128)
    same = (p[:, None] // 64) == (p[None, :] // 64)
    mf = (same & (p[:, None] <= p[None, :])).astype(np.float32).astype(NPBF)
    mb = (same & (p[:, None] >= p[None, :])).astype(np.float32).astype(NPBF)
    return dict(cs=cs.astype(NPBF), cosm=cosm, nsinm=nsinm, mf=mf, mb=mb)


def even_weights(j, w_in, gu_f, gb_f, gu_b, gb_b, gla_g):
    cols = np.r_[512 * j:512 * j + 512, 1024 + 256 * j:1024 + 256 * j + 256, 1536 + 256 * j:1536 + 256 * j + 256,
                 2048 + 512 * j:2048 + 512 * j + 512, 3072 + 512 * j:3072 + 512 * j + 512, 4096:4128]
    hc = slice(256 * j, 256 * j + 256)
    return dict(w_e=np.ascontiguousarray(w_in[:, cols]),
                gua_f=np.ascontiguousarray(np.concatenate([gu_f[:, hc], gb_f[None, hc]], 0)),
                gua_b=np.ascontiguousarray(np.concatenate([gu_b[:, hc], gb_b[None, hc]], 0)),
                glag=lay128(gla_g))


NBU = 3072
NKT = 20


def emit_odd(nc, T, stack, ps, hload, g1, w_o, qg, kg, rb, onehot, cmult, mixT, scr, on_rows=None):
    sb = lambda name, shape, dt: stack.enter_context(nc.sbuf_tensor(uniq(name), shape, dt))
    PS = Rot(ps)
    xn_s, qT_s, kT_s, v_s, u_s = scr
    NCH = SEQ // 512
    ones = sb("ones", [128, 128], F32); b_ones = Buf()
    T.op("dve", lambda e: e.memset(ones[:], 1.0), writes=[b_ones])
    onesb = sb("onesb", [128, 128], BF16); b_onesb = Buf()
    T.op("dve", lambda e: e.memset(onesb[:], 1.0), writes=[b_onesb])
    g1s = sb("g1s", [128, KT], F32); b_g1 = Buf()
    T.dma("sp", g1s[:], g1, writes=[b_g1])
    qgs = sb("qgs", [128, 1], F32); b_qg = Buf()
    T.dma("sp", qgs[:], qg, writes=[b_qg])
    kgs = sb("kgs", [128, 1], F32); b_kg = Buf()
    T.dma("sp", kgs[:], kg, writes=[b_kg])

    with contextlib.ExitStack() as st2:
        sb2 = lambda name, shape, dt: st2.enter_context(nc.sbuf_tensor(uniq(name), shape, dt))
        rbs = sb2("rbs", [32, 8], F32); b_rb = Buf()
        T.dma("sp", rbs[:], rb, writes=[b_rb])
        oh = sb2("oh", [32, NBU], F32); b_oh = Buf()
        T.dma("sp", oh[:], onehot, writes=[b_oh])
        cm = sb2("cm", [8, NBU], F32); b_cm = Buf()
        T.dma("sp", cm[:], cmult, writes=[b_cm])
        ue = sb2("ue", [8, NBU], F32); b_ue = Buf()
        ub = sb2("ub", [8, NBU], BF16); b_ub = Buf()
        for c in range(NBU // 512):
            pt, pb = PS.next()
            T.op("pe", lambda e: e.matmul(pt[0:8, :], rbs[:, :], oh[:, c * 512:(c + 1) * 512], start=True, stop=True),
                 reads=[b_rb, b_oh], writes=[pb])
            T.op("act", lambda e: e.activation(out=ue[:, c * 512:(c + 1) * 512], in_=pt[0:8, :], func=AF.Exp),
                 reads=[pb], writes=[b_ue])
        T.op("dve", lambda e: e.tensor_tensor(out=ub[:, :], in0=ue[:, :], in1=cm[:, :], op=ALU.mult),
             reads=[b_ue, b_cm], writes=[b_ub])
        b_us = Buf()
        T.dma("sp", u_s, ub[:, :], reads=[b_ub], writes=[b_us])

        hcs = [sb2("hc%d" % i, [128, KT, 512], F32) for i in range(2)]; b_hcs = [Buf(), Buf()]
        xcs = [sb2("xc%d" % i, [128, KT, 512], BF16) for i in range(2)]; b_xcs = [Buf(), Buf()]
        sqr = Rot([sb2("sq%d" % i, [128, 512], F32) for i in range(2)])
        rstd_t = sb2("rstd", [128, 512], F32); b_rstd = Buf()
        xv = xn_s.rearrange("(k p) t -> p k t", p=128)
        b_xn = [Buf() for _ in range(NCH)]
        hload(T, hcs[0], 0, b_hcs[0])
        for c in range(NCH):
            t0 = c * 512
            if c + 1 < NCH:
                hload(T, hcs[(c + 1) % 2], c + 1, b_hcs[(c + 1) % 2])
            emit_norm_chunk(nc, T, hcs[c % 2], b_hcs[c % 2], g1s, b_g1, ones, b_ones, PS, sqr, rstd_t, b_rstd,
                            xcs[c % 2], b_xcs[c % 2], 512)
            T.dma("sp", xv[:, :, t0:t0 + 512], xcs[c % 2][:], reads=[b_xcs[c % 2]], writes=[b_xn[c]])
    T.barrier()

    with contextlib.ExitStack() as st2:
        sb2 = lambda name, shape, dt: st2.enter_context(nc.sbuf_tensor(uniq(name), shape, dt))
        wp = [sb2("wp%d" % i, [128, KT, 1024], BF16) for i in range(2)]; b_wp = [Buf(), Buf()]
        xcs = [sb2("xc%d" % i, [128, KT, 512], BF16) for i in range(2)]; b_xcs = [Buf(), Buf()]
        sqr = Rot([sb2("sq%d" % i, [128, 512], F32) for i in range(3)])
        rsr = Rot([sb2("rs%d" % i, [128, 512], F32) for i in range(3)])
        stg = Rot([sb2("stg%d" % i, [128, 512], BF16) for i in range(4)])
        wv = w_o.rearrange("(k p) n -> p k n", p=128)
        xv = xn_s.rearrange("(k p) t -> p k t", p=128)
        T.dma("pool", wp[0][:], wv[:, :, 0:1024], writes=[b_wp[0]])
        items = [(part, c) for part in range(3) for c in range(NCH)]
        T.dma("sp", xcs[0][:], xv[:, :, 0:512], reads=[b_xn[0]], writes=[b_xcs[0]])
        for idx, (part, c) in enumerate(items):
            t0 = c * 512
            if c == 0 and part + 1 < 3:
                T.dma("pool", wp[(part + 1) % 2][:], wv[:, :, (part + 1) * 1024:(part + 2) * 1024],
                      writes=[b_wp[(part + 1) % 2]])
            if idx + 1 < len(items):
                c2 = items[idx + 1][1]
                T.dma("sp", xcs[(idx + 1) % 2][:], xv[:, :, c2 * 512:(c2 + 1) * 512], reads=[b_xn[c2]],
                      writes=[b_xcs[(idx + 1) % 2]])
            xc, b_xc = xcs[idx % 2], b_xcs[idx % 2]
            w_, b_w = wp[part % 2], b_wp[part % 2]
            if part < 2:
                gs, b_gs = (qgs, b_qg) if part == 0 else (kgs, b_kg)
                dst = qT_s if part == 0 else kT_s
                sc_, bi_ = (1.0, 128 * EPS) if part == 0 else (1.0 / 128, EPS)
                pend_n = None
                for hd in range(8):
                    pt, pb = PS.next()
                    mm_group(T, pt[:, :], [(w_[:, k, hd * 128:(hd + 1) * 128], xc[:, k, :]) for k in range(KT)], pb,
                             reads=[b_w, b_xc])
                    sq, b_sq = sqr.next()
                    T.op("act", lambda e: e.activation(out=sq[:, :], in_=pt[:, :], func=AF.Square), reads=[pb], writes=[b_sq])
                    if pend_n is not None:
                        pend_n()

                    def fin(hd=hd, pt=pt, pb=pb, sq=sq, b_sq=b_sq):
                        p2, pb2 = PS.next()
                        T.op("pe", lambda e: e.matmul(p2[:, :], ones[:], sq[:, :], start=True, stop=True),
                             reads=[b_ones, b_sq], writes=[pb2])
                        rs, b_rs = rsr.next()
                        T.op("act", lambda e: e.activation(out=rs[:, :], in_=p2[:, :], func=AF.Sqrt, scale=sc_, bias=bi_),
                             reads=[pb2], writes=[b_rs])
                        T.op("dve", lambda e: e.reciprocal(out=rs[:, :], in_=rs[:, :]), reads=[b_rs], writes=[b_rs])
                        s, b_s = stg.next()
                        T.op("dve", lambda e: e.scalar_tensor_tensor(out=s[:, :], in0=pt[:, :], scalar=gs[:, 0:1], in1=rs[:, :],
                                                                     op0=ALU.mult, op1=ALU.mult),
                             reads=[pb, b_gs, b_rs], writes=[b_s])
                        T.dma("sp", dst[hd * 128:(hd + 1) * 128, t0:t0 + 512], s[:, :], reads=[b_s])
                    pend_n = fin
                pend_n()
            else:
                for ts in range(4):
                    for half in range(2):
                        pt, pb = PS.next()
                        mm_group(T, pt[:, :], [(xc[:, k, ts * 128:(ts + 1) * 128], w_[:, k, half * 512:(half + 1) * 512])
                                               for k in range(KT)], pb, reads=[b_w, b_xc])
                        s, b_s = stg.next()
                        if half == 0:
                            T.op("act", lambda e: e.activation(out=s[:, :], in_=pt[:, :], func=AF.Copy), reads=[pb], writes=[b_s])
                        else:
                            T.op("dve", lambda e: e.tensor_copy(out=s[:, :], in_=pt[:, :]), reads=[pb], writes=[b_s])
                        T.dma("sp", v_s[t0 + ts * 128:t0 + (ts + 1) * 128, half * 512:(half + 1) * 512], s[:, :], reads=[b_s])
    T.barrier()

    with contextlib.ExitStack() as st2:
        sb2 = lambda name, shape, dt: st2.enter_context(nc.sbuf_tensor(uniq(name), shape, dt))
        qTs = [sb2("qT%d" % i, [128, SEQ], BF16) for i in range(2)]
        kTs = [sb2("kT%d" % i, [128, SEQ], BF16) for i in range(2)]
        vts = [sb2("vt%d" % i, [128, 32, 128], BF16) for i in range(2)]
        Es = [sb2("E%d" % i, [128, NKT, 512], BF16) for i in range(2)]
        b_q = [Buf(), Buf()]; b_k = [Buf(), Buf()]; b_v = [Buf(), Buf()]; b_E = [Buf(), Buf()]
        pex = Rot([sb2("pex%d" % i, [128, 512], BF16) for i in range(6)])
        pmr = Rot([sb2("pm%d" % i, [128, 512], BF16) for i in range(20)])
        rdr = Rot([sb2("rd%d" % i, [128, 512], F32) for i in range(2)])
        stg = Rot([sb2("stg%d" % i, [128, 512], BF16) for i in range(2)])
        vv_ = v_s.rearrange("(i p) c -> p i c", p=128)
        SPS = Rot(ps[0:4]); OPS = Rot(ps[4:6]); DPS = Rot(ps[6:8])

        def loadh(h):
            s = h % 2
            T.dma("sp", qTs[s][:], qT_s[h * 128:(h + 1) * 128, :], writes=[b_q[s]])
            T.dma("sp", kTs[s][:], kT_s[h * 128:(h + 1) * 128, :], writes=[b_k[s]])
            T.dma("sp", vts[s][:], vv_[:, :, h * 128:(h + 1) * 128], writes=[b_v[s]])
            T.dma("sp", Es[s][:], bass.AP(u_s.tensor, u_s.offset + h * NBU, [[1, 128], [128, NKT], [1, 512]]),
                  reads=[b_us], writes=[b_E[s]])

        def rev(ap):
            return bass.AP(ap.tensor, ap.offset + 511, [list(ap.ap[0]), [-1, 512]])

        loadh(0)
        loadh(1)
        iters = []
        for h in range(8):
            for qt in range(NCH):
                tiles = [a for a in range(NKT) if 0 <= qt * 512 - 1024 + 128 * a < SEQ]
                for n_, a in enumerate(tiles):
                    iters.append((h, qt, a, n_ == 0, n_ == len(tiles) - 1))
        live = {}
        acc = {}
        row_toks = []

        def emit_qk(i):
            h, qt, a, first, last = iters[i]
            s = h % 2
            t0 = qt * 512
            s0 = t0 - 1024 + 128 * a
            pt, pb = SPS.next()
            T.op("pe", lambda e: e.matmul(pt[:, :], kTs[s][:, s0:s0 + 128], rev(qTs[s][:, t0:t0 + 512]),
                                          start=True, stop=True), reads=[b_k[s], b_q[s]], writes=[pb])
            px, b_px = pex.next()
            T.op("act", lambda e: e.activation(out=px[:, :], in_=pt[:, :], func=AF.Exp), reads=[pb], writes=[b_px])
            pm, b_pm = pmr.next()
            T.op("dve", lambda e: e.tensor_tensor(out=pm[:, :], in0=px[:, :], in1=Es[s][:, a, :], op=ALU.mult),
                 reads=[b_px, b_E[s]], writes=[b_pm])
            live[i] = (pm, b_pm)

        def emit_pv(i):
            h, qt, a, first, last = iters[i]
            s = h % 2
            t0 = qt * 512
            s0 = t0 - 1024 + 128 * a
            if first:
                acc[(h, qt)] = (OPS.next(), DPS.next())
                if qt == 0 and 1 <= h < 7:
                    loadh(h + 1)
            (po, bo), (pd, bd) = acc[(h, qt)]
            pm, b_pm = live.pop(i)
            T.op("pe", lambda e: e.matmul(po[:, :], vts[s][:, s0 // 128, :], pm[:, :], start=first, stop=last),
                 reads=[b_v[s], b_pm], writes=[bo], inc=last)
            T.op("pe", lambda e: e.matmul(pd[:, :], onesb[:, :], pm[:, :], start=first, stop=last),
                 reads=[b_onesb, b_pm], writes=[bd], inc=True)
            if last:
                rd, b_rd = rdr.next()
                T.op("dve", lambda e: e.reciprocal(out=rd[:, :], in_=pd[:, :]), reads=[bd], writes=[b_rd])
                st_, b_st = stg.next()
                T.op("dve", lambda e: e.tensor_tensor(out=st_[:, :], in0=rev(po[:, :]), in1=rev(rd[:, :]), op=ALU.mult),
                     reads=[bo, b_rd], writes=[b_st])
                row_toks.append(T.dma("sp", mixT[h * 128:(h + 1) * 128, t0:t0 + 512], st_[:, :], reads=[b_st]))
                del acc[(h, qt)]
                if qt == NCH - 1 and h % 2 == 1 and on_rows is not None:
                    on_rows(h // 2, list(row_toks))
                    del row_toks[:]

        LOOK = 16
        for i in range(len(iters) + LOOK):
            if i < len(iters):
                emit_qk(i)
            if i - LOOK >= 0:
                emit_pv(i - LOOK)


def build_odd(debug=False):
    nc = bass.Bass("TRN2", target_bir_lowering=False)
    dt = lambda name, shape, dtype, kind="ExternalInput": nc.dram_tensor(name, shape, dtype, kind=kind).ap()
    hT = dt("hT", [D, SEQ], F32)
    g1 = dt("g1", [128, KT], F32)
    w_o = dt("w_o", [D, 3072], F32)
    qg = dt("qg", [128, 1], F32)
    kg = dt("kg", [128, 1], F32)
    rb = dt("rb", [32, 8], F32)
    onehot = dt("onehot", [32, NBU], F32)
    cmult = dt("cmult", [8, NBU], F32)
    mixT = dt("mixT", [1024, SEQ], BF16, "ExternalOutput")
    kind = "ExternalOutput" if debug else "Internal"
    scr = (dt("xn_s", [D, SEQ], BF16, kind), dt("qT_s", [1024, SEQ], BF16, kind), dt("kT_s", [1024, SEQ], BF16, kind),
           dt("v_s", [SEQ, 1024], BF16, kind), dt("u_s", [8, NBU], BF16, kind))
    with contextlib.ExitStack() as stack:
        T = Trk(nc, stack)
        ps = [stack.enter_context(nc.psum_tensor("ps%d" % i, [128, 512], F32)) for i in range(8)]
        hv = hT.rearrange("(k p) t -> p k t", p=128)
        hload = lambda T, t, c, b: T.dma("sp", t[:], hv[:, :, c * 512:(c + 1) * 512], writes=[b])
        emit_odd(nc, T, stack, ps, hload, g1, w_o, qg, kg, rb, onehot, cmult, mixT, scr)
        T.finish()
    return nc


def t5_bucket_np(rel):
    nb = 16
    ret = (rel > 0).astype(np.int32) * nb
    n = np.abs(rel)
    max_exact = nb // 2
    large = max_exact + (np.log(np.maximum(n, 1) / max_exact) / np.log(1024 / max_exact) * (nb - max_exact)).astype(np.int32)
    large = np.minimum(large, nb - 1)
    return (ret + np.where(n < max_exact, n, large)).astype(np.int32)


def odd_consts():
    m = np.arange(NBU)
    delta = m - 1535
    bkt = t5_bucket_np(delta)
    onehot = (bkt[None, :] == np.arange(32)[:, None]).astype(np.float32)
    mult = np.zeros(NBU, np.float32)
    for (w, d) in ((128, 1), (512, 4), (2048, 16)):
        mult += ((delta % d == 0) & (np.abs(delta) <= w // 2)).astype(np.float32)
    return dict(onehot=onehot, cmult=np.ascontiguousarray(np.broadcast_to(mult, (8, NBU))))


def odd_weights(j, w_qkv, q_g, k_g, rel_bias):
    cols = np.r_[1024 * j:1024 * j + 1024, 2048 + 1024 * j:2048 + 1024 * j + 1024, 4096 + 1024 * j:4096 + 1024 * j + 1024]
    return dict(w_o=np.ascontiguousarray(w_qkv[:, cols]), qg=np.ascontiguousarray(q_g.reshape(128, 1)),
                kg=np.ascontiguousarray(k_g.reshape(128, 1)), rb=np.ascontiguousarray(rel_bias[:, 8 * j:8 * j + 8]))


_PROGS = {}


def _prog(name):
    if name not in _PROGS:
        _PROGS[name] = {"even": build_even, "odd": build_odd, "post": build_post}[name]()
    return _PROGS[name]


def _run(name, in_maps):
    res = run_bass_kernel_spmd(_prog(name), in_maps, core_ids=list(range(NCORES)))
    return res.results


def kernel_unfused(x, mix_norm_g, w_in_even, gate_up_fwd, gate_bias_fwd, gate_up_bwd, gate_bias_bwd, gla_norm_g, w_out_even,
           w_qkv_odd, q_norm_g, k_norm_g, rel_bias, w_out_odd, ffn_norm_g, w_gate, w_up, conv_w, conv_b, w_down):
    f32 = lambda a: np.ascontiguousarray(np.asarray(a, dtype=np.float32))
    x = f32(x)
    hT = [np.ascontiguousarray(x[b].T) for b in range(BATCH)]
    ec = even_consts()
    oc = odd_consts()
    for layer in range(4):
        i = layer // 2
        g1 = lay128(f32(mix_norm_g[layer]))
        ims = []
        for b in range(BATCH):
            for j in range(2):
                im = dict(hT=hT[b], g1=g1)
                if layer % 2 == 0:
                    im.update(even_weights(j, f32(w_in_even[i]), f32(gate_up_fwd[i]), f32(gate_bias_fwd[i]),
                                           f32(gate_up_bwd[i]), f32(gate_bias_bwd[i]), f32(gla_norm_g[i])))
                    im.update(ec)
                else:
                    im.update(odd_weights(j, f32(w_qkv_odd[i]), f32(q_norm_g[i]), f32(k_norm_g[i]), f32(rel_bias)))
                    im.update(oc)
                ims.append(im)
        res = _run("even" if layer % 2 == 0 else "odd", ims)
        mT = []
        for b in range(BATCH):
            m0 = np.asarray(res[2 * b]["mixT"]); m1 = np.asarray(res[2 * b + 1]["mixT"])
            if layer % 2 == 0:
                mT.append(np.concatenate([m0[:512], m1[:512], m0[512:], m1[512:]], axis=0))
            else:
                mT.append(np.concatenate([m0, m1], axis=0))
        w_out = f32(w_out_even[i]) if layer % 2 == 0 else f32(w_out_odd[i])
        cw = np.ascontiguousarray(f32(conv_w[layer]).T.reshape(FT, 128, 3).transpose(1, 0, 2).reshape(128, FT * 3))
        common = dict(w_out=w_out, g2=lay128(f32(ffn_norm_g[layer])), w_gate=f32(w_gate[layer]), w_up=f32(w_up[layer]),
                      cw=cw, cb=lay128(f32(conv_b[layer])), w_down=f32(w_down[layer]))
        ims = []
        for b in range(BATCH):
            for half in range(2):
                t0 = half * TOK
                he = np.zeros((D, EXT), np.float32); me = np.zeros((D, EXT), mT[b].dtype)
                lo, hi = max(t0 - 1, 0), min(t0 + TOK + 1, SEQ)
                he[:, lo - (t0 - 1):hi - (t0 - 1)] = hT[b][:, lo:hi]
                me[:, lo - (t0 - 1):hi - (t0 - 1)] = mT[b][:, lo:hi]
                im = dict(hT_ext=he, mT_ext=me)
                im.update(common)
                ims.append(im)
        res = _run("post", ims)
        hT = [np.ascontiguousarray(np.concatenate([np.asarray(res[2 * b]["hT_out"]), np.asarray(res[2 * b + 1]["hT_out"])], axis=1))
              for b in range(BATCH)]
    return np.ascontiguousarray(np.stack([h.T for h in hT], axis=0)).astype(np.float32)


PAIRS = [[0, 1], [2, 3], [4, 5], [6, 7]]


def build_fused():
    nc = bass.Bass("TRN2", target_bir_lowering=False)
    dt = lambda name, shape, dtype, kind="ExternalInput": nc.dram_tensor(name, shape, dtype, kind=kind).ap()
    xT = dt("xT", [D, TOK], F32)
    sel = dt("sel", [128, 2], F32)
    cs = dt("cs", [256, 512], BF16); cosm = dt("cosm", [SEQ, SEQ], BF16); nsinm = dt("nsinm", [SEQ, SEQ], BF16)
    mf = dt("mf", [128, 128], BF16); mb = dt("mb", [128, 128], BF16)
    onehot = dt("onehot", [32, NBU], F32); cmult = dt("cmult", [8, NBU], F32); rb = dt("rb", [32, 8], F32)
    L = []
    for l in range(4):
        d_ = dict(g1=dt("g1_%d" % l, [128, KT], F32), w_out=dt("w_out_%d" % l, [D, D], F32), g2=dt("g2_%d" % l, [128, KT], F32),
                  w_gate=dt("w_gate_%d" % l, [D, FF], F32), w_up=dt("w_up_%d" % l, [D, FF], F32),
                  cw=dt("cw_%d" % l, [128, FT * 3], F32), cb=dt("cb_%d" % l, [128, FT], F32),
                  w_down=dt("w_down_%d" % l, [FF, D], F32))
        if l % 2 == 0:
            d_.update(w_e=dt("w_e_%d" % l, [D, NW_E], F32), gua_f=dt("gua_f_%d" % l, [17, 256], F32),
                      gua_b=dt("gua_b_%d" % l, [17, 256], F32), glag=dt("glag_%d" % l, [128, 2], F32))
        else:
            d_.update(w_o=dt("w_o_%d" % l, [D, 3072], F32), qg=dt("qg_%d" % l, [128, 1], F32), kg=dt("kg_%d" % l, [128, 1], F32))
        L.append(d_)
    outT = dt("outT", [D, TOK], F32, "ExternalOutput")
    I = "Internal"
    hown = dt("hown", [D, TOK], F32, I)
    hfull = dt("hfull", [8, 2, 2, 128, TOK], F32, I)
    mcore = dt("mcore", [1024, SEQ], BF16, I)
    mfull = dt("mfull", [4, 2, 256, SEQ], BF16, I)
    hmid_scr = dt("hmid_scr", [FF, TOK], BF16, I)
    hm_scr = dt("hm_scr", [D, EXT], F32, I)
    scr_e = (dt("aT_s", [512, SEQ], BF16, I), dt("qT_s", [256, SEQ], BF16, I), dt("kT_s", [256, SEQ], BF16, I),
             dt("k_s", [SEQ, 256], BF16, I), dt("v_s", [SEQ, 512], BF16, I), dt("sgT_s", [512, SEQ], BF16, I),
             dt("la_s", [2, SEQ, 256], BF16, I))
    scr_o = (dt("xn_s", [D, SEQ], BF16, I), dt("qTo_s", [1024, SEQ], BF16, I), dt("kTo_s", [1024, SEQ], BF16, I),
             dt("vo_s", [SEQ, 1024], BF16, I), dt("u_s", [8, NBU], BF16, I))

    with contextlib.ExitStack() as stack:
        T = Trk(nc, stack)
        ps = [stack.enter_context(nc.psum_tensor("ps%d" % i, [128, 512], F32)) for i in range(8)]
        sels = stack.enter_context(nc.sbuf_tensor("sels", [128, 2], F32)); b_sel = Buf()
        T.dma("sp", sels[:], sel, writes=[b_sel])
        T.dma("sp", hown, xT)

        def hload(T, t, c, b):
            r, tc = c // 4, (c % 4) * 512
            toks = []
            for q in range(8):
                toks.append(T.dma("sp", t[:, 2 * q:2 * q + 2, :], hfull[q, r].rearrange("k p t -> p k t")[:, :, tc:tc + 512],
                                  writes=([b] if q == 0 else [])))
            b.multi = toks

        def make_loaders():
            st = {}

            def get(sb2, key, shape, dtype):
                if key not in st:
                    t = sb2(key, shape, dtype); bb = Buf()
                    T.op("dve", lambda e: e.memset(t[:], 0.0), writes=[bb])
                    st[key] = (t, bb)
                return st[key]

            def load_m(T, sb2, mT, k0, b):
                kg = k0 // 4
                r_src, q0 = kg // 2, 2 * (kg % 2)
                A, bA = get(sb2, "mA", [128, 2, EXT], BF16)
                B, bB = get(sb2, "mB", [128, 2, EXT], BF16)
                for qq in range(2):
                    src = mfull[q0 + qq, r_src].rearrange("(k p) t -> p k t", p=128)
                    T.dma("sp", A[:, :, 1:EXT], src[:, :, 0:TOK + 1], writes=[bA])
                    T.dma("sp", B[:, :, 0:EXT - 1], src[:, :, TOK - 1:SEQ], writes=[bB])
                    T.op("dve", lambda e: e.tensor_scalar(out=A[:], in0=A[:], scalar1=sels[:, 0:1], scalar2=None, op0=ALU.mult),
                         reads=[b_sel], writes=[bA])
                    T.op("dve", lambda e: e.scalar_tensor_tensor(out=mT[:, k0 + 2 * qq:k0 + 2 * qq + 2, :], in0=B[:],
                                                                 scalar=sels[:, 1:2], in1=A[:], op0=ALU.mult, op1=ALU.add),
                         reads=[bA, bB, b_sel], writes=[b])

            def load_h(T, sb2, hb, n, b):
                q, k2 = n // 2, n % 2
                T.dma("sp", hb[:, 1:TOK + 1], hown[n * 128:(n + 1) * 128, :], writes=[b])
                T.dma("sp", hb[:, 0:1], hfull[q, 0, k2][:, TOK - 1:TOK], writes=[b], slow=True)
                T.dma("sp", hb[:, EXT - 1:EXT], hfull[q, 1, k2][:, 0:1], writes=[b], slow=True)
                T.op("dve", lambda e: e.tensor_scalar(out=hb[:, 0:1], in0=hb[:, 0:1], scalar1=sels[:, 1:2], scalar2=None,
                                                      op0=ALU.mult), reads=[b_sel], writes=[b])
                T.op("dve", lambda e: e.tensor_scalar(out=hb[:, EXT - 1:EXT], in0=hb[:, EXT - 1:EXT], scalar1=sels[:, 0:1],
                                                      scalar2=None, op0=ALU.mult), reads=[b_sel], writes=[b])
            return load_m, load_h

        def gather_h(q, extra=()):
            T.collective(hown[256 * q:256 * (q + 1), :].opt(), hfull[q].rearrange("r k p t -> (r k p) t").opt(), PAIRS,
                         extra=extra)

        def gather_m(q, extra=()):
            T.collective(mcore[256 * q:256 * (q + 1), :].opt(), mfull[q].rearrange("r p t -> (r p) t").opt(), PAIRS,
                         extra=extra)

        T.barrier()
        for q in range(8):
            gather_h(q)
        for l in range(4):
            W = L[l]
            T.barrier(); T.new_epoch()
            with contextlib.ExitStack() as st_l:
                if l % 2 == 0:
                    emit_even(nc, T, st_l, ps, hload, W["g1"], W["w_e"], W["gua_f"], W["gua_b"], W["glag"],
                              cs, cosm, nsinm, mf, mb, mcore, scr_e, on_rows=gather_m)
                else:
                    emit_odd(nc, T, st_l, ps, hload, W["g1"], W["w_o"], W["qg"], W["kg"], rb, onehot, cmult, mcore, scr_o,
                             on_rows=gather_m)
            T.barrier(); T.new_epoch()
            store_toks = {}

            def on_store(n, c, nchunk, tk):
                store_toks.setdefault(n // 2, []).append(tk)
                if c == nchunk - 1 and n % 2 == 1:
                    gather_h(n // 2, store_toks[n // 2])

            with contextlib.ExitStack() as st_l:
                load_m, load_h = make_loaders()
                emit_post(nc, T, st_l, ps, load_m, load_h, W["w_out"], W["g2"], W["w_gate"], W["w_up"], W["cw"], W["cb"],
                          W["w_down"], outT if l == 3 else hown, hmid_scr, hm_scr, on_store=(on_store if l < 3 else None))
        T.finish()
    return nc, T


def wout_perm(layer):
    perm = np.zeros(D, np.int64)
    for r in range(2):
        for row in range(1024):
            if layer % 2 == 0:
                ch = 512 * r + row if row < 512 else 1024 + 512 * r + (row - 512)
            else:
                ch = 1024 * r + row
            perm[r * 1024 + row] = ch
    return perm


def kernel(x, mix_norm_g, w_in_even, gate_up_fwd, gate_bias_fwd, gate_up_bwd, gate_bias_bwd, gla_norm_g, w_out_even,
                 w_qkv_odd, q_norm_g, k_norm_g, rel_bias, w_out_odd, ffn_norm_g, w_gate, w_up, conv_w, conv_b, w_down):
    f32 = lambda a: np.ascontiguousarray(np.asarray(a, dtype=np.float32))
    x = f32(x)
    if "fused" not in _PROGS:
        _PROGS["fused"] = build_fused()[0]
    common = {}
    common.update(even_consts()); common.update(odd_consts())
    for l in range(4):
        i = l // 2
        w_out = f32(w_out_even[i]) if l % 2 == 0 else f32(w_out_odd[i])
        common["w_out_%d" % l] = np.ascontiguousarray(w_out[wout_perm(l), :])
        common["g1_%d" % l] = lay128(f32(mix_norm_g[l])); common["g2_%d" % l] = lay128(f32(ffn_norm_g[l]))
        common["w_gate_%d" % l] = f32(w_gate[l]); common["w_up_%d" % l] = f32(w_up[l]); common["w_down_%d" % l] = f32(w_down[l])
        common["cw_%d" % l] = np.ascontiguousarray(f32(conv_w[l]).T.reshape(FT, 128, 3).transpose(1, 0, 2).reshape(128, FT * 3))
        common["cb_%d" % l] = lay128(f32(conv_b[l]))
    ims = []
    for b in range(BATCH):
        for r in range(2):
            im = dict(common)
            im["xT"] = np.ascontiguousarray(x[b, r * TOK:(r + 1) * TOK, :].T)
            s = np.zeros((128, 2), np.float32); s[:, r] = 1.0
            im["sel"] = s
            im["rb"] = np.ascontiguousarray(f32(rel_bias)[:, 8 * r:8 * r + 8])
            for l in range(4):
                i = l // 2
                if l % 2 == 0:
                    ew = even_weights(r, f32(w_in_even[i]), f32(gate_up_fwd[i]), f32(gate_bias_fwd[i]), f32(gate_up_bwd[i]),
                                      f32(gate_bias_bwd[i]), f32(gla_norm_g[i]))
                    for k_, v_ in ew.items():
                        im["%s_%d" % (k_, l)] = v_
                else:
                    ow = odd_weights(r, f32(w_qkv_odd[i]), f32(q_norm_g[i]), f32(k_norm_g[i]), f32(rel_bias))
                    for k_ in ("w_o", "qg", "kg"):
                        im["%s_%d" % (k_, l)] = ow[k_]
            ims.append(im)
    res = run_bass_kernel_spmd(_PROGS["fused"], ims, core_ids=list(range(NCORES))).results
    out = np.empty((BATCH, SEQ, D), np.float32)
    for b in range(BATCH):
        for r in range(2):
            out[b, r * TOK:(r + 1) * TOK, :] = np.asarray(res[2 * b + r]["outT"]).T
    return out
```
